# Optimizing a Trainium2 kernel written in Bass

```python
import math
import jax, jax.numpy as jnp
from jax import lax
import numpy as np

D_MODEL = 1024
BATCH = 8
SEQ = 8192
DEPTH = 2
DEC_BATCH = 8
DEC_SEQ = 16
PAST_LEN = 2048

CHUNK = 64
PLE_DIM = 256
MIX_WIDTH = D_MODEL
D_FF = 4 * D_MODEL
ROPE_THETA = 10000.0
NORM_EPS = 1e-6
L2_EPS = 1e-6
Q_BLOCK = 128

A_HEADS = 4
A_DK = MIX_WIDTH // 4 // A_HEADS
A_DV = A_DK
A_QK = A_HEADS * A_DK
CONV_W = 4
A_CONV_CH = 2 * A_QK + A_HEADS * A_DV

B_HEADS = 4
B_DV = MIX_WIDTH // 2 // B_HEADS
B_DQK = B_DV // 2

C_HEADS = 4
C_HD = MIX_WIDTH // 4 // C_HEADS
C_WIDTH = C_HEADS * C_HD
C_W_RANK = 64
C_A_RANK = 64
C_G_RANK = 128
C_PROJ = 3 * C_WIDTH + C_W_RANK + C_A_RANK + C_G_RANK
C_LN_EPS = 64e-5
RWKV_DECAY_OFFSET = 0.5

PROJ_SIZES = (A_CONV_CH, A_HEADS * A_DV, A_HEADS, A_HEADS,
              B_HEADS * 2 * B_DQK, B_HEADS * 2 * B_DQK, B_HEADS * B_DV, C_PROJ)
IN_WIDTH = (A_CONV_CH + A_HEADS * A_DV + 2 * A_HEADS
            + 2 * B_HEADS * 2 * B_DQK + B_HEADS * B_DV + C_PROJ)

kernel_name = 'hybrid_streaming_encoder_step'

F32 = jnp.float32


def rmsnorm(x, g):
    xf = x.astype(F32)
    y = xf * lax.rsqrt(jnp.mean(xf * xf, axis=-1, keepdims=True) + NORM_EPS)
    return (y * g.astype(F32)).astype(x.dtype)


def l2norm(x):
    xf = x.astype(F32)
    return (xf * lax.rsqrt(jnp.sum(xf * xf, axis=-1, keepdims=True) + L2_EPS)).astype(x.dtype)


def rope(x, pos):
    d = x.shape[-1]
    half = d // 2
    inv = ROPE_THETA ** (-2.0 * jnp.arange(half, dtype=F32) / d)
    ang = pos.astype(F32)[:, None] * inv[None, :]
    shape = (1, pos.shape[0]) + (1,) * (x.ndim - 3) + (half,)
    cos = jnp.cos(ang).reshape(shape)
    sin = jnp.sin(ang).reshape(shape)
    xf = x.astype(F32)
    x1, x2 = xf[..., :half], xf[..., half:]
    return jnp.concatenate([x1 * cos - x2 * sin, x2 * cos + x1 * sin], axis=-1).astype(x.dtype)


def causal_conv(x, buf, w):
    L = x.shape[1]
    xp = jnp.concatenate([buf.astype(x.dtype), x], axis=1)
    y = xp[:, 0:L] * w[0]
    for j in range(1, CONV_W):
        y = y + xp[:, j:j + L] * w[j]
    return y, xp[:, L:]


def split_proj(proj):
    idx, acc = [], 0
    for s in PROJ_SIZES[:-1]:
        acc += s
        idx.append(acc)
    return jnp.split(proj, idx, axis=-1)


def gated_delta_chunked(q, k, v, g, beta, s0):
    bn, L, nh, dk = q.shape
    dv = v.shape[-1]
    C = min(CHUNK, L)
    N = L // C

    def blk(t):
        t = t.astype(F32).reshape((bn, N, C, nh) + t.shape[3:])
        return jnp.moveaxis(t, 3, 1)

    q, k, v, g, beta = blk(q), blk(k), blk(v), blk(g), blk(beta)
    q = q * (dk ** -0.5)
    g = jnp.cumsum(g, axis=-1)
    idx = jnp.arange(C)
    lower = idx[:, None] >= idx[None, :]
    strict = idx[:, None] > idx[None, :]
    diff = g[..., :, None] - g[..., None, :]
    decay = jnp.where(lower, jnp.exp(jnp.where(lower, diff, 0.0)), 0.0)
    kk = jnp.einsum('bhnid,bhnjd->bhnij', k, k)
    a_mat = jnp.where(strict, beta[..., :, None] * kk * decay, 0.0)
    eye = jnp.eye(C, dtype=F32)
    rhs = jnp.concatenate([v * beta[..., None], k * (beta * jnp.exp(g))[..., None]], axis=-1)
    sol = lax.linalg.triangular_solve(jnp.broadcast_to(eye, a_mat.shape) + a_mat, rhs,
                                      left_side=True, lower=True)
    u_val, w_k = sol[..., :dv], sol[..., dv:]
    qk = jnp.where(lower, jnp.einsum('bhnid,bhnjd->bhnij', q, k) * decay, 0.0)
    q_dec = q * jnp.exp(g)[..., None]
    k_tail = k * jnp.exp(g[..., -1:] - g)[..., None]
    g_last = jnp.exp(g[..., -1])

    def step(s, xs):
        u_c, w_c, qk_c, qd_c, kt_c, gl_c = xs
        u_new = u_c - jnp.einsum('bhcd,bhde->bhce', w_c, s)
        o = jnp.einsum('bhcd,bhde->bhce', qd_c, s) + jnp.einsum('bhij,bhje->bhie', qk_c, u_new)
        s = s * gl_c[..., None, None] + jnp.einsum('bhcd,bhce->bhde', kt_c, u_new)
        return s, o

    xs = (jnp.moveaxis(u_val, 2, 0), jnp.moveaxis(w_k, 2, 0), jnp.moveaxis(qk, 2, 0),
          jnp.moveaxis(q_dec, 2, 0), jnp.moveaxis(k_tail, 2, 0), jnp.moveaxis(g_last, 2, 0))
    s_fin, o = lax.scan(step, s0.astype(F32), xs)
    o = jnp.transpose(o, (1, 0, 3, 2, 4)).reshape(bn, L, nh, dv)
    return o, s_fin


def rwkv7_scan(r, w_log, k, v, kk, a, s0):
    def step(s, xs):
        r_t, w_t, k_t, v_t, kk_t, a_t = xs
        sa = jnp.einsum('bhvk,bhk->bhv', s, -kk_t)
        s = (s * jnp.exp(w_t)[:, :, None, :] + sa[..., :, None] * (kk_t * a_t)[..., None, :]
             + v_t[..., :, None] * k_t[..., None, :])
        return s, jnp.einsum('bhvk,bhk->bhv', s, r_t)

    xs = (jnp.moveaxis(r.astype(F32), 1, 0), jnp.moveaxis(w_log.astype(F32), 1, 0),
          jnp.moveaxis(k.astype(F32), 1, 0), jnp.moveaxis(v.astype(F32), 1, 0),
          jnp.moveaxis(kk.astype(F32), 1, 0), jnp.moveaxis(a.astype(F32), 1, 0))
    s_fin, y = lax.scan(step, s0.astype(F32), xs)
    return jnp.moveaxis(y, 0, 1), s_fin


def diff_attention(q, k, v, q_pos, k_pos, lam):
    bn, lq, nh, _, d = q.shape
    dv = v.shape[-1]
    blk = min(Q_BLOCK, lq)
    nb = lq // blk
    kf = k.astype(F32)
    vf = v.astype(F32)
    k_chunk = k_pos // CHUNK
    neg = jnp.finfo(F32).min

    def one_block(args):
        qb, qp = args
        s = jnp.einsum('bqhmd,bkhmd->bhmqk', qb.astype(F32) * (d ** -0.5), kf)
        visible = k_chunk[None, :] <= (qp // CHUNK)[:, None]
        p = jax.nn.softmax(jnp.where(visible, s, neg), axis=-1)
        pd = p[:, :, 0] - lam * p[:, :, 1]
        return jnp.einsum('bhqk,bkhe->bqhe', pd, vf)

    qb = jnp.moveaxis(q.reshape(bn, nb, blk, nh, 2, d), 1, 0)
    o = lax.map(one_block, (qb, q_pos.reshape(nb, blk)))
    return jnp.moveaxis(o, 0, 1).reshape(bn, lq, nh, dv)


def trunk_layer(h, p_l, past_len, cache_k, cache_v, conv_buf, delta_s, shift_prev, wkv_s,
                norm_mix, w_in, a_conv_w, a_A_log, a_dt_bias, a_norm,
                b_lam_q1, b_lam_k1, b_lam_q2, b_lam_k2, b_norm,
                c_mu, c_w0, c_w_up, c_a0, c_a_up, c_g_up, c_k_k, c_k_a, c_r_k, c_ln_w, c_ln_b,
                w_out, norm_ffn, w_ff1, w_ff2, norm_ple, w_ple_gate, w_ple_proj, layer_idx):
    dt = h.dtype
    bn, L, _ = h.shape
    x = rmsnorm(h, norm_mix)
    a_qkv, a_z, a_a, a_b, b_q, b_k, b_v, c_raw = split_proj(x @ w_in)

    a_c, new_conv = causal_conv(a_qkv, conv_buf, a_conv_w)
    a_c = jax.nn.silu(a_c)
    aq, ak, av = jnp.split(a_c, [A_QK, 2 * A_QK], axis=-1)
    aq = l2norm(aq.reshape(bn, L, A_HEADS, A_DK))
    ak = l2norm(ak.reshape(bn, L, A_HEADS, A_DK))
    av = av.reshape(bn, L, A_HEADS, A_DV)
    a_logdecay = -jnp.exp(a_A_log.astype(F32)) * jax.nn.softplus(a_a.astype(F32) + a_dt_bias)
    a_beta = jax.nn.sigmoid(a_b.astype(F32))
    ao, new_delta = gated_delta_chunked(aq, ak, av, a_logdecay, a_beta, delta_s)
    ao = rmsnorm(ao, a_norm) * jax.nn.silu(a_z.astype(F32)).reshape(bn, L, A_HEADS, A_DV)

    q_pos = past_len + jnp.arange(L, dtype=jnp.int32)
    k_pos = jnp.arange(past_len + L, dtype=jnp.int32)
    bq = rope(b_q.reshape(bn, L, B_HEADS, 2, B_DQK), q_pos)
    bk = rope(b_k.reshape(bn, L, B_HEADS, 2, B_DQK), q_pos)
    bv = b_v.reshape(bn, L, B_HEADS, B_DV)
    new_k = bk.reshape(bn, L, B_HEADS, 2 * B_DQK)
    k_all = jnp.concatenate([cache_k.astype(dt), new_k], axis=1).reshape(bn, past_len + L, B_HEADS, 2, B_DQK)
    v_all = jnp.concatenate([cache_v.astype(dt), bv], axis=1)
    lam_init = 0.8 - 0.6 * math.exp(-0.3 * layer_idx)
    lam = (jnp.exp(jnp.sum(b_lam_q1.astype(F32) * b_lam_k1.astype(F32)))
           - jnp.exp(jnp.sum(b_lam_q2.astype(F32) * b_lam_k2.astype(F32))) + lam_init)
    bo = diff_attention(bq, k_all, v_all, q_pos, k_pos, lam)
    bo = rmsnorm(bo, b_norm) * (1.0 - lam_init)

    prev = jnp.concatenate([shift_prev[:, None, :].astype(dt), c_raw[:, :-1]], axis=1)
    c = c_raw + (prev - c_raw) * c_mu
    new_shift = c_raw[:, -1]
    cr, ck, cv, cwd, cad, cgd = jnp.split(
        c, [C_WIDTH, 2 * C_WIDTH, 3 * C_WIDTH, 3 * C_WIDTH + C_W_RANK,
            3 * C_WIDTH + C_W_RANK + C_A_RANK], axis=-1)
    hd = (bn, L, C_HEADS, C_HD)
    w_log = -jnp.exp(-jax.nn.softplus(-(c_w0 + jnp.tanh(cwd.astype(F32)) @ c_w_up)) - RWKV_DECAY_OFFSET)
    ca = jax.nn.sigmoid(c_a0 + cad.astype(F32) @ c_a_up).reshape(hd)
    cg = jax.nn.sigmoid(cgd.astype(F32)) @ c_g_up
    cr = cr.astype(F32).reshape(hd)
    cv = cv.astype(F32).reshape(hd)
    ck = ck.astype(F32).reshape(hd)
    kk = l2norm(ck * c_k_k.reshape(C_HEADS, C_HD))
    ck = ck * (1.0 + (ca - 1.0) * c_k_a.reshape(C_HEADS, C_HD))
    cy, new_wkv = rwkv7_scan(cr, w_log.reshape(hd), ck, cv, kk, ca, wkv_s)
    mu = jnp.mean(cy, axis=-1, keepdims=True)
    var = jnp.mean(jnp.square(cy - mu), axis=-1, keepdims=True)
    cy = ((cy - mu) * lax.rsqrt(var + C_LN_EPS)).reshape(bn, L, C_WIDTH) * c_ln_w + c_ln_b
    bonus = jnp.sum(cr * ck * c_r_k, axis=-1, keepdims=True) * cv
    co = (cy + bonus.reshape(bn, L, C_WIDTH)) * cg

    mix = jnp.concatenate([ao.reshape(bn, L, -1).astype(dt), bo.reshape(bn, L, -1).astype(dt),
                           co.astype(dt)], axis=-1)
    h = h + mix @ w_out

    u = jax.nn.relu(rmsnorm(h, norm_ffn) @ w_ff1)
    h = h + (u * u) @ w_ff2

    gate = jax.nn.sigmoid(rmsnorm(h, norm_ple) @ w_ple_gate)
    h = h + gate * (p_l @ w_ple_proj)

    return h, (new_conv, new_delta.astype(dt), new_k, bv, new_shift, new_wkv.astype(dt))


def run_trunk(x, p, cache_k, cache_v, conv_buf, delta_s, shift_prev, wkv_s, layer_params, norm_final):
    past_len = cache_k.shape[2]
    h = x
    states = []
    for i in range(DEPTH):
        lp = [t[i] for t in layer_params]
        h, st = trunk_layer(h, p[i], past_len, cache_k[i], cache_v[i], conv_buf[i], delta_s[i],
                            shift_prev[i], wkv_s[i], *lp, layer_idx=i)
        states.append(st)
    new_state = [jnp.stack([st[j] for st in states], axis=0) for j in range(6)]
    return rmsnorm(h, norm_final), new_state


def setup_inputs(seed: int = 0) -> dict:
    key = jax.random.key(seed)
    ks = iter(jax.random.split(key, 48))

    def nrm(shape, s):
        return jax.random.normal(next(ks), shape, F32) * s

    def gain(shape):
        return 1.0 + nrm(shape, 0.01)

    dt0 = jnp.exp(jax.random.uniform(next(ks), (DEPTH, A_HEADS), F32,
                                     minval=math.log(1e-3), maxval=math.log(1e-1)))
    return {
        'x_prompt': nrm((BATCH, SEQ, D_MODEL), 1.0),
        'x_sample': nrm((DEC_BATCH, DEC_SEQ, D_MODEL), 1.0),
        'cache_b_k': nrm((DEPTH, DEC_BATCH, PAST_LEN, B_HEADS, 2 * B_DQK), 1.0),
        'cache_b_v': nrm((DEPTH, DEC_BATCH, PAST_LEN, B_HEADS, B_DV), 1.0),
        'state_a_conv': nrm((DEPTH, DEC_BATCH, CONV_W - 1, A_CONV_CH), 1.0),
        'state_a_delta': nrm((DEPTH, DEC_BATCH, A_HEADS, A_DK, A_DV), 0.1),
        'state_c_shift': nrm((DEPTH, DEC_BATCH, C_PROJ), 1.0),
        'state_c_wkv': nrm((DEPTH, DEC_BATCH, C_HEADS, C_HD, C_HD), 0.5),
        'p_prompt': nrm((DEPTH, BATCH, SEQ, PLE_DIM), 1.0),
        'p_sample': nrm((DEPTH, DEC_BATCH, DEC_SEQ, PLE_DIM), 1.0),
        'norm_mix': gain((DEPTH, D_MODEL)),
        'w_in': nrm((DEPTH, D_MODEL, IN_WIDTH), D_MODEL ** -0.5),
        'a_conv_w': nrm((DEPTH, CONV_W, A_CONV_CH), CONV_W ** -0.5),
        'a_A_log': jnp.log(jax.random.uniform(next(ks), (DEPTH, A_HEADS), F32, minval=1.0, maxval=16.0)),
        'a_dt_bias': dt0 + jnp.log(-jnp.expm1(-dt0)),
        'a_norm': gain((DEPTH, A_DV)),
        'b_lam_q1': nrm((DEPTH, B_DQK), 0.1),
        'b_lam_k1': nrm((DEPTH, B_DQK), 0.1),
        'b_lam_q2': nrm((DEPTH, B_DQK), 0.1),
        'b_lam_k2': nrm((DEPTH, B_DQK), 0.1),
        'b_norm': gain((DEPTH, B_DV)),
        'c_mu': jax.random.uniform(next(ks), (DEPTH, C_PROJ), F32),
        'c_w0': nrm((DEPTH, C_WIDTH), 0.5) - 0.5,
        'c_w_up': nrm((DEPTH, C_W_RANK, C_WIDTH), 0.5 * C_W_RANK ** -0.5),
        'c_a0': nrm((DEPTH, C_WIDTH), 0.1),
        'c_a_up': nrm((DEPTH, C_A_RANK, C_WIDTH), 0.5 * C_A_RANK ** -0.5),
        'c_g_up': nrm((DEPTH, C_G_RANK, C_WIDTH), C_G_RANK ** -0.5),
        'c_k_k': 0.85 + nrm((DEPTH, C_WIDTH), 0.02),
        'c_k_a': 1.0 + nrm((DEPTH, C_WIDTH), 0.02),
        'c_r_k': nrm((DEPTH, C_HEADS, C_HD), 0.1),
        'c_ln_w': gain((DEPTH, C_WIDTH)),
        'c_ln_b': nrm((DEPTH, C_WIDTH), 0.01),
        'w_out': nrm((DEPTH, MIX_WIDTH, D_MODEL), MIX_WIDTH ** -0.5),
        'norm_ffn': gain((DEPTH, D_MODEL)),
        'w_ff1': nrm((DEPTH, D_MODEL, D_FF), D_MODEL ** -0.5),
        'w_ff2': nrm((DEPTH, D_FF, D_MODEL), D_FF ** -0.5),
        'norm_ple': gain((DEPTH, D_MODEL)),
        'w_ple_gate': nrm((DEPTH, D_MODEL, D_MODEL), D_MODEL ** -0.5),
        'w_ple_proj': nrm((DEPTH, PLE_DIM, D_MODEL), PLE_DIM ** -0.5),
        'norm_final': gain((D_MODEL,)),
    }


def reference(x_prompt, x_sample, cache_b_k, cache_b_v, state_a_conv, state_a_delta,
              state_c_shift, state_c_wkv, p_prompt, p_sample,
              norm_mix, w_in, a_conv_w, a_A_log, a_dt_bias, a_norm,
              b_lam_q1, b_lam_k1, b_lam_q2, b_lam_k2, b_norm,
              c_mu, c_w0, c_w_up, c_a0, c_a_up, c_g_up, c_k_k, c_k_a, c_r_k, c_ln_w, c_ln_b,
              w_out, norm_ffn, w_ff1, w_ff2, norm_ple, w_ple_gate, w_ple_proj, norm_final):
    layer_params = (norm_mix, w_in, a_conv_w, a_A_log, a_dt_bias, a_norm,
                    b_lam_q1, b_lam_k1, b_lam_q2, b_lam_k2, b_norm,
                    c_mu, c_w0, c_w_up, c_a0, c_a_up, c_g_up, c_k_k, c_k_a, c_r_k, c_ln_w, c_ln_b,
                    w_out, norm_ffn, w_ff1, w_ff2, norm_ple, w_ple_gate, w_ple_proj)

    bp = x_prompt.shape[0]
    dt = x_prompt.dtype
    zk = jnp.zeros((DEPTH, bp, 0, B_HEADS, 2 * B_DQK), dt)
    zv = jnp.zeros((DEPTH, bp, 0, B_HEADS, B_DV), dt)
    zc = jnp.zeros((DEPTH, bp, CONV_W - 1, A_CONV_CH), dt)
    zd = jnp.zeros((DEPTH, bp, A_HEADS, A_DK, A_DV), dt)
    zs = jnp.zeros((DEPTH, bp, C_PROJ), dt)
    zw = jnp.zeros((DEPTH, bp, C_HEADS, C_HD, C_HD), dt)
    y_prompt, st_p = run_trunk(x_prompt, p_prompt, zk, zv, zc, zd, zs, zw, layer_params, norm_final)
    a_conv_p, a_delta_p, b_k_p, b_v_p, c_shift_p, c_wkv_p = st_p

    y_sample, st_s = run_trunk(x_sample, p_sample, cache_b_k, cache_b_v, state_a_conv, state_a_delta,
                               state_c_shift, state_c_wkv, layer_params, norm_final)
    a_conv_s, a_delta_s, b_k_s, b_v_s, c_shift_s, c_wkv_s = st_s

    return (y_prompt, y_sample,
            a_conv_p, a_delta_p, b_k_p, b_v_p, c_shift_p, c_wkv_p,
            a_conv_s, a_delta_s, b_k_s, b_v_s, c_shift_s, c_wkv_s)
```

```python
import math
from contextlib import ExitStack

import numpy as np
import ml_dtypes
import concourse.bass as bass
import concourse.mybir as mybir
from concourse.bass_utils import run_bass_kernel_spmd

F32 = mybir.dt.float32
BF16 = mybir.dt.bfloat16
AF = mybir.ActivationFunctionType
ALU = mybir.AluOpType
AX = mybir.AxisListType

D_MODEL = 1024
DEPTH = 2
PLE_DIM = 256
D_FF = 4096
IN_WIDTH = 3592
NORM_EPS = 1e-6
L2_EPS = 1e-6
C_LN_EPS = 64e-5
O_AQKV, O_AZ, O_AA, O_AB, O_BQ, O_BK, O_BV, O_C = 0, 768, 1024, 1028, 1032, 1544, 2056, 2568

ENGS = ("pe", "act", "dve", "pool", "sp")
SEM_WRAP = 30000


class Buf:
    __slots__ = ("name", "writer", "readers", "chan", "multi")

    def __init__(self, name, multi=False):
        self.name = name
        self.writer = [] if multi else None
        self.readers = []
        self.chan = None
        self.multi = multi


class Chan:
    __slots__ = ("sem", "count", "name")

    def __init__(self, name):
        self.name = name
        self.sem = None
        self.count = 0


class Op:
    __slots__ = ("eng", "fn", "deps", "signal", "semidx", "semval", "is_dma", "chan", "chan_val")

    def __init__(self, eng, fn):
        self.eng = eng
        self.fn = fn
        self.deps = []
        self.signal = False
        self.semidx = 0
        self.semval = 0
        self.is_dma = False
        self.chan = None
        self.chan_val = 0


class Sched:
    def __init__(self, nc):
        self.nc = nc
        self.ops = {e: [] for e in ENGS}
        self.chans = []
        self.final_waits = []

    def _collect(self, op, reads, writes):
        deps = []
        for b in reads:
            if b.multi:
                deps.extend(b.writer)
            elif b.writer is not None:
                deps.append(b.writer)
        for b in writes:
            if b.multi:
                deps.extend(b.writer)
            elif b.writer is not None:
                deps.append(b.writer)
            deps.extend(b.readers)
        seen = set()
        for d in deps:
            if d is op or id(d) in seen:
                continue
            seen.add(id(d))
            if d.eng == "pe" and op.eng == "pe" and not d.is_dma and not op.is_dma:
                continue
            op.deps.append(d)
        for b in writes:
            if b.multi:
                b.writer.append(op)
            else:
                b.writer = op
                b.readers = []
        for b in reads:
            b.readers.append(op)

    def op(self, eng, fn, reads=(), writes=()):
        o = Op(eng, fn)
        self.ops[eng].append(o)
        self._collect(o, reads, writes)
        return o

    def dma(self, eng, out, in_, reads=(), writes=(), chan_buf=None, final=False, **kw):
        if chan_buf is None:
            chan_buf = (list(writes) + list(reads))[0]
        if chan_buf.chan is None:
            chan_buf.chan = {}
        if eng not in chan_buf.chan:
            chan_buf.chan[eng] = Chan(chan_buf.name + "_" + eng)
            self.chans.append(chan_buf.chan[eng])
        ch = chan_buf.chan[eng]
        o = Op(eng, None)
        o.is_dma = True
        o.chan = ch
        ch.count += 1
        o.chan_val = 16 * ch.count
        o.fn = lambda e, out=out, in_=in_, kw=kw: e.dma_start(out=out, in_=in_, **kw)
        self.ops[eng].append(o)
        self._collect(o, reads, writes)
        if final:
            self.final_waits.append(o)
        return o

    def emit(self, stack):
        nc = self.nc
        for e in ENGS:
            for o in self.ops[e]:
                for d in o.deps:
                    if not d.is_dma:
                        d.signal = True
        nsem = {}
        for e in ENGS:
            cnt = 0
            for o in self.ops[e]:
                if o.signal and not o.is_dma:
                    o.semidx = cnt // SEM_WRAP
                    o.semval = cnt % SEM_WRAP + 1
                    cnt += 1
            nsem[e] = cnt // SEM_WRAP + 1
        esems = {e: [stack.enter_context(nc.semaphore(f"s_{e}{i}")) for i in range(nsem[e])] for e in ENGS}
        for i, ch in enumerate(self.chans):
            ch.sem = stack.enter_context(nc.semaphore(f"c{i}_{ch.name}"))
        block = stack.enter_context(nc.Block())

        def run(e, eng):
            seen = {}
            maxidx = {}
            for o in self.ops[e]:
                need = {}
                for d in o.deps:
                    if d.is_dma:
                        key = ("c", id(d.chan)); sem = d.chan.sem; val = d.chan_val
                    else:
                        key = (d.eng, d.semidx); sem = esems[d.eng][d.semidx]; val = d.semval
                    if key not in need or need[key][1] < val:
                        need[key] = (sem, val)
                for key, (sem, val) in need.items():
                    if key[0] != "c":
                        if maxidx.get(key[0], -1) > key[1]:
                            continue
                    if seen.get(key, 0) >= val:
                        continue
                    seen[key] = val
                    if key[0] != "c":
                        maxidx[key[0]] = max(maxidx.get(key[0], -1), key[1])
                    eng.wait_ge(sem, val)
                ins = o.fn(eng)
                if o.is_dma:
                    ins.then_inc(o.chan.sem, 16)
                elif o.signal:
                    ins.then_inc(esems[e][o.semidx], 1)
            if e == "sp":
                fin = {}
                for o in self.final_waits:
                    fin[id(o.chan)] = (o.chan, max(o.chan_val, fin.get(id(o.chan), (None, 0))[1]))
                for ch, val in fin.values():
                    eng.wait_ge(ch.sem, val)

        @block.tensor
        def _(eng):
            run("pe", eng)

        @block.scalar
        def _(eng):
            run("act", eng)

        @block.vector
        def _(eng):
            run("dve", eng)

        @block.gpsimd
        def _(eng):
            run("pool", eng)

        @block.sync
        def _(eng):
            run("sp", eng)


class V:
    __slots__ = ("ap", "bufs")

    def __init__(self, ap, bufs):
        self.ap = ap
        self.bufs = bufs

    def __getitem__(self, idx):
        return V(self.ap[idx], self.bufs)

    def r(self, s, **kw):
        return V(self.ap.rearrange(s, **kw), self.bufs)

    def bc(self, shape):
        return V(self.ap.to_broadcast(shape), self.bufs)

    def unsq(self, ax):
        return V(self.ap.unsqueeze(ax), self.bufs)


class T:
    def __init__(self, t, name, track=True, multi=False, bufs=None):
        self.t = t
        self.name = name
        self.buf = Buf(name, multi=multi) if track else None
        self.bufs = bufs

    def __getitem__(self, idx):
        if self.bufs is not None:
            b = self.bufs() if callable(self.bufs) else self.bufs
            return V(self.t[idx], tuple(b))
        return V(self.t[idx], (self.buf,) if self.buf is not None else ())

    def v(self):
        return self[:]


class Builder:
    def __init__(self, seq, dec_seq=16, past=2048, stages=("prompt", "A", "C", "sample")):
        self.SEQ = seq
        self.DEC = dec_seq
        self.PAST = past
        self.TT = min(512, seq)
        self.CH = min(64, seq)
        self.stages = stages
        self.nc = bass.Bass("TRN2", target_bir_lowering=False)
        self.S = Sched(self.nc)
        self.st = ExitStack()
        self.in_names = []
        self.out_names = []
        import os
        self.bsub = int(os.environ.get('BSUB', '9'))
        self.poolsum = os.environ.get('POOLSUM', '0') == '1'
        self.wcache = os.environ.get('WCACHE', '1') == '1'
        self.asub = int(os.environ.get('ASUB', '9'))
        self.a1 = int(os.environ.get('A1', '9'))
        self.a3 = int(os.environ.get('A3', '9'))

    def sb(self, name, shape, dt=F32):
        import os
        if os.environ.get("ALLOCDBG"):
            print("ALLOC", name, shape, dt, int(np.prod(shape[1:])) * (4 if dt == F32 else 2))
        return T(self.st.enter_context(self.nc.sbuf_tensor("s_" + name, list(shape), dt)), name)

    def arena(self, name, nfloats):
        t = self.st.enter_context(self.nc.sbuf_tensor("s_" + name, [128, nfloats], F32))
        return {"t": t, "off": 0, "bufs": [], "n": nfloats, "parts": {}}

    def _arena_ap(self, ar, off, shape, dt):
        n = int(np.prod(shape[1:]))
        nfl = n if dt == F32 else n // 2
        assert off + nfl <= ar["n"], (off, nfl, ar["n"])
        ap = ar["t"][:, off:off + nfl]
        if dt != F32:
            ap = ap.bitcast(dt)
        if len(shape) == 3:
            ap = ap.rearrange("p (a b) -> p a b", a=shape[1])
        elif len(shape) == 4:
            ap = ap.rearrange("p (a b c) -> p a b c", a=shape[1], b=shape[2])
        return ap, nfl

    def sub(self, ar, name, shape, dt=F32, part=None):
        if part is None:
            ap, nfl = self._arena_ap(ar, ar["off"], shape, dt)
            ar["off"] += nfl
            tt = T(ap, name)
            ar["bufs"].append(tt.buf)
            return tt
        pd = ar["parts"].setdefault(part, {"off": 0, "bufs": []})
        ap, nfl = self._arena_ap(ar, pd["off"], shape, dt)
        pd["off"] += nfl
        tt = T(ap, name)
        pd["bufs"].append(tt.buf)
        own = tt.buf
        tt.bufs = lambda: [own] + [b for p, d in ar["parts"].items() if p != part for b in d["bufs"]]
        return tt

    def whole(self, ar, name, shape, dt=F32):
        ap, _ = self._arena_ap(ar, 0, shape, dt)
        return T(ap, name, track=False, bufs=ar["bufs"])

    def psum(self, name, shape, dt=F32):
        return T(self.st.enter_context(self.nc.psum_tensor("p_" + name, list(shape), dt)), name)

    def din(self, name, shape, dt=F32):
        self.in_names.append(name)
        return T(self.nc.dram_tensor(name, list(shape), dt, kind="ExternalInput").ap(), name, track=False)

    def dout(self, name, shape, dt=F32):
        self.out_names.append(name)
        return T(self.nc.dram_tensor(name, list(shape), dt, kind="ExternalOutput").ap(), name, track=False)

    def dscr(self, name, shape, dt):
        return T(self.nc.dram_tensor(name, list(shape), dt).ap(), name, multi=True)

    def _op(self, eng, meth, *args, _r=(), _w=(), **kw):
        reads, writes = list(_r), list(_w)
        cargs = []
        for i, a in enumerate(args):
            if isinstance(a, V):
                (writes if i == 0 else reads).extend(a.bufs)
                cargs.append(a.ap)
            else:
                cargs.append(a)
        ckw = {}
        for k, a in kw.items():
            if isinstance(a, V):
                (writes if k in ("out", "accum_out") else reads).extend(a.bufs)
                ckw[k] = a.ap
            else:
                ckw[k] = a
        return self.S.op(eng, lambda e: getattr(e, meth)(*cargs, **ckw), reads=reads, writes=writes)

    def pe(self, meth, *a, **k):
        return self._op("pe", meth, *a, **k)

    def act(self, meth, *a, **k):
        return self._op("act", meth, *a, **k)

    def dve(self, meth, *a, **k):
        return self._op("dve", meth, *a, **k)

    def dma(self, q, out, in_, final=False, **kw):
        import os
        if os.environ.get("NOSCR") and any(b.multi for b in list(out.bufs) + list(in_.bufs)):
            return
        if os.environ.get("NOOUT") and final and "b_" in str(out.ap):
            return
        reads = list(in_.bufs)
        writes = list(out.bufs)
        cands = [b for b in writes + reads if not b.multi]
        return self.S.dma(q, out.ap, in_.ap, reads=reads, writes=writes, chan_buf=cands[0], final=final, **kw)

    def build(self):
        nc = self.nc
        SEQ, TT = self.SEQ, self.TT
        NT = SEQ // TT
        NB = TT // 128
        st = self.st

        x_in = self.din("x", [SEQ, D_MODEL])
        p_in = self.din("p", [DEPTH, SEQ, PLE_DIM])
        W = {}
        for nm, shp in [("w_in", [DEPTH, D_MODEL, IN_WIDTH]), ("w_out", [DEPTH, D_MODEL, D_MODEL]),
                        ("w_ff1", [DEPTH, D_MODEL, D_FF]), ("w_ff2", [DEPTH, D_FF, D_MODEL]),
                        ("w_ple_gate", [DEPTH, D_MODEL, D_MODEL]), ("w_ple_proj", [DEPTH, PLE_DIM, D_MODEL])]:
            W[nm] = self.din(nm, shp)
        cols_in = self.din("cols", [128, 64])
        NCONST = 11
        consts_in = self.din("consts", [128, NCONST, 128])
        cols2_in = self.din("cols2", [128, DEPTH, 64])
        cw3_in = self.din("cw3", [128, DEPTH, 3, 256])
        cshift_out = self.dout("c_shift", [DEPTH, 1024])
        cwkv_out = self.dout("c_wkv", [DEPTH, 4, 64, 64])
        rowp_in = self.din("rowp", [DEPTH, 8])
        aconv_out = self.dout("a_conv", [DEPTH, 3, 768])
        adelta_out = self.dout("a_delta", [DEPTH, 4, 64, 64])
        rope_in = self.din("rope", [SEQ, 64])
        amask_in = self.din("amask", [128, 4, 512], BF16)
        lam_in = self.din("lamrow", [DEPTH, 4, 64])
        y_out = self.dout("y", [SEQ, D_MODEL])
        bk_out = self.dout("b_k", [DEPTH, SEQ, 512])
        bv_out = self.dout("b_v", [DEPTH, SEQ, 512])
        kT_scr = [self.dscr(f"kT_scr{l}", [512, SEQ], BF16) for l in range(DEPTH)]
        v_scr = [self.dscr(f"v_scr{l}", [SEQ, 512], BF16) for l in range(DEPTH)]

        consts = self.sb("consts", [128, NCONST, 128])
        cols2 = self.sb("cols2", [128, DEPTH, 64])
        rowp = self.sb("rowp", [128, DEPTH, 8])
        cU, cSU, cL, cLI, cSame = (consts[:, i, :] for i in (2, 3, 4, 5, 6))
        arA = self.arena("arA", 8192); arB = self.arena("arB", 4096); arC = self.arena("arC", 4096)
        aq = self.sub(arA, "aq", [128, 6, 3 + TT], part="A")
        acarry = [self.sb(f"acarry{l}", [128, 6, 3]) for l in range(DEPTH)]
        Sd = [self.sb(f"Sd{l}", [128, 2, 128]) for l in range(DEPTH)]
        ac = self.sub(arA, "ac", [128, 6, TT], part="A")
        qkn = self.sub(arB, "qkn", [128, 4, TT], part="A")
        zs = self.sub(arB, "zs", [128, 2, TT], part="A")
        abtm = self.sb("abtm", [128, NB, 8])
        astep = self.sb("astep", [128, NB, 4])
        beta = self.sb("beta", [128, NB, 4])
        arD = self.arena("arD", 4096)
        kvtm = self.sub(arD, "kvtm", [128, NB, 512], part="A")
        ontm = self.sub(arD, "ontm", [128, NB, 256], part="A")
        aL = self.sub(arD, "aL", [128, 4, 128], part="A"); aU = self.sub(arD, "aU", [128, 4, 128], part="A")
        decA = self.sub(arB, "decA", [128, 4, 128], part="A"); decT = self.sub(arB, "decT", [128, 4, 128], part="A")
        eg = self.sb("eg", [128, 16]); bkg = self.sb("bkg", [128, 4]); egt = self.sb("egt", [128, 2, 2])
        Am = self.sub(arA, "Am", [128, 4, 128], part="A")
        Xa = [self.sb("Xa0", [128, 4, 128])] * 2
        XTa = [self.sb("XTa0", [128, 4, 128])] * 2
        Rm = self.sub(arA, "Rm", [128, 4, 128], part="A")
        vb_ = self.sb("vb_", [128, 4, 64]); kbz = self.sb("kbz", [128, 4, 128]); ktz = self.sb("ktz", [128, 2, 4, 128]); kzb = self.sb("kzb", [128, 4, 128])
        unew = self.sb("unew", [128, 256]); wT = self.sb("wT", [128, 2, 128])
        qkT = self.sub(arA, "qkT", [128, 4, 128], part="A"); ssq = self.sb("ssq", [128, 4])
        ident = consts[:, 0, :]
        identb = self.sb("identb", [128, 128], BF16)
        onesb = self.sb("onesb", [128, 128], BF16)
        cols = self.sb("cols", [128, 64])
        amask = self.sb("amask", [128, 4, 512], BF16)
        lamt = self.sb("lamt", [128, 8])
        h = self.sb("h", [128, 8, TT])
        xb = self.sb("xb", [128, 8, TT], BF16)
        rstd = self.sb("rstd", [128, TT])
        sq = [self.sb(f"sq{i}", [128, TT], BF16) for i in range(2)]
        NSLOT = 3
        ring = [self.sb(f"wr{i}", [128, 4096], BF16) for i in range(NSLOT)]
        self.ring_i = 0
        mixT = self.sb("mixT", [128, 8, TT], BF16)
        pT = self.sb("pT", [128, 2, TT], BF16)
        tmpf = [self.sb(f"tmpf{i}", [128, TT]) for i in range(2)]
        ropet = self.sb("ropet", [128, NB, 64])
        kfs = [self.sb(f"kfs{i}", [128, 512]) for i in range(2)]
        vfs = [self.sb(f"vfs{i}", [128, 512]) for i in range(2)]
        qT = self.sb("qT", [128, 4, 2, TT], BF16)

        ptok = self.sub(arD, "ptok", [128, NB, PLE_DIM], part="P")
        kblk = [self.sub(arD, f"kblk{i}", [128, 512], BF16, part="B") for i in range(2)]
        vblk = [self.sub(arD, f"vblk{i}", [128, 4, 128], BF16, part="B") for i in range(2)]
        PT = [self.sub(arD, f"PT{i}", [128, TT], BF16, part="B") for i in range(4)]
        osb = [self.sub(arD, f"osb{i}", [128, TT], part="B") for i in range(3)]
        qtm = self.sub(arC, "qtm", [128, NB, 512], BF16, part="B")
        ktm = self.sub(arC, "ktm", [128, NB, 512], BF16, part="B")
        vtb = self.sub(arC, "vtb", [128, NB, 512], BF16, part="B")
        kTt = self.sub(arC, "kTt", [128, 4, TT], BF16, part="B")
        xtok = self.sub(arC, "xtok", [128, NB, D_MODEL], part="X")
        uT = self.sub(arA, "uT", [128, 32, TT], BF16, part="F")
        gate = self.sub(arB, "gate", [128, 8, TT], part="F")
        c_r = self.sub(arA, "c_r", [128, 2, TT], part="C"); c_v = self.sub(arA, "c_v", [128, 2, TT], part="C")
        c_wl = self.sub(arA, "c_wl", [128, 2, TT], part="C"); c_g = self.sub(arA, "c_g", [128, 2, TT], part="C")
        c_kk = self.sub(arA, "c_kk", [128, 2, TT], part="C"); c_km = self.sub(arA, "c_km", [128, 2, TT], part="C")
        c_b = self.sub(arA, "c_b", [128, 2, TT], part="C"); c_bon = self.sub(arA, "c_bon", [128, 2, TT], part="C")
        c_a = self.sub(arB, "c_a", [128, 2, TT], part="C"); c_k = self.sub(arB, "c_k", [128, 2, TT], part="C")
        c_6 = self.sub(arB, "c_6", [128, TT], part="C"); c_7 = self.sub(arB, "c_7", [128, TT], part="C")
        C1T = self.sub(arB, "C1T", [128, 4, 128], part="C"); C2T = self.sub(arB, "C2T", [128, 4, 128], part="C")
        crawm = [self.sub(arD, f"crawm{i}", [128, 1 + TT], part="C") for i in range(2)]
        tm5 = self.sub(arD, "tm5", [128, 5, 256], part="C")
        rTt = self.sub(arD, "rTt", [128, 2, 128], part="C"); kapTt = self.sub(arD, "kapTt", [128, 2, 128], part="C")
        kTm = self.sub(arD, "kTm", [128, 4, 128], part="C"); bTm = self.sub(arD, "bTm", [128, 4, 128], part="C")
        uval = self.sub(arC, "uval", [128, 256], part="A"); oq = self.sub(arC, "oq", [128, 256], part="A")
        otm = self.sub(arC, "otm", [128, 256], part="A"); o2 = self.sub(arC, "o2", [128, 256], part="A")
        kapz = self.sub(arC, "kapz", [128, 4, 128], part="C")
        cktz = self.sub(arC, "cktz", [128, 2, 4, 128], part="C"); cbtz = self.sub(arC, "cbtz", [128, 2, 4, 128], part="C")
        cx0 = self.sub(arC, "cx0", [128, 256], part="C"); cU0 = self.sub(arC, "cU0", [128, 256], part="C")
        cU_ = self.sub(arC, "cU_", [128, 256], part="C"); cyq = self.sub(arC, "cyq", [128, 256], part="C")
        cytm = self.sub(arC, "cytm", [128, 256], part="C")
        cR2 = T(vfs[0].t[:, :].rearrange("p (h s) -> p h s", h=4), "cR2", track=False, bufs=[vfs[0].buf])
        cBmT = T(vfs[1].t[:, :].rearrange("p (h s) -> p h s", h=4), "cBmT", track=False, bufs=[vfs[1].buf])
        Tst = [self.sb(f"Tst{l}", [128, 2, 128]) for l in range(DEPTH)]
        ccarry = [self.sb(f"ccarry{l}", [128, 8]) for l in range(DEPTH)]
        acarry_s = [self.sb(f"acarry_s{l}", [128, 6, 3]) for l in range(DEPTH)]
        Sd_s = [self.sb(f"Sd_s{l}", [128, 2, 128]) for l in range(DEPTH)]
        Tst_s = [self.sb(f"Tst_s{l}", [128, 2, 128]) for l in range(DEPTH)]
        ccarry_s = [self.sb(f"ccarry_s{l}", [128, 8]) for l in range(DEPTH)]
        kvp = self.sb("kvp", [128, 2, 2, 64]); pcc = self.sb("pcc", [128, 2, 2])
        cw3 = self.sb("cw3", [128, DEPTH, 3, 256], BF16)
        cst = self.sb("cst", [128, 8])

        def cyn(blk):
            return kfs[blk // 2][:, (blk % 2) * 256:(blk % 2 + 1) * 256]
        cgt = self.sb("cgt", [128, 2, 256])
        ps = [self.psum(f"ps{i}", [128, 512]) for i in range(7)]
        psbT = self.psum("psb", [128, 1024], BF16)

        self.dma("sp", consts.v(), consts_in.v())
        self.dma("sp", cols.v(), cols_in.v())
        self.dma("sp", cols2.v(), cols2_in.v())
        self.dma("sp", rowp.v(), V(rowp_in.t.rearrange("l a -> (l a)").partition_broadcast(128)
                                  .rearrange("p (l a) -> p l a", l=DEPTH), ()))
        self.act("activation", rowp[:, :, 0:4], rowp[:, :, 0:4], AF.Exp)
        self.dve("tensor_scalar_mul", rowp[:, :, 0:4], rowp[:, :, 0:4], -1.0)
        for l in range(DEPTH):
            self.dve("memset", acarry[l].v(), 0.0)
            self.dve("memset", Sd[l].v(), 0.0)
        self.dma("pool", cw3.v(), cw3_in.v())
        for l in range(DEPTH):
            self.dve("memset", Tst[l].v(), 0.0)
            self.dve("memset", ccarry[l].v(), 0.0)
            self.dve("tensor_scalar", cols2[:, l, 54:56], cols2[:, l, 46:48], -1.0, 1.0, ALU.mult, ALU.add)
        self.dve("memset", kbz.v(), 0.0)
        self.dve("memset", ktz.v(), 0.0)
        self.dve("memset", unew.v(), 0.0)
        self.dma("sp", amask.v(), amask_in.v())
        self.dve("tensor_copy", identb.v(), consts[:, 0, :])
        self.dve("tensor_copy", onesb.v(), consts[:, 1, :])
        lrow = T(ptok.t[:, 0:2, :].rearrange("p a (b d) -> p a b d", b=4), "lrow", track=False, bufs=ptok.bufs)
        self.dma("sp", lrow.v(), V(lam_in.t.rearrange("l a d -> (l a d)").partition_broadcast(128)
                                  .rearrange("p (l a d) -> p l a d", l=DEPTH, a=4), ()))
        lsum = self.sb("lsum", [128, 4])
        for l in range(DEPTH):
            for m in range(2):
                self.dve("tensor_tensor", tmpf[0][:, 0:64], lrow[:, l, 2 * m, :], lrow[:, l, 2 * m + 1, :], ALU.mult)
                self.dve("reduce_sum", lsum[:, 2 * l + m:2 * l + m + 1], tmpf[0][:, 0:64], AX.X)
        self.act("activation", lsum.v(), lsum.v(), AF.Exp)
        for l in range(DEPTH):
            lam_init = 0.8 - 0.6 * math.exp(-0.3 * l)
            self.dve("scalar_tensor_tensor", lamt[:, l:l + 1], lsum[:, 2 * l + 1:2 * l + 2], -lam_init,
                     lsum[:, 2 * l:2 * l + 1], ALU.add, ALU.subtract)

        bns = self.sb("bns", [128, DEPTH])
        for l in range(DEPTH):
            self.dve("tensor_scalar_mul", bns[:, l:l + 1], cols[:, 48 + l:49 + l], float(1.0 - (0.8 - 0.6 * math.exp(-0.3 * l))))
        def col(name, l, k=0):
            base = {"norm_mix": 0, "norm_ffn": 8, "norm_ple": 16, "b_norm": 24}[name]
            if name == "b_norm":
                return bns[:, l:l + 1]
            return cols[:, l * 24 + base + k: l * 24 + base + k + 1]

        def colf(k):
            return cols[:, 50 + k:51 + k]

        def rmsnorm_to_xb(ntok, gcol):
            for k in range(8):
                s = sq[k % 2]
                self.act("activation", s[:, :ntok], h[:, k, :ntok], AF.Square)
                self.pe("matmul", ps[2][:, :ntok], onesb.v(), s[:, :ntok], start=(k == 0), stop=(k == 7))
            self.act("activation", rstd[:, :ntok], ps[2][:, :ntok], AF.Ln, bias=colf(0), scale=1.0 / D_MODEL)
            self.act("activation", rstd[:, :ntok], rstd[:, :ntok], AF.Exp, scale=-0.5)
            for k in range(8):
                self.dve("scalar_tensor_tensor", xb[:, k, :ntok], h[:, k, :ntok], gcol(k), rstd[:, :ntok],
                         ALU.mult, ALU.mult)

        wscr = {}

        def load_piece(wref, KC, c0, ncols):
            wname, l = wref
            wv = W[wname].t[l]
            slot = ring[self.ring_i % NSLOT]
            self.ring_i += 1
            flat = slot[:, 0:KC * ncols]
            sv = flat.r("p (k n) -> p k n", k=KC)
            key = (wname, l, KC, c0, ncols)
            if key in wscr and self.wcache:
                self.dma("sp", flat, wscr[key].v())
                return sv
            for k0 in range(0, KC, 8):
                k1 = min(KC, k0 + 8)
                src = V(wv.rearrange("(kc p) n -> p kc n", p=128)[:, k0:k1, c0:c0 + ncols], ())
                self.dma("pool", sv[:, k0:k1, :], src)
            if self.wcache:
                scr = self.dscr(f"wb_{wname}_{l}_{c0}_{ncols}", [128, KC * ncols], BF16)
                wscr[key] = scr
                self.dma("sp", scr.v(), flat)
            return sv

        self.psd = 0

        def dense_fm(wv, KC, c0, ncols_total, rhs, ntok, epi, mchunk=128):
            per = max(128, min(512, 4096 // KC))
            m = 0
            for pc0 in range(0, ncols_total, per):
                pn = min(per, ncols_total - pc0)
                sv = load_piece(wv, KC, c0 + pc0, pn)
                for cc in range(0, pn, mchunk):
                    mc = min(mchunk, pn - cc)
                    pst = ps[self.psd % 2]
                    self.psd += 1
                    for k in range(KC):
                        self.pe("matmul", pst[:mc, :ntok], sv[:, k, cc:cc + mc], rhs[:, k, :ntok],
                                start=(k == 0), stop=(k == KC - 1))
                    epi(m, pst[:mc, :ntok])
                    m += 1

        def dense_tm(wv, c0, ncols, ntok, epi):
            sv = load_piece(wv, 8, c0, ncols)
            for b0 in range(0, ntok, 128):
                bt = min(128, ntok - b0)
                pst = ps[self.psd % 2]
                self.psd += 1
                for k in range(8):
                    self.pe("matmul", pst[:bt, :ncols], xb[:, k, b0:b0 + bt], sv[:, k, :], start=(k == 0), stop=(k == 7))
                epi(b0 // 128, bt, pst[:bt, :ncols])

        def rope_tm(dst, src_sb, bt, blk):
            import os
            if os.environ.get("NOROPE"):
                self.dve("tensor_copy", dst, src_sb[:bt, :])
                return
            s4 = src_sb[:bt, :].r("p (g t f) -> p g t f", g=8, t=2)
            d4 = dst.r("p (g t f) -> p g t f", g=8, t=2)
            cos = ropet[:bt, blk, 0:32].unsq(1).bc([bt, 8, 32])
            sin = ropet[:bt, blk, 32:64].unsq(1).bc([bt, 8, 32])
            ta = tmpf[0][:bt, 0:256].r("p (g f) -> p g f", g=8)
            tb = tmpf[1][:bt, 0:256].r("p (g f) -> p g f", g=8)
            self.dve("tensor_tensor", ta, s4[:, :, 0, :], cos, ALU.mult)
            self.dve("tensor_tensor", tb, s4[:, :, 1, :], sin, ALU.mult)
            self.dve("tensor_tensor", d4[:, :, 0, :], ta, tb, ALU.subtract)
            self.dve("tensor_tensor", ta, s4[:, :, 1, :], cos, ALU.mult)
            self.dve("tensor_tensor", tb, s4[:, :, 0, :], sin, ALU.mult)
            self.dve("tensor_tensor", d4[:, :, 1, :], ta, tb, ALU.add)

        def neumann(bt, X, XT, R, CH):
            nlev = 5 if CH > 16 else 3
            nlev = min(nlev, self.a3 - 2)
            for lev in range(nlev):
                pX = ps[0][:bt, :].r("p (h s) -> p h s", h=4)[:, :, :bt]
                pXT = ps[1][:bt, :].r("p (h s) -> p h s", h=4)[:, :, :bt]
                pR = ps[3][:bt, :].r("p (h s) -> p h s", h=4)[:, :, :bt]
                last = lev == nlev - 1
                for hh in range(4):
                    self.pe("matmul", pXT[:, hh, :], X[:bt, hh, :bt], XT[:bt, hh, :bt], start=True, stop=True)
                if not last:
                    for hh in range(4):
                        self.pe("matmul", pX[:, hh, :], XT[:bt, hh, :bt], X[:bt, hh, :bt], start=True, stop=True)
                self.dve("tensor_copy", XT[:bt, :, :bt], pXT)
                if not last:
                    self.act("copy", X[:bt, :, :bt], pX)
                for hh in range(4):
                    self.pe("matmul", pR[:, hh, :], XT[:bt, hh, :bt], R[:bt, hh, :bt], start=True, stop=True)
                self.dve("tensor_tensor", R[:bt, :, :bt], R[:bt, :, :bt], pR, ALU.add)

        def tile_body(cx):
            ntok = cx.ntok
            nbk = (ntok + 127) // 128
            bl = min(128, ntok)
            kb0 = cx.key_base
            self.dma("sp", xtok[:bl, :nbk, :], V(cx.x_src.rearrange("(b p) d -> p b d", p=bl), ()))
            for k in range(8):
                for b in range(nbk):
                    self.pe("transpose", ps[3][:, b * 128:b * 128 + bl], xtok[:bl, b, k * 128:(k + 1) * 128], ident[:bl, :bl])
                self.act("copy", h[:, k, :ntok], ps[3][:, :ntok])
            self.dma("sp", ropet[:bl, :nbk, :], V(cx.rope_src.rearrange("(b p) d -> p b d", p=bl), ()))

            for l in range(DEPTH):
                lam_init = 0.8 - 0.6 * math.exp(-0.3 * l)
                rmsnorm_to_xb(ntok, lambda k: col("norm_mix", l, k))
                w_in = ("w_in", l)
                kT_s, v_s = cx.kT_scr[l], cx.v_scr[l]

                if True:
                    def epi_q(blk, bt, pv):
                        self.act("copy", osb[0][:bt, :], pv)
                        rope_tm(qtm[:bt, blk, :], osb[0], bt, blk)
                    dense_tm(w_in, O_BQ, 512, ntok, epi_q)

                    def epi_k(blk, bt, pv):
                        kf = kfs[blk % 2]
                        self.act("copy", osb[1][:bt, :], pv)
                        rope_tm(kf[:bt, :], osb[1], bt, blk)
                        self.act("copy", ktm[:bt, blk, :], kf[:bt, :])
                        self.dma("sp", V(cx.bk_dst(l)[blk * 128:blk * 128 + bt, :], ()), kf[:bt, :], final=True)
                    dense_tm(w_in, O_BK, 512, ntok, epi_k)

                    def epi_v(blk, bt, pv):
                        vf = vfs[blk % 2]
                        self.act("copy", vf[:bt, :], pv)
                        self.dve("tensor_copy", vtb[:bt, blk, :], vf[:bt, :])
                        self.dma("sp", V(cx.bv_dst(l)[blk * 128:blk * 128 + bt, :], ()), vf[:bt, :], final=True)
                    dense_tm(w_in, O_BV, 512, ntok, epi_v)
                    self.dma("sp", V(v_s.t[kb0:kb0 + ntok, :].rearrange("(b p) d -> p b d", p=bl), (v_s.buf,)), vtb[:bl, :nbk, :])
                    psb = psbT.v()
                    for c in range(4):
                        for b in range(nbk):
                            self.pe("transpose", psb[:, b * 128:b * 128 + bl], qtm[:bl, b, c * 128:(c + 1) * 128], identb[:bl, :bl])
                        for m in range(2):
                            self.act("mul", qT[:, c, m, :ntok], psb[:, 0:ntok], cSame[:, m * 64:m * 64 + 1])
                        for b in range(nbk):
                            self.pe("transpose", psb[:, 512 + b * 128:512 + b * 128 + bl], ktm[:bl, b, c * 128:(c + 1) * 128], identb[:bl, :bl])
                        self.dve("tensor_copy", kTt[:, c, :ntok], psb[:, 512:512 + ntok])
                    self.dma("sp", V(kT_s.t[:, kb0:kb0 + ntok].rearrange("(c p) s -> p c s", p=128), (kT_s.buf,)), kTt[:, :, :ntok])

                    nk = kb0 + ntok
                    nkb = (nk + 127) // 128
                    sbank = (ps[0], ps[1], ps[2], ps[5]) if self.poolsum else (ps[0], ps[1], ps[2], ps[0])
                    for hh in range(4):
                        units = []
                        for ks in range(0, nk, 512):
                            kn = min(512, nk - ks)
                            for j in range((kn + 127) // 128):
                                units.append((ks, kn, j, min(128, kn - j * 128), (ks + j * 128) // 128))

                        def stage1(ui):
                            ks, kn, j, kr, kb = units[ui]
                            sbi = (ks // 512) % 2
                            cur_k, cur_v = kblk[sbi], vblk[sbi]
                            if j == 0:
                                self.dma("sp", cur_k[:, :kn], V(kT_s.t[hh * 128:(hh + 1) * 128, ks:ks + kn], (kT_s.buf,)))
                                if kn == 512:
                                    self.dma("sp", cur_v.v(), V(v_s.t[ks:ks + 512, hh * 128:(hh + 1) * 128]
                                                                .rearrange("(j p) d -> p j d", p=128), (v_s.buf,)))
                                else:
                                    for jj in range((kn + 127) // 128):
                                        krr = min(128, kn - jj * 128)
                                        self.dma("sp", cur_v[:krr, jj, :], V(v_s.t[ks + jj * 128:ks + jj * 128 + krr, hh * 128:(hh + 1) * 128], (v_s.buf,)))
                            diag = cx.masked and kb * 128 >= kb0
                            for m in range(2):
                                pss = sbank[(ui % 2) * 2 + m] if self.poolsum else ps[(2 * ui + m) % 3]
                                pt = PT[(ui % 2) * 2 + m]
                                self.pe("matmul", pss[:kr, :ntok], cur_k[:, j * 128:j * 128 + kr],
                                        qT[:, hh, m, :ntok], start=True, stop=True)
                                self.act("activation", pt[:kr, :ntok], pss[:kr, :ntok], AF.Exp, scale=0.125)
                                if diag:
                                    self.dve("tensor_tensor", pt[:kr, :ntok], pt[:kr, :ntok], amask[:kr, kb - kb0 // 128, :ntok], ALU.mult)

                        def stage2(ui):
                            ks, kn, j, kr, kb = units[ui]
                            cur_v = vblk[(ks // 512) % 2]
                            first, last = ui == 0, ui == len(units) - 1
                            for m in range(2):
                                pt = PT[(ui % 2) * 2 + m]
                                self.pe("matmul", ps[3 + m][:, :ntok], cur_v[:kr, j, :], pt[:kr, :ntok], start=first, stop=last)
                                if not self.poolsum:
                                    self.pe("matmul", ps[5 + m][:, :ntok], onesb[:kr, :], pt[:kr, :ntok], start=first, stop=last)
                                elif first:
                                    self._op("pool", "tensor_copy", osb[m][:, :ntok], pt[:, :ntok])
                                else:
                                    self._op("pool", "tensor_tensor", osb[m][:kr, :ntok], osb[m][:kr, :ntok], pt[:kr, :ntok], ALU.add)

                        stage1(0)
                        for ui in range(1, len(units)):
                            stage1(ui)
                            stage2(ui - 1)
                        stage2(len(units) - 1)
                        n_ = slice(0, ntok)
                        if self.poolsum:
                            for m in range(2):
                                self.pe("matmul", ps[m][:, n_], consts[:, 1, :], osb[m][:, n_], start=True, stop=True)
                            self.dve("reciprocal", osb[0][:, n_], ps[0][:, n_])
                            self.dve("reciprocal", osb[1][:, n_], ps[1][:, n_])
                        else:
                            for m in range(2):
                                self.act("activation", osb[m][:, n_], ps[5 + m][:, n_], AF.Ln)
                                self.act("activation", osb[m][:, n_], osb[m][:, n_], AF.Exp, scale=-1.0)
                        self.dve("tensor_tensor", osb[0][:, n_], osb[0][:, n_], ps[3][:, n_], ALU.mult)
                        self.dve("tensor_tensor", osb[1][:, n_], osb[1][:, n_], ps[4][:, n_], ALU.mult)
                        self.dve("scalar_tensor_tensor", osb[2][:, n_], osb[1][:, n_], lamt[:, l:l + 1], osb[0][:, n_], ALU.mult, ALU.add)
                        self.act("activation", sq[0][:, n_], osb[2][:, n_], AF.Square)
                        self.pe("matmul", ps[2][:, n_], onesb.v(), sq[0][:, n_], start=True, stop=True)
                        self.act("activation", osb[0][:, n_], ps[2][:, n_], AF.Ln, bias=colf(0), scale=1.0 / 128)
                        self.act("activation", osb[0][:, n_], osb[0][:, n_], AF.Exp, scale=-0.5)
                        self.dve("scalar_tensor_tensor", mixT[:, 2 + hh, n_], osb[2][:, n_], col("b_norm", l), osb[0][:, n_],
                                 ALU.mult, ALU.mult)

                if "A" not in self.stages:
                    for c in (0, 1):
                        self.dve("memset", mixT[:, c, :], 0.0)
                else:
                    one_col = consts[:, 1, 0:1]
                    self.dve("tensor_copy", aq[:, :, 0:3], cx.acarry[l].v())

                    def epi_aqkv(m, pv):
                        self.act("copy", aq[:, m, 3:3 + ntok], pv)
                    dense_fm(w_in, 8, O_AQKV, 768, xb, ntok, epi_aqkv)
                    self.dve("tensor_copy", cx.acarry[l].v(), aq[:, :, ntok:ntok + 3])
                    if cx.is_last:
                        for m in range(6):
                            self.dma("sp", V(cx.aconv_out.t[l, :, m * 128:(m + 1) * 128].rearrange("j p -> p j"), ()),
                                     cx.acarry[l][:, m, :], final=True, allow_slow_non_contiguous=True)

                    def epi_az(m, pv):
                        self.act("activation", zs[:, m, :ntok], pv, AF.Silu)
                    if self.a1 >= 2: dense_fm(w_in, 8, O_AZ, 256, xb, ntok, epi_az)

                    def epi_ab(blk, bt, pv):
                        self.act("copy", abtm[:bt, blk, :], pv)
                    if self.a1 >= 3: dense_tm(w_in, O_AA, 8, ntok, epi_ab)
                    for m in (range(6) if self.a1 >= 4 else ()):
                        tf = tmpf[m % 2]
                        self.dve("tensor_scalar_mul", tf[:, :ntok], aq[:, m, 0:ntok], cols2[:, l, m * 4:m * 4 + 1])
                        for j in (1, 2, 3):
                            self.dve("scalar_tensor_tensor", tf[:, :ntok], aq[:, m, j:j + ntok],
                                     cols2[:, l, m * 4 + j:m * 4 + j + 1], tf[:, :ntok], ALU.mult, ALU.add)
                        self.act("activation", ac[:, m, :ntok], tf[:, :ntok], AF.Silu)
                    for m in (range(4) if self.a1 >= 5 else ()):
                        tf = tmpf[m % 2]
                        self.act("activation", tf[:, :ntok], ac[:, m, :ntok], AF.Square)
                        self.pe("matmul", ps[2][:, :ntok], cSame, tf[:, :ntok], start=True, stop=True)
                        self.act("activation", tf[:, :ntok], ps[2][:, :ntok], AF.Ln, bias=colf(0), scale=1.0)
                        self.act("activation", tf[:, :ntok], tf[:, :ntok], AF.Exp, scale=-0.5)
                        self.dve("scalar_tensor_tensor", qkn[:, m, :ntok], ac[:, m, :ntok], 0.125 if m < 2 else 1.0,
                                 tf[:, :ntok], ALU.mult, ALU.mult)
                    nbk = (ntok + 127) // 128
                    btl = min(128, ntok)
                    if self.a1 >= 6: self.dve("tensor_tensor", astep[:btl, :nbk, :], abtm[:btl, :nbk, 0:4],
                             rowp[:btl, l, 4:8].unsq(1).bc([btl, nbk, 4]), ALU.add)
                    if self.a1 >= 7: self.act("activation", astep[:btl, :nbk, :], astep[:btl, :nbk, :], AF.Exp)
                    if self.a1 >= 8: self.act("activation", astep[:btl, :nbk, :], astep[:btl, :nbk, :], AF.Ln, bias=one_col[:btl, :], scale=1.0)
                    if self.a1 >= 9: self.dve("tensor_tensor", astep[:btl, :nbk, :], astep[:btl, :nbk, :],
                             rowp[:btl, l, 0:4].unsq(1).bc([btl, nbk, 4]), ALU.mult)
                    if self.a1 >= 9: self.act("activation", beta[:btl, :nbk, :], abtm[:btl, :nbk, 4:8], AF.Sigmoid)

                    for blk in (range(nbk) if self.asub >= 2 else ()):
                        bt = min(128, ntok - blk * 128)
                        c0 = blk * 128
                        chunks = [(r0, min(r0 + cx.CH, bt)) for r0 in range(0, bt, cx.CH)]
                        for i, (src, m) in enumerate(((qkn, 2), (qkn, 3), (ac, 4), (ac, 5))):
                            self.pe("transpose", ps[3][:bt, i * 128:(i + 1) * 128], src[:, m, c0:c0 + bt], ident)
                        self.act("copy", kvtm[:bt, blk, :], ps[3][:bt, :])
                        ktm_ = kvtm[:bt, blk, 0:256].r("p (h d) -> p h d", h=4)
                        vtm_ = kvtm[:bt, blk, 256:512].r("p (h d) -> p h d", h=4)
                        a_b = astep[:bt, blk, :]
                        self.dve("tensor_tensor", aL[:bt, :, :bt], cL[:bt, :bt].unsq(1).bc([bt, 4, bt]),
                                 a_b.unsq(2).bc([bt, 4, bt]), ALU.mult)
                        self.dve("tensor_tensor", aU[:bt, :, :bt], cU[:bt, :bt].unsq(1).bc([bt, 4, bt]),
                                 a_b.unsq(2).bc([bt, 4, bt]), ALU.mult)
                        p0 = ps[0][:bt, :].r("p (h s) -> p h s", h=4)[:, :, :bt]
                        p1 = ps[1][:bt, :].r("p (h s) -> p h s", h=4)[:, :, :bt]
                        for hh in range(4):
                            self.pe("matmul", p0[:, hh, :], cU[:bt, :bt], aL[:bt, hh, :bt], start=True, stop=True)
                        for hh in range(4):
                            self.pe("matmul", p1[:, hh, :], cL[:bt, :bt], aU[:bt, hh, :bt], start=True, stop=True)
                        self.act("activation", decA[:bt, :, :bt], p0, AF.Exp)
                        self.act("activation", decT[:bt, :, :bt], p1, AF.Exp)
                        self.dve("tensor_tensor", decA[:bt, :, :bt], decA[:bt, :, :bt], cL[:bt, :bt].unsq(1).bc([bt, 4, bt]), ALU.mult)
                        self.dve("tensor_tensor", decT[:bt, :, :bt], decT[:bt, :, :bt], cU[:bt, :bt].unsq(1).bc([bt, 4, bt]), ALU.mult)
                        self.pe("matmul", ps[2][:bt, 0:4], cU[:bt, :bt], a_b, start=True, stop=True)
                        self.pe("matmul", ps[2][:bt, 4:8], cL[:bt, :bt], a_b, start=True, stop=True)
                        for ci in range(len(chunks)):
                            for e in range(2):
                                self.pe("matmul", ps[2][:, 8 + 2 * ci:10 + 2 * ci], consts[:bt, 7 + 2 * ci + e, :],
                                        astep[:bt, blk, e::2], start=(e == 0), stop=(e == 1))
                        self.act("activation", eg[:bt, 0:8], ps[2][:bt, 0:8], AF.Exp)
                        self.act("activation", egt.v().r("p c r -> p (c r)")[:, 0:2 * len(chunks)],
                                 ps[2][:, 8:8 + 2 * len(chunks)], AF.Exp)
                        self.dve("tensor_tensor", bkg[:bt, :], beta[:bt, blk, :], eg[:bt, 0:4], ALU.mult)
                        if self.asub < 3: continue
                        pk = ps[3][:bt, :].r("p (h s) -> p h s", h=4)[:, :, :bt]
                        import os
                        kkv = os.environ.get("KKV", "")
                        for hh in range(4):
                            e, pr = hh % 2, hh // 2
                            if kkv == "even" and e == 1:
                                continue
                            if kkv == "odd" and e == 0:
                                continue
                            self.dve("tensor_scalar_mul", kzb[:, hh, :bt], qkn[:, 2 + pr, c0:c0 + bt], cSame[:, e * 64:e * 64 + 1])
                            self.pe("matmul", pk[:, hh, :], kzb[:, hh, :bt], qkn[:, 2 + pr, c0:c0 + bt], start=True, stop=True)
                        import os
                        av = os.environ.get("AV", "")
                        for hh in range(4):
                            if av == "nodve":
                                continue
                            if av == "fsc":
                                self.dve("scalar_tensor_tensor", Am[:bt, hh, :bt], pk[:, hh, :], 1.0,
                                         decA[:bt, hh, :bt], ALU.mult, ALU.mult)
                                continue
                            if av == "tt":
                                self.dve("tensor_tensor", Am[:bt, hh, :bt], pk[:, hh, :], decA[:bt, hh, :bt], ALU.mult)
                                continue
                            self.dve("scalar_tensor_tensor", Am[:bt, hh, :bt], pk[:, hh, :], beta[:bt, blk, hh:hh + 1],
                                     decA[:bt, hh, :bt], ALU.mult, ALU.mult)
                        if self.a3 < 2: continue
                        pB = ps[0][:bt, :].r("p (h s) -> p h s", h=4)[:, :, :bt]
                        for hh in range(4):
                            self.pe("transpose", pB[:, hh, :], Am[:bt, hh, :bt], ident[:bt, :bt])
                        self.act("copy", Xa[0][:bt, :, :bt], pB)
                        self.dve("tensor_tensor", Rm[:bt, :, :bt], ident[:bt, :bt].unsq(1).bc([bt, 4, bt]), Xa[0][:bt, :, :bt], ALU.subtract)
                        self.dve("tensor_copy", XTa[0][:bt, :, :bt], Am[:bt, :, :bt])
                        neumann(bt, Xa[0], XTa[0], Rm, cx.CH)
                        self.dve("tensor_tensor", vb_[:bt, :, :], vtm_, beta[:bt, blk, :].unsq(2).bc([bt, 4, 64]), ALU.mult)
                        for hh in range(4):
                            e = hh % 2
                            self.dve("tensor_tensor", kbz[:bt, hh, e * 64:(e + 1) * 64], ktm_[:, hh, :],
                                     bkg[:bt, hh:hh + 1].bc([bt, 64]), ALU.mult)
                            for ci, (r0, r1) in enumerate(chunks):
                                self.dve("tensor_tensor", ktz[r0:r1, ci, hh, e * 64:(e + 1) * 64], ktm_[r0:r1, hh, :],
                                         eg[r0:r1, 4 + hh:5 + hh].bc([r1 - r0, 64]), ALU.mult)
                        for hh in range(4):
                            self.pe("matmul", ps[4][:bt, hh * 64:(hh + 1) * 64], Rm[:bt, hh, :bt], vb_[:bt, hh, :], start=True, stop=True)
                        for pr in range(2):
                            for e in range(2):
                                self.pe("matmul", ps[4][:, 256 + pr * 128:256 + pr * 128 + bt], kbz[:bt, 2 * pr + e, :],
                                        Rm[:bt, 2 * pr + e, :bt], start=(e == 0), stop=(e == 1))
                        self.act("copy", uval[:bt, :], ps[4][:bt, 0:256])
                        self.act("copy", wT[:, :, :bt], ps[4][:, 256:512].r("p (a t) -> p a t", a=2)[:, :, :bt])
                        pq = ps[5][:bt, :].r("p (h s) -> p h s", h=4)[:, :, :bt]
                        for hh in range(4):
                            e, pr = hh % 2, hh // 2
                            self.pe("matmul", pq[:, hh, :], kzb[:, hh, :bt], qkn[:, pr, c0:c0 + bt], start=True, stop=True)
                        self.dve("tensor_tensor", qkT[:bt, :, :bt], pq, decT[:bt, :, :bt], ALU.mult)
                        if self.asub < 5: continue
                        for ci, (r0, r1) in enumerate(chunks):
                            for pr in range(2):
                                self.pe("matmul", ps[6][:bt, pr * 128:(pr + 1) * 128], wT[:, pr, :bt], cx.Sd[l][:, pr, :], start=True, stop=True)
                            for pr in range(2):
                                self.pe("matmul", ps[6][:bt, 256 + pr * 128:256 + (pr + 1) * 128], qkn[:, pr, c0:c0 + bt],
                                        cx.Sd[l][:, pr, :], start=True, stop=True)
                            self.dve("tensor_tensor", unew[r0:r1, :], uval[r0:r1, :], ps[6][r0:r1, 0:256], ALU.subtract)
                            self.dve("tensor_tensor", oq[r0:r1, :].r("p (h d) -> p h d", h=4),
                                     ps[6][r0:r1, 256:512].r("p (h d) -> p h d", h=4),
                                     eg[r0:r1, 0:4].unsq(2).bc([r1 - r0, 4, 64]), ALU.mult)
                            for pr in range(2):
                                for e in range(2):
                                    hh = 2 * pr + e
                                    self.pe("matmul", ps[2][:, 64 + pr * 64:128 + pr * 64], ktz[:bt, ci, hh, :],
                                            unew[:bt, hh * 64:(hh + 1) * 64], start=(e == 0), stop=(e == 1))
                            for e in range(2):
                                sdv = cx.Sd[l][e * 64:(e + 1) * 64, :, e * 64:(e + 1) * 64]
                                self.dve("tensor_tensor", sdv, sdv, egt[e * 64:(e + 1) * 64, ci, :].unsq(2).bc([64, 2, 64]), ALU.mult)
                                self.dve("tensor_tensor", sdv, sdv, ps[2][e * 64:(e + 1) * 64, 64:192].r("p (a d) -> p a d", a=2), ALU.add)
                        if self.asub < 6: continue
                        for hh in range(4):
                            self.pe("matmul", ps[5][:bt, hh * 64:(hh + 1) * 64], qkT[:bt, hh, :bt], unew[:bt, hh * 64:(hh + 1) * 64],
                                    start=True, stop=True)
                        self.dve("tensor_tensor", otm[:bt, :], oq[:bt, :], ps[5][:bt, 0:256], ALU.add)
                        self.dve("tensor_tensor", o2[:bt, :], otm[:bt, :], otm[:bt, :], ALU.mult)
                        self.dve("tensor_reduce", ssq[:bt, :], o2[:bt, :].r("p (h d) -> p h d", h=4), AX.X, ALU.add)
                        self.act("activation", ssq[:bt, :], ssq[:bt, :], AF.Ln, bias=colf(0)[:bt, :], scale=1.0 / 64)
                        self.act("activation", ssq[:bt, :], ssq[:bt, :], AF.Exp, scale=-0.5)
                        self.dve("tensor_tensor", ontm[:bt, blk, :].r("p (h d) -> p h d", h=4),
                                 otm[:bt, :].r("p (h d) -> p h d", h=4), ssq[:bt, :].unsq(2).bc([bt, 4, 64]), ALU.mult)
                    if self.asub < 9:
                        for c in (0, 1):
                            self.dve("memset", mixT[:, c, :], 0.0)
                    for pr in (range(2) if self.asub >= 9 else ()):
                        for blk in range(nbk):
                            bt = min(128, ntok - blk * 128)
                            self.pe("transpose", ps[3][:, blk * 128:blk * 128 + bt], ontm[:bt, blk, pr * 128:(pr + 1) * 128], ident[:bt, :bt])
                        self.dve("scalar_tensor_tensor", mixT[:, pr, :ntok], ps[3][:, :ntok], cols2[:, l, 24:25], zs[:, pr, :ntok],
                                 ALU.mult, ALU.mult)
                    if cx.is_last:
                        for e in range(2):
                            self.dma("sp", V(cx.adelta_out.t[l].rearrange("(pr e) k v -> e k pr v", e=2)[e], ()),
                                     cx.Sd[l][e * 64:(e + 1) * 64, :, e * 64:(e + 1) * 64], final=True)

                if "C" not in self.stages:
                    for c in (6, 7):
                        self.dve("memset", mixT[:, c, :], 0.0)
                else:
                    nbk = (ntok + 127) // 128
                    self.dve("memset", kapz.v(), 0.0)
                    self.dve("memset", cktz.v(), 0.0)
                    self.dve("memset", cbtz.v(), 0.0)
                    c2 = lambda a, b=None: cols2[:, l, a:(a + 1 if b is None else b)]
                    dests = [c_r[:, 0, :ntok], c_r[:, 1, :ntok], c_k[:, 0, :ntok], c_k[:, 1, :ntok],
                             c_v[:, 0, :ntok], c_v[:, 1, :ntok], c_6[:, :ntok], c_7[:, :ntok]]

                    def epi_c(m, pv):
                        cm = crawm[m % 2]
                        self.act("copy", cm[:, 1:1 + ntok], pv)
                        self.act("copy", cm[:, 0:1], cx.ccarry[l][:, m:m + 1])
                        self.act("copy", cx.ccarry[l][:, m:m + 1], cm[:, ntok:ntok + 1])
                        tf = tmpf[m % 2]
                        self.dve("tensor_tensor", tf[:, :ntok], cm[:, 0:ntok], cm[:, 1:1 + ntok], ALU.subtract)
                        self.dve("scalar_tensor_tensor", dests[m], tf[:, :ntok], c2(32 + m), cm[:, 1:1 + ntok], ALU.mult, ALU.add)
                    dense_fm(w_in, 8, O_C, 1024, xb, ntok, epi_c)
                    if cx.is_last:
                        self.dma("sp", V(cx.cshift_out.t[l].rearrange("(c p) -> p c", p=128), ()), cx.ccarry[l].v(), final=True,
                                 allow_slow_non_contiguous=True)
                    self.act("activation", sq[0][:, :ntok], c_6[:, :ntok], AF.Tanh)
                    self.act("copy", sq[1][:, :ntok], c_6[:, :ntok])
                    for m in range(2):
                        self.pe("matmul", ps[0][:, :ntok], cw3[:, l, 0, m * 128:(m + 1) * 128], sq[0][:, :ntok], start=True, stop=True)
                        self.act("activation", c_wl[:, m, :ntok], ps[0][:, :ntok], AF.Sigmoid, bias=c2(40 + m), scale=1.0)
                        self.dve("tensor_scalar_mul", c_wl[:, m, :ntok], c_wl[:, m, :ntok], -math.exp(-0.5))
                        self.pe("matmul", ps[1][:, :ntok], cw3[:, l, 1, m * 128:(m + 1) * 128], sq[1][:, :ntok], start=True, stop=True)
                        self.act("activation", c_a[:, m, :ntok], ps[1][:, :ntok], AF.Sigmoid, bias=c2(42 + m), scale=1.0)
                    self.act("activation", sq[0][:, :ntok], c_7[:, :ntok], AF.Sigmoid)
                    for m in range(2):
                        self.pe("matmul", ps[0][:, :ntok], cw3[:, l, 2, m * 128:(m + 1) * 128], sq[0][:, :ntok], start=True, stop=True)
                        self.act("copy", c_g[:, m, :ntok], ps[0][:, :ntok])
                    for m in range(2):
                        tf = tmpf[m % 2]
                        self.dve("tensor_scalar_mul", c_kk[:, m, :ntok], c_k[:, m, :ntok], c2(44 + m))
                        self.act("activation", tf[:, :ntok], c_kk[:, m, :ntok], AF.Square)
                        self.pe("matmul", ps[2][:, :ntok], cSame, tf[:, :ntok], start=True, stop=True)
                        self.act("activation", tf[:, :ntok], ps[2][:, :ntok], AF.Ln, bias=colf(0), scale=1.0)
                        self.act("activation", tf[:, :ntok], tf[:, :ntok], AF.Exp, scale=-0.5)
                        self.dve("tensor_tensor", c_kk[:, m, :ntok], c_kk[:, m, :ntok], tf[:, :ntok], ALU.mult)
                        self.dve("tensor_scalar", tf[:, :ntok], c_a[:, m, :ntok], c2(46 + m), c2(54 + m), ALU.mult, ALU.add)
                        self.dve("tensor_tensor", c_km[:, m, :ntok], c_k[:, m, :ntok], tf[:, :ntok], ALU.mult)
                        self.dve("tensor_tensor", c_b[:, m, :ntok], c_a[:, m, :ntok], c_kk[:, m, :ntok], ALU.mult)
                        self.dve("scalar_tensor_tensor", tf[:, :ntok], c_r[:, m, :ntok], c2(48 + m), c_km[:, m, :ntok], ALU.mult, ALU.mult)
                        self.pe("matmul", ps[2][:, :ntok], cSame, tf[:, :ntok], start=True, stop=True)
                        self.dve("tensor_tensor", c_bon[:, m, :ntok], ps[2][:, :ntok], c_v[:, m, :ntok], ALU.mult)

                    for blk in range(nbk):
                        bt = min(128, ntok - blk * 128)
                        c0 = blk * 128
                        chunks = [(r0, min(r0 + cx.CH, bt)) for r0 in range(0, bt, cx.CH)]
                        for i, src in enumerate((c_wl, c_km, c_b, c_kk, c_v)):
                            pst = ps[3] if i % 2 == 0 else ps[4]
                            for m in range(2):
                                self.pe("transpose", pst[:bt, m * 128:(m + 1) * 128], src[:, m, c0:c0 + bt], ident)
                            self.act("copy", tm5[:bt, i, :], pst[:bt, 0:256])
                        wl_tm, km_tm, b_tm, kk_tm, v_tm = (tm5[:bt, i, :] for i in range(5))
                        self.pe("matmul", ps[0][:bt, 0:256], cU[:bt, :bt], wl_tm, start=True, stop=True)
                        for m in range(2):
                            self.pe("matmul", ps[1][:, m * 128:m * 128 + bt], tm5[:bt, 0, m * 128:(m + 1) * 128], cU[:bt, :bt],
                                    start=True, stop=True)
                        eng = cgt[:bt, 0, :]
                        egm = cgt[:bt, 1, :]
                        self.act("activation", eng, ps[0][:bt, 0:256], AF.Exp, scale=-1.0)
                        self.act("activation", egm, ps[0][:bt, 0:256], AF.Exp)
                        self.act("activation", cx0[:bt, :], wl_tm, AF.Exp, scale=-1.0)
                        self.dve("tensor_tensor", egm, egm, cx0[:bt, :], ALU.mult)
                        for hh in range(4):
                            e = hh % 2
                            hs = slice(hh * 64, (hh + 1) * 64)
                            self.dve("tensor_tensor", kapz[:bt, hh, e * 64:(e + 1) * 64], kk_tm[:, hs], egm[:, hs], ALU.mult)
                            for ci, (r0, r1) in enumerate(chunks):
                                self.dve("tensor_tensor", cktz[r0:r1, ci, hh, e * 64:(e + 1) * 64], km_tm[r0:r1, hs], eng[r0:r1, hs], ALU.mult)
                                self.dve("tensor_tensor", cbtz[r0:r1, ci, hh, e * 64:(e + 1) * 64], b_tm[r0:r1, hs], eng[r0:r1, hs], ALU.mult)
                        pgT = ps[1][:, 0:256].r("p (m t) -> p m t", m=2)[:, :, :bt]
                        egT = C1T[:, 0:2, :bt]
                        engT = C1T[:, 2:4, :bt]
                        egmT = C2T[:, 0:2, :bt]
                        self.act("activation", egT, pgT, AF.Exp)
                        self.act("activation", engT, pgT, AF.Exp, scale=-1.0)
                        self.dve("tensor_tensor", egmT, pgT, c_wl[:, :, c0:c0 + bt], ALU.subtract)
                        self.act("activation", egmT, egmT, AF.Exp)
                        self.dve("tensor_tensor", rTt[:, :, :bt], c_r[:, :, c0:c0 + bt], egT, ALU.mult)
                        self.dve("tensor_tensor", kapTt[:, :, :bt], c_kk[:, :, c0:c0 + bt], egmT, ALU.mult)
                        for ci, (r0, r1) in enumerate(chunks):
                            self.act("copy", pcc[:, ci, :], egT[:, :, r1 - 1])
                        for hh in range(4):
                            e, pr = hh % 2, hh // 2
                            self.dve("scalar_tensor_tensor", kTm[:, hh, :bt], c_km[:, pr, c0:c0 + bt], cSame[:, e * 64:e * 64 + 1],
                                     engT[:, pr, :], ALU.mult, ALU.mult)
                            self.dve("scalar_tensor_tensor", bTm[:, hh, :bt], c_b[:, pr, c0:c0 + bt], cSame[:, e * 64:e * 64 + 1],
                                     engT[:, pr, :], ALU.mult, ALU.mult)
                        def hview(pt):
                            return pt[:bt, :].r("p (h s) -> p h s", h=4)[:, :, :bt]
                        pB, pBm, pC1, pC2 = hview(ps[0]), hview(ps[3]), hview(ps[4]), hview(ps[5])
                        for hh in range(4):
                            pr = hh // 2
                            self.pe("matmul", pB[:, hh, :], bTm[:, hh, :bt], kapTt[:, pr, :bt], start=True, stop=True)
                        for hh in range(4):
                            pr = hh // 2
                            self.pe("matmul", pBm[:, hh, :], kTm[:, hh, :bt], kapTt[:, pr, :bt], start=True, stop=True)
                        for hh in range(4):
                            pr = hh // 2
                            self.pe("matmul", pC1[:, hh, :], kTm[:, hh, :bt], rTt[:, pr, :bt], start=True, stop=True)
                        for hh in range(4):
                            pr = hh // 2
                            self.pe("matmul", pC2[:, hh, :], bTm[:, hh, :bt], rTt[:, pr, :bt], start=True, stop=True)
                        msu = cSU[:bt, :bt].unsq(1).bc([bt, 4, bt])
                        mu_ = cU[:bt, :bt].unsq(1).bc([bt, 4, bt])
                        X, XT, R = Xa[0], XTa[0], cR2
                        self.dve("tensor_tensor", X[:bt, :, :bt], pB, msu, ALU.mult)
                        self.dve("tensor_tensor", cBmT[:bt, :, :bt], pBm, msu, ALU.mult)
                        self.dve("tensor_tensor", C1T[:bt, :, :bt], pC1, mu_, ALU.mult)
                        self.dve("scalar_tensor_tensor", C2T[:bt, :, :bt], pC2, -1.0, mu_, ALU.mult, ALU.mult)
                        pA = hview(ps[1])
                        for hh in range(4):
                            self.pe("transpose", pA[:, hh, :], X[:bt, hh, :bt], ident[:bt, :bt])
                        self.act("copy", XT[:bt, :, :bt], pA)
                        self.dve("tensor_tensor", R[:bt, :, :bt], ident[:bt, :bt].unsq(1).bc([bt, 4, bt]), X[:bt, :, :bt], ALU.subtract)
                        neumann(bt, X, XT, R, cx.CH)
                        for hh in range(4):
                            hs = slice(hh * 64, (hh + 1) * 64)
                            self.pe("matmul", ps[4][:bt, hs], cBmT[:bt, hh, :bt], v_tm[:, hs], start=True, stop=True)
                        self.act("copy", cx0[:bt, :], ps[4][:bt, 0:256])
                        for hh in range(4):
                            hs = slice(hh * 64, (hh + 1) * 64)
                            self.pe("matmul", ps[4][:bt, 256 + hh * 64:256 + (hh + 1) * 64], R[:bt, hh, :bt], cx0[:bt, hs], start=True, stop=True)
                        self.act("copy", cU0[:bt, :], ps[4][:bt, 256:512])
                        for pr in range(2):
                            for e in range(2):
                                self.pe("matmul", ps[5][:, pr * 128:pr * 128 + bt], kapz[:bt, 2 * pr + e, :], R[:bt, 2 * pr + e, :bt],
                                        start=(e == 0), stop=(e == 1))
                        self.act("copy", wT[:, :, :bt], ps[5][:, 0:256].r("p (a t) -> p a t", a=2)[:, :, :bt])
                        for ci in range(len(chunks)):
                            for pr in range(2):
                                for e in range(2):
                                    hh = 2 * pr + e
                                    self.pe("matmul", ps[5][:, 256 + ci * 128 + pr * 64:256 + ci * 128 + (pr + 1) * 64],
                                            cktz[:bt, ci, hh, :], v_tm[:, hh * 64:(hh + 1) * 64], start=(e == 0), stop=(e == 1))
                        self.act("copy", kvp[:, 0:len(chunks), :, :], ps[5][:, 256:256 + 128 * len(chunks)].r("p (c a d) -> p c a d", c=len(chunks), a=2))
                        for ci, (r0, r1) in enumerate(chunks):
                            for pr in range(2):
                                self.pe("matmul", ps[6][:bt, pr * 128:(pr + 1) * 128], wT[:, pr, :bt], cx.Tst[l][:, pr, :], start=True, stop=True)
                            for pr in range(2):
                                self.pe("matmul", ps[6][:bt, 256 + pr * 128:256 + (pr + 1) * 128], rTt[:, pr, :bt], cx.Tst[l][:, pr, :],
                                        start=True, stop=True)
                            self.dve("tensor_tensor", cU_[r0:r1, :], cU0[r0:r1, :], ps[6][r0:r1, 0:256], ALU.add)
                            self.dve("tensor_copy", cyq[r0:r1, :], ps[6][r0:r1, 256:512])
                            for e in range(2):
                                tdv = cx.Tst[l][e * 64:(e + 1) * 64, :, e * 64:(e + 1) * 64]
                                self.dve("tensor_tensor", tdv, tdv, kvp[e * 64:(e + 1) * 64, ci, :, :], ALU.add)
                            for pr in range(2):
                                for e in range(2):
                                    hh = 2 * pr + e
                                    self.pe("matmul", ps[2][:, 64 + pr * 64:128 + pr * 64], cbtz[:bt, ci, hh, :],
                                            cU_[:bt, hh * 64:(hh + 1) * 64], start=(e == 0), stop=(e == 1))
                            for e in range(2):
                                tdv = cx.Tst[l][e * 64:(e + 1) * 64, :, e * 64:(e + 1) * 64]
                                self.dve("tensor_tensor", tdv, tdv, ps[2][e * 64:(e + 1) * 64, 64:192].r("p (a d) -> p a d", a=2), ALU.subtract)
                                self.dve("tensor_tensor", tdv, tdv, pcc[e * 64:(e + 1) * 64, ci, :].unsq(2).bc([64, 2, 64]), ALU.mult)
                        for hh in range(4):
                            hs = slice(hh * 64, (hh + 1) * 64)
                            self.pe("matmul", ps[4][:bt, hs], C1T[:bt, hh, :bt], v_tm[:, hs], start=True, stop=False)
                            self.pe("matmul", ps[4][:bt, hs], C2T[:bt, hh, :bt], cU_[:bt, hs], start=False, stop=True)
                        self.dve("tensor_tensor", cytm[:bt, :], cyq[:bt, :], ps[4][:bt, 0:256], ALU.add)
                        y3 = cytm[:bt, :].r("p (h d) -> p h d", h=4)
                        self.dve("tensor_reduce", cst[:bt, 0:4], y3, AX.X, ALU.add)
                        self.dve("tensor_scalar_mul", cst[:bt, 0:4], cst[:bt, 0:4], 1.0 / 64)
                        self.dve("tensor_tensor", y3, y3, cst[:bt, 0:4].unsq(2).bc([bt, 4, 64]), ALU.subtract)
                        self.dve("tensor_tensor", cyq[:bt, :], cytm[:bt, :], cytm[:bt, :], ALU.mult)
                        self.dve("tensor_reduce", cst[:bt, 4:8], cyq[:bt, :].r("p (h d) -> p h d", h=4), AX.X, ALU.add)
                        self.act("activation", cst[:bt, 4:8], cst[:bt, 4:8], AF.Ln, bias=colf(9)[:bt, :], scale=1.0 / 64)
                        self.act("activation", cst[:bt, 4:8], cst[:bt, 4:8], AF.Exp, scale=-0.5)
                        self.dve("tensor_tensor", cyn(blk)[:bt, :].r("p (h d) -> p h d", h=4), y3,
                                 cst[:bt, 4:8].unsq(2).bc([bt, 4, 64]), ALU.mult)
                    for pr in range(2):
                        for blk in range(nbk):
                            bt = min(128, ntok - blk * 128)
                            self.pe("transpose", ps[3][:, blk * 128:blk * 128 + bt], cyn(blk)[:bt, pr * 128:(pr + 1) * 128], ident[:bt, :bt])
                        tf = tmpf[pr]
                        self.dve("scalar_tensor_tensor", tf[:, :ntok], ps[3][:, :ntok], c2(50 + pr), c_bon[:, pr, :ntok], ALU.mult, ALU.add)
                        self.dve("scalar_tensor_tensor", mixT[:, 6 + pr, :ntok], tf[:, :ntok], c2(52 + pr), c_g[:, pr, :ntok], ALU.add, ALU.mult)
                    if cx.is_last:
                        for pr in range(2):
                            self.pe("transpose", ps[3][:, pr * 128:(pr + 1) * 128], cx.Tst[l][:, pr, :], ident)
                        self.act("copy", wT.v(), ps[3][:, 0:256].r("p (a t) -> p a t", a=2))
                        for e in range(2):
                            self.dma("sp", V(cx.cwkv_out.t[l].rearrange("(pr e) v k -> e v pr k", e=2)[e], ()),
                                     wT[e * 64:(e + 1) * 64, :, e * 64:(e + 1) * 64], final=True)

                def epi_res(m, pv):
                    self.dve("tensor_tensor", h[:, m, :ntok], h[:, m, :ntok], pv, ALU.add)
                dense_fm(("w_out", l), 8, 0, D_MODEL, mixT, ntok, epi_res)

                rmsnorm_to_xb(ntok, lambda k: col("norm_ffn", l, k))

                def epi_ff1(m, pv):
                    tf = tmpf[m % 2]
                    self.act("activation", tf[:, :ntok], pv, AF.Relu)
                    self.dve("tensor_tensor", uT[:, m, :ntok], tf[:, :ntok], tf[:, :ntok], ALU.mult)
                dense_fm(("w_ff1", l), 8, 0, D_FF, xb, ntok, epi_ff1)
                dense_fm(("w_ff2", l), 32, 0, D_MODEL, uT, ntok, epi_res)

                rmsnorm_to_xb(ntok, lambda k: col("norm_ple", l, k))

                def epi_gate(m, pv):
                    self.act("activation", gate[:, m, :ntok], pv, AF.Sigmoid)
                dense_fm(("w_ple_gate", l), 8, 0, D_MODEL, xb, ntok, epi_gate)
                self.dma("sp", ptok[:bl, :nbk, :], V(cx.p_src(l).rearrange("(b p) d -> p b d", p=bl), ()))
                for k in range(2):
                    for b in range(nbk):
                        self.pe("transpose", ps[3][:, b * 128:b * 128 + bl], ptok[:bl, b, k * 128:(k + 1) * 128], ident[:bl, :bl])
                    self.act("copy", pT[:, k, :ntok], ps[3][:, :ntok])

                def epi_ple(m, pv):
                    self.dve("tensor_tensor", tmpf[m % 2][:, :ntok], gate[:, m, :ntok], pv, ALU.mult)
                    self.dve("tensor_tensor", h[:, m, :ntok], h[:, m, :ntok], tmpf[m % 2][:, :ntok], ALU.add)
                dense_fm(("w_ple_proj", l), 2, 0, D_MODEL, pT, ntok, epi_ple)

            for k in range(8):
                s_ = sq[k % 2]
                self.act("activation", s_[:, :ntok], h[:, k, :ntok], AF.Square)
                self.pe("matmul", ps[2][:, :ntok], onesb.v(), s_[:, :ntok], start=(k == 0), stop=(k == 7))
            self.act("activation", rstd[:, :ntok], ps[2][:, :ntok], AF.Ln, bias=colf(0), scale=1.0 / D_MODEL)
            self.act("activation", rstd[:, :ntok], rstd[:, :ntok], AF.Exp, scale=-0.5)
            for k in range(8):
                tf = tmpf[k % 2]
                self.dve("scalar_tensor_tensor", tf[:, :ntok], h[:, k, :ntok], cols[:, 51 + k:52 + k], rstd[:, :ntok], ALU.mult, ALU.mult)
                for b in range(nbk):
                    self.pe("transpose", ps[3][:bl, b * 128:(b + 1) * 128], tf[:, b * 128:b * 128 + bl], ident)
                self.act("copy", xtok[:bl, :nbk, k * 128:(k + 1) * 128], ps[3][:bl, :nbk * 128].r("p (b f) -> p b f", b=nbk))
            self.dma("sp", V(cx.y_dst.rearrange("(b p) d -> p b d", p=bl), ()), xtok[:bl, :nbk, :], final=True)

        class Cx:
            pass

        if "prompt" in self.stages:
            for t in range(NT):
                cx = Cx()
                t0 = t * TT
                cx.ntok = TT; cx.CH = min(64, SEQ); cx.key_base = t0; cx.masked = True; cx.is_last = (t == NT - 1)
                cx.x_src = x_in.t[t0:t0 + TT, :]
                cx.rope_src = rope_in.t[t0:t0 + TT, :]
                cx.p_src = lambda l, t0=t0: p_in.t[l, t0:t0 + TT, :]
                cx.y_dst = y_out.t[t0:t0 + TT, :]
                cx.bk_dst = lambda l, t0=t0: bk_out.t[l, t0:t0 + TT, :]
                cx.bv_dst = lambda l, t0=t0: bv_out.t[l, t0:t0 + TT, :]
                cx.kT_scr, cx.v_scr = kT_scr, v_scr
                cx.acarry, cx.Sd, cx.Tst, cx.ccarry = acarry, Sd, Tst, ccarry
                cx.aconv_out, cx.adelta_out, cx.cshift_out, cx.cwkv_out = aconv_out, adelta_out, cshift_out, cwkv_out
                tile_body(cx)

        if "sample" in self.stages:
            DEC, PAST = self.DEC, self.PAST
            xs_in = self.din("xs", [DEC, D_MODEL])
            ps_in = self.din("psm", [DEPTH, DEC, PLE_DIM])
            ropes_in = self.din("rope_s", [DEC, 64])
            ck_in = self.din("cache_k", [DEPTH, PAST, 512])
            cv_in = self.din("cache_v", [DEPTH, PAST, 512])
            sconv_in = self.din("st_conv", [DEPTH, 3, 768])
            sdelta_in = self.din("st_delta", [DEPTH, 4, 64, 64])
            sshift_in = self.din("st_shift", [DEPTH, 1024])
            swkv_in = self.din("st_wkv", [DEPTH, 4, 64, 64])
            ys_out = self.dout("y_s", [DEC, D_MODEL])
            bks_out = self.dout("b_k_s", [DEPTH, DEC, 512])
            bvs_out = self.dout("b_v_s", [DEPTH, DEC, 512])
            aconvs_out = self.dout("a_conv_s", [DEPTH, 3, 768])
            adeltas_out = self.dout("a_delta_s", [DEPTH, 4, 64, 64])
            cshifts_out = self.dout("c_shift_s", [DEPTH, 1024])
            cwkvs_out = self.dout("c_wkv_s", [DEPTH, 4, 64, 64])
            kT_scr_s = [self.dscr(f"kT_scr_s{l}", [512, PAST + DEC], BF16) for l in range(DEPTH)]
            v_scr_s = [self.dscr(f"v_scr_s{l}", [PAST + DEC, 512], BF16) for l in range(DEPTH)]
            psb = psbT.v()
            for l in range(DEPTH):
                for m in range(6):
                    self.dma("sp", acarry_s[l][:, m, :], V(sconv_in.t[l, :, m * 128:(m + 1) * 128].rearrange("j p -> p j"), ()),
                             allow_slow_non_contiguous=True)
                self.dma("sp", ccarry_s[l].v(), V(sshift_in.t[l].rearrange("(c p) -> p c", p=128), ()), allow_slow_non_contiguous=True)
                self.dve("memset", Sd_s[l].v(), 0.0)
                self.dve("memset", wT.v(), 0.0)
                for e in range(2):
                    self.dma("sp", Sd_s[l][e * 64:(e + 1) * 64, :, e * 64:(e + 1) * 64],
                             V(sdelta_in.t[l].rearrange("(pr e) k v -> e k pr v", e=2)[e], ()))
                    self.dma("sp", wT[e * 64:(e + 1) * 64, :, e * 64:(e + 1) * 64],
                             V(swkv_in.t[l].rearrange("(pr e) v k -> e v pr k", e=2)[e], ()))
                for pr in range(2):
                    self.pe("transpose", ps[3][:, pr * 128:(pr + 1) * 128], wT[:, pr, :], ident)
                self.act("copy", Tst_s[l].v(), ps[3][:, 0:256].r("p (a t) -> p a t", a=2))
                for g in range(PAST // 512):
                    for b in range(4):
                        r0 = g * 512 + b * 128
                        kf, vf = kfs[b % 2], vfs[b % 2]
                        self.dma("sp", kf.v(), V(ck_in.t[l, r0:r0 + 128, :], ()))
                        self.act("copy", ktm[:, b, :], kf.v())
                        self.dma("sp", vf.v(), V(cv_in.t[l, r0:r0 + 128, :], ()))
                        self.dve("tensor_copy", vtb[:, b, :], vf.v())
                    for c in range(4):
                        for b in range(4):
                            self.pe("transpose", psb[:, 512 + b * 128:512 + (b + 1) * 128], ktm[:, b, c * 128:(c + 1) * 128], identb.v())
                        self.dve("tensor_copy", kTt[:, c, :], psb[:, 512:1024])
                    self.dma("sp", V(kT_scr_s[l].t[:, g * 512:(g + 1) * 512].rearrange("(c p) s -> p c s", p=128), (kT_scr_s[l].buf,)), kTt.v())
                    self.dma("sp", V(v_scr_s[l].t[g * 512:(g + 1) * 512, :].rearrange("(b p) d -> p b d", p=128), (v_scr_s[l].buf,)), vtb.v())
            cx = Cx()
            cx.ntok = DEC; cx.CH = min(64, DEC); cx.key_base = PAST; cx.masked = False; cx.is_last = True
            cx.x_src = xs_in.t[:, :]
            cx.rope_src = ropes_in.t[:, :]
            cx.p_src = lambda l: ps_in.t[l, :, :]
            cx.y_dst = ys_out.t[:, :]
            cx.bk_dst = lambda l: bks_out.t[l, :, :]
            cx.bv_dst = lambda l: bvs_out.t[l, :, :]
            cx.kT_scr, cx.v_scr = kT_scr_s, v_scr_s
            cx.acarry, cx.Sd, cx.Tst, cx.ccarry = acarry_s, Sd_s, Tst_s, ccarry_s
            cx.aconv_out, cx.adelta_out, cx.cshift_out, cx.cwkv_out = aconvs_out, adeltas_out, cshifts_out, cwkvs_out
            tile_body(cx)

        self.S.emit(st)
        st.close()
        return nc


def _consts():
    c = np.zeros((128, 11, 128), np.float32)
    i = np.arange(128)
    same = (i[:, None] // 64) == (i[None, :] // 64)
    c[:, 0, :] = np.eye(128)
    c[:, 1, :] = 1.0
    c[:, 2, :] = (i[:, None] <= i[None, :]) & same
    c[:, 3, :] = (i[:, None] < i[None, :]) & same
    c[:, 4, :] = (i[:, None] > i[None, :]) & same
    c[:, 5, :] = (i[:, None] >= i[None, :]) & same
    c[:, 6, :] = same
    for ci in range(2):
        for e in range(2):
            c[:, 7 + 2 * ci + e, :] = ((i[:, None] // 64) == ci) & ((i[None, :] // 64) == e)
    return c


def _cols2(inp):
    c = np.zeros((128, DEPTH, 64), np.float32)
    for l in range(DEPTH):
        cw = inp["a_conv_w"][l]
        for m in range(6):
            c[:, l, m * 4:m * 4 + 4] = cw[:, m * 128:(m + 1) * 128].T
        c[:, l, 24] = np.tile(inp["a_norm"][l], 2)
        c[:, l, 32:40] = inp["c_mu"][l].reshape(8, 128).T
        for nm, base in (("c_w0", 40), ("c_a0", 42), ("c_k_k", 44), ("c_k_a", 46), ("c_r_k", 48), ("c_ln_w", 50), ("c_ln_b", 52)):
            c[:, l, base:base + 2] = inp[nm][l].reshape(2, 128).T
    return c


def _cw3(inp):
    w = np.zeros((128, DEPTH, 3, 256), np.float32)
    for l in range(DEPTH):
        w[0:64, l, 0, :] = inp["c_w_up"][l]
        w[64:128, l, 1, :] = inp["c_a_up"][l]
        w[:, l, 2, :] = inp["c_g_up"][l]
    return w


def _rowp(inp):
    return np.concatenate([inp["a_A_log"], inp["a_dt_bias"]], axis=1).astype(np.float32)


def _amask():
    kp = np.arange(128)[:, None]
    q = np.arange(512)[None, :]
    m = np.zeros((128, 4, 512), np.float32)
    for j in range(4):
        m[:, j, :] = (2 * j + kp // 64) <= (q // 64)
    return m.astype(ml_dtypes.bfloat16)


def _rope_table(pos):
    half = 32
    inv = (10000.0 ** (-2.0 * np.arange(half, dtype=np.float32) / 64)).astype(np.float32)
    ang = pos.astype(np.float32)[:, None] * inv[None, :]
    return np.concatenate([np.cos(ang), np.sin(ang)], axis=1).astype(np.float32)


def _cols(inp):
    c = np.zeros((128, 64), np.float32)
    for l in range(DEPTH):
        for nm, base in (("norm_mix", 0), ("norm_ffn", 8), ("norm_ple", 16)):
            c[:, l * 24 + base:l * 24 + base + 8] = inp[nm][l].reshape(8, 128).T
        lam_init = 0.8 - 0.6 * math.exp(-0.3 * l)
        c[:, 48 + l] = inp["b_norm"][l]
    c[:, 50] = NORM_EPS
    c[:, 59] = C_LN_EPS
    c[:, 51:59] = inp["norm_final"].reshape(8, 128).T
    return c


_CACHE = {}


def run(inputs, seq, n_cores, stages=("prompt", "A", "C", "sample"), trace=False):
    key = (seq, stages)
    if key not in _CACHE:
        b = Builder(seq, stages=stages)
        b.build()
        _CACHE[key] = b
    b = _CACHE[key]
    cols = _cols(inputs)
    consts = _consts()
    amask = _amask()
    rope = _rope_table(np.arange(seq))
    lamrow = np.stack([np.stack([inputs[n][l] for n in ("b_lam_q1", "b_lam_k1", "b_lam_q2", "b_lam_k2")])
                       for l in range(DEPTH)]).astype(np.float32)
    shared = {"cols": cols, "consts": consts, "amask": amask, "rope": rope, "lamrow": lamrow,
              "cols2": _cols2(inputs), "rowp": _rowp(inputs), "cw3": _cw3(inputs)}
    for nm in ("w_in", "w_out", "w_ff1", "w_ff2", "w_ple_gate", "w_ple_proj"):
        shared[nm] = inputs[nm]
    if "sample" in stages:
        past = inputs["cache_b_k"].shape[2]
        dec = inputs["x_sample"].shape[1]
        shared["rope_s"] = _rope_table(np.arange(past, past + dec))
    in_maps = []
    for c in range(n_cores):
        m = dict(shared)
        m["x"] = np.ascontiguousarray(inputs["x_prompt"][c])
        m["p"] = np.ascontiguousarray(inputs["p_prompt"][:, c])
        if "sample" in stages:
            m["xs"] = np.ascontiguousarray(inputs["x_sample"][c])
            m["psm"] = np.ascontiguousarray(inputs["p_sample"][:, c])
            m["cache_k"] = np.ascontiguousarray(inputs["cache_b_k"][:, c]).reshape(DEPTH, past, 512)
            m["cache_v"] = np.ascontiguousarray(inputs["cache_b_v"][:, c]).reshape(DEPTH, past, 512)
            m["st_conv"] = np.ascontiguousarray(inputs["state_a_conv"][:, c])
            m["st_delta"] = np.ascontiguousarray(inputs["state_a_delta"][:, c])
            m["st_shift"] = np.ascontiguousarray(inputs["state_c_shift"][:, c])
            m["st_wkv"] = np.ascontiguousarray(inputs["state_c_wkv"][:, c])
        in_maps.append(m)
    res = run_bass_kernel_spmd(b.nc, in_maps, core_ids=list(range(n_cores)), trace=trace)
    return res


def kernel(**inputs):
    inputs = {k: np.asarray(v) for k, v in inputs.items()}
    n, seq = inputs["x_prompt"].shape[0], inputs["x_prompt"].shape[1]
    dec = inputs["x_sample"].shape[1]
    r = run(inputs, seq, n).results

    def st(name, axis):
        return np.stack([np.asarray(r[c][name]) for c in range(n)], axis=axis)
    return (st("y", 0), st("y_s", 0),
            st("a_conv", 1), st("a_delta", 1),
            st("b_k", 1).reshape(DEPTH, n, seq, 4, 128), st("b_v", 1).reshape(DEPTH, n, seq, 4, 128),
            st("c_shift", 1), st("c_wkv", 1),
            st("a_conv_s", 1), st("a_delta_s", 1),
            st("b_k_s", 1).reshape(DEPTH, n, dec, 4, 128), st("b_v_s", 1).reshape(DEPTH, n, dec, 4, 128),
            st("c_shift_s", 1), st("c_wkv_s", 1))
```

```python
import math
from contextlib import ExitStack

import numpy as np
import ml_dtypes
import concourse.bass as bass
import concourse.mybir as mybir
from concourse.bass_utils import run_bass_kernel_spmd

F32 = mybir.dt.float32
BF16 = mybir.dt.bfloat16
AF = mybir.ActivationFunctionType
ALU = mybir.AluOpType
AX = mybir.AxisListType

D_MODEL = 1024
DEPTH = 2
PLE_DIM = 256
D_FF = 4096
IN_WIDTH = 3592
NORM_EPS = 1e-6
L2_EPS = 1e-6
C_LN_EPS = 64e-5
O_AQKV, O_AZ, O_AA, O_AB, O_BQ, O_BK, O_BV, O_C = 0, 768, 1024, 1028, 1032, 1544, 2056, 2568

ENGS = ("pe", "act", "dve", "pool", "sp")
SEM_WRAP = 30000


class Buf:
    __slots__ = ("name", "writer", "readers", "chan", "multi")

    def __init__(self, name, multi=False):
        self.name = name
        self.writer = [] if multi else None
        self.readers = []
        self.chan = None
        self.multi = multi


class Chan:
    __slots__ = ("sem", "count", "name")

    def __init__(self, name):
        self.name = name
        self.sem = None
        self.count = 0


class Op:
    __slots__ = ("eng", "fn", "deps", "signal", "semidx", "semval", "is_dma", "chan", "chan_val")

    def __init__(self, eng, fn):
        self.eng = eng
        self.fn = fn
        self.deps = []
        self.signal = False
        self.semidx = 0
        self.semval = 0
        self.is_dma = False
        self.chan = None
        self.chan_val = 0


class Sched:
    def __init__(self, nc):
        self.nc = nc
        self.ops = {e: [] for e in ENGS}
        self.chans = []
        self.final_waits = []

    def _collect(self, op, reads, writes, waits=()):
        deps = []
        for b in waits:
            if b.multi:
                deps.extend(b.writer)
            elif b.writer is not None:
                deps.append(b.writer)
            deps.extend(b.readers)
        for b in reads:
            if b.multi:
                deps.extend(b.writer)
            elif b.writer is not None:
                deps.append(b.writer)
        for b in writes:
            if b.multi:
                deps.extend(b.writer)
            elif b.writer is not None:
                deps.append(b.writer)
            deps.extend(b.readers)
        seen = set()
        for d in deps:
            if d is op or id(d) in seen:
                continue
            seen.add(id(d))
            if d.eng == "pe" and op.eng == "pe" and not d.is_dma and not op.is_dma:
                continue
            op.deps.append(d)
        for b in writes:
            if b.multi:
                b.writer.append(op)
            else:
                b.writer = op
                b.readers = []
        for b in reads:
            b.readers.append(op)

    def op(self, eng, fn, reads=(), writes=(), waits=()):
        o = Op(eng, fn)
        self.ops[eng].append(o)
        self._collect(o, reads, writes, waits)
        return o

    def dma(self, eng, out, in_, reads=(), writes=(), chan_buf=None, final=False, waits=(), **kw):
        if chan_buf is None:
            chan_buf = (list(writes) + list(reads))[0]
        if chan_buf.chan is None:
            chan_buf.chan = {}
        if eng not in chan_buf.chan:
            chan_buf.chan[eng] = Chan(chan_buf.name + "_" + eng)
            self.chans.append(chan_buf.chan[eng])
        ch = chan_buf.chan[eng]
        o = Op(eng, None)
        o.is_dma = True
        o.chan = ch
        ch.count += 1
        o.chan_val = 16 * ch.count
        o.fn = lambda e, out=out, in_=in_, kw=kw: e.dma_start(out=out, in_=in_, **kw)
        self.ops[eng].append(o)
        self._collect(o, reads, writes, waits)
        if final:
            self.final_waits.append(o)
        return o

    def emit(self, stack):
        nc = self.nc
        for e in ENGS:
            for o in self.ops[e]:
                for d in o.deps:
                    if not d.is_dma:
                        d.signal = True
        nsem = {}
        for e in ENGS:
            cnt = 0
            for o in self.ops[e]:
                if o.signal and not o.is_dma:
                    o.semidx = cnt // SEM_WRAP
                    o.semval = cnt % SEM_WRAP + 1
                    cnt += 1
            nsem[e] = cnt // SEM_WRAP + 1
        esems = {e: [stack.enter_context(nc.semaphore(f"s_{e}{i}")) for i in range(nsem[e])] for e in ENGS}
        for i, ch in enumerate(self.chans):
            ch.sem = stack.enter_context(nc.semaphore(f"c{i}_{ch.name}"))
        block = stack.enter_context(nc.Block())

        def run(e, eng):
            seen = {}
            maxidx = {}
            for o in self.ops[e]:
                need = {}
                for d in o.deps:
                    if d.is_dma:
                        key = ("c", id(d.chan)); sem = d.chan.sem; val = d.chan_val
                    else:
                        key = (d.eng, d.semidx); sem = esems[d.eng][d.semidx]; val = d.semval
                    if key not in need or need[key][1] < val:
                        need[key] = (sem, val)
                for key, (sem, val) in need.items():
                    if key[0] != "c":
                        if maxidx.get(key[0], -1) > key[1]:
                            continue
                    if seen.get(key, 0) >= val:
                        continue
                    seen[key] = val
                    if key[0] != "c":
                        maxidx[key[0]] = max(maxidx.get(key[0], -1), key[1])
                    eng.wait_ge(sem, val)
                ins = o.fn(eng)
                if o.is_dma:
                    ins.then_inc(o.chan.sem, 16)
                elif o.signal:
                    ins.then_inc(esems[e][o.semidx], 1)
            if e == "sp":
                fin = {}
                for o in self.final_waits:
                    fin[id(o.chan)] = (o.chan, max(o.chan_val, fin.get(id(o.chan), (None, 0))[1]))
                for ch, val in fin.values():
                    eng.wait_ge(ch.sem, val)

        @block.tensor
        def _(eng):
            run("pe", eng)

        @block.scalar
        def _(eng):
            run("act", eng)

        @block.vector
        def _(eng):
            run("dve", eng)

        @block.gpsimd
        def _(eng):
            run("pool", eng)

        @block.sync
        def _(eng):
            run("sp", eng)


class V:
    __slots__ = ("ap", "bufs", "wb")

    def __init__(self, ap, bufs, wb=()):
        self.ap = ap
        self.bufs = bufs
        self.wb = wb

    def __getitem__(self, idx):
        return V(self.ap[idx], self.bufs, self.wb)

    def r(self, s, **kw):
        return V(self.ap.rearrange(s, **kw), self.bufs, self.wb)

    def bc(self, shape):
        return V(self.ap.to_broadcast(shape), self.bufs, self.wb)

    def unsq(self, ax):
        return V(self.ap.unsqueeze(ax), self.bufs, self.wb)


class T:
    def __init__(self, t, name, track=True, multi=False, bufs=None):
        self.t = t
        self.name = name
        self.buf = Buf(name, multi=multi) if track else None
        self.bufs = bufs
        self.wbufs = None

    def __getitem__(self, idx):
        if self.wbufs is not None:
            return V(self.t[idx], (self.buf,), tuple(self.wbufs()))
        if self.bufs is not None:
            b = self.bufs() if callable(self.bufs) else self.bufs
            return V(self.t[idx], tuple(b))
        return V(self.t[idx], (self.buf,) if self.buf is not None else ())

    def v(self):
        return self[:]


class Builder:
    def __init__(self, seq, dec_seq=16, past=2048, stages=("prompt", "A", "C", "sample")):
        self.SEQ = seq
        self.DEC = dec_seq
        self.PAST = past
        self.TT = min(512, seq)
        self.CH = min(64, seq)
        self.stages = stages
        self.nc = bass.Bass("TRN2", target_bir_lowering=False)
        self.S = Sched(self.nc)
        self.st = ExitStack()
        self.in_names = []
        self.out_names = []
        import os
        self.bsub = int(os.environ.get('BSUB', '9'))
        self.poolsum = os.environ.get('POOLSUM', '0') == '1'
        self.wcache = os.environ.get('WCACHE', '1') == '1'
        self.asub = int(os.environ.get('ASUB', '9'))
        self.a1 = int(os.environ.get('A1', '9'))
        self.a3 = int(os.environ.get('A3', '9'))

    def sb(self, name, shape, dt=F32):
        import os
        if os.environ.get("ALLOCDBG"):
            print("ALLOC", name, shape, dt, int(np.prod(shape[1:])) * (4 if dt == F32 else 2))
        return T(self.st.enter_context(self.nc.sbuf_tensor("s_" + name, list(shape), dt)), name)

    def arena(self, name, nfloats):
        t = self.st.enter_context(self.nc.sbuf_tensor("s_" + name, [128, nfloats], F32))
        return {"t": t, "off": 0, "bufs": [], "n": nfloats, "parts": {}}

    def _arena_ap(self, ar, off, shape, dt):
        n = int(np.prod(shape[1:]))
        nfl = n if dt == F32 else n // 2
        assert off + nfl <= ar["n"], (off, nfl, ar["n"])
        ap = ar["t"][:, off:off + nfl]
        if dt != F32:
            ap = ap.bitcast(dt)
        if len(shape) == 3:
            ap = ap.rearrange("p (a b) -> p a b", a=shape[1])
        elif len(shape) == 4:
            ap = ap.rearrange("p (a b c) -> p a b c", a=shape[1], b=shape[2])
        return ap, nfl

    def sub(self, ar, name, shape, dt=F32, part=None):
        if part is None:
            ap, nfl = self._arena_ap(ar, ar["off"], shape, dt)
            ar["off"] += nfl
            tt = T(ap, name)
            ar["bufs"].append(tt.buf)
            return tt
        pd = ar["parts"].setdefault(part, {"off": 0, "bufs": []})
        ap, nfl = self._arena_ap(ar, pd["off"], shape, dt)
        pd["off"] += nfl
        tt = T(ap, name)
        pd["bufs"].append(tt.buf)
        tt.wbufs = lambda: [b for p, d in ar["parts"].items() if p != part for b in d["bufs"]]
        return tt

    def whole(self, ar, name, shape, dt=F32):
        ap, _ = self._arena_ap(ar, 0, shape, dt)
        return T(ap, name, track=False, bufs=ar["bufs"])

    def psum(self, name, shape, dt=F32):
        return T(self.st.enter_context(self.nc.psum_tensor("p_" + name, list(shape), dt)), name)

    def din(self, name, shape, dt=F32):
        self.in_names.append(name)
        return T(self.nc.dram_tensor(name, list(shape), dt, kind="ExternalInput").ap(), name, track=False)

    def dout(self, name, shape, dt=F32):
        self.out_names.append(name)
        return T(self.nc.dram_tensor(name, list(shape), dt, kind="ExternalOutput").ap(), name, track=False)

    def dscr(self, name, shape, dt):
        return T(self.nc.dram_tensor(name, list(shape), dt).ap(), name, multi=True)

    def _op(self, eng, meth, *args, _r=(), _w=(), **kw):
        reads, writes = list(_r), list(_w)
        waits = []
        cargs = []
        for i, a in enumerate(args):
            if isinstance(a, V):
                (writes if i == 0 else reads).extend(a.bufs)
                waits.extend(a.wb)
                cargs.append(a.ap)
            else:
                cargs.append(a)
        ckw = {}
        for k, a in kw.items():
            if isinstance(a, V):
                (writes if k in ("out", "accum_out") else reads).extend(a.bufs)
                waits.extend(a.wb)
                ckw[k] = a.ap
            else:
                ckw[k] = a
        return self.S.op(eng, lambda e: getattr(e, meth)(*cargs, **ckw), reads=reads, writes=writes, waits=waits)

    def pe(self, meth, *a, **k):
        return self._op("pe", meth, *a, **k)

    def act(self, meth, *a, **k):
        return self._op("act", meth, *a, **k)

    def dve(self, meth, *a, **k):
        return self._op("dve", meth, *a, **k)

    def dma(self, q, out, in_, final=False, **kw):
        import os
        if os.environ.get("NOSCR") and any(b.multi for b in list(out.bufs) + list(in_.bufs)):
            return
        if os.environ.get("NOOUT") and final and "b_" in str(out.ap):
            return
        reads = list(in_.bufs)
        writes = list(out.bufs)
        cands = [b for b in writes + reads if not b.multi]
        return self.S.dma(q, out.ap, in_.ap, reads=reads, writes=writes, chan_buf=cands[0], final=final,
                          waits=list(out.wb) + list(in_.wb), **kw)

    def build(self):
        nc = self.nc
        SEQ, TT = self.SEQ, self.TT
        NT = SEQ // TT
        NB = TT // 128
        st = self.st

        x_in = self.din("x", [SEQ, D_MODEL])
        p_in = self.din("p", [DEPTH, SEQ, PLE_DIM])
        W = {}
        for nm, shp in [("w_in", [DEPTH, D_MODEL, IN_WIDTH]), ("w_out", [DEPTH, D_MODEL, D_MODEL]),
                        ("w_ff1", [DEPTH, D_MODEL, D_FF]), ("w_ff2", [DEPTH, D_FF, D_MODEL]),
                        ("w_ple_gate", [DEPTH, D_MODEL, D_MODEL]), ("w_ple_proj", [DEPTH, PLE_DIM, D_MODEL])]:
            W[nm] = self.din(nm, shp)
        cols_in = self.din("cols", [128, 64])
        NCONST = 11
        consts_in = self.din("consts", [128, NCONST, 128])
        cols2_in = self.din("cols2", [128, DEPTH, 64])
        cw3_in = self.din("cw3", [128, DEPTH, 3, 256])
        cshift_out = self.dout("c_shift", [DEPTH, 1024])
        cwkv_out = self.dout("c_wkv", [DEPTH, 4, 64, 64])
        rowp_in = self.din("rowp", [DEPTH, 8])
        aconv_out = self.dout("a_conv", [DEPTH, 3, 768])
        adelta_out = self.dout("a_delta", [DEPTH, 4, 64, 64])
        rope_in = self.din("rope", [SEQ, 64])
        amask_in = self.din("amask", [128, 4, 512], BF16)
        lam_in = self.din("lamrow", [DEPTH, 4, 64])
        y_out = self.dout("y", [SEQ, D_MODEL])
        bk_out = self.dout("b_k", [DEPTH, SEQ, 512])
        bv_out = self.dout("b_v", [DEPTH, SEQ, 512])
        kT_scr = [self.dscr(f"kT_scr{l}", [512, SEQ], BF16) for l in range(DEPTH)]
        v_scr = [self.dscr(f"v_scr{l}", [SEQ, 512], BF16) for l in range(DEPTH)]

        consts = self.sb("consts", [128, NCONST, 128])
        cols2 = self.sb("cols2", [128, DEPTH, 64])
        rowp = self.sb("rowp", [128, DEPTH, 8])
        cU, cSU, cL, cLI, cSame = (consts[:, i, :] for i in (2, 3, 4, 5, 6))
        arA = self.arena("arA", 8192); arB = self.arena("arB", 4096); arC = self.arena("arC", 4096)
        aq = self.sub(arA, "aq", [128, 6, 3 + TT], part="A")
        acarry = [self.sb(f"acarry{l}", [128, 6, 3]) for l in range(DEPTH)]
        Sd = [self.sb(f"Sd{l}", [128, 2, 128]) for l in range(DEPTH)]
        ac = self.sub(arA, "ac", [128, 6, TT], part="A")
        qkn = self.sub(arB, "qkn", [128, 4, TT], part="A")
        zs = self.sub(arB, "zs", [128, 2, TT], part="A")
        abtm = self.sb("abtm", [128, NB, 8])
        astep = self.sb("astep", [128, NB, 4])
        beta = self.sb("beta", [128, NB, 4])
        arD = self.arena("arD", 4096)
        kvtm = self.sub(arD, "kvtm", [128, NB, 512], part="A")
        ontm = self.sub(arD, "ontm", [128, NB, 256], part="A")
        aL = self.sub(arD, "aL", [128, 4, 128], part="A"); aU = self.sub(arD, "aU", [128, 4, 128], part="A")
        decA = self.sub(arB, "decA", [128, 4, 128], part="A"); decT = self.sub(arB, "decT", [128, 4, 128], part="A")
        eg = self.sb("eg", [128, 16]); bkg = self.sb("bkg", [128, 4]); egt = self.sb("egt", [128, 2, 2])
        Am = self.sub(arA, "Am", [128, 4, 128], part="A")
        Xa = [self.sb("Xa0", [128, 4, 128])] * 2
        XTa = [self.sb("XTa0", [128, 4, 128])] * 2
        Rm = self.sub(arA, "Rm", [128, 4, 128], part="A")
        vb_ = self.sb("vb_", [128, 4, 64]); kbz = self.sb("kbz", [128, 4, 128]); ktz = self.sb("ktz", [128, 2, 4, 128]); kzb = self.sb("kzb", [128, 4, 128])
        unew = self.sb("unew", [128, 256]); wT = self.sb("wT", [128, 2, 128])
        qkT = self.sub(arA, "qkT", [128, 4, 128], part="A"); ssq = self.sb("ssq", [128, 4])
        ident = consts[:, 0, :]
        identb = self.sb("identb", [128, 128], BF16)
        onesb = self.sb("onesb", [128, 128], BF16)
        cols = self.sb("cols", [128, 64])
        amask = self.sb("amask", [128, 4, 512], BF16)
        lamt = self.sb("lamt", [128, 8])
        h = self.sb("h", [128, 8, TT])
        xb = self.sb("xb", [128, 8, TT], BF16)
        rstd = self.sb("rstd", [128, TT])
        sq = [self.sb(f"sq{i}", [128, TT], BF16) for i in range(2)]
        NSLOT = 3
        ring = [self.sb(f"wr{i}", [128, 4096], BF16) for i in range(NSLOT)]
        self.ring_i = 0
        mixT = self.sb("mixT", [128, 8, TT], BF16)
        pT = self.sb("pT", [128, 2, TT], BF16)
        tmpf = [self.sb(f"tmpf{i}", [128, TT]) for i in range(2)]
        ropet = self.sb("ropet", [128, NB, 64])
        kfs = [self.sb(f"kfs{i}", [128, 512]) for i in range(2)]
        vfs = [self.sb(f"vfs{i}", [128, 512]) for i in range(2)]
        qT = self.sb("qT", [128, 4, 2, TT], BF16)

        ptok = self.sub(arD, "ptok", [128, NB, PLE_DIM], part="P")
        kblk = [self.sub(arD, f"kblk{i}", [128, 512], BF16, part="B") for i in range(2)]
        vblk = [self.sub(arD, f"vblk{i}", [128, 4, 128], BF16, part="B") for i in range(2)]
        PT = [self.sub(arD, f"PT{i}", [128, TT], BF16, part="B") for i in range(4)]
        osb = [self.sub(arD, f"osb{i}", [128, TT], part="B") for i in range(3)]
        qtm = self.sub(arC, "qtm", [128, NB, 512], BF16, part="B")
        ktm = self.sub(arC, "ktm", [128, NB, 512], BF16, part="B")
        vtb = self.sub(arC, "vtb", [128, NB, 512], BF16, part="B")
        kTt = self.sub(arC, "kTt", [128, 4, TT], BF16, part="B")
        xtok = self.sub(arC, "xtok", [128, NB, D_MODEL], part="X")
        uT = self.sub(arA, "uT", [128, 32, TT], BF16, part="F")
        gate = self.sub(arB, "gate", [128, 8, TT], part="F")
        c_r = self.sub(arA, "c_r", [128, 2, TT], part="C"); c_v = self.sub(arA, "c_v", [128, 2, TT], part="C")
        c_wl = self.sub(arA, "c_wl", [128, 2, TT], part="C"); c_g = self.sub(arA, "c_g", [128, 2, TT], part="C")
        c_kk = self.sub(arA, "c_kk", [128, 2, TT], part="C"); c_km = self.sub(arA, "c_km", [128, 2, TT], part="C")
        c_b = self.sub(arA, "c_b", [128, 2, TT], part="C"); c_bon = self.sub(arA, "c_bon", [128, 2, TT], part="C")
        c_a = self.sub(arB, "c_a", [128, 2, TT], part="C"); c_k = self.sub(arB, "c_k", [128, 2, TT], part="C")
        c_6 = self.sub(arB, "c_6", [128, TT], part="C"); c_7 = self.sub(arB, "c_7", [128, TT], part="C")
        C1T = self.sub(arB, "C1T", [128, 4, 128], part="C"); C2T = self.sub(arB, "C2T", [128, 4, 128], part="C")
        crawm = [self.sub(arD, f"crawm{i}", [128, 1 + TT], part="C") for i in range(2)]
        tm5 = self.sub(arD, "tm5", [128, 5, 256], part="C")
        rTt = self.sub(arD, "rTt", [128, 2, 128], part="C"); kapTt = self.sub(arD, "kapTt", [128, 2, 128], part="C")
        kTm = self.sub(arD, "kTm", [128, 4, 128], part="C"); bTm = self.sub(arD, "bTm", [128, 4, 128], part="C")
        uval = self.sub(arC, "uval", [128, 256], part="A"); oq = self.sub(arC, "oq", [128, 256], part="A")
        otm = self.sub(arC, "otm", [128, 256], part="A"); o2 = self.sub(arC, "o2", [128, 256], part="A")
        kapz = self.sub(arC, "kapz", [128, 4, 128], part="C")
        cktz = self.sub(arC, "cktz", [128, 2, 4, 128], part="C"); cbtz = self.sub(arC, "cbtz", [128, 2, 4, 128], part="C")
        cx0 = self.sub(arC, "cx0", [128, 256], part="C"); cU0 = self.sub(arC, "cU0", [128, 256], part="C")
        cU_ = self.sub(arC, "cU_", [128, 256], part="C"); cyq = self.sub(arC, "cyq", [128, 256], part="C")
        cytm = self.sub(arC, "cytm", [128, 256], part="C")
        cR2 = T(vfs[0].t[:, :].rearrange("p (h s) -> p h s", h=4), "cR2", track=False, bufs=[vfs[0].buf])
        cBmT = T(vfs[1].t[:, :].rearrange("p (h s) -> p h s", h=4), "cBmT", track=False, bufs=[vfs[1].buf])
        Tst = [self.sb(f"Tst{l}", [128, 2, 128]) for l in range(DEPTH)]
        ccarry = [self.sb(f"ccarry{l}", [128, 8]) for l in range(DEPTH)]
        acarry_s = [self.sb(f"acarry_s{l}", [128, 6, 3]) for l in range(DEPTH)]
        Sd_s = [self.sb(f"Sd_s{l}", [128, 2, 128]) for l in range(DEPTH)]
        Tst_s = [self.sb(f"Tst_s{l}", [128, 2, 128]) for l in range(DEPTH)]
        ccarry_s = [self.sb(f"ccarry_s{l}", [128, 8]) for l in range(DEPTH)]
        kvp = self.sb("kvp", [128, 2, 2, 64]); pcc = self.sb("pcc", [128, 2, 2])
        cw3 = self.sb("cw3", [128, DEPTH, 3, 256], BF16)
        cst = self.sb("cst", [128, 8])

        def cyn(blk):
            return kfs[blk // 2][:, (blk % 2) * 256:(blk % 2 + 1) * 256]
        cgt = self.sb("cgt", [128, 2, 256])
        ps = [self.psum(f"ps{i}", [128, 512]) for i in range(7)]
        psbT = self.psum("psb", [128, 1024], BF16)

        self.dma("sp", consts.v(), consts_in.v())
        self.dma("sp", cols.v(), cols_in.v())
        self.dma("sp", cols2.v(), cols2_in.v())
        self.dma("sp", rowp.v(), V(rowp_in.t.rearrange("l a -> (l a)").partition_broadcast(128)
                                  .rearrange("p (l a) -> p l a", l=DEPTH), ()))
        self.act("activation", rowp[:, :, 0:4], rowp[:, :, 0:4], AF.Exp)
        self.dve("tensor_scalar_mul", rowp[:, :, 0:4], rowp[:, :, 0:4], -1.0)
        for l in range(DEPTH):
            self.dve("memset", acarry[l].v(), 0.0)
            self.dve("memset", Sd[l].v(), 0.0)
        self.dma("pool", cw3.v(), cw3_in.v())
        for l in range(DEPTH):
            self.dve("memset", Tst[l].v(), 0.0)
            self.dve("memset", ccarry[l].v(), 0.0)
            self.dve("tensor_scalar", cols2[:, l, 54:56], cols2[:, l, 46:48], -1.0, 1.0, ALU.mult, ALU.add)
        self.dve("memset", kbz.v(), 0.0)
        self.dve("memset", ktz.v(), 0.0)
        self.dve("memset", unew.v(), 0.0)
        self.dma("sp", amask.v(), amask_in.v())
        self.dve("tensor_copy", identb.v(), consts[:, 0, :])
        self.dve("tensor_copy", onesb.v(), consts[:, 1, :])
        lrow = T(ptok.t[:, 0:2, :].rearrange("p a (b d) -> p a b d", b=4), "lrow", track=False, bufs=[ptok.buf])
        self.dma("sp", lrow.v(), V(lam_in.t.rearrange("l a d -> (l a d)").partition_broadcast(128)
                                  .rearrange("p (l a d) -> p l a d", l=DEPTH, a=4), ()))
        lsum = self.sb("lsum", [128, 4])
        for l in range(DEPTH):
            for m in range(2):
                self.dve("tensor_tensor", tmpf[0][:, 0:64], lrow[:, l, 2 * m, :], lrow[:, l, 2 * m + 1, :], ALU.mult)
                self.dve("reduce_sum", lsum[:, 2 * l + m:2 * l + m + 1], tmpf[0][:, 0:64], AX.X)
        self.act("activation", lsum.v(), lsum.v(), AF.Exp)
        for l in range(DEPTH):
            lam_init = 0.8 - 0.6 * math.exp(-0.3 * l)
            self.dve("scalar_tensor_tensor", lamt[:, l:l + 1], lsum[:, 2 * l + 1:2 * l + 2], -lam_init,
                     lsum[:, 2 * l:2 * l + 1], ALU.add, ALU.subtract)

        bns = self.sb("bns", [128, DEPTH])
        for l in range(DEPTH):
            self.dve("tensor_scalar_mul", bns[:, l:l + 1], cols[:, 48 + l:49 + l], float(1.0 - (0.8 - 0.6 * math.exp(-0.3 * l))))
        def col(name, l, k=0):
            base = {"norm_mix": 0, "norm_ffn": 8, "norm_ple": 16, "b_norm": 24}[name]
            if name == "b_norm":
                return bns[:, l:l + 1]
            return cols[:, l * 24 + base + k: l * 24 + base + k + 1]

        def colf(k):
            return cols[:, 50 + k:51 + k]

        def rmsnorm_to_xb(ntok, gcol):
            for k in range(8):
                s = sq[k % 2]
                self.act("activation", s[:, :ntok], h[:, k, :ntok], AF.Square)
                self.pe("matmul", ps[2][:, :ntok], onesb.v(), s[:, :ntok], start=(k == 0), stop=(k == 7))
            self.act("activation", rstd[:, :ntok], ps[2][:, :ntok], AF.Ln, bias=colf(0), scale=1.0 / D_MODEL)
            self.act("activation", rstd[:, :ntok], rstd[:, :ntok], AF.Exp, scale=-0.5)
            for k in range(8):
                self.dve("scalar_tensor_tensor", xb[:, k, :ntok], h[:, k, :ntok], gcol(k), rstd[:, :ntok],
                         ALU.mult, ALU.mult)

        wscr = {}

        def load_piece(wref, KC, c0, ncols):
            wname, l = wref
            wv = W[wname].t[l]
            slot = ring[self.ring_i % NSLOT]
            self.ring_i += 1
            flat = slot[:, 0:KC * ncols]
            sv = flat.r("p (k n) -> p k n", k=KC)
            key = (wname, l, KC, c0, ncols)
            if key in wscr and self.wcache:
                self.dma("sp", flat, wscr[key].v())
                return sv
            for k0 in range(0, KC, 8):
                k1 = min(KC, k0 + 8)
                src = V(wv.rearrange("(kc p) n -> p kc n", p=128)[:, k0:k1, c0:c0 + ncols], ())
                self.dma("pool", sv[:, k0:k1, :], src)
            if self.wcache:
                scr = self.dscr(f"wb_{wname}_{l}_{c0}_{ncols}", [128, KC * ncols], BF16)
                wscr[key] = scr
                self.dma("sp", scr.v(), flat)
            return sv

        self.psd = 0

        def dense_fm(wv, KC, c0, ncols_total, rhs, ntok, epi, mchunk=128):
            per = max(128, min(512, 4096 // KC))
            m = 0
            for pc0 in range(0, ncols_total, per):
                pn = min(per, ncols_total - pc0)
                sv = load_piece(wv, KC, c0 + pc0, pn)
                for cc in range(0, pn, mchunk):
                    mc = min(mchunk, pn - cc)
                    pst = ps[self.psd % 2]
                    self.psd += 1
                    for k in range(KC):
                        self.pe("matmul", pst[:mc, :ntok], sv[:, k, cc:cc + mc], rhs[:, k, :ntok],
                                start=(k == 0), stop=(k == KC - 1))
                    epi(m, pst[:mc, :ntok])
                    m += 1

        def dense_tm(wv, c0, ncols, ntok, epi):
            sv = load_piece(wv, 8, c0, ncols)
            for b0 in range(0, ntok, 128):
                bt = min(128, ntok - b0)
                pst = ps[self.psd % 2]
                self.psd += 1
                for k in range(8):
                    self.pe("matmul", pst[:bt, :ncols], xb[:, k, b0:b0 + bt], sv[:, k, :], start=(k == 0), stop=(k == 7))
                epi(b0 // 128, bt, pst[:bt, :ncols])

        def rope_tm(dst, src_sb, bt, blk):
            import os
            if os.environ.get("NOROPE"):
                self.dve("tensor_copy", dst, src_sb[:bt, :])
                return
            s4 = src_sb[:bt, :].r("p (g t f) -> p g t f", g=8, t=2)
            d4 = dst.r("p (g t f) -> p g t f", g=8, t=2)
            cos = ropet[:bt, blk, 0:32].unsq(1).bc([bt, 8, 32])
            sin = ropet[:bt, blk, 32:64].unsq(1).bc([bt, 8, 32])
            ta = tmpf[0][:bt, 0:256].r("p (g f) -> p g f", g=8)
            tb = tmpf[1][:bt, 0:256].r("p (g f) -> p g f", g=8)
            self.dve("tensor_tensor", ta, s4[:, :, 0, :], cos, ALU.mult)
            self.dve("tensor_tensor", tb, s4[:, :, 1, :], sin, ALU.mult)
            self.dve("tensor_tensor", d4[:, :, 0, :], ta, tb, ALU.subtract)
            self.dve("tensor_tensor", ta, s4[:, :, 1, :], cos, ALU.mult)
            self.dve("tensor_tensor", tb, s4[:, :, 0, :], sin, ALU.mult)
            self.dve("tensor_tensor", d4[:, :, 1, :], ta, tb, ALU.add)

        def neumann(bt, X, XT, R, CH):
            nlev = 5 if CH > 16 else 3
            nlev = min(nlev, self.a3 - 2)
            for lev in range(nlev):
                pX = ps[0][:bt, :].r("p (h s) -> p h s", h=4)[:, :, :bt]
                pXT = ps[1][:bt, :].r("p (h s) -> p h s", h=4)[:, :, :bt]
                pR = ps[3][:bt, :].r("p (h s) -> p h s", h=4)[:, :, :bt]
                last = lev == nlev - 1
                for hh in range(4):
                    self.pe("matmul", pXT[:, hh, :], X[:bt, hh, :bt], XT[:bt, hh, :bt], start=True, stop=True)
                if not last:
                    for hh in range(4):
                        self.pe("matmul", pX[:, hh, :], XT[:bt, hh, :bt], X[:bt, hh, :bt], start=True, stop=True)
                self.dve("tensor_copy", XT[:bt, :, :bt], pXT)
                if not last:
                    self.act("copy", X[:bt, :, :bt], pX)
                for hh in range(4):
                    self.pe("matmul", pR[:, hh, :], XT[:bt, hh, :bt], R[:bt, hh, :bt], start=True, stop=True)
                self.dve("tensor_tensor", R[:bt, :, :bt], R[:bt, :, :bt], pR, ALU.add)

        def tile_body(cx):
            ntok = cx.ntok
            nbk = (ntok + 127) // 128
            bl = min(128, ntok)
            kb0 = cx.key_base
            self.dma("sp", xtok[:bl, :nbk, :], V(cx.x_src.rearrange("(b p) d -> p b d", p=bl), ()))
            for k in range(8):
                for b in range(nbk):
                    self.pe("transpose", ps[3][:, b * 128:b * 128 + bl], xtok[:bl, b, k * 128:(k + 1) * 128], ident[:bl, :bl])
                self.act("copy", h[:, k, :ntok], ps[3][:, :ntok])
            self.dma("sp", ropet[:bl, :nbk, :], V(cx.rope_src.rearrange("(b p) d -> p b d", p=bl), ()))

            for l in range(DEPTH):
                lam_init = 0.8 - 0.6 * math.exp(-0.3 * l)
                rmsnorm_to_xb(ntok, lambda k: col("norm_mix", l, k))
                w_in = ("w_in", l)
                kT_s, v_s = cx.kT_scr[l], cx.v_scr[l]

                if True:
                    def epi_q(blk, bt, pv):
                        self.act("copy", osb[0][:bt, :], pv)
                        rope_tm(qtm[:bt, blk, :], osb[0], bt, blk)
                    dense_tm(w_in, O_BQ, 512, ntok, epi_q)

                    def epi_k(blk, bt, pv):
                        kf = kfs[blk % 2]
                        self.act("copy", osb[1][:bt, :], pv)
                        rope_tm(kf[:bt, :], osb[1], bt, blk)
                        self.act("copy", ktm[:bt, blk, :], kf[:bt, :])
                        self.dma("sp", V(cx.bk_dst(l)[blk * 128:blk * 128 + bt, :], ()), kf[:bt, :], final=True)
                    dense_tm(w_in, O_BK, 512, ntok, epi_k)

                    def epi_v(blk, bt, pv):
                        vf = vfs[blk % 2]
                        self.act("copy", vf[:bt, :], pv)
                        self.dve("tensor_copy", vtb[:bt, blk, :], vf[:bt, :])
                        self.dma("sp", V(cx.bv_dst(l)[blk * 128:blk * 128 + bt, :], ()), vf[:bt, :], final=True)
                    dense_tm(w_in, O_BV, 512, ntok, epi_v)
                    self.dma("sp", V(v_s.t[kb0:kb0 + ntok, :].rearrange("(b p) d -> p b d", p=bl), (v_s.buf,)), vtb[:bl, :nbk, :])
                    psb = psbT.v()
                    for c in range(4):
                        for b in range(nbk):
                            self.pe("transpose", psb[:, b * 128:b * 128 + bl], qtm[:bl, b, c * 128:(c + 1) * 128], identb[:bl, :bl])
                        for m in range(2):
                            self.act("mul", qT[:, c, m, :ntok], psb[:, 0:ntok], cSame[:, m * 64:m * 64 + 1])
                        for b in range(nbk):
                            self.pe("transpose", psb[:, 512 + b * 128:512 + b * 128 + bl], ktm[:bl, b, c * 128:(c + 1) * 128], identb[:bl, :bl])
                        self.dve("tensor_copy", kTt[:, c, :ntok], psb[:, 512:512 + ntok])
                    self.dma("sp", V(kT_s.t[:, kb0:kb0 + ntok].rearrange("(c p) s -> p c s", p=128), (kT_s.buf,)), kTt[:, :, :ntok])

                    nk = kb0 + ntok
                    nkb = (nk + 127) // 128
                    sbank = (ps[0], ps[1], ps[2], ps[5]) if self.poolsum else (ps[0], ps[1], ps[2], ps[0])
                    for hh in range(4):
                        units = []
                        for ks in range(0, nk, 512):
                            kn = min(512, nk - ks)
                            for j in range((kn + 127) // 128):
                                units.append((ks, kn, j, min(128, kn - j * 128), (ks + j * 128) // 128))

                        def stage1(ui):
                            ks, kn, j, kr, kb = units[ui]
                            sbi = (ks // 512) % 2
                            cur_k, cur_v = kblk[sbi], vblk[sbi]
                            if j == 0:
                                self.dma("sp", cur_k[:, :kn], V(kT_s.t[hh * 128:(hh + 1) * 128, ks:ks + kn], (kT_s.buf,)))
                                if kn == 512:
                                    self.dma("sp", cur_v.v(), V(v_s.t[ks:ks + 512, hh * 128:(hh + 1) * 128]
                                                                .rearrange("(j p) d -> p j d", p=128), (v_s.buf,)))
                                else:
                                    for jj in range((kn + 127) // 128):
                                        krr = min(128, kn - jj * 128)
                                        self.dma("sp", cur_v[:krr, jj, :], V(v_s.t[ks + jj * 128:ks + jj * 128 + krr, hh * 128:(hh + 1) * 128], (v_s.buf,)))
                            diag = cx.masked and kb * 128 >= kb0
                            for m in range(2):
                                pss = sbank[(ui % 2) * 2 + m] if self.poolsum else ps[(2 * ui + m) % 3]
                                pt = PT[(ui % 2) * 2 + m]
                                self.pe("matmul", pss[:kr, :ntok], cur_k[:, j * 128:j * 128 + kr],
                                        qT[:, hh, m, :ntok], start=True, stop=True)
                                self.act("activation", pt[:kr, :ntok], pss[:kr, :ntok], AF.Exp, scale=0.125)
                                if diag:
                                    self.dve("tensor_tensor", pt[:kr, :ntok], pt[:kr, :ntok], amask[:kr, kb - kb0 // 128, :ntok], ALU.mult)

                        def stage2(ui):
                            ks, kn, j, kr, kb = units[ui]
                            cur_v = vblk[(ks // 512) % 2]
                            first, last = ui == 0, ui == len(units) - 1
                            for m in range(2):
                                pt = PT[(ui % 2) * 2 + m]
                                self.pe("matmul", ps[3 + m][:, :ntok], cur_v[:kr, j, :], pt[:kr, :ntok], start=first, stop=last)
                                if not self.poolsum:
                                    self.pe("matmul", ps[5 + m][:, :ntok], onesb[:kr, :], pt[:kr, :ntok], start=first, stop=last)
                                elif first:
                                    self._op("pool", "tensor_copy", osb[m][:, :ntok], pt[:, :ntok])
                                else:
                                    self._op("pool", "tensor_tensor", osb[m][:kr, :ntok], osb[m][:kr, :ntok], pt[:kr, :ntok], ALU.add)

                        stage1(0)
                        for ui in range(1, len(units)):
                            stage1(ui)
                            stage2(ui - 1)
                        stage2(len(units) - 1)
                        n_ = slice(0, ntok)
                        if self.poolsum:
                            for m in range(2):
                                self.pe("matmul", ps[m][:, n_], consts[:, 1, :], osb[m][:, n_], start=True, stop=True)
                            self.dve("reciprocal", osb[0][:, n_], ps[0][:, n_])
                            self.dve("reciprocal", osb[1][:, n_], ps[1][:, n_])
                        else:
                            for m in range(2):
                                self.act("activation", osb[m][:, n_], ps[5 + m][:, n_], AF.Ln)
                                self.act("activation", osb[m][:, n_], osb[m][:, n_], AF.Exp, scale=-1.0)
                        self.dve("tensor_tensor", osb[0][:, n_], osb[0][:, n_], ps[3][:, n_], ALU.mult)
                        self.dve("tensor_tensor", osb[1][:, n_], osb[1][:, n_], ps[4][:, n_], ALU.mult)
                        self.dve("scalar_tensor_tensor", osb[2][:, n_], osb[1][:, n_], lamt[:, l:l + 1], osb[0][:, n_], ALU.mult, ALU.add)
                        self.act("activation", sq[0][:, n_], osb[2][:, n_], AF.Square)
                        self.pe("matmul", ps[2][:, n_], onesb.v(), sq[0][:, n_], start=True, stop=True)
                        self.act("activation", osb[0][:, n_], ps[2][:, n_], AF.Ln, bias=colf(0), scale=1.0 / 128)
                        self.act("activation", osb[0][:, n_], osb[0][:, n_], AF.Exp, scale=-0.5)
                        self.dve("scalar_tensor_tensor", mixT[:, 2 + hh, n_], osb[2][:, n_], col("b_norm", l), osb[0][:, n_],
                                 ALU.mult, ALU.mult)

                if "A" not in self.stages:
                    for c in (0, 1):
                        self.dve("memset", mixT[:, c, :], 0.0)
                else:
                    one_col = consts[:, 1, 0:1]
                    self.dve("tensor_copy", aq[:, :, 0:3], cx.acarry[l].v())

                    def epi_aqkv(m, pv):
                        self.act("copy", aq[:, m, 3:3 + ntok], pv)
                    dense_fm(w_in, 8, O_AQKV, 768, xb, ntok, epi_aqkv)
                    self.dve("tensor_copy", cx.acarry[l].v(), aq[:, :, ntok:ntok + 3])
                    if cx.is_last:
                        for m in range(6):
                            self.dma("sp", V(cx.aconv_out.t[l, :, m * 128:(m + 1) * 128].rearrange("j p -> p j"), ()),
                                     cx.acarry[l][:, m, :], final=True, allow_slow_non_contiguous=True)

                    def epi_az(m, pv):
                        self.act("activation", zs[:, m, :ntok], pv, AF.Silu)
                    if self.a1 >= 2: dense_fm(w_in, 8, O_AZ, 256, xb, ntok, epi_az)

                    def epi_ab(blk, bt, pv):
                        self.act("copy", abtm[:bt, blk, :], pv)
                    if self.a1 >= 3: dense_tm(w_in, O_AA, 8, ntok, epi_ab)
                    for m in (range(6) if self.a1 >= 4 else ()):
                        tf = tmpf[m % 2]
                        self.dve("tensor_scalar_mul", tf[:, :ntok], aq[:, m, 0:ntok], cols2[:, l, m * 4:m * 4 + 1])
                        for j in (1, 2, 3):
                            self.dve("scalar_tensor_tensor", tf[:, :ntok], aq[:, m, j:j + ntok],
                                     cols2[:, l, m * 4 + j:m * 4 + j + 1], tf[:, :ntok], ALU.mult, ALU.add)
                        self.act("activation", ac[:, m, :ntok], tf[:, :ntok], AF.Silu)
                    for m in (range(4) if self.a1 >= 5 else ()):
                        tf = tmpf[m % 2]
                        self.act("activation", tf[:, :ntok], ac[:, m, :ntok], AF.Square)
                        self.pe("matmul", ps[2][:, :ntok], cSame, tf[:, :ntok], start=True, stop=True)
                        self.act("activation", tf[:, :ntok], ps[2][:, :ntok], AF.Ln, bias=colf(0), scale=1.0)
                        self.act("activation", tf[:, :ntok], tf[:, :ntok], AF.Exp, scale=-0.5)
                        self.dve("scalar_tensor_tensor", qkn[:, m, :ntok], ac[:, m, :ntok], 0.125 if m < 2 else 1.0,
                                 tf[:, :ntok], ALU.mult, ALU.mult)
                    nbk = (ntok + 127) // 128
                    btl = min(128, ntok)
                    if self.a1 >= 6: self.dve("tensor_tensor", astep[:btl, :nbk, :], abtm[:btl, :nbk, 0:4],
                             rowp[:btl, l, 4:8].unsq(1).bc([btl, nbk, 4]), ALU.add)
                    if self.a1 >= 7: self.act("activation", astep[:btl, :nbk, :], astep[:btl, :nbk, :], AF.Exp)
                    if self.a1 >= 8: self.act("activation", astep[:btl, :nbk, :], astep[:btl, :nbk, :], AF.Ln, bias=one_col[:btl, :], scale=1.0)
                    if self.a1 >= 9: self.dve("tensor_tensor", astep[:btl, :nbk, :], astep[:btl, :nbk, :],
                             rowp[:btl, l, 0:4].unsq(1).bc([btl, nbk, 4]), ALU.mult)
                    if self.a1 >= 9: self.act("activation", beta[:btl, :nbk, :], abtm[:btl, :nbk, 4:8], AF.Sigmoid)

                    for blk in (range(nbk) if self.asub >= 2 else ()):
                        bt = min(128, ntok - blk * 128)
                        c0 = blk * 128
                        chunks = [(r0, min(r0 + cx.CH, bt)) for r0 in range(0, bt, cx.CH)]
                        for i, (src, m) in enumerate(((qkn, 2), (qkn, 3), (ac, 4), (ac, 5))):
                            self.pe("transpose", ps[3][:bt, i * 128:(i + 1) * 128], src[:, m, c0:c0 + bt], ident)
                        self.act("copy", kvtm[:bt, blk, :], ps[3][:bt, :])
                        ktm_ = kvtm[:bt, blk, 0:256].r("p (h d) -> p h d", h=4)
                        vtm_ = kvtm[:bt, blk, 256:512].r("p (h d) -> p h d", h=4)
                        a_b = astep[:bt, blk, :]
                        self.dve("tensor_tensor", aL[:bt, :, :bt], cL[:bt, :bt].unsq(1).bc([bt, 4, bt]),
                                 a_b.unsq(2).bc([bt, 4, bt]), ALU.mult)
                        self.dve("tensor_tensor", aU[:bt, :, :bt], cU[:bt, :bt].unsq(1).bc([bt, 4, bt]),
                                 a_b.unsq(2).bc([bt, 4, bt]), ALU.mult)
                        p0 = ps[0][:bt, :].r("p (h s) -> p h s", h=4)[:, :, :bt]
                        p1 = ps[1][:bt, :].r("p (h s) -> p h s", h=4)[:, :, :bt]
                        for hh in range(4):
                            self.pe("matmul", p0[:, hh, :], cU[:bt, :bt], aL[:bt, hh, :bt], start=True, stop=True)
                        for hh in range(4):
                            self.pe("matmul", p1[:, hh, :], cL[:bt, :bt], aU[:bt, hh, :bt], start=True, stop=True)
                        self.act("activation", decA[:bt, :, :bt], p0, AF.Exp)
                        self.act("activation", decT[:bt, :, :bt], p1, AF.Exp)
                        self.dve("tensor_tensor", decA[:bt, :, :bt], decA[:bt, :, :bt], cL[:bt, :bt].unsq(1).bc([bt, 4, bt]), ALU.mult)
                        self.dve("tensor_tensor", decT[:bt, :, :bt], decT[:bt, :, :bt], cU[:bt, :bt].unsq(1).bc([bt, 4, bt]), ALU.mult)
                        self.pe("matmul", ps[2][:bt, 0:4], cU[:bt, :bt], a_b, start=True, stop=True)
                        self.pe("matmul", ps[2][:bt, 4:8], cL[:bt, :bt], a_b, start=True, stop=True)
                        for ci in range(len(chunks)):
                            for e in range(2):
                                self.pe("matmul", ps[2][:, 8 + 2 * ci:10 + 2 * ci], consts[:bt, 7 + 2 * ci + e, :],
                                        astep[:bt, blk, e::2], start=(e == 0), stop=(e == 1))
                        self.act("activation", eg[:bt, 0:8], ps[2][:bt, 0:8], AF.Exp)
                        self.act("activation", egt.v().r("p c r -> p (c r)")[:, 0:2 * len(chunks)],
                                 ps[2][:, 8:8 + 2 * len(chunks)], AF.Exp)
                        self.dve("tensor_tensor", bkg[:bt, :], beta[:bt, blk, :], eg[:bt, 0:4], ALU.mult)
                        if self.asub < 3: continue
                        pk = ps[3][:bt, :].r("p (h s) -> p h s", h=4)[:, :, :bt]
                        import os
                        kkv = os.environ.get("KKV", "")
                        for hh in range(4):
                            e, pr = hh % 2, hh // 2
                            if kkv == "even" and e == 1:
                                continue
                            if kkv == "odd" and e == 0:
                                continue
                            self.dve("tensor_scalar_mul", kzb[:, hh, :bt], qkn[:, 2 + pr, c0:c0 + bt], cSame[:, e * 64:e * 64 + 1])
                            self.pe("matmul", pk[:, hh, :], kzb[:, hh, :bt], qkn[:, 2 + pr, c0:c0 + bt], start=True, stop=True)
                        import os
                        av = os.environ.get("AV", "")
                        for hh in range(4):
                            if av == "nodve":
                                continue
                            if av == "fsc":
                                self.dve("scalar_tensor_tensor", Am[:bt, hh, :bt], pk[:, hh, :], 1.0,
                                         decA[:bt, hh, :bt], ALU.mult, ALU.mult)
                                continue
                            if av == "tt":
                                self.dve("tensor_tensor", Am[:bt, hh, :bt], pk[:, hh, :], decA[:bt, hh, :bt], ALU.mult)
                                continue
                            self.dve("scalar_tensor_tensor", Am[:bt, hh, :bt], pk[:, hh, :], beta[:bt, blk, hh:hh + 1],
                                     decA[:bt, hh, :bt], ALU.mult, ALU.mult)
                        if self.a3 < 2: continue
                        pB = ps[0][:bt, :].r("p (h s) -> p h s", h=4)[:, :, :bt]
                        for hh in range(4):
                            self.pe("transpose", pB[:, hh, :], Am[:bt, hh, :bt], ident[:bt, :bt])
                        self.act("copy", Xa[0][:bt, :, :bt], pB)
                        self.dve("tensor_tensor", Rm[:bt, :, :bt], ident[:bt, :bt].unsq(1).bc([bt, 4, bt]), Xa[0][:bt, :, :bt], ALU.subtract)
                        self.dve("tensor_copy", XTa[0][:bt, :, :bt], Am[:bt, :, :bt])
                        neumann(bt, Xa[0], XTa[0], Rm, cx.CH)
                        self.dve("tensor_tensor", vb_[:bt, :, :], vtm_, beta[:bt, blk, :].unsq(2).bc([bt, 4, 64]), ALU.mult)
                        for hh in range(4):
                            e = hh % 2
                            self.dve("tensor_tensor", kbz[:bt, hh, e * 64:(e + 1) * 64], ktm_[:, hh, :],
                                     bkg[:bt, hh:hh + 1].bc([bt, 64]), ALU.mult)
                            for ci, (r0, r1) in enumerate(chunks):
                                self.dve("tensor_tensor", ktz[r0:r1, ci, hh, e * 64:(e + 1) * 64], ktm_[r0:r1, hh, :],
                                         eg[r0:r1, 4 + hh:5 + hh].bc([r1 - r0, 64]), ALU.mult)
                        for hh in range(4):
                            self.pe("matmul", ps[4][:bt, hh * 64:(hh + 1) * 64], Rm[:bt, hh, :bt], vb_[:bt, hh, :], start=True, stop=True)
                        for pr in range(2):
                            for e in range(2):
                                self.pe("matmul", ps[4][:, 256 + pr * 128:256 + pr * 128 + bt], kbz[:bt, 2 * pr + e, :],
                                        Rm[:bt, 2 * pr + e, :bt], start=(e == 0), stop=(e == 1))
                        self.act("copy", uval[:bt, :], ps[4][:bt, 0:256])
                        self.act("copy", wT[:, :, :bt], ps[4][:, 256:512].r("p (a t) -> p a t", a=2)[:, :, :bt])
                        pq = ps[5][:bt, :].r("p (h s) -> p h s", h=4)[:, :, :bt]
                        for hh in range(4):
                            e, pr = hh % 2, hh // 2
                            self.pe("matmul", pq[:, hh, :], kzb[:, hh, :bt], qkn[:, pr, c0:c0 + bt], start=True, stop=True)
                        self.dve("tensor_tensor", qkT[:bt, :, :bt], pq, decT[:bt, :, :bt], ALU.mult)
                        if self.asub < 5: continue
                        for ci, (r0, r1) in enumerate(chunks):
                            for pr in range(2):
                                self.pe("matmul", ps[6][:bt, pr * 128:(pr + 1) * 128], wT[:, pr, :bt], cx.Sd[l][:, pr, :], start=True, stop=True)
                            for pr in range(2):
                                self.pe("matmul", ps[6][:bt, 256 + pr * 128:256 + (pr + 1) * 128], qkn[:, pr, c0:c0 + bt],
                                        cx.Sd[l][:, pr, :], start=True, stop=True)
                            self.dve("tensor_tensor", unew[r0:r1, :], uval[r0:r1, :], ps[6][r0:r1, 0:256], ALU.subtract)
                            self.dve("tensor_tensor", oq[r0:r1, :].r("p (h d) -> p h d", h=4),
                                     ps[6][r0:r1, 256:512].r("p (h d) -> p h d", h=4),
                                     eg[r0:r1, 0:4].unsq(2).bc([r1 - r0, 4, 64]), ALU.mult)
                            for pr in range(2):
                                for e in range(2):
                                    hh = 2 * pr + e
                                    self.pe("matmul", ps[2][:, 64 + pr * 64:128 + pr * 64], ktz[:bt, ci, hh, :],
                                            unew[:bt, hh * 64:(hh + 1) * 64], start=(e == 0), stop=(e == 1))
                            for e in range(2):
                                sdv = cx.Sd[l][e * 64:(e + 1) * 64, :, e * 64:(e + 1) * 64]
                                self.dve("tensor_tensor", sdv, sdv, egt[e * 64:(e + 1) * 64, ci, :].unsq(2).bc([64, 2, 64]), ALU.mult)
                                self.dve("tensor_tensor", sdv, sdv, ps[2][e * 64:(e + 1) * 64, 64:192].r("p (a d) -> p a d", a=2), ALU.add)
                        if self.asub < 6: continue
                        for hh in range(4):
                            self.pe("matmul", ps[5][:bt, hh * 64:(hh + 1) * 64], qkT[:bt, hh, :bt], unew[:bt, hh * 64:(hh + 1) * 64],
                                    start=True, stop=True)
                        self.dve("tensor_tensor", otm[:bt, :], oq[:bt, :], ps[5][:bt, 0:256], ALU.add)
                        self.dve("tensor_tensor", o2[:bt, :], otm[:bt, :], otm[:bt, :], ALU.mult)
                        self.dve("tensor_reduce", ssq[:bt, :], o2[:bt, :].r("p (h d) -> p h d", h=4), AX.X, ALU.add)
                        self.act("activation", ssq[:bt, :], ssq[:bt, :], AF.Ln, bias=colf(0)[:bt, :], scale=1.0 / 64)
                        self.act("activation", ssq[:bt, :], ssq[:bt, :], AF.Exp, scale=-0.5)
                        self.dve("tensor_tensor", ontm[:bt, blk, :].r("p (h d) -> p h d", h=4),
                                 otm[:bt, :].r("p (h d) -> p h d", h=4), ssq[:bt, :].unsq(2).bc([bt, 4, 64]), ALU.mult)
                    if self.asub < 9:
                        for c in (0, 1):
                            self.dve("memset", mixT[:, c, :], 0.0)
                    for pr in (range(2) if self.asub >= 9 else ()):
                        for blk in range(nbk):
                            bt = min(128, ntok - blk * 128)
                            self.pe("transpose", ps[3][:, blk * 128:blk * 128 + bt], ontm[:bt, blk, pr * 128:(pr + 1) * 128], ident[:bt, :bt])
                        self.dve("scalar_tensor_tensor", mixT[:, pr, :ntok], ps[3][:, :ntok], cols2[:, l, 24:25], zs[:, pr, :ntok],
                                 ALU.mult, ALU.mult)
                    if cx.is_last:
                        for e in range(2):
                            self.dma("sp", V(cx.adelta_out.t[l].rearrange("(pr e) k v -> e k pr v", e=2)[e], ()),
                                     cx.Sd[l][e * 64:(e + 1) * 64, :, e * 64:(e + 1) * 64], final=True)

                if "C" not in self.stages:
                    for c in (6, 7):
                        self.dve("memset", mixT[:, c, :], 0.0)
                else:
                    nbk = (ntok + 127) // 128
                    self.dve("memset", kapz.v(), 0.0)
                    self.dve("memset", cktz.v(), 0.0)
                    self.dve("memset", cbtz.v(), 0.0)
                    c2 = lambda a, b=None: cols2[:, l, a:(a + 1 if b is None else b)]
                    dests = [c_r[:, 0, :ntok], c_r[:, 1, :ntok], c_k[:, 0, :ntok], c_k[:, 1, :ntok],
                             c_v[:, 0, :ntok], c_v[:, 1, :ntok], c_6[:, :ntok], c_7[:, :ntok]]

                    def epi_c(m, pv):
                        cm = crawm[m % 2]
                        self.act("copy", cm[:, 1:1 + ntok], pv)
                        self.act("copy", cm[:, 0:1], cx.ccarry[l][:, m:m + 1])
                        self.act("copy", cx.ccarry[l][:, m:m + 1], cm[:, ntok:ntok + 1])
                        tf = tmpf[m % 2]
                        self.dve("tensor_tensor", tf[:, :ntok], cm[:, 0:ntok], cm[:, 1:1 + ntok], ALU.subtract)
                        self.dve("scalar_tensor_tensor", dests[m], tf[:, :ntok], c2(32 + m), cm[:, 1:1 + ntok], ALU.mult, ALU.add)
                    dense_fm(w_in, 8, O_C, 1024, xb, ntok, epi_c)
                    if cx.is_last:
                        self.dma("sp", V(cx.cshift_out.t[l].rearrange("(c p) -> p c", p=128), ()), cx.ccarry[l].v(), final=True,
                                 allow_slow_non_contiguous=True)
                    self.act("activation", sq[0][:, :ntok], c_6[:, :ntok], AF.Tanh)
                    self.act("copy", sq[1][:, :ntok], c_6[:, :ntok])
                    for m in range(2):
                        self.pe("matmul", ps[0][:, :ntok], cw3[:, l, 0, m * 128:(m + 1) * 128], sq[0][:, :ntok], start=True, stop=True)
                        self.act("activation", c_wl[:, m, :ntok], ps[0][:, :ntok], AF.Sigmoid, bias=c2(40 + m), scale=1.0)
                        self.dve("tensor_scalar_mul", c_wl[:, m, :ntok], c_wl[:, m, :ntok], -math.exp(-0.5))
                        self.pe("matmul", ps[1][:, :ntok], cw3[:, l, 1, m * 128:(m + 1) * 128], sq[1][:, :ntok], start=True, stop=True)
                        self.act("activation", c_a[:, m, :ntok], ps[1][:, :ntok], AF.Sigmoid, bias=c2(42 + m), scale=1.0)
                    self.act("activation", sq[0][:, :ntok], c_7[:, :ntok], AF.Sigmoid)
                    for m in range(2):
                        self.pe("matmul", ps[0][:, :ntok], cw3[:, l, 2, m * 128:(m + 1) * 128], sq[0][:, :ntok], start=True, stop=True)
                        self.act("copy", c_g[:, m, :ntok], ps[0][:, :ntok])
                    for m in range(2):
                        tf = tmpf[m % 2]
                        self.dve("tensor_scalar_mul", c_kk[:, m, :ntok], c_k[:, m, :ntok], c2(44 + m))
                        self.act("activation", tf[:, :ntok], c_kk[:, m, :ntok], AF.Square)
                        self.pe("matmul", ps[2][:, :ntok], cSame, tf[:, :ntok], start=True, stop=True)
                        self.act("activation", tf[:, :ntok], ps[2][:, :ntok], AF.Ln, bias=colf(0), scale=1.0)
                        self.act("activation", tf[:, :ntok], tf[:, :ntok], AF.Exp, scale=-0.5)
                        self.dve("tensor_tensor", c_kk[:, m, :ntok], c_kk[:, m, :ntok], tf[:, :ntok], ALU.mult)
                        self.dve("tensor_scalar", tf[:, :ntok], c_a[:, m, :ntok], c2(46 + m), c2(54 + m), ALU.mult, ALU.add)
                        self.dve("tensor_tensor", c_km[:, m, :ntok], c_k[:, m, :ntok], tf[:, :ntok], ALU.mult)
                        self.dve("tensor_tensor", c_b[:, m, :ntok], c_a[:, m, :ntok], c_kk[:, m, :ntok], ALU.mult)
                        self.dve("scalar_tensor_tensor", tf[:, :ntok], c_r[:, m, :ntok], c2(48 + m), c_km[:, m, :ntok], ALU.mult, ALU.mult)
                        self.pe("matmul", ps[2][:, :ntok], cSame, tf[:, :ntok], start=True, stop=True)
                        self.dve("tensor_tensor", c_bon[:, m, :ntok], ps[2][:, :ntok], c_v[:, m, :ntok], ALU.mult)

                    for blk in range(nbk):
                        bt = min(128, ntok - blk * 128)
                        c0 = blk * 128
                        chunks = [(r0, min(r0 + cx.CH, bt)) for r0 in range(0, bt, cx.CH)]
                        for i, src in enumerate((c_wl, c_km, c_b, c_kk, c_v)):
                            pst = ps[3] if i % 2 == 0 else ps[4]
                            for m in range(2):
                                self.pe("transpose", pst[:bt, m * 128:(m + 1) * 128], src[:, m, c0:c0 + bt], ident)
                            self.act("copy", tm5[:bt, i, :], pst[:bt, 0:256])
                        wl_tm, km_tm, b_tm, kk_tm, v_tm = (tm5[:bt, i, :] for i in range(5))
                        self.pe("matmul", ps[0][:bt, 0:256], cU[:bt, :bt], wl_tm, start=True, stop=True)
                        for m in range(2):
                            self.pe("matmul", ps[1][:, m * 128:m * 128 + bt], tm5[:bt, 0, m * 128:(m + 1) * 128], cU[:bt, :bt],
                                    start=True, stop=True)
                        eng = cgt[:bt, 0, :]
                        egm = cgt[:bt, 1, :]
                        self.act("activation", eng, ps[0][:bt, 0:256], AF.Exp, scale=-1.0)
                        self.act("activation", egm, ps[0][:bt, 0:256], AF.Exp)
                        self.act("activation", cx0[:bt, :], wl_tm, AF.Exp, scale=-1.0)
                        self.dve("tensor_tensor", egm, egm, cx0[:bt, :], ALU.mult)
                        for hh in range(4):
                            e = hh % 2
                            hs = slice(hh * 64, (hh + 1) * 64)
                            self.dve("tensor_tensor", kapz[:bt, hh, e * 64:(e + 1) * 64], kk_tm[:, hs], egm[:, hs], ALU.mult)
                            for ci, (r0, r1) in enumerate(chunks):
                                self.dve("tensor_tensor", cktz[r0:r1, ci, hh, e * 64:(e + 1) * 64], km_tm[r0:r1, hs], eng[r0:r1, hs], ALU.mult)
                                self.dve("tensor_tensor", cbtz[r0:r1, ci, hh, e * 64:(e + 1) * 64], b_tm[r0:r1, hs], eng[r0:r1, hs], ALU.mult)
                        pgT = ps[1][:, 0:256].r("p (m t) -> p m t", m=2)[:, :, :bt]
                        egT = C1T[:, 0:2, :bt]
                        engT = C1T[:, 2:4, :bt]
                        egmT = C2T[:, 0:2, :bt]
                        self.act("activation", egT, pgT, AF.Exp)
                        self.act("activation", engT, pgT, AF.Exp, scale=-1.0)
                        self.dve("tensor_tensor", egmT, pgT, c_wl[:, :, c0:c0 + bt], ALU.subtract)
                        self.act("activation", egmT, egmT, AF.Exp)
                        self.dve("tensor_tensor", rTt[:, :, :bt], c_r[:, :, c0:c0 + bt], egT, ALU.mult)
                        self.dve("tensor_tensor", kapTt[:, :, :bt], c_kk[:, :, c0:c0 + bt], egmT, ALU.mult)
                        for ci, (r0, r1) in enumerate(chunks):
                            self.act("copy", pcc[:, ci, :], egT[:, :, r1 - 1])
                        for hh in range(4):
                            e, pr = hh % 2, hh // 2
                            self.dve("scalar_tensor_tensor", kTm[:, hh, :bt], c_km[:, pr, c0:c0 + bt], cSame[:, e * 64:e * 64 + 1],
                                     engT[:, pr, :], ALU.mult, ALU.mult)
                            self.dve("scalar_tensor_tensor", bTm[:, hh, :bt], c_b[:, pr, c0:c0 + bt], cSame[:, e * 64:e * 64 + 1],
                                     engT[:, pr, :], ALU.mult, ALU.mult)
                        def hview(pt):
                            return pt[:bt, :].r("p (h s) -> p h s", h=4)[:, :, :bt]
                        pB, pBm, pC1, pC2 = hview(ps[0]), hview(ps[3]), hview(ps[4]), hview(ps[5])
                        for hh in range(4):
                            pr = hh // 2
                            self.pe("matmul", pB[:, hh, :], bTm[:, hh, :bt], kapTt[:, pr, :bt], start=True, stop=True)
                        for hh in range(4):
                            pr = hh // 2
                            self.pe("matmul", pBm[:, hh, :], kTm[:, hh, :bt], kapTt[:, pr, :bt], start=True, stop=True)
                        for hh in range(4):
                            pr = hh // 2
                            self.pe("matmul", pC1[:, hh, :], kTm[:, hh, :bt], rTt[:, pr, :bt], start=True, stop=True)
                        for hh in range(4):
                            pr = hh // 2
                            self.pe("matmul", pC2[:, hh, :], bTm[:, hh, :bt], rTt[:, pr, :bt], start=True, stop=True)
                        msu = cSU[:bt, :bt].unsq(1).bc([bt, 4, bt])
                        mu_ = cU[:bt, :bt].unsq(1).bc([bt, 4, bt])
                        X, XT, R = Xa[0], XTa[0], cR2
                        self.dve("tensor_tensor", X[:bt, :, :bt], pB, msu, ALU.mult)
                        self.dve("tensor_tensor", cBmT[:bt, :, :bt], pBm, msu, ALU.mult)
                        self.dve("tensor_tensor", C1T[:bt, :, :bt], pC1, mu_, ALU.mult)
                        self.dve("scalar_tensor_tensor", C2T[:bt, :, :bt], pC2, -1.0, mu_, ALU.mult, ALU.mult)
                        pA = hview(ps[1])
                        for hh in range(4):
                            self.pe("transpose", pA[:, hh, :], X[:bt, hh, :bt], ident[:bt, :bt])
                        self.act("copy", XT[:bt, :, :bt], pA)
                        self.dve("tensor_tensor", R[:bt, :, :bt], ident[:bt, :bt].unsq(1).bc([bt, 4, bt]), X[:bt, :, :bt], ALU.subtract)
                        neumann(bt, X, XT, R, cx.CH)
                        for hh in range(4):
                            hs = slice(hh * 64, (hh + 1) * 64)
                            self.pe("matmul", ps[4][:bt, hs], cBmT[:bt, hh, :bt], v_tm[:, hs], start=True, stop=True)
                        self.act("copy", cx0[:bt, :], ps[4][:bt, 0:256])
                        for hh in range(4):
                            hs = slice(hh * 64, (hh + 1) * 64)
                            self.pe("matmul", ps[4][:bt, 256 + hh * 64:256 + (hh + 1) * 64], R[:bt, hh, :bt], cx0[:bt, hs], start=True, stop=True)
                        self.act("copy", cU0[:bt, :], ps[4][:bt, 256:512])
                        for pr in range(2):
                            for e in range(2):
                                self.pe("matmul", ps[5][:, pr * 128:pr * 128 + bt], kapz[:bt, 2 * pr + e, :], R[:bt, 2 * pr + e, :bt],
                                        start=(e == 0), stop=(e == 1))
                        self.act("copy", wT[:, :, :bt], ps[5][:, 0:256].r("p (a t) -> p a t", a=2)[:, :, :bt])
                        for ci in range(len(chunks)):
                            for pr in range(2):
                                for e in range(2):
                                    hh = 2 * pr + e
                                    self.pe("matmul", ps[5][:, 256 + ci * 128 + pr * 64:256 + ci * 128 + (pr + 1) * 64],
                                            cktz[:bt, ci, hh, :], v_tm[:, hh * 64:(hh + 1) * 64], start=(e == 0), stop=(e == 1))
                        self.act("copy", kvp[:, 0:len(chunks), :, :], ps[5][:, 256:256 + 128 * len(chunks)].r("p (c a d) -> p c a d", c=len(chunks), a=2))
                        for ci, (r0, r1) in enumerate(chunks):
                            for pr in range(2):
                                self.pe("matmul", ps[6][:bt, pr * 128:(pr + 1) * 128], wT[:, pr, :bt], cx.Tst[l][:, pr, :], start=True, stop=True)
                            for pr in range(2):
                                self.pe("matmul", ps[6][:bt, 256 + pr * 128:256 + (pr + 1) * 128], rTt[:, pr, :bt], cx.Tst[l][:, pr, :],
                                        start=True, stop=True)
                            self.dve("tensor_tensor", cU_[r0:r1, :], cU0[r0:r1, :], ps[6][r0:r1, 0:256], ALU.add)
                            self.dve("tensor_copy", cyq[r0:r1, :], ps[6][r0:r1, 256:512])
                            for e in range(2):
                                tdv = cx.Tst[l][e * 64:(e + 1) * 64, :, e * 64:(e + 1) * 64]
                                self.dve("tensor_tensor", tdv, tdv, kvp[e * 64:(e + 1) * 64, ci, :, :], ALU.add)
                            for pr in range(2):
                                for e in range(2):
                                    hh = 2 * pr + e
                                    self.pe("matmul", ps[2][:, 64 + pr * 64:128 + pr * 64], cbtz[:bt, ci, hh, :],
                                            cU_[:bt, hh * 64:(hh + 1) * 64], start=(e == 0), stop=(e == 1))
                            for e in range(2):
                                tdv = cx.Tst[l][e * 64:(e + 1) * 64, :, e * 64:(e + 1) * 64]
                                self.dve("tensor_tensor", tdv, tdv, ps[2][e * 64:(e + 1) * 64, 64:192].r("p (a d) -> p a d", a=2), ALU.subtract)
                                self.dve("tensor_tensor", tdv, tdv, pcc[e * 64:(e + 1) * 64, ci, :].unsq(2).bc([64, 2, 64]), ALU.mult)
                        for hh in range(4):
                            hs = slice(hh * 64, (hh + 1) * 64)
                            self.pe("matmul", ps[4][:bt, hs], C1T[:bt, hh, :bt], v_tm[:, hs], start=True, stop=False)
                            self.pe("matmul", ps[4][:bt, hs], C2T[:bt, hh, :bt], cU_[:bt, hs], start=False, stop=True)
                        self.dve("tensor_tensor", cytm[:bt, :], cyq[:bt, :], ps[4][:bt, 0:256], ALU.add)
                        y3 = cytm[:bt, :].r("p (h d) -> p h d", h=4)
                        self.dve("tensor_reduce", cst[:bt, 0:4], y3, AX.X, ALU.add)
                        self.dve("tensor_scalar_mul", cst[:bt, 0:4], cst[:bt, 0:4], 1.0 / 64)
                        self.dve("tensor_tensor", y3, y3, cst[:bt, 0:4].unsq(2).bc([bt, 4, 64]), ALU.subtract)
                        self.dve("tensor_tensor", cyq[:bt, :], cytm[:bt, :], cytm[:bt, :], ALU.mult)
                        self.dve("tensor_reduce", cst[:bt, 4:8], cyq[:bt, :].r("p (h d) -> p h d", h=4), AX.X, ALU.add)
                        self.act("activation", cst[:bt, 4:8], cst[:bt, 4:8], AF.Ln, bias=colf(9)[:bt, :], scale=1.0 / 64)
                        self.act("activation", cst[:bt, 4:8], cst[:bt, 4:8], AF.Exp, scale=-0.5)
                        self.dve("tensor_tensor", cyn(blk)[:bt, :].r("p (h d) -> p h d", h=4), y3,
                                 cst[:bt, 4:8].unsq(2).bc([bt, 4, 64]), ALU.mult)
                    for pr in range(2):
                        for blk in range(nbk):
                            bt = min(128, ntok - blk * 128)
                            self.pe("transpose", ps[3][:, blk * 128:blk * 128 + bt], cyn(blk)[:bt, pr * 128:(pr + 1) * 128], ident[:bt, :bt])
                        tf = tmpf[pr]
                        self.dve("scalar_tensor_tensor", tf[:, :ntok], ps[3][:, :ntok], c2(50 + pr), c_bon[:, pr, :ntok], ALU.mult, ALU.add)
                        self.dve("scalar_tensor_tensor", mixT[:, 6 + pr, :ntok], tf[:, :ntok], c2(52 + pr), c_g[:, pr, :ntok], ALU.add, ALU.mult)
                    if cx.is_last:
                        for pr in range(2):
                            self.pe("transpose", ps[3][:, pr * 128:(pr + 1) * 128], cx.Tst[l][:, pr, :], ident)
                        self.act("copy", wT.v(), ps[3][:, 0:256].r("p (a t) -> p a t", a=2))
                        for e in range(2):
                            self.dma("sp", V(cx.cwkv_out.t[l].rearrange("(pr e) v k -> e v pr k", e=2)[e], ()),
                                     wT[e * 64:(e + 1) * 64, :, e * 64:(e + 1) * 64], final=True)

                def epi_res(m, pv):
                    self.dve("tensor_tensor", h[:, m, :ntok], h[:, m, :ntok], pv, ALU.add)
                dense_fm(("w_out", l), 8, 0, D_MODEL, mixT, ntok, epi_res)

                rmsnorm_to_xb(ntok, lambda k: col("norm_ffn", l, k))

                def epi_ff1(m, pv):
                    tf = tmpf[m % 2]
                    self.act("activation", tf[:, :ntok], pv, AF.Relu)
                    self.dve("tensor_tensor", uT[:, m, :ntok], tf[:, :ntok], tf[:, :ntok], ALU.mult)
                dense_fm(("w_ff1", l), 8, 0, D_FF, xb, ntok, epi_ff1)
                dense_fm(("w_ff2", l), 32, 0, D_MODEL, uT, ntok, epi_res)

                rmsnorm_to_xb(ntok, lambda k: col("norm_ple", l, k))

                def epi_gate(m, pv):
                    self.act("activation", gate[:, m, :ntok], pv, AF.Sigmoid)
                dense_fm(("w_ple_gate", l), 8, 0, D_MODEL, xb, ntok, epi_gate)
                self.dma("sp", ptok[:bl, :nbk, :], V(cx.p_src(l).rearrange("(b p) d -> p b d", p=bl), ()))
                for k in range(2):
                    for b in range(nbk):
                        self.pe("transpose", ps[3][:, b * 128:b * 128 + bl], ptok[:bl, b, k * 128:(k + 1) * 128], ident[:bl, :bl])
                    self.act("copy", pT[:, k, :ntok], ps[3][:, :ntok])

                def epi_ple(m, pv):
                    self.dve("tensor_tensor", tmpf[m % 2][:, :ntok], gate[:, m, :ntok], pv, ALU.mult)
                    self.dve("tensor_tensor", h[:, m, :ntok], h[:, m, :ntok], tmpf[m % 2][:, :ntok], ALU.add)
                dense_fm(("w_ple_proj", l), 2, 0, D_MODEL, pT, ntok, epi_ple)

            for k in range(8):
                s_ = sq[k % 2]
                self.act("activation", s_[:, :ntok], h[:, k, :ntok], AF.Square)
                self.pe("matmul", ps[2][:, :ntok], onesb.v(), s_[:, :ntok], start=(k == 0), stop=(k == 7))
            self.act("activation", rstd[:, :ntok], ps[2][:, :ntok], AF.Ln, bias=colf(0), scale=1.0 / D_MODEL)
            self.act("activation", rstd[:, :ntok], rstd[:, :ntok], AF.Exp, scale=-0.5)
            for k in range(8):
                tf = tmpf[k % 2]
                self.dve("scalar_tensor_tensor", tf[:, :ntok], h[:, k, :ntok], cols[:, 51 + k:52 + k], rstd[:, :ntok], ALU.mult, ALU.mult)
                for b in range(nbk):
                    self.pe("transpose", ps[3][:bl, b * 128:(b + 1) * 128], tf[:, b * 128:b * 128 + bl], ident)
                self.act("copy", xtok[:bl, :nbk, k * 128:(k + 1) * 128], ps[3][:bl, :nbk * 128].r("p (b f) -> p b f", b=nbk))
            self.dma("sp", V(cx.y_dst.rearrange("(b p) d -> p b d", p=bl), ()), xtok[:bl, :nbk, :], final=True)

        class Cx:
            pass

        if "prompt" in self.stages:
            for t in range(NT):
                cx = Cx()
                t0 = t * TT
                cx.ntok = TT; cx.CH = min(64, SEQ); cx.key_base = t0; cx.masked = True; cx.is_last = (t == NT - 1)
                cx.x_src = x_in.t[t0:t0 + TT, :]
                cx.rope_src = rope_in.t[t0:t0 + TT, :]
                cx.p_src = lambda l, t0=t0: p_in.t[l, t0:t0 + TT, :]
                cx.y_dst = y_out.t[t0:t0 + TT, :]
                cx.bk_dst = lambda l, t0=t0: bk_out.t[l, t0:t0 + TT, :]
                cx.bv_dst = lambda l, t0=t0: bv_out.t[l, t0:t0 + TT, :]
                cx.kT_scr, cx.v_scr = kT_scr, v_scr
                cx.acarry, cx.Sd, cx.Tst, cx.ccarry = acarry, Sd, Tst, ccarry
                cx.aconv_out, cx.adelta_out, cx.cshift_out, cx.cwkv_out = aconv_out, adelta_out, cshift_out, cwkv_out
                tile_body(cx)

        if "sample" in self.stages:
            DEC, PAST = self.DEC, self.PAST
            xs_in = self.din("xs", [DEC, D_MODEL])
            ps_in = self.din("psm", [DEPTH, DEC, PLE_DIM])
            ropes_in = self.din("rope_s", [DEC, 64])
            ck_in = self.din("cache_k", [DEPTH, PAST, 512])
            cv_in = self.din("cache_v", [DEPTH, PAST, 512])
            sconv_in = self.din("st_conv", [DEPTH, 3, 768])
            sdelta_in = self.din("st_delta", [DEPTH, 4, 64, 64])
            sshift_in = self.din("st_shift", [DEPTH, 1024])
            swkv_in = self.din("st_wkv", [DEPTH, 4, 64, 64])
            ys_out = self.dout("y_s", [DEC, D_MODEL])
            bks_out = self.dout("b_k_s", [DEPTH, DEC, 512])
            bvs_out = self.dout("b_v_s", [DEPTH, DEC, 512])
            aconvs_out = self.dout("a_conv_s", [DEPTH, 3, 768])
            adeltas_out = self.dout("a_delta_s", [DEPTH, 4, 64, 64])
            cshifts_out = self.dout("c_shift_s", [DEPTH, 1024])
            cwkvs_out = self.dout("c_wkv_s", [DEPTH, 4, 64, 64])
            kT_scr_s = [self.dscr(f"kT_scr_s{l}", [512, PAST + DEC], BF16) for l in range(DEPTH)]
            v_scr_s = [self.dscr(f"v_scr_s{l}", [PAST + DEC, 512], BF16) for l in range(DEPTH)]
            psb = psbT.v()
            for l in range(DEPTH):
                for m in range(6):
                    self.dma("sp", acarry_s[l][:, m, :], V(sconv_in.t[l, :, m * 128:(m + 1) * 128].rearrange("j p -> p j"), ()),
                             allow_slow_non_contiguous=True)
                self.dma("sp", ccarry_s[l].v(), V(sshift_in.t[l].rearrange("(c p) -> p c", p=128), ()), allow_slow_non_contiguous=True)
                self.dve("memset", Sd_s[l].v(), 0.0)
                self.dve("memset", wT.v(), 0.0)
                for e in range(2):
                    self.dma("sp", Sd_s[l][e * 64:(e + 1) * 64, :, e * 64:(e + 1) * 64],
                             V(sdelta_in.t[l].rearrange("(pr e) k v -> e k pr v", e=2)[e], ()))
                    self.dma("sp", wT[e * 64:(e + 1) * 64, :, e * 64:(e + 1) * 64],
                             V(swkv_in.t[l].rearrange("(pr e) v k -> e v pr k", e=2)[e], ()))
                for pr in range(2):
                    self.pe("transpose", ps[3][:, pr * 128:(pr + 1) * 128], wT[:, pr, :], ident)
                self.act("copy", Tst_s[l].v(), ps[3][:, 0:256].r("p (a t) -> p a t", a=2))
                for g in range(PAST // 512):
                    for b in range(4):
                        r0 = g * 512 + b * 128
                        kf, vf = kfs[b % 2], vfs[b % 2]
                        self.dma("sp", kf.v(), V(ck_in.t[l, r0:r0 + 128, :], ()))
                        self.act("copy", ktm[:, b, :], kf.v())
                        self.dma("sp", vf.v(), V(cv_in.t[l, r0:r0 + 128, :], ()))
                        self.dve("tensor_copy", vtb[:, b, :], vf.v())
                    for c in range(4):
                        for b in range(4):
                            self.pe("transpose", psb[:, 512 + b * 128:512 + (b + 1) * 128], ktm[:, b, c * 128:(c + 1) * 128], identb.v())
                        self.dve("tensor_copy", kTt[:, c, :], psb[:, 512:1024])
                    self.dma("sp", V(kT_scr_s[l].t[:, g * 512:(g + 1) * 512].rearrange("(c p) s -> p c s", p=128), (kT_scr_s[l].buf,)), kTt.v())
                    self.dma("sp", V(v_scr_s[l].t[g * 512:(g + 1) * 512, :].rearrange("(b p) d -> p b d", p=128), (v_scr_s[l].buf,)), vtb.v())
            cx = Cx()
            cx.ntok = DEC; cx.CH = min(64, DEC); cx.key_base = PAST; cx.masked = False; cx.is_last = True
            cx.x_src = xs_in.t[:, :]
            cx.rope_src = ropes_in.t[:, :]
            cx.p_src = lambda l: ps_in.t[l, :, :]
            cx.y_dst = ys_out.t[:, :]
            cx.bk_dst = lambda l: bks_out.t[l, :, :]
            cx.bv_dst = lambda l: bvs_out.t[l, :, :]
            cx.kT_scr, cx.v_scr = kT_scr_s, v_scr_s
            cx.acarry, cx.Sd, cx.Tst, cx.ccarry = acarry_s, Sd_s, Tst_s, ccarry_s
            cx.aconv_out, cx.adelta_out, cx.cshift_out, cx.cwkv_out = aconvs_out, adeltas_out, cshifts_out, cwkvs_out
            tile_body(cx)

        self.S.emit(st)
        st.close()
        return nc


def _consts():
    c = np.zeros((128, 11, 128), np.float32)
    i = np.arange(128)
    same = (i[:, None] // 64) == (i[None, :] // 64)
    c[:, 0, :] = np.eye(128)
    c[:, 1, :] = 1.0
    c[:, 2, :] = (i[:, None] <= i[None, :]) & same
    c[:, 3, :] = (i[:, None] < i[None, :]) & same
    c[:, 4, :] = (i[:, None] > i[None, :]) & same
    c[:, 5, :] = (i[:, None] >= i[None, :]) & same
    c[:, 6, :] = same
    for ci in range(2):
        for e in range(2):
            c[:, 7 + 2 * ci + e, :] = ((i[:, None] // 64) == ci) & ((i[None, :] // 64) == e)
    return c


def _cols2(inp):
    c = np.zeros((128, DEPTH, 64), np.float32)
    for l in range(DEPTH):
        cw = inp["a_conv_w"][l]
        for m in range(6):
            c[:, l, m * 4:m * 4 + 4] = cw[:, m * 128:(m + 1) * 128].T
        c[:, l, 24] = np.tile(inp["a_norm"][l], 2)
        c[:, l, 32:40] = inp["c_mu"][l].reshape(8, 128).T
        for nm, base in (("c_w0", 40), ("c_a0", 42), ("c_k_k", 44), ("c_k_a", 46), ("c_r_k", 48), ("c_ln_w", 50), ("c_ln_b", 52)):
            c[:, l, base:base + 2] = inp[nm][l].reshape(2, 128).T
    return c


def _cw3(inp):
    w = np.zeros((128, DEPTH, 3, 256), np.float32)
    for l in range(DEPTH):
        w[0:64, l, 0, :] = inp["c_w_up"][l]
        w[64:128, l, 1, :] = inp["c_a_up"][l]
        w[:, l, 2, :] = inp["c_g_up"][l]
    return w


def _rowp(inp):
    return np.concatenate([inp["a_A_log"], inp["a_dt_bias"]], axis=1).astype(np.float32)


def _amask():
    kp = np.arange(128)[:, None]
    q = np.arange(512)[None, :]
    m = np.zeros((128, 4, 512), np.float32)
    for j in range(4):
        m[:, j, :] = (2 * j + kp // 64) <= (q // 64)
    return m.astype(ml_dtypes.bfloat16)


def _rope_table(pos):
    half = 32
    inv = (10000.0 ** (-2.0 * np.arange(half, dtype=np.float32) / 64)).astype(np.float32)
    ang = pos.astype(np.float32)[:, None] * inv[None, :]
    return np.concatenate([np.cos(ang), np.sin(ang)], axis=1).astype(np.float32)


def _cols(inp):
    c = np.zeros((128, 64), np.float32)
    for l in range(DEPTH):
        for nm, base in (("norm_mix", 0), ("norm_ffn", 8), ("norm_ple", 16)):
            c[:, l * 24 + base:l * 24 + base + 8] = inp[nm][l].reshape(8, 128).T
        lam_init = 0.8 - 0.6 * math.exp(-0.3 * l)
        c[:, 48 + l] = inp["b_norm"][l]
    c[:, 50] = NORM_EPS
    c[:, 59] = C_LN_EPS
    c[:, 51:59] = inp["norm_final"].reshape(8, 128).T
    return c


_CACHE = {}


def run(inputs, seq, n_cores, stages=("prompt", "A", "C", "sample"), trace=False):
    key = (seq, stages)
    if key not in _CACHE:
        b = Builder(seq, stages=stages)
        b.build()
        _CACHE[key] = b
    b = _CACHE[key]
    cols = _cols(inputs)
    consts = _consts()
    amask = _amask()
    rope = _rope_table(np.arange(seq))
    lamrow = np.stack([np.stack([inputs[n][l] for n in ("b_lam_q1", "b_lam_k1", "b_lam_q2", "b_lam_k2")])
                       for l in range(DEPTH)]).astype(np.float32)
    shared = {"cols": cols, "consts": consts, "amask": amask, "rope": rope, "lamrow": lamrow,
              "cols2": _cols2(inputs), "rowp": _rowp(inputs), "cw3": _cw3(inputs)}
    for nm in ("w_in", "w_out", "w_ff1", "w_ff2", "w_ple_gate", "w_ple_proj"):
        shared[nm] = inputs[nm]
    if "sample" in stages:
        past = inputs["cache_b_k"].shape[2]
        dec = inputs["x_sample"].shape[1]
        shared["rope_s"] = _rope_table(np.arange(past, past + dec))
    in_maps = []
    for c in range(n_cores):
        m = dict(shared)
        m["x"] = np.ascontiguousarray(inputs["x_prompt"][c])
        m["p"] = np.ascontiguousarray(inputs["p_prompt"][:, c])
        if "sample" in stages:
            m["xs"] = np.ascontiguousarray(inputs["x_sample"][c])
            m["psm"] = np.ascontiguousarray(inputs["p_sample"][:, c])
            m["cache_k"] = np.ascontiguousarray(inputs["cache_b_k"][:, c]).reshape(DEPTH, past, 512)
            m["cache_v"] = np.ascontiguousarray(inputs["cache_b_v"][:, c]).reshape(DEPTH, past, 512)
            m["st_conv"] = np.ascontiguousarray(inputs["state_a_conv"][:, c])
            m["st_delta"] = np.ascontiguousarray(inputs["state_a_delta"][:, c])
            m["st_shift"] = np.ascontiguousarray(inputs["state_c_shift"][:, c])
            m["st_wkv"] = np.ascontiguousarray(inputs["state_c_wkv"][:, c])
        in_maps.append(m)
    res = run_bass_kernel_spmd(b.nc, in_maps, core_ids=list(range(n_cores)), trace=trace)
    return res


def kernel(**inputs):
    inputs = {k: np.asarray(v) for k, v in inputs.items()}
    n, seq = inputs["x_prompt"].shape[0], inputs["x_prompt"].shape[1]
    dec = inputs["x_sample"].shape[1]
    r = run(inputs, seq, n).results

    def st(name, axis):
        return np.stack([np.asarray(r[c][name]) for c in range(n)], axis=axis)
    return (st("y", 0), st("y_s", 0),
            st("a_conv", 1), st("a_delta", 1),
            st("b_k", 1).reshape(DEPTH, n, seq, 4, 128), st("b_v", 1).reshape(DEPTH, n, seq, 4, 128),
            st("c_shift", 1), st("c_wkv", 1),
            st("a_conv_s", 1), st("a_delta_s", 1),
            st("b_k_s", 1).reshape(DEPTH, n, dec, 4, 128), st("b_v_s", 1).reshape(DEPTH, n, dec, 4, 128),
            st("c_shift_s", 1), st("c_wkv_s", 1))
```

```python
import math
from contextlib import ExitStack

import numpy as np
import ml_dtypes
import concourse.bass as bass
import concourse.mybir as mybir
from concourse.bass_utils import run_bass_kernel_spmd

F32 = mybir.dt.float32
BF16 = mybir.dt.bfloat16
AF = mybir.ActivationFunctionType
ALU = mybir.AluOpType
AX = mybir.AxisListType

D_MODEL = 1024
DEPTH = 2
PLE_DIM = 256
D_FF = 4096
IN_WIDTH = 3592
NORM_EPS = 1e-6
L2_EPS = 1e-6
C_LN_EPS = 64e-5
O_AQKV, O_AZ, O_AA, O_AB, O_BQ, O_BK, O_BV, O_C = 0, 768, 1024, 1028, 1032, 1544, 2056, 2568

ENGS = ("pe", "act", "dve", "pool", "sp")
SEM_WRAP = 30000


class Buf:
    __slots__ = ("name", "writer", "readers", "chan", "multi")

    def __init__(self, name, multi=False):
        self.name = name
        self.writer = [] if multi else None
        self.readers = []
        self.chan = None
        self.multi = multi


class Chan:
    __slots__ = ("sem", "count", "name")

    def __init__(self, name):
        self.name = name
        self.sem = None
        self.count = 0


class Op:
    __slots__ = ("eng", "fn", "deps", "signal", "semidx", "semval", "is_dma", "chan", "chan_val")

    def __init__(self, eng, fn):
        self.eng = eng
        self.fn = fn
        self.deps = []
        self.signal = False
        self.semidx = 0
        self.semval = 0
        self.is_dma = False
        self.chan = None
        self.chan_val = 0


class Sched:
    def __init__(self, nc):
        self.nc = nc
        self.ops = {e: [] for e in ENGS}
        self.chans = []
        self.final_waits = []

    def _collect(self, op, reads, writes, waits=()):
        deps = []
        for b in waits:
            if b.multi:
                deps.extend(b.writer)
            elif b.writer is not None:
                deps.append(b.writer)
            deps.extend(b.readers)
        for b in reads:
            if b.multi:
                deps.extend(b.writer)
            elif b.writer is not None:
                deps.append(b.writer)
        for b in writes:
            if b.multi:
                deps.extend(b.writer)
            elif b.writer is not None:
                deps.append(b.writer)
            deps.extend(b.readers)
        seen = set()
        for d in deps:
            if d is op or id(d) in seen:
                continue
            seen.add(id(d))
            if d.eng == "pe" and op.eng == "pe" and not d.is_dma and not op.is_dma:
                continue
            op.deps.append(d)
        for b in writes:
            if b.multi:
                b.writer.append(op)
            else:
                b.writer = op
                b.readers = []
        for b in reads:
            b.readers.append(op)

    def op(self, eng, fn, reads=(), writes=(), waits=()):
        o = Op(eng, fn)
        self.ops[eng].append(o)
        self._collect(o, reads, writes, waits)
        return o

    def dma(self, eng, out, in_, reads=(), writes=(), chan_buf=None, final=False, waits=(), **kw):
        if chan_buf is None:
            chan_buf = (list(writes) + list(reads))[0]
        if chan_buf.chan is None:
            chan_buf.chan = {}
        if eng not in chan_buf.chan:
            chan_buf.chan[eng] = Chan(chan_buf.name + "_" + eng)
            self.chans.append(chan_buf.chan[eng])
        ch = chan_buf.chan[eng]
        o = Op(eng, None)
        o.is_dma = True
        o.chan = ch
        ch.count += 1
        o.chan_val = 16 * ch.count
        o.fn = lambda e, out=out, in_=in_, kw=kw: e.dma_start(out=out, in_=in_, **kw)
        self.ops[eng].append(o)
        self._collect(o, reads, writes, waits)
        if final:
            self.final_waits.append(o)
        return o

    def emit(self, stack):
        nc = self.nc
        for e in ENGS:
            for o in self.ops[e]:
                for d in o.deps:
                    if not d.is_dma:
                        d.signal = True
        nsem = {}
        for e in ENGS:
            cnt = 0
            for o in self.ops[e]:
                if o.signal and not o.is_dma:
                    o.semidx = cnt // SEM_WRAP
                    o.semval = cnt % SEM_WRAP + 1
                    cnt += 1
            nsem[e] = cnt // SEM_WRAP + 1
        esems = {e: [stack.enter_context(nc.semaphore(f"s_{e}{i}")) for i in range(nsem[e])] for e in ENGS}
        for i, ch in enumerate(self.chans):
            ch.sem = stack.enter_context(nc.semaphore(f"c{i}_{ch.name}"))
        block = stack.enter_context(nc.Block())

        def run(e, eng):
            seen = {}
            maxidx = {}
            for o in self.ops[e]:
                need = {}
                for d in o.deps:
                    if d.is_dma:
                        key = ("c", id(d.chan)); sem = d.chan.sem; val = d.chan_val
                    else:
                        key = (d.eng, d.semidx); sem = esems[d.eng][d.semidx]; val = d.semval
                    if key not in need or need[key][1] < val:
                        need[key] = (sem, val)
                for key, (sem, val) in need.items():
                    if key[0] != "c":
                        if maxidx.get(key[0], -1) > key[1]:
                            continue
                    if seen.get(key, 0) >= val:
                        continue
                    seen[key] = val
                    if key[0] != "c":
                        maxidx[key[0]] = max(maxidx.get(key[0], -1), key[1])
                    eng.wait_ge(sem, val)
                ins = o.fn(eng)
                if o.is_dma:
                    ins.then_inc(o.chan.sem, 16)
                elif o.signal:
                    ins.then_inc(esems[e][o.semidx], 1)
            if e == "sp":
                fin = {}
                for o in self.final_waits:
                    fin[id(o.chan)] = (o.chan, max(o.chan_val, fin.get(id(o.chan), (None, 0))[1]))
                for ch, val in fin.values():
                    eng.wait_ge(ch.sem, val)

        @block.tensor
        def _(eng):
            run("pe", eng)

        @block.scalar
        def _(eng):
            run("act", eng)

        @block.vector
        def _(eng):
            run("dve", eng)

        @block.gpsimd
        def _(eng):
            run("pool", eng)

        @block.sync
        def _(eng):
            run("sp", eng)


class V:
    __slots__ = ("ap", "bufs", "wb")

    def __init__(self, ap, bufs, wb=()):
        self.ap = ap
        self.bufs = bufs
        self.wb = wb

    def __getitem__(self, idx):
        return V(self.ap[idx], self.bufs, self.wb)

    def r(self, s, **kw):
        return V(self.ap.rearrange(s, **kw), self.bufs, self.wb)

    def bc(self, shape):
        return V(self.ap.to_broadcast(shape), self.bufs, self.wb)

    def unsq(self, ax):
        return V(self.ap.unsqueeze(ax), self.bufs, self.wb)


class T:
    def __init__(self, t, name, track=True, multi=False, bufs=None):
        self.t = t
        self.name = name
        self.buf = Buf(name, multi=multi) if track else None
        self.bufs = bufs
        self.wbufs = None

    def __getitem__(self, idx):
        if self.wbufs is not None:
            own = (self.buf,) if self.buf is not None else tuple(self.bufs)
            return V(self.t[idx], own, tuple(self.wbufs()))
        if self.bufs is not None:
            b = self.bufs() if callable(self.bufs) else self.bufs
            return V(self.t[idx], tuple(b))
        return V(self.t[idx], (self.buf,) if self.buf is not None else ())

    def v(self):
        return self[:]


class Builder:
    def __init__(self, seq, dec_seq=16, past=2048, stages=("prompt", "A", "C", "sample")):
        self.SEQ = seq
        self.DEC = dec_seq
        self.PAST = past
        self.TT = min(512, seq)
        self.CH = min(64, seq)
        self.stages = stages
        self.nc = bass.Bass("TRN2", target_bir_lowering=False)
        self.S = Sched(self.nc)
        self.st = ExitStack()
        self.in_names = []
        self.out_names = []
        import os
        self.bsub = int(os.environ.get('BSUB', '9'))
        self.poolsum = os.environ.get('POOLSUM', '0') == '1'
        self.wcache = os.environ.get('WCACHE', '1') == '1'
        self.asub = int(os.environ.get('ASUB', '9'))
        self.a1 = int(os.environ.get('A1', '9'))
        self.a3 = int(os.environ.get('A3', '9'))

    def sb(self, name, shape, dt=F32):
        import os
        if os.environ.get("ALLOCDBG"):
            print("ALLOC", name, shape, dt, int(np.prod(shape[1:])) * (4 if dt == F32 else 2))
        return T(self.st.enter_context(self.nc.sbuf_tensor("s_" + name, list(shape), dt)), name)

    def arena(self, name, nfloats):
        t = self.st.enter_context(self.nc.sbuf_tensor("s_" + name, [128, nfloats], F32))
        return {"t": t, "off": 0, "bufs": [], "n": nfloats, "parts": {}}

    def _arena_ap(self, ar, off, shape, dt):
        n = int(np.prod(shape[1:]))
        nfl = n if dt == F32 else n // 2
        assert off + nfl <= ar["n"], (off, nfl, ar["n"])
        ap = ar["t"][:, off:off + nfl]
        if dt != F32:
            ap = ap.bitcast(dt)
        if len(shape) == 3:
            ap = ap.rearrange("p (a b) -> p a b", a=shape[1])
        elif len(shape) == 4:
            ap = ap.rearrange("p (a b c) -> p a b c", a=shape[1], b=shape[2])
        return ap, nfl

    def sub(self, ar, name, shape, dt=F32, part=None):
        if part is None:
            ap, nfl = self._arena_ap(ar, ar["off"], shape, dt)
            ar["off"] += nfl
            tt = T(ap, name)
            ar["bufs"].append(tt.buf)
            return tt
        pd = ar["parts"].setdefault(part, {"off": 0, "bufs": []})
        ap, nfl = self._arena_ap(ar, pd["off"], shape, dt)
        pd["off"] += nfl
        tt = T(ap, name)
        pd["bufs"].append(tt.buf)
        tt.wbufs = lambda: [b for p, d in ar["parts"].items() if p != part for b in d["bufs"]]
        return tt

    def whole(self, ar, name, shape, dt=F32):
        ap, _ = self._arena_ap(ar, 0, shape, dt)
        return T(ap, name, track=False, bufs=ar["bufs"])

    def psum(self, name, shape, dt=F32):
        return T(self.st.enter_context(self.nc.psum_tensor("p_" + name, list(shape), dt)), name)

    def din(self, name, shape, dt=F32):
        self.in_names.append(name)
        return T(self.nc.dram_tensor(name, list(shape), dt, kind="ExternalInput").ap(), name, track=False)

    def dout(self, name, shape, dt=F32):
        self.out_names.append(name)
        return T(self.nc.dram_tensor(name, list(shape), dt, kind="ExternalOutput").ap(), name, track=False)

    def dscr(self, name, shape, dt):
        return T(self.nc.dram_tensor(name, list(shape), dt).ap(), name, multi=True)

    def _op(self, eng, meth, *args, _r=(), _w=(), **kw):
        reads, writes = list(_r), list(_w)
        waits = []
        cargs = []
        for i, a in enumerate(args):
            if isinstance(a, V):
                (writes if i == 0 else reads).extend(a.bufs)
                waits.extend(a.wb)
                cargs.append(a.ap)
            else:
                cargs.append(a)
        ckw = {}
        for k, a in kw.items():
            if isinstance(a, V):
                (writes if k in ("out", "accum_out") else reads).extend(a.bufs)
                waits.extend(a.wb)
                ckw[k] = a.ap
            else:
                ckw[k] = a
        return self.S.op(eng, lambda e: getattr(e, meth)(*cargs, **ckw), reads=reads, writes=writes, waits=waits)

    def pe(self, meth, *a, **k):
        return self._op("pe", meth, *a, **k)

    def act(self, meth, *a, **k):
        return self._op("act", meth, *a, **k)

    def dve(self, meth, *a, **k):
        return self._op("dve", meth, *a, **k)

    def dma(self, q, out, in_, final=False, **kw):
        import os
        if os.environ.get("NOSCR") and any(b.multi for b in list(out.bufs) + list(in_.bufs)):
            return
        if os.environ.get("NOOUT") and final and "b_" in str(out.ap):
            return
        reads = list(in_.bufs)
        writes = list(out.bufs)
        cands = [b for b in writes + reads if not b.multi]
        return self.S.dma(q, out.ap, in_.ap, reads=reads, writes=writes, chan_buf=cands[0], final=final,
                          waits=list(out.wb) + list(in_.wb), **kw)

    def build(self):
        nc = self.nc
        SEQ, TT = self.SEQ, self.TT
        NT = SEQ // TT
        NB = TT // 128
        st = self.st

        x_in = self.din("x", [SEQ, D_MODEL])
        p_in = self.din("p", [DEPTH, SEQ, PLE_DIM])
        W = {}
        for nm, shp in [("w_in", [DEPTH, D_MODEL, IN_WIDTH]), ("w_out", [DEPTH, D_MODEL, D_MODEL]),
                        ("w_ff1", [DEPTH, D_MODEL, D_FF]), ("w_ff2", [DEPTH, D_FF, D_MODEL]),
                        ("w_ple_gate", [DEPTH, D_MODEL, D_MODEL]), ("w_ple_proj", [DEPTH, PLE_DIM, D_MODEL])]:
            W[nm] = self.din(nm, shp)
        cols_in = self.din("cols", [128, 64])
        NCONST = 11
        consts_in = self.din("consts", [128, NCONST, 128])
        cols2_in = self.din("cols2", [128, DEPTH, 64])
        cw3_in = self.din("cw3", [128, DEPTH, 3, 256])
        cshift_out = self.dout("c_shift", [DEPTH, 1024])
        cwkv_out = self.dout("c_wkv", [DEPTH, 4, 64, 64])
        rowp_in = self.din("rowp", [DEPTH, 8])
        aconv_out = self.dout("a_conv", [DEPTH, 3, 768])
        adelta_out = self.dout("a_delta", [DEPTH, 4, 64, 64])
        rope_in = self.din("rope", [SEQ, 64])
        amask_in = self.din("amask", [128, 4, 512], BF16)
        lam_in = self.din("lamrow", [DEPTH, 4, 64])
        y_out = self.dout("y", [SEQ, D_MODEL])
        bk_out = self.dout("b_k", [DEPTH, SEQ, 512])
        bv_out = self.dout("b_v", [DEPTH, SEQ, 512])
        kT_scr = [self.dscr(f"kT_scr{l}", [512, SEQ], BF16) for l in range(DEPTH)]
        v_scr = [self.dscr(f"v_scr{l}", [SEQ, 512], BF16) for l in range(DEPTH)]

        consts = self.sb("consts", [128, NCONST, 128])
        cols2 = self.sb("cols2", [128, DEPTH, 64])
        rowp = self.sb("rowp", [128, DEPTH, 8])
        cU, cSU, cL, cLI, cSame = (consts[:, i, :] for i in (2, 3, 4, 5, 6))
        arA = self.arena("arA", 8192); arB = self.arena("arB", 4096); arC = self.arena("arC", 4096)
        aq = self.sub(arA, "aq", [128, 6, 3 + TT], part="A")
        acarry = [self.sb(f"acarry{l}", [128, 6, 3]) for l in range(DEPTH)]
        Sd = [self.sb(f"Sd{l}", [128, 2, 128]) for l in range(DEPTH)]
        ac = self.sub(arA, "ac", [128, 6, TT], part="A")
        qkn = self.sub(arB, "qkn", [128, 4, TT], part="A")
        zs = self.sub(arB, "zs", [128, 2, TT], part="A")
        abtm = self.sb("abtm", [128, NB, 8])
        astep = self.sb("astep", [128, NB, 4])
        beta = self.sb("beta", [128, NB, 4])
        arD = self.arena("arD", 4096)
        kvtm = self.sub(arD, "kvtm", [128, NB, 512], part="A")
        ontm = self.sub(arD, "ontm", [128, NB, 256], part="A")
        aL = self.sub(arD, "aL", [128, 4, 128], part="A"); aU = self.sub(arD, "aU", [128, 4, 128], part="A")
        decA = self.sub(arB, "decA", [128, 4, 128], part="A"); decT = self.sub(arB, "decT", [128, 4, 128], part="A")
        eg = self.sb("eg", [128, 16]); bkg = self.sb("bkg", [128, 4]); egt = self.sb("egt", [128, 2, 2])
        Am = self.sub(arA, "Am", [128, 4, 128], part="A")
        Xa = [self.sb("Xa0", [128, 4, 128])] * 2
        XTa = [self.sb("XTa0", [128, 4, 128])] * 2
        Rm = self.sub(arA, "Rm", [128, 4, 128], part="A")
        vb_ = self.sb("vb_", [128, 4, 64]); kbz = self.sb("kbz", [128, 4, 128]); ktz = self.sb("ktz", [128, 2, 4, 128]); kzb = self.sb("kzb", [128, 4, 128])
        unew = self.sb("unew", [128, 256]); wT = self.sb("wT", [128, 2, 128])
        qkT = self.sub(arA, "qkT", [128, 4, 128], part="A"); ssq = self.sb("ssq", [128, 4])
        ident = consts[:, 0, :]
        identb = self.sb("identb", [128, 128], BF16)
        onesb = self.sb("onesb", [128, 128], BF16)
        cols = self.sb("cols", [128, 64])
        amask = self.sb("amask", [128, 4, 512], BF16)
        lamt = self.sb("lamt", [128, 8])
        h = self.sb("h", [128, 8, TT])
        xb = self.sb("xb", [128, 8, TT], BF16)
        rstd = self.sb("rstd", [128, TT])
        sq = [self.sb(f"sq{i}", [128, TT], BF16) for i in range(2)]
        NSLOT = 3
        ring = [self.sb(f"wr{i}", [128, 4096], BF16) for i in range(NSLOT)]
        self.ring_i = 0
        mixT = self.sb("mixT", [128, 8, TT], BF16)
        pT = self.sb("pT", [128, 2, TT], BF16)
        tmpf = [self.sb(f"tmpf{i}", [128, TT]) for i in range(2)]
        ropet = self.sb("ropet", [128, NB, 64])
        kfs = [self.sb(f"kfs{i}", [128, 512]) for i in range(2)]
        vfs = [self.sb(f"vfs{i}", [128, 512]) for i in range(2)]
        qT = self.sb("qT", [128, 4, 2, TT], BF16)

        ptok = self.sub(arD, "ptok", [128, NB, PLE_DIM], part="P")
        kblk = [self.sub(arD, f"kblk{i}", [128, 512], BF16, part="B") for i in range(2)]
        vblk = [self.sub(arD, f"vblk{i}", [128, 4, 128], BF16, part="B") for i in range(2)]
        PT = [self.sub(arD, f"PT{i}", [128, TT], BF16, part="B") for i in range(4)]
        osb = [self.sub(arD, f"osb{i}", [128, TT], part="B") for i in range(3)]
        qtm = self.sub(arC, "qtm", [128, NB, 512], BF16, part="B")
        ktm = self.sub(arC, "ktm", [128, NB, 512], BF16, part="B")
        vtb = self.sub(arC, "vtb", [128, NB, 512], BF16, part="B")
        kTt = self.sub(arC, "kTt", [128, 4, TT], BF16, part="B")
        xtok = self.sub(arC, "xtok", [128, NB, D_MODEL], part="X")
        uT = self.sub(arA, "uT", [128, 32, TT], BF16, part="F")
        gate = self.sub(arB, "gate", [128, 8, TT], part="F")
        c_r = self.sub(arA, "c_r", [128, 2, TT], part="C"); c_v = self.sub(arA, "c_v", [128, 2, TT], part="C")
        c_wl = self.sub(arA, "c_wl", [128, 2, TT], part="C"); c_g = self.sub(arA, "c_g", [128, 2, TT], part="C")
        c_kk = self.sub(arA, "c_kk", [128, 2, TT], part="C"); c_km = self.sub(arA, "c_km", [128, 2, TT], part="C")
        c_b = self.sub(arA, "c_b", [128, 2, TT], part="C"); c_bon = self.sub(arA, "c_bon", [128, 2, TT], part="C")
        c_a = self.sub(arB, "c_a", [128, 2, TT], part="C"); c_k = self.sub(arB, "c_k", [128, 2, TT], part="C")
        c_6 = self.sub(arB, "c_6", [128, TT], part="C"); c_7 = self.sub(arB, "c_7", [128, TT], part="C")
        C1T = self.sub(arB, "C1T", [128, 4, 128], part="C"); C2T = self.sub(arB, "C2T", [128, 4, 128], part="C")
        cXb = T(c_6.t[:, 0:256].bitcast(BF16).rearrange("p (h s) -> p h s", h=4), "cXb", track=False, bufs=[c_6.buf])
        cXb.wbufs = c_6.wbufs
        cXTb = T(c_7.t[:, 0:256].bitcast(BF16).rearrange("p (h s) -> p h s", h=4), "cXTb", track=False, bufs=[c_7.buf])
        cXTb.wbufs = c_7.wbufs
        crawm = [self.sub(arD, f"crawm{i}", [128, 1 + TT], part="C") for i in range(2)]
        tm5 = self.sub(arD, "tm5", [128, 5, 256], part="C")
        rTt = self.sub(arD, "rTt", [128, 2, 128], part="C"); kapTt = self.sub(arD, "kapTt", [128, 2, 128], part="C")
        kTm = self.sub(arD, "kTm", [128, 4, 128], part="C"); bTm = self.sub(arD, "bTm", [128, 4, 128], part="C")
        uval = self.sub(arC, "uval", [128, 256], part="A"); oq = self.sub(arC, "oq", [128, 256], part="A")
        otm = self.sub(arC, "otm", [128, 256], part="A"); o2 = self.sub(arC, "o2", [128, 256], part="A")
        kapz = self.sub(arC, "kapz", [128, 4, 128], BF16, part="C")
        cktz = self.sub(arC, "cktz", [128, 2, 4, 128], part="C"); cbtz = self.sub(arC, "cbtz", [128, 2, 4, 128], part="C")
        cx0 = self.sub(arC, "cx0", [128, 256], part="C"); cx0b = self.sub(arC, "cx0b", [128, 256], BF16, part="C"); cU0 = self.sub(arC, "cU0", [128, 256], part="C")
        cU_ = self.sub(arC, "cU_", [128, 256], part="C"); cyq = self.sub(arC, "cyq", [128, 256], part="C")
        cytm = self.sub(arC, "cytm", [128, 256], part="C")
        cR2 = T(vfs[0].t[:, 0:256].bitcast(BF16).rearrange("p (h s) -> p h s", h=4), "cR2", track=False, bufs=[vfs[0].buf])
        cBmT = T(vfs[1].t[:, :].rearrange("p (h s) -> p h s", h=4), "cBmT", track=False, bufs=[vfs[1].buf])
        Tst = [self.sb(f"Tst{l}", [128, 2, 128]) for l in range(DEPTH)]
        ccarry = [self.sb(f"ccarry{l}", [128, 8]) for l in range(DEPTH)]
        acarry_s = [self.sb(f"acarry_s{l}", [128, 6, 3]) for l in range(DEPTH)]
        Sd_s = [self.sb(f"Sd_s{l}", [128, 2, 128]) for l in range(DEPTH)]
        Tst_s = [self.sb(f"Tst_s{l}", [128, 2, 128]) for l in range(DEPTH)]
        ccarry_s = [self.sb(f"ccarry_s{l}", [128, 8]) for l in range(DEPTH)]
        kvp = self.sb("kvp", [128, 2, 2, 64]); pcc = self.sb("pcc", [128, 2, 2])
        cw3 = self.sb("cw3", [128, DEPTH, 3, 256], BF16)
        cst = self.sb("cst", [128, 8])

        def cyn(blk):
            return kfs[blk // 2][:, (blk % 2) * 256:(blk % 2 + 1) * 256]
        cgt = self.sb("cgt", [128, 2, 256])
        ps = [self.psum(f"ps{i}", [128, 512]) for i in range(7)]
        psbT = self.psum("psb", [128, 1024], BF16)

        self.dma("sp", consts.v(), consts_in.v())
        self.dma("sp", cols.v(), cols_in.v())
        self.dma("sp", cols2.v(), cols2_in.v())
        self.dma("sp", rowp.v(), V(rowp_in.t.rearrange("l a -> (l a)").partition_broadcast(128)
                                  .rearrange("p (l a) -> p l a", l=DEPTH), ()))
        self.act("activation", rowp[:, :, 0:4], rowp[:, :, 0:4], AF.Exp)
        self.dve("tensor_scalar_mul", rowp[:, :, 0:4], rowp[:, :, 0:4], -1.0)
        for l in range(DEPTH):
            self.dve("memset", acarry[l].v(), 0.0)
            self.dve("memset", Sd[l].v(), 0.0)
        self.dma("pool", cw3.v(), cw3_in.v())
        for l in range(DEPTH):
            self.dve("memset", Tst[l].v(), 0.0)
            self.dve("memset", ccarry[l].v(), 0.0)
            self.dve("tensor_scalar", cols2[:, l, 54:56], cols2[:, l, 46:48], -1.0, 1.0, ALU.mult, ALU.add)
        self.dve("memset", kbz.v(), 0.0)
        self.dve("memset", ktz.v(), 0.0)
        self.dve("memset", unew.v(), 0.0)
        self.dma("sp", amask.v(), amask_in.v())
        self.dve("tensor_copy", identb.v(), consts[:, 0, :])
        self.dve("tensor_copy", onesb.v(), consts[:, 1, :])
        lrow = T(ptok.t[:, 0:2, :].rearrange("p a (b d) -> p a b d", b=4), "lrow", track=False, bufs=[ptok.buf])
        self.dma("sp", lrow.v(), V(lam_in.t.rearrange("l a d -> (l a d)").partition_broadcast(128)
                                  .rearrange("p (l a d) -> p l a d", l=DEPTH, a=4), ()))
        lsum = self.sb("lsum", [128, 4])
        for l in range(DEPTH):
            for m in range(2):
                self.dve("tensor_tensor", tmpf[0][:, 0:64], lrow[:, l, 2 * m, :], lrow[:, l, 2 * m + 1, :], ALU.mult)
                self.dve("reduce_sum", lsum[:, 2 * l + m:2 * l + m + 1], tmpf[0][:, 0:64], AX.X)
        self.act("activation", lsum.v(), lsum.v(), AF.Exp)
        for l in range(DEPTH):
            lam_init = 0.8 - 0.6 * math.exp(-0.3 * l)
            self.dve("scalar_tensor_tensor", lamt[:, l:l + 1], lsum[:, 2 * l + 1:2 * l + 2], -lam_init,
                     lsum[:, 2 * l:2 * l + 1], ALU.add, ALU.subtract)

        bns = self.sb("bns", [128, DEPTH])
        for l in range(DEPTH):
            self.dve("tensor_scalar_mul", bns[:, l:l + 1], cols[:, 48 + l:49 + l], float(1.0 - (0.8 - 0.6 * math.exp(-0.3 * l))))
        def col(name, l, k=0):
            base = {"norm_mix": 0, "norm_ffn": 8, "norm_ple": 16, "b_norm": 24}[name]
            if name == "b_norm":
                return bns[:, l:l + 1]
            return cols[:, l * 24 + base + k: l * 24 + base + k + 1]

        def colf(k):
            return cols[:, 50 + k:51 + k]

        def rmsnorm_to_xb(ntok, gcol):
            for k in range(8):
                s = sq[k % 2]
                self.act("activation", s[:, :ntok], h[:, k, :ntok], AF.Square)
                self.pe("matmul", ps[2][:, :ntok], onesb.v(), s[:, :ntok], start=(k == 0), stop=(k == 7))
            self.act("activation", rstd[:, :ntok], ps[2][:, :ntok], AF.Ln, bias=colf(0), scale=1.0 / D_MODEL)
            self.act("activation", rstd[:, :ntok], rstd[:, :ntok], AF.Exp, scale=-0.5)
            for k in range(8):
                self.dve("scalar_tensor_tensor", xb[:, k, :ntok], h[:, k, :ntok], gcol(k), rstd[:, :ntok],
                         ALU.mult, ALU.mult)

        wscr = {}

        def load_piece(wref, KC, c0, ncols):
            wname, l = wref
            wv = W[wname].t[l]
            slot = ring[self.ring_i % NSLOT]
            self.ring_i += 1
            flat = slot[:, 0:KC * ncols]
            sv = flat.r("p (k n) -> p k n", k=KC)
            key = (wname, l, KC, c0, ncols)
            if key in wscr and self.wcache:
                self.dma("sp", flat, wscr[key].v())
                return sv
            for k0 in range(0, KC, 8):
                k1 = min(KC, k0 + 8)
                src = V(wv.rearrange("(kc p) n -> p kc n", p=128)[:, k0:k1, c0:c0 + ncols], ())
                self.dma("pool", sv[:, k0:k1, :], src)
            if self.wcache:
                scr = self.dscr(f"wb_{wname}_{l}_{c0}_{ncols}", [128, KC * ncols], BF16)
                wscr[key] = scr
                self.dma("sp", scr.v(), flat)
            return sv

        self.psd = 0

        def dense_fm(wv, KC, c0, ncols_total, rhs, ntok, epi, mchunk=128):
            per = max(128, min(512, 4096 // KC))
            m = 0
            for pc0 in range(0, ncols_total, per):
                pn = min(per, ncols_total - pc0)
                sv = load_piece(wv, KC, c0 + pc0, pn)
                for cc in range(0, pn, mchunk):
                    mc = min(mchunk, pn - cc)
                    pst = ps[self.psd % 2]
                    self.psd += 1
                    for k in range(KC):
                        self.pe("matmul", pst[:mc, :ntok], sv[:, k, cc:cc + mc], rhs[:, k, :ntok],
                                start=(k == 0), stop=(k == KC - 1))
                    epi(m, pst[:mc, :ntok])
                    m += 1

        def dense_tm(wv, c0, ncols, ntok, epi):
            sv = load_piece(wv, 8, c0, ncols)
            for b0 in range(0, ntok, 128):
                bt = min(128, ntok - b0)
                pst = ps[self.psd % 2]
                self.psd += 1
                for k in range(8):
                    self.pe("matmul", pst[:bt, :ncols], xb[:, k, b0:b0 + bt], sv[:, k, :], start=(k == 0), stop=(k == 7))
                epi(b0 // 128, bt, pst[:bt, :ncols])

        def rope_tm(dst, src_sb, bt, blk):
            import os
            if os.environ.get("NOROPE"):
                self.dve("tensor_copy", dst, src_sb[:bt, :])
                return
            s4 = src_sb[:bt, :].r("p (g t f) -> p g t f", g=8, t=2)
            d4 = dst.r("p (g t f) -> p g t f", g=8, t=2)
            cos = ropet[:bt, blk, 0:32].unsq(1).bc([bt, 8, 32])
            sin = ropet[:bt, blk, 32:64].unsq(1).bc([bt, 8, 32])
            ta = tmpf[0][:bt, 0:256].r("p (g f) -> p g f", g=8)
            tb = tmpf[1][:bt, 0:256].r("p (g f) -> p g f", g=8)
            self.dve("tensor_tensor", ta, s4[:, :, 0, :], cos, ALU.mult)
            self.dve("tensor_tensor", tb, s4[:, :, 1, :], sin, ALU.mult)
            self.dve("tensor_tensor", d4[:, :, 0, :], ta, tb, ALU.subtract)
            self.dve("tensor_tensor", ta, s4[:, :, 1, :], cos, ALU.mult)
            self.dve("tensor_tensor", tb, s4[:, :, 0, :], sin, ALU.mult)
            self.dve("tensor_tensor", d4[:, :, 1, :], ta, tb, ALU.add)

        def neumann(bt, X, XT, R, CH):
            nlev = 5 if CH > 16 else 3
            nlev = min(nlev, self.a3 - 2)
            for lev in range(nlev):
                pX = ps[0][:bt, :].r("p (h s) -> p h s", h=4)[:, :, :bt]
                pXT = ps[1][:bt, :].r("p (h s) -> p h s", h=4)[:, :, :bt]
                pR = ps[3][:bt, :].r("p (h s) -> p h s", h=4)[:, :, :bt]
                last = lev == nlev - 1
                for hh in range(4):
                    self.pe("matmul", pXT[:, hh, :], X[:bt, hh, :bt], XT[:bt, hh, :bt], start=True, stop=True)
                if not last:
                    for hh in range(4):
                        self.pe("matmul", pX[:, hh, :], XT[:bt, hh, :bt], X[:bt, hh, :bt], start=True, stop=True)
                self.dve("tensor_copy", XT[:bt, :, :bt], pXT)
                if not last:
                    self.act("copy", X[:bt, :, :bt], pX)
                for hh in range(4):
                    self.pe("matmul", pR[:, hh, :], XT[:bt, hh, :bt], R[:bt, hh, :bt], start=True, stop=True)
                self.dve("tensor_tensor", R[:bt, :, :bt], R[:bt, :, :bt], pR, ALU.add)

        def tile_body(cx):
            ntok = cx.ntok
            nbk = (ntok + 127) // 128
            bl = min(128, ntok)
            kb0 = cx.key_base
            self.dma("sp", xtok[:bl, :nbk, :], V(cx.x_src.rearrange("(b p) d -> p b d", p=bl), ()))
            for k in range(8):
                for b in range(nbk):
                    self.pe("transpose", ps[3][:, b * 128:b * 128 + bl], xtok[:bl, b, k * 128:(k + 1) * 128], ident[:bl, :bl])
                self.act("copy", h[:, k, :ntok], ps[3][:, :ntok])
            self.dma("sp", ropet[:bl, :nbk, :], V(cx.rope_src.rearrange("(b p) d -> p b d", p=bl), ()))

            for l in range(DEPTH):
                lam_init = 0.8 - 0.6 * math.exp(-0.3 * l)
                rmsnorm_to_xb(ntok, lambda k: col("norm_mix", l, k))
                w_in = ("w_in", l)
                kT_s, v_s = cx.kT_scr[l], cx.v_scr[l]

                if True:
                    def epi_q(blk, bt, pv):
                        self.act("copy", osb[0][:bt, :], pv)
                        rope_tm(qtm[:bt, blk, :], osb[0], bt, blk)
                    dense_tm(w_in, O_BQ, 512, ntok, epi_q)

                    def epi_k(blk, bt, pv):
                        kf = kfs[blk % 2]
                        self.act("copy", osb[1][:bt, :], pv)
                        rope_tm(kf[:bt, :], osb[1], bt, blk)
                        self.act("copy", ktm[:bt, blk, :], kf[:bt, :])
                        self.dma("sp", V(cx.bk_dst(l)[blk * 128:blk * 128 + bt, :], ()), kf[:bt, :], final=True)
                    dense_tm(w_in, O_BK, 512, ntok, epi_k)

                    def epi_v(blk, bt, pv):
                        vf = vfs[blk % 2]
                        self.act("copy", vf[:bt, :], pv)
                        self.dve("tensor_copy", vtb[:bt, blk, :], vf[:bt, :])
                        self.dma("sp", V(cx.bv_dst(l)[blk * 128:blk * 128 + bt, :], ()), vf[:bt, :], final=True)
                    dense_tm(w_in, O_BV, 512, ntok, epi_v)
                    self.dma("sp", V(v_s.t[kb0:kb0 + ntok, :].rearrange("(b p) d -> p b d", p=bl), (v_s.buf,)), vtb[:bl, :nbk, :])
                    psb = psbT.v()
                    for c in range(4):
                        for b in range(nbk):
                            self.pe("transpose", psb[:, b * 128:b * 128 + bl], qtm[:bl, b, c * 128:(c + 1) * 128], identb[:bl, :bl])
                        for m in range(2):
                            self.act("mul", qT[:, c, m, :ntok], psb[:, 0:ntok], cSame[:, m * 64:m * 64 + 1])
                        for b in range(nbk):
                            self.pe("transpose", psb[:, 512 + b * 128:512 + b * 128 + bl], ktm[:bl, b, c * 128:(c + 1) * 128], identb[:bl, :bl])
                        self.dve("tensor_copy", kTt[:, c, :ntok], psb[:, 512:512 + ntok])
                    self.dma("sp", V(kT_s.t[:, kb0:kb0 + ntok].rearrange("(c p) s -> p c s", p=128), (kT_s.buf,)), kTt[:, :, :ntok])

                    nk = kb0 + ntok
                    nkb = (nk + 127) // 128
                    sbank = (ps[0], ps[1], ps[2], ps[5]) if self.poolsum else (ps[0], ps[1], ps[2], ps[0])
                    for hh in range(4):
                        units = []
                        for ks in range(0, nk, 512):
                            kn = min(512, nk - ks)
                            for j in range((kn + 127) // 128):
                                units.append((ks, kn, j, min(128, kn - j * 128), (ks + j * 128) // 128))

                        def stage1(ui):
                            ks, kn, j, kr, kb = units[ui]
                            sbi = (ks // 512) % 2
                            cur_k, cur_v = kblk[sbi], vblk[sbi]
                            if j == 0:
                                self.dma("sp", cur_k[:, :kn], V(kT_s.t[hh * 128:(hh + 1) * 128, ks:ks + kn], (kT_s.buf,)))
                                if kn == 512:
                                    self.dma("sp", cur_v.v(), V(v_s.t[ks:ks + 512, hh * 128:(hh + 1) * 128]
                                                                .rearrange("(j p) d -> p j d", p=128), (v_s.buf,)))
                                else:
                                    for jj in range((kn + 127) // 128):
                                        krr = min(128, kn - jj * 128)
                                        self.dma("sp", cur_v[:krr, jj, :], V(v_s.t[ks + jj * 128:ks + jj * 128 + krr, hh * 128:(hh + 1) * 128], (v_s.buf,)))
                            diag = cx.masked and kb * 128 >= kb0
                            for m in range(2):
                                pss = sbank[(ui % 2) * 2 + m] if self.poolsum else ps[(2 * ui + m) % 3]
                                pt = PT[(ui % 2) * 2 + m]
                                self.pe("matmul", pss[:kr, :ntok], cur_k[:, j * 128:j * 128 + kr],
                                        qT[:, hh, m, :ntok], start=True, stop=True)
                                self.act("activation", pt[:kr, :ntok], pss[:kr, :ntok], AF.Exp, scale=0.125)
                                if diag:
                                    self.dve("tensor_tensor", pt[:kr, :ntok], pt[:kr, :ntok], amask[:kr, kb - kb0 // 128, :ntok], ALU.mult)

                        def stage2(ui):
                            ks, kn, j, kr, kb = units[ui]
                            cur_v = vblk[(ks // 512) % 2]
                            first, last = ui == 0, ui == len(units) - 1
                            for m in range(2):
                                pt = PT[(ui % 2) * 2 + m]
                                self.pe("matmul", ps[3 + m][:, :ntok], cur_v[:kr, j, :], pt[:kr, :ntok], start=first, stop=last)
                                if not self.poolsum:
                                    self.pe("matmul", ps[5 + m][:, :ntok], onesb[:kr, :], pt[:kr, :ntok], start=first, stop=last)
                                elif first:
                                    self._op("pool", "tensor_copy", osb[m][:, :ntok], pt[:, :ntok])
                                else:
                                    self._op("pool", "tensor_tensor", osb[m][:kr, :ntok], osb[m][:kr, :ntok], pt[:kr, :ntok], ALU.add)

                        stage1(0)
                        for ui in range(1, len(units)):
                            stage1(ui)
                            stage2(ui - 1)
                        stage2(len(units) - 1)
                        n_ = slice(0, ntok)
                        if self.poolsum:
                            for m in range(2):
                                self.pe("matmul", ps[m][:, n_], consts[:, 1, :], osb[m][:, n_], start=True, stop=True)
                            self.dve("reciprocal", osb[0][:, n_], ps[0][:, n_])
                            self.dve("reciprocal", osb[1][:, n_], ps[1][:, n_])
                        else:
                            for m in range(2):
                                self.act("activation", osb[m][:, n_], ps[5 + m][:, n_], AF.Ln)
                                self.act("activation", osb[m][:, n_], osb[m][:, n_], AF.Exp, scale=-1.0)
                        self.dve("tensor_tensor", osb[0][:, n_], osb[0][:, n_], ps[3][:, n_], ALU.mult)
                        self.dve("tensor_tensor", osb[1][:, n_], osb[1][:, n_], ps[4][:, n_], ALU.mult)
                        self.dve("scalar_tensor_tensor", osb[2][:, n_], osb[1][:, n_], lamt[:, l:l + 1], osb[0][:, n_], ALU.mult, ALU.add)
                        self.act("activation", sq[0][:, n_], osb[2][:, n_], AF.Square)
                        self.pe("matmul", ps[2][:, n_], onesb.v(), sq[0][:, n_], start=True, stop=True)
                        self.act("activation", osb[0][:, n_], ps[2][:, n_], AF.Ln, bias=colf(0), scale=1.0 / 128)
                        self.act("activation", osb[0][:, n_], osb[0][:, n_], AF.Exp, scale=-0.5)
                        self.dve("scalar_tensor_tensor", mixT[:, 2 + hh, n_], osb[2][:, n_], col("b_norm", l), osb[0][:, n_],
                                 ALU.mult, ALU.mult)

                if "A" not in self.stages:
                    for c in (0, 1):
                        self.dve("memset", mixT[:, c, :], 0.0)
                else:
                    one_col = consts[:, 1, 0:1]
                    self.dve("tensor_copy", aq[:, :, 0:3], cx.acarry[l].v())

                    def epi_aqkv(m, pv):
                        self.act("copy", aq[:, m, 3:3 + ntok], pv)
                    dense_fm(w_in, 8, O_AQKV, 768, xb, ntok, epi_aqkv)
                    self.dve("tensor_copy", cx.acarry[l].v(), aq[:, :, ntok:ntok + 3])
                    if cx.is_last:
                        for m in range(6):
                            self.dma("sp", V(cx.aconv_out.t[l, :, m * 128:(m + 1) * 128].rearrange("j p -> p j"), ()),
                                     cx.acarry[l][:, m, :], final=True, allow_slow_non_contiguous=True)

                    def epi_az(m, pv):
                        self.act("activation", zs[:, m, :ntok], pv, AF.Silu)
                    if self.a1 >= 2: dense_fm(w_in, 8, O_AZ, 256, xb, ntok, epi_az)

                    def epi_ab(blk, bt, pv):
                        self.act("copy", abtm[:bt, blk, :], pv)
                    if self.a1 >= 3: dense_tm(w_in, O_AA, 8, ntok, epi_ab)
                    for m in (range(6) if self.a1 >= 4 else ()):
                        tf = tmpf[m % 2]
                        self.dve("tensor_scalar_mul", tf[:, :ntok], aq[:, m, 0:ntok], cols2[:, l, m * 4:m * 4 + 1])
                        for j in (1, 2, 3):
                            self.dve("scalar_tensor_tensor", tf[:, :ntok], aq[:, m, j:j + ntok],
                                     cols2[:, l, m * 4 + j:m * 4 + j + 1], tf[:, :ntok], ALU.mult, ALU.add)
                        self.act("activation", ac[:, m, :ntok], tf[:, :ntok], AF.Silu)
                    for m in (range(4) if self.a1 >= 5 else ()):
                        tf = tmpf[m % 2]
                        self.act("activation", tf[:, :ntok], ac[:, m, :ntok], AF.Square)
                        self.pe("matmul", ps[2][:, :ntok], cSame, tf[:, :ntok], start=True, stop=True)
                        self.act("activation", tf[:, :ntok], ps[2][:, :ntok], AF.Ln, bias=colf(0), scale=1.0)
                        self.act("activation", tf[:, :ntok], tf[:, :ntok], AF.Exp, scale=-0.5)
                        self.dve("scalar_tensor_tensor", qkn[:, m, :ntok], ac[:, m, :ntok], 0.125 if m < 2 else 1.0,
                                 tf[:, :ntok], ALU.mult, ALU.mult)
                    nbk = (ntok + 127) // 128
                    btl = min(128, ntok)
                    if self.a1 >= 6: self.dve("tensor_tensor", astep[:btl, :nbk, :], abtm[:btl, :nbk, 0:4],
                             rowp[:btl, l, 4:8].unsq(1).bc([btl, nbk, 4]), ALU.add)
                    if self.a1 >= 7: self.act("activation", astep[:btl, :nbk, :], astep[:btl, :nbk, :], AF.Exp)
                    if self.a1 >= 8: self.act("activation", astep[:btl, :nbk, :], astep[:btl, :nbk, :], AF.Ln, bias=one_col[:btl, :], scale=1.0)
                    if self.a1 >= 9: self.dve("tensor_tensor", astep[:btl, :nbk, :], astep[:btl, :nbk, :],
                             rowp[:btl, l, 0:4].unsq(1).bc([btl, nbk, 4]), ALU.mult)
                    if self.a1 >= 9: self.act("activation", beta[:btl, :nbk, :], abtm[:btl, :nbk, 4:8], AF.Sigmoid)

                    for blk in (range(nbk) if self.asub >= 2 else ()):
                        bt = min(128, ntok - blk * 128)
                        c0 = blk * 128
                        chunks = [(r0, min(r0 + cx.CH, bt)) for r0 in range(0, bt, cx.CH)]
                        for i, (src, m) in enumerate(((qkn, 2), (qkn, 3), (ac, 4), (ac, 5))):
                            self.pe("transpose", ps[3][:bt, i * 128:(i + 1) * 128], src[:, m, c0:c0 + bt], ident)
                        self.act("copy", kvtm[:bt, blk, :], ps[3][:bt, :])
                        ktm_ = kvtm[:bt, blk, 0:256].r("p (h d) -> p h d", h=4)
                        vtm_ = kvtm[:bt, blk, 256:512].r("p (h d) -> p h d", h=4)
                        a_b = astep[:bt, blk, :]
                        self.dve("tensor_tensor", aL[:bt, :, :bt], cL[:bt, :bt].unsq(1).bc([bt, 4, bt]),
                                 a_b.unsq(2).bc([bt, 4, bt]), ALU.mult)
                        self.dve("tensor_tensor", aU[:bt, :, :bt], cU[:bt, :bt].unsq(1).bc([bt, 4, bt]),
                                 a_b.unsq(2).bc([bt, 4, bt]), ALU.mult)
                        p0 = ps[0][:bt, :].r("p (h s) -> p h s", h=4)[:, :, :bt]
                        p1 = ps[1][:bt, :].r("p (h s) -> p h s", h=4)[:, :, :bt]
                        for hh in range(4):
                            self.pe("matmul", p0[:, hh, :], cU[:bt, :bt], aL[:bt, hh, :bt], start=True, stop=True)
                        for hh in range(4):
                            self.pe("matmul", p1[:, hh, :], cL[:bt, :bt], aU[:bt, hh, :bt], start=True, stop=True)
                        self.act("activation", decA[:bt, :, :bt], p0, AF.Exp)
                        self.act("activation", decT[:bt, :, :bt], p1, AF.Exp)
                        self.dve("tensor_tensor", decA[:bt, :, :bt], decA[:bt, :, :bt], cL[:bt, :bt].unsq(1).bc([bt, 4, bt]), ALU.mult)
                        self.dve("tensor_tensor", decT[:bt, :, :bt], decT[:bt, :, :bt], cU[:bt, :bt].unsq(1).bc([bt, 4, bt]), ALU.mult)
                        self.pe("matmul", ps[2][:bt, 0:4], cU[:bt, :bt], a_b, start=True, stop=True)
                        self.pe("matmul", ps[2][:bt, 4:8], cL[:bt, :bt], a_b, start=True, stop=True)
                        for ci in range(len(chunks)):
                            for e in range(2):
                                self.pe("matmul", ps[2][:, 8 + 2 * ci:10 + 2 * ci], consts[:bt, 7 + 2 * ci + e, :],
                                        astep[:bt, blk, e::2], start=(e == 0), stop=(e == 1))
                        self.act("activation", eg[:bt, 0:8], ps[2][:bt, 0:8], AF.Exp)
                        self.act("activation", egt.v().r("p c r -> p (c r)")[:, 0:2 * len(chunks)],
                                 ps[2][:, 8:8 + 2 * len(chunks)], AF.Exp)
                        self.dve("tensor_tensor", bkg[:bt, :], beta[:bt, blk, :], eg[:bt, 0:4], ALU.mult)
                        if self.asub < 3: continue
                        pk = ps[3][:bt, :].r("p (h s) -> p h s", h=4)[:, :, :bt]
                        import os
                        kkv = os.environ.get("KKV", "")
                        for hh in range(4):
                            e, pr = hh % 2, hh // 2
                            if kkv == "even" and e == 1:
                                continue
                            if kkv == "odd" and e == 0:
                                continue
                            self.dve("tensor_scalar_mul", kzb[:, hh, :bt], qkn[:, 2 + pr, c0:c0 + bt], cSame[:, e * 64:e * 64 + 1])
                            self.pe("matmul", pk[:, hh, :], kzb[:, hh, :bt], qkn[:, 2 + pr, c0:c0 + bt], start=True, stop=True)
                        import os
                        av = os.environ.get("AV", "")
                        for hh in range(4):
                            if av == "nodve":
                                continue
                            if av == "fsc":
                                self.dve("scalar_tensor_tensor", Am[:bt, hh, :bt], pk[:, hh, :], 1.0,
                                         decA[:bt, hh, :bt], ALU.mult, ALU.mult)
                                continue
                            if av == "tt":
                                self.dve("tensor_tensor", Am[:bt, hh, :bt], pk[:, hh, :], decA[:bt, hh, :bt], ALU.mult)
                                continue
                            self.dve("scalar_tensor_tensor", Am[:bt, hh, :bt], pk[:, hh, :], beta[:bt, blk, hh:hh + 1],
                                     decA[:bt, hh, :bt], ALU.mult, ALU.mult)
                        if self.a3 < 2: continue
                        pB = ps[0][:bt, :].r("p (h s) -> p h s", h=4)[:, :, :bt]
                        for hh in range(4):
                            self.pe("transpose", pB[:, hh, :], Am[:bt, hh, :bt], ident[:bt, :bt])
                        self.act("copy", Xa[0][:bt, :, :bt], pB)
                        self.dve("tensor_tensor", Rm[:bt, :, :bt], ident[:bt, :bt].unsq(1).bc([bt, 4, bt]), Xa[0][:bt, :, :bt], ALU.subtract)
                        self.dve("tensor_copy", XTa[0][:bt, :, :bt], Am[:bt, :, :bt])
                        neumann(bt, Xa[0], XTa[0], Rm, cx.CH)
                        self.dve("tensor_tensor", vb_[:bt, :, :], vtm_, beta[:bt, blk, :].unsq(2).bc([bt, 4, 64]), ALU.mult)
                        for hh in range(4):
                            e = hh % 2
                            self.dve("tensor_tensor", kbz[:bt, hh, e * 64:(e + 1) * 64], ktm_[:, hh, :],
                                     bkg[:bt, hh:hh + 1].bc([bt, 64]), ALU.mult)
                            for ci, (r0, r1) in enumerate(chunks):
                                self.dve("tensor_tensor", ktz[r0:r1, ci, hh, e * 64:(e + 1) * 64], ktm_[r0:r1, hh, :],
                                         eg[r0:r1, 4 + hh:5 + hh].bc([r1 - r0, 64]), ALU.mult)
                        for hh in range(4):
                            self.pe("matmul", ps[4][:bt, hh * 64:(hh + 1) * 64], Rm[:bt, hh, :bt], vb_[:bt, hh, :], start=True, stop=True)
                        for pr in range(2):
                            for e in range(2):
                                self.pe("matmul", ps[4][:, 256 + pr * 128:256 + pr * 128 + bt], kbz[:bt, 2 * pr + e, :],
                                        Rm[:bt, 2 * pr + e, :bt], start=(e == 0), stop=(e == 1))
                        self.act("copy", uval[:bt, :], ps[4][:bt, 0:256])
                        self.act("copy", wT[:, :, :bt], ps[4][:, 256:512].r("p (a t) -> p a t", a=2)[:, :, :bt])
                        pq = ps[5][:bt, :].r("p (h s) -> p h s", h=4)[:, :, :bt]
                        for hh in range(4):
                            e, pr = hh % 2, hh // 2
                            self.pe("matmul", pq[:, hh, :], kzb[:, hh, :bt], qkn[:, pr, c0:c0 + bt], start=True, stop=True)
                        self.dve("tensor_tensor", qkT[:bt, :, :bt], pq, decT[:bt, :, :bt], ALU.mult)
                        if self.asub < 5: continue
                        for ci, (r0, r1) in enumerate(chunks):
                            for pr in range(2):
                                self.pe("matmul", ps[6][:bt, pr * 128:(pr + 1) * 128], wT[:, pr, :bt], cx.Sd[l][:, pr, :], start=True, stop=True)
                            for pr in range(2):
                                self.pe("matmul", ps[6][:bt, 256 + pr * 128:256 + (pr + 1) * 128], qkn[:, pr, c0:c0 + bt],
                                        cx.Sd[l][:, pr, :], start=True, stop=True)
                            self.dve("tensor_tensor", unew[r0:r1, :], uval[r0:r1, :], ps[6][r0:r1, 0:256], ALU.subtract)
                            self.dve("tensor_tensor", oq[r0:r1, :].r("p (h d) -> p h d", h=4),
                                     ps[6][r0:r1, 256:512].r("p (h d) -> p h d", h=4),
                                     eg[r0:r1, 0:4].unsq(2).bc([r1 - r0, 4, 64]), ALU.mult)
                            for pr in range(2):
                                for e in range(2):
                                    hh = 2 * pr + e
                                    self.pe("matmul", ps[2][:, 64 + pr * 64:128 + pr * 64], ktz[:bt, ci, hh, :],
                                            unew[:bt, hh * 64:(hh + 1) * 64], start=(e == 0), stop=(e == 1))
                            for e in range(2):
                                sdv = cx.Sd[l][e * 64:(e + 1) * 64, :, e * 64:(e + 1) * 64]
                                self.dve("tensor_tensor", sdv, sdv, egt[e * 64:(e + 1) * 64, ci, :].unsq(2).bc([64, 2, 64]), ALU.mult)
                                self.dve("tensor_tensor", sdv, sdv, ps[2][e * 64:(e + 1) * 64, 64:192].r("p (a d) -> p a d", a=2), ALU.add)
                        if self.asub < 6: continue
                        for hh in range(4):
                            self.pe("matmul", ps[5][:bt, hh * 64:(hh + 1) * 64], qkT[:bt, hh, :bt], unew[:bt, hh * 64:(hh + 1) * 64],
                                    start=True, stop=True)
                        self.dve("tensor_tensor", otm[:bt, :], oq[:bt, :], ps[5][:bt, 0:256], ALU.add)
                        self.dve("tensor_tensor", o2[:bt, :], otm[:bt, :], otm[:bt, :], ALU.mult)
                        self.dve("tensor_reduce", ssq[:bt, :], o2[:bt, :].r("p (h d) -> p h d", h=4), AX.X, ALU.add)
                        self.act("activation", ssq[:bt, :], ssq[:bt, :], AF.Ln, bias=colf(0)[:bt, :], scale=1.0 / 64)
                        self.act("activation", ssq[:bt, :], ssq[:bt, :], AF.Exp, scale=-0.5)
                        self.dve("tensor_tensor", ontm[:bt, blk, :].r("p (h d) -> p h d", h=4),
                                 otm[:bt, :].r("p (h d) -> p h d", h=4), ssq[:bt, :].unsq(2).bc([bt, 4, 64]), ALU.mult)
                    if self.asub < 9:
                        for c in (0, 1):
                            self.dve("memset", mixT[:, c, :], 0.0)
                    for pr in (range(2) if self.asub >= 9 else ()):
                        for blk in range(nbk):
                            bt = min(128, ntok - blk * 128)
                            self.pe("transpose", ps[3][:, blk * 128:blk * 128 + bt], ontm[:bt, blk, pr * 128:(pr + 1) * 128], ident[:bt, :bt])
                        self.dve("scalar_tensor_tensor", mixT[:, pr, :ntok], ps[3][:, :ntok], cols2[:, l, 24:25], zs[:, pr, :ntok],
                                 ALU.mult, ALU.mult)
                    if cx.is_last:
                        for e in range(2):
                            self.dma("sp", V(cx.adelta_out.t[l].rearrange("(pr e) k v -> e k pr v", e=2)[e], ()),
                                     cx.Sd[l][e * 64:(e + 1) * 64, :, e * 64:(e + 1) * 64], final=True)

                if "C" not in self.stages:
                    for c in (6, 7):
                        self.dve("memset", mixT[:, c, :], 0.0)
                else:
                    nbk = (ntok + 127) // 128
                    self.dve("memset", kapz.v(), 0.0)
                    self.dve("memset", cktz.v(), 0.0)
                    self.dve("memset", cbtz.v(), 0.0)
                    c2 = lambda a, b=None: cols2[:, l, a:(a + 1 if b is None else b)]
                    dests = [c_r[:, 0, :ntok], c_r[:, 1, :ntok], c_k[:, 0, :ntok], c_k[:, 1, :ntok],
                             c_v[:, 0, :ntok], c_v[:, 1, :ntok], c_6[:, :ntok], c_7[:, :ntok]]

                    def epi_c(m, pv):
                        cm = crawm[m % 2]
                        self.act("copy", cm[:, 1:1 + ntok], pv)
                        self.act("copy", cm[:, 0:1], cx.ccarry[l][:, m:m + 1])
                        self.act("copy", cx.ccarry[l][:, m:m + 1], cm[:, ntok:ntok + 1])
                        tf = tmpf[m % 2]
                        self.dve("tensor_tensor", tf[:, :ntok], cm[:, 0:ntok], cm[:, 1:1 + ntok], ALU.subtract)
                        self.dve("scalar_tensor_tensor", dests[m], tf[:, :ntok], c2(32 + m), cm[:, 1:1 + ntok], ALU.mult, ALU.add)
                    dense_fm(w_in, 8, O_C, 1024, xb, ntok, epi_c)
                    if cx.is_last:
                        self.dma("sp", V(cx.cshift_out.t[l].rearrange("(c p) -> p c", p=128), ()), cx.ccarry[l].v(), final=True,
                                 allow_slow_non_contiguous=True)
                    self.act("activation", sq[0][:, :ntok], c_6[:, :ntok], AF.Tanh)
                    self.act("copy", sq[1][:, :ntok], c_6[:, :ntok])
                    for m in range(2):
                        self.pe("matmul", ps[0][:, :ntok], cw3[:, l, 0, m * 128:(m + 1) * 128], sq[0][:, :ntok], start=True, stop=True)
                        self.act("activation", c_wl[:, m, :ntok], ps[0][:, :ntok], AF.Sigmoid, bias=c2(40 + m), scale=1.0)
                        self.dve("tensor_scalar_mul", c_wl[:, m, :ntok], c_wl[:, m, :ntok], -math.exp(-0.5))
                        self.pe("matmul", ps[1][:, :ntok], cw3[:, l, 1, m * 128:(m + 1) * 128], sq[1][:, :ntok], start=True, stop=True)
                        self.act("activation", c_a[:, m, :ntok], ps[1][:, :ntok], AF.Sigmoid, bias=c2(42 + m), scale=1.0)
                    self.act("activation", sq[0][:, :ntok], c_7[:, :ntok], AF.Sigmoid)
                    for m in range(2):
                        self.pe("matmul", ps[0][:, :ntok], cw3[:, l, 2, m * 128:(m + 1) * 128], sq[0][:, :ntok], start=True, stop=True)
                        self.act("copy", c_g[:, m, :ntok], ps[0][:, :ntok])
                    for m in range(2):
                        tf = tmpf[m % 2]
                        self.dve("tensor_scalar_mul", c_kk[:, m, :ntok], c_k[:, m, :ntok], c2(44 + m))
                        self.act("activation", tf[:, :ntok], c_kk[:, m, :ntok], AF.Square)
                        self.pe("matmul", ps[2][:, :ntok], cSame, tf[:, :ntok], start=True, stop=True)
                        self.act("activation", tf[:, :ntok], ps[2][:, :ntok], AF.Ln, bias=colf(0), scale=1.0)
                        self.act("activation", tf[:, :ntok], tf[:, :ntok], AF.Exp, scale=-0.5)
                        self.dve("tensor_tensor", c_kk[:, m, :ntok], c_kk[:, m, :ntok], tf[:, :ntok], ALU.mult)
                        self.dve("tensor_scalar", tf[:, :ntok], c_a[:, m, :ntok], c2(46 + m), c2(54 + m), ALU.mult, ALU.add)
                        self.dve("tensor_tensor", c_km[:, m, :ntok], c_k[:, m, :ntok], tf[:, :ntok], ALU.mult)
                        self.dve("tensor_tensor", c_b[:, m, :ntok], c_a[:, m, :ntok], c_kk[:, m, :ntok], ALU.mult)
                        self.dve("scalar_tensor_tensor", tf[:, :ntok], c_r[:, m, :ntok], c2(48 + m), c_km[:, m, :ntok], ALU.mult, ALU.mult)
                        self.pe("matmul", ps[2][:, :ntok], cSame, tf[:, :ntok], start=True, stop=True)
                        self.dve("tensor_tensor", c_bon[:, m, :ntok], ps[2][:, :ntok], c_v[:, m, :ntok], ALU.mult)

                    for blk in range(nbk):
                        bt = min(128, ntok - blk * 128)
                        c0 = blk * 128
                        chunks = [(r0, min(r0 + cx.CH, bt)) for r0 in range(0, bt, cx.CH)]
                        for i, src in enumerate((c_wl, c_km, c_b, c_kk, c_v)):
                            pst = ps[3] if i % 2 == 0 else ps[4]
                            for m in range(2):
                                self.pe("transpose", pst[:bt, m * 128:(m + 1) * 128], src[:, m, c0:c0 + bt], ident)
                            self.act("copy", tm5[:bt, i, :], pst[:bt, 0:256])
                        wl_tm, km_tm, b_tm, kk_tm, v_tm = (tm5[:bt, i, :] for i in range(5))
                        self.pe("matmul", ps[0][:bt, 0:256], cU[:bt, :bt], wl_tm, start=True, stop=True)
                        for m in range(2):
                            self.pe("matmul", ps[1][:, m * 128:m * 128 + bt], tm5[:bt, 0, m * 128:(m + 1) * 128], cU[:bt, :bt],
                                    start=True, stop=True)
                        eng = cgt[:bt, 0, :]
                        egm = cgt[:bt, 1, :]
                        self.act("activation", eng, ps[0][:bt, 0:256], AF.Exp, scale=-1.0)
                        self.act("activation", egm, ps[0][:bt, 0:256], AF.Exp)
                        self.act("activation", cx0[:bt, :], wl_tm, AF.Exp, scale=-1.0)
                        self.dve("tensor_tensor", egm, egm, cx0[:bt, :], ALU.mult)
                        for hh in range(4):
                            e = hh % 2
                            hs = slice(hh * 64, (hh + 1) * 64)
                            self.dve("tensor_tensor", kapz[:bt, hh, e * 64:(e + 1) * 64], kk_tm[:, hs], egm[:, hs], ALU.mult)
                            for ci, (r0, r1) in enumerate(chunks):
                                self.dve("tensor_tensor", cktz[r0:r1, ci, hh, e * 64:(e + 1) * 64], km_tm[r0:r1, hs], eng[r0:r1, hs], ALU.mult)
                                self.dve("tensor_tensor", cbtz[r0:r1, ci, hh, e * 64:(e + 1) * 64], b_tm[r0:r1, hs], eng[r0:r1, hs], ALU.mult)
                        pgT = ps[1][:, 0:256].r("p (m t) -> p m t", m=2)[:, :, :bt]
                        egT = C1T[:, 0:2, :bt]
                        engT = C1T[:, 2:4, :bt]
                        egmT = C2T[:, 0:2, :bt]
                        self.act("activation", egT, pgT, AF.Exp)
                        self.act("activation", engT, pgT, AF.Exp, scale=-1.0)
                        self.dve("tensor_tensor", egmT, pgT, c_wl[:, :, c0:c0 + bt], ALU.subtract)
                        self.act("activation", egmT, egmT, AF.Exp)
                        self.dve("tensor_tensor", rTt[:, :, :bt], c_r[:, :, c0:c0 + bt], egT, ALU.mult)
                        self.dve("tensor_tensor", kapTt[:, :, :bt], c_kk[:, :, c0:c0 + bt], egmT, ALU.mult)
                        for ci, (r0, r1) in enumerate(chunks):
                            self.act("copy", pcc[:, ci, :], egT[:, :, r1 - 1])
                        for hh in range(4):
                            e, pr = hh % 2, hh // 2
                            self.dve("scalar_tensor_tensor", kTm[:, hh, :bt], c_km[:, pr, c0:c0 + bt], cSame[:, e * 64:e * 64 + 1],
                                     engT[:, pr, :], ALU.mult, ALU.mult)
                            self.dve("scalar_tensor_tensor", bTm[:, hh, :bt], c_b[:, pr, c0:c0 + bt], cSame[:, e * 64:e * 64 + 1],
                                     engT[:, pr, :], ALU.mult, ALU.mult)
                        def hview(pt):
                            return pt[:bt, :].r("p (h s) -> p h s", h=4)[:, :, :bt]
                        pB, pBm, pC1, pC2 = hview(ps[0]), hview(ps[3]), hview(ps[4]), hview(ps[5])
                        for hh in range(4):
                            pr = hh // 2
                            self.pe("matmul", pB[:, hh, :], bTm[:, hh, :bt], kapTt[:, pr, :bt], start=True, stop=True)
                        for hh in range(4):
                            pr = hh // 2
                            self.pe("matmul", pBm[:, hh, :], kTm[:, hh, :bt], kapTt[:, pr, :bt], start=True, stop=True)
                        for hh in range(4):
                            pr = hh // 2
                            self.pe("matmul", pC1[:, hh, :], kTm[:, hh, :bt], rTt[:, pr, :bt], start=True, stop=True)
                        for hh in range(4):
                            pr = hh // 2
                            self.pe("matmul", pC2[:, hh, :], bTm[:, hh, :bt], rTt[:, pr, :bt], start=True, stop=True)
                        msu = cSU[:bt, :bt].unsq(1).bc([bt, 4, bt])
                        mu_ = cU[:bt, :bt].unsq(1).bc([bt, 4, bt])
                        X, XT, R = cXb, cXTb, cR2
                        self.dve("tensor_tensor", X[:bt, :, :bt], pB, msu, ALU.mult)
                        self.dve("tensor_tensor", cBmT[:bt, :, :bt], pBm, msu, ALU.mult)
                        self.dve("tensor_tensor", C1T[:bt, :, :bt], pC1, mu_, ALU.mult)
                        self.dve("scalar_tensor_tensor", C2T[:bt, :, :bt], pC2, -1.0, mu_, ALU.mult, ALU.mult)
                        pA = psbT[:bt, 0:512].r("p (h s) -> p h s", h=4)[:, :, :bt]
                        for hh in range(4):
                            self.pe("transpose", pA[:, hh, :], X[:bt, hh, :bt], identb[:bt, :bt])
                        self.act("copy", XT[:bt, :, :bt], pA)
                        self.dve("tensor_tensor", R[:bt, :, :bt], ident[:bt, :bt].unsq(1).bc([bt, 4, bt]), X[:bt, :, :bt], ALU.subtract)
                        neumann(bt, X, XT, R, cx.CH)
                        for hh in range(4):
                            hs = slice(hh * 64, (hh + 1) * 64)
                            self.pe("matmul", ps[4][:bt, hs], cBmT[:bt, hh, :bt], v_tm[:, hs], start=True, stop=True)
                        self.act("copy", cx0b[:bt, :], ps[4][:bt, 0:256])
                        for hh in range(4):
                            hs = slice(hh * 64, (hh + 1) * 64)
                            self.pe("matmul", ps[4][:bt, 256 + hh * 64:256 + (hh + 1) * 64], R[:bt, hh, :bt], cx0b[:bt, hs], start=True, stop=True)
                        self.act("copy", cU0[:bt, :], ps[4][:bt, 256:512])
                        for pr in range(2):
                            for e in range(2):
                                self.pe("matmul", ps[5][:, pr * 128:pr * 128 + bt], kapz[:bt, 2 * pr + e, :], R[:bt, 2 * pr + e, :bt],
                                        start=(e == 0), stop=(e == 1))
                        self.act("copy", wT[:, :, :bt], ps[5][:, 0:256].r("p (a t) -> p a t", a=2)[:, :, :bt])
                        for ci in range(len(chunks)):
                            for pr in range(2):
                                for e in range(2):
                                    hh = 2 * pr + e
                                    self.pe("matmul", ps[5][:, 256 + ci * 128 + pr * 64:256 + ci * 128 + (pr + 1) * 64],
                                            cktz[:bt, ci, hh, :], v_tm[:, hh * 64:(hh + 1) * 64], start=(e == 0), stop=(e == 1))
                        self.act("copy", kvp[:, 0:len(chunks), :, :], ps[5][:, 256:256 + 128 * len(chunks)].r("p (c a d) -> p c a d", c=len(chunks), a=2))
                        for ci, (r0, r1) in enumerate(chunks):
                            for pr in range(2):
                                self.pe("matmul", ps[6][:bt, pr * 128:(pr + 1) * 128], wT[:, pr, :bt], cx.Tst[l][:, pr, :], start=True, stop=True)
                            for pr in range(2):
                                self.pe("matmul", ps[6][:bt, 256 + pr * 128:256 + (pr + 1) * 128], rTt[:, pr, :bt], cx.Tst[l][:, pr, :],
                                        start=True, stop=True)
                            self.dve("tensor_tensor", cU_[r0:r1, :], cU0[r0:r1, :], ps[6][r0:r1, 0:256], ALU.add)
                            self.dve("tensor_copy", cyq[r0:r1, :], ps[6][r0:r1, 256:512])
                            for e in range(2):
                                tdv = cx.Tst[l][e * 64:(e + 1) * 64, :, e * 64:(e + 1) * 64]
                                self.dve("tensor_tensor", tdv, tdv, kvp[e * 64:(e + 1) * 64, ci, :, :], ALU.add)
                            for pr in range(2):
                                for e in range(2):
                                    hh = 2 * pr + e
                                    self.pe("matmul", ps[2][:, 64 + pr * 64:128 + pr * 64], cbtz[:bt, ci, hh, :],
                                            cU_[:bt, hh * 64:(hh + 1) * 64], start=(e == 0), stop=(e == 1))
                            for e in range(2):
                                tdv = cx.Tst[l][e * 64:(e + 1) * 64, :, e * 64:(e + 1) * 64]
                                self.dve("tensor_tensor", tdv, tdv, ps[2][e * 64:(e + 1) * 64, 64:192].r("p (a d) -> p a d", a=2), ALU.subtract)
                                self.dve("tensor_tensor", tdv, tdv, pcc[e * 64:(e + 1) * 64, ci, :].unsq(2).bc([64, 2, 64]), ALU.mult)
                        for hh in range(4):
                            hs = slice(hh * 64, (hh + 1) * 64)
                            self.pe("matmul", ps[4][:bt, hs], C1T[:bt, hh, :bt], v_tm[:, hs], start=True, stop=False)
                            self.pe("matmul", ps[4][:bt, hs], C2T[:bt, hh, :bt], cU_[:bt, hs], start=False, stop=True)
                        self.dve("tensor_tensor", cytm[:bt, :], cyq[:bt, :], ps[4][:bt, 0:256], ALU.add)
                        y3 = cytm[:bt, :].r("p (h d) -> p h d", h=4)
                        self.dve("tensor_reduce", cst[:bt, 0:4], y3, AX.X, ALU.add)
                        self.dve("tensor_scalar_mul", cst[:bt, 0:4], cst[:bt, 0:4], 1.0 / 64)
                        self.dve("tensor_tensor", y3, y3, cst[:bt, 0:4].unsq(2).bc([bt, 4, 64]), ALU.subtract)
                        self.dve("tensor_tensor", cyq[:bt, :], cytm[:bt, :], cytm[:bt, :], ALU.mult)
                        self.dve("tensor_reduce", cst[:bt, 4:8], cyq[:bt, :].r("p (h d) -> p h d", h=4), AX.X, ALU.add)
                        self.act("activation", cst[:bt, 4:8], cst[:bt, 4:8], AF.Ln, bias=colf(9)[:bt, :], scale=1.0 / 64)
                        self.act("activation", cst[:bt, 4:8], cst[:bt, 4:8], AF.Exp, scale=-0.5)
                        self.dve("tensor_tensor", cyn(blk)[:bt, :].r("p (h d) -> p h d", h=4), y3,
                                 cst[:bt, 4:8].unsq(2).bc([bt, 4, 64]), ALU.mult)
                    for pr in range(2):
                        for blk in range(nbk):
                            bt = min(128, ntok - blk * 128)
                            self.pe("transpose", ps[3][:, blk * 128:blk * 128 + bt], cyn(blk)[:bt, pr * 128:(pr + 1) * 128], ident[:bt, :bt])
                        tf = tmpf[pr]
                        self.dve("scalar_tensor_tensor", tf[:, :ntok], ps[3][:, :ntok], c2(50 + pr), c_bon[:, pr, :ntok], ALU.mult, ALU.add)
                        self.dve("scalar_tensor_tensor", mixT[:, 6 + pr, :ntok], tf[:, :ntok], c2(52 + pr), c_g[:, pr, :ntok], ALU.add, ALU.mult)
                    if cx.is_last:
                        for pr in range(2):
                            self.pe("transpose", ps[3][:, pr * 128:(pr + 1) * 128], cx.Tst[l][:, pr, :], ident)
                        self.act("copy", wT.v(), ps[3][:, 0:256].r("p (a t) -> p a t", a=2))
                        for e in range(2):
                            self.dma("sp", V(cx.cwkv_out.t[l].rearrange("(pr e) v k -> e v pr k", e=2)[e], ()),
                                     wT[e * 64:(e + 1) * 64, :, e * 64:(e + 1) * 64], final=True)

                def epi_res(m, pv):
                    self.dve("tensor_tensor", h[:, m, :ntok], h[:, m, :ntok], pv, ALU.add)
                dense_fm(("w_out", l), 8, 0, D_MODEL, mixT, ntok, epi_res)

                rmsnorm_to_xb(ntok, lambda k: col("norm_ffn", l, k))

                def epi_ff1(m, pv):
                    tf = tmpf[m % 2]
                    self.act("activation", tf[:, :ntok], pv, AF.Relu)
                    self.dve("tensor_tensor", uT[:, m, :ntok], tf[:, :ntok], tf[:, :ntok], ALU.mult)
                dense_fm(("w_ff1", l), 8, 0, D_FF, xb, ntok, epi_ff1)
                dense_fm(("w_ff2", l), 32, 0, D_MODEL, uT, ntok, epi_res)

                rmsnorm_to_xb(ntok, lambda k: col("norm_ple", l, k))

                def epi_gate(m, pv):
                    self.act("activation", gate[:, m, :ntok], pv, AF.Sigmoid)
                dense_fm(("w_ple_gate", l), 8, 0, D_MODEL, xb, ntok, epi_gate)
                self.dma("sp", ptok[:bl, :nbk, :], V(cx.p_src(l).rearrange("(b p) d -> p b d", p=bl), ()))
                for k in range(2):
                    for b in range(nbk):
                        self.pe("transpose", ps[3][:, b * 128:b * 128 + bl], ptok[:bl, b, k * 128:(k + 1) * 128], ident[:bl, :bl])
                    self.act("copy", pT[:, k, :ntok], ps[3][:, :ntok])

                def epi_ple(m, pv):
                    self.dve("tensor_tensor", tmpf[m % 2][:, :ntok], gate[:, m, :ntok], pv, ALU.mult)
                    self.dve("tensor_tensor", h[:, m, :ntok], h[:, m, :ntok], tmpf[m % 2][:, :ntok], ALU.add)
                dense_fm(("w_ple_proj", l), 2, 0, D_MODEL, pT, ntok, epi_ple)

            for k in range(8):
                s_ = sq[k % 2]
                self.act("activation", s_[:, :ntok], h[:, k, :ntok], AF.Square)
                self.pe("matmul", ps[2][:, :ntok], onesb.v(), s_[:, :ntok], start=(k == 0), stop=(k == 7))
            self.act("activation", rstd[:, :ntok], ps[2][:, :ntok], AF.Ln, bias=colf(0), scale=1.0 / D_MODEL)
            self.act("activation", rstd[:, :ntok], rstd[:, :ntok], AF.Exp, scale=-0.5)
            for k in range(8):
                tf = tmpf[k % 2]
                self.dve("scalar_tensor_tensor", tf[:, :ntok], h[:, k, :ntok], cols[:, 51 + k:52 + k], rstd[:, :ntok], ALU.mult, ALU.mult)
                for b in range(nbk):
                    self.pe("transpose", ps[3][:bl, b * 128:(b + 1) * 128], tf[:, b * 128:b * 128 + bl], ident)
                self.act("copy", xtok[:bl, :nbk, k * 128:(k + 1) * 128], ps[3][:bl, :nbk * 128].r("p (b f) -> p b f", b=nbk))
            self.dma("sp", V(cx.y_dst.rearrange("(b p) d -> p b d", p=bl), ()), xtok[:bl, :nbk, :], final=True)

        class Cx:
            pass

        if "prompt" in self.stages:
            for t in range(NT):
                cx = Cx()
                t0 = t * TT
                cx.ntok = TT; cx.CH = min(64, SEQ); cx.key_base = t0; cx.masked = True; cx.is_last = (t == NT - 1)
                cx.x_src = x_in.t[t0:t0 + TT, :]
                cx.rope_src = rope_in.t[t0:t0 + TT, :]
                cx.p_src = lambda l, t0=t0: p_in.t[l, t0:t0 + TT, :]
                cx.y_dst = y_out.t[t0:t0 + TT, :]
                cx.bk_dst = lambda l, t0=t0: bk_out.t[l, t0:t0 + TT, :]
                cx.bv_dst = lambda l, t0=t0: bv_out.t[l, t0:t0 + TT, :]
                cx.kT_scr, cx.v_scr = kT_scr, v_scr
                cx.acarry, cx.Sd, cx.Tst, cx.ccarry = acarry, Sd, Tst, ccarry
                cx.aconv_out, cx.adelta_out, cx.cshift_out, cx.cwkv_out = aconv_out, adelta_out, cshift_out, cwkv_out
                tile_body(cx)

        if "sample" in self.stages:
            DEC, PAST = self.DEC, self.PAST
            xs_in = self.din("xs", [DEC, D_MODEL])
            ps_in = self.din("psm", [DEPTH, DEC, PLE_DIM])
            ropes_in = self.din("rope_s", [DEC, 64])
            ck_in = self.din("cache_k", [DEPTH, PAST, 512])
            cv_in = self.din("cache_v", [DEPTH, PAST, 512])
            sconv_in = self.din("st_conv", [DEPTH, 3, 768])
            sdelta_in = self.din("st_delta", [DEPTH, 4, 64, 64])
            sshift_in = self.din("st_shift", [DEPTH, 1024])
            swkv_in = self.din("st_wkv", [DEPTH, 4, 64, 64])
            ys_out = self.dout("y_s", [DEC, D_MODEL])
            bks_out = self.dout("b_k_s", [DEPTH, DEC, 512])
            bvs_out = self.dout("b_v_s", [DEPTH, DEC, 512])
            aconvs_out = self.dout("a_conv_s", [DEPTH, 3, 768])
            adeltas_out = self.dout("a_delta_s", [DEPTH, 4, 64, 64])
            cshifts_out = self.dout("c_shift_s", [DEPTH, 1024])
            cwkvs_out = self.dout("c_wkv_s", [DEPTH, 4, 64, 64])
            kT_scr_s = [self.dscr(f"kT_scr_s{l}", [512, PAST + DEC], BF16) for l in range(DEPTH)]
            v_scr_s = [self.dscr(f"v_scr_s{l}", [PAST + DEC, 512], BF16) for l in range(DEPTH)]
            psb = psbT.v()
            for l in range(DEPTH):
                for m in range(6):
                    self.dma("sp", acarry_s[l][:, m, :], V(sconv_in.t[l, :, m * 128:(m + 1) * 128].rearrange("j p -> p j"), ()),
                             allow_slow_non_contiguous=True)
                self.dma("sp", ccarry_s[l].v(), V(sshift_in.t[l].rearrange("(c p) -> p c", p=128), ()), allow_slow_non_contiguous=True)
                self.dve("memset", Sd_s[l].v(), 0.0)
                self.dve("memset", wT.v(), 0.0)
                for e in range(2):
                    self.dma("sp", Sd_s[l][e * 64:(e + 1) * 64, :, e * 64:(e + 1) * 64],
                             V(sdelta_in.t[l].rearrange("(pr e) k v -> e k pr v", e=2)[e], ()))
                    self.dma("sp", wT[e * 64:(e + 1) * 64, :, e * 64:(e + 1) * 64],
                             V(swkv_in.t[l].rearrange("(pr e) v k -> e v pr k", e=2)[e], ()))
                for pr in range(2):
                    self.pe("transpose", ps[3][:, pr * 128:(pr + 1) * 128], wT[:, pr, :], ident)
                self.act("copy", Tst_s[l].v(), ps[3][:, 0:256].r("p (a t) -> p a t", a=2))
                for g in range(PAST // 512):
                    for b in range(4):
                        r0 = g * 512 + b * 128
                        kf, vf = kfs[b % 2], vfs[b % 2]
                        self.dma("sp", kf.v(), V(ck_in.t[l, r0:r0 + 128, :], ()))
                        self.act("copy", ktm[:, b, :], kf.v())
                        self.dma("sp", vf.v(), V(cv_in.t[l, r0:r0 + 128, :], ()))
                        self.dve("tensor_copy", vtb[:, b, :], vf.v())
                    for c in range(4):
                        for b in range(4):
                            self.pe("transpose", psb[:, 512 + b * 128:512 + (b + 1) * 128], ktm[:, b, c * 128:(c + 1) * 128], identb.v())
                        self.dve("tensor_copy", kTt[:, c, :], psb[:, 512:1024])
                    self.dma("sp", V(kT_scr_s[l].t[:, g * 512:(g + 1) * 512].rearrange("(c p) s -> p c s", p=128), (kT_scr_s[l].buf,)), kTt.v())
                    self.dma("sp", V(v_scr_s[l].t[g * 512:(g + 1) * 512, :].rearrange("(b p) d -> p b d", p=128), (v_scr_s[l].buf,)), vtb.v())
            cx = Cx()
            cx.ntok = DEC; cx.CH = min(64, DEC); cx.key_base = PAST; cx.masked = False; cx.is_last = True
            cx.x_src = xs_in.t[:, :]
            cx.rope_src = ropes_in.t[:, :]
            cx.p_src = lambda l: ps_in.t[l, :, :]
            cx.y_dst = ys_out.t[:, :]
            cx.bk_dst = lambda l: bks_out.t[l, :, :]
            cx.bv_dst = lambda l: bvs_out.t[l, :, :]
            cx.kT_scr, cx.v_scr = kT_scr_s, v_scr_s
            cx.acarry, cx.Sd, cx.Tst, cx.ccarry = acarry_s, Sd_s, Tst_s, ccarry_s
            cx.aconv_out, cx.adelta_out, cx.cshift_out, cx.cwkv_out = aconvs_out, adeltas_out, cshifts_out, cwkvs_out
            tile_body(cx)

        self.S.emit(st)
        st.close()
        return nc


def _consts():
    c = np.zeros((128, 11, 128), np.float32)
    i = np.arange(128)
    same = (i[:, None] // 64) == (i[None, :] // 64)
    c[:, 0, :] = np.eye(128)
    c[:, 1, :] = 1.0
    c[:, 2, :] = (i[:, None] <= i[None, :]) & same
    c[:, 3, :] = (i[:, None] < i[None, :]) & same
    c[:, 4, :] = (i[:, None] > i[None, :]) & same
    c[:, 5, :] = (i[:, None] >= i[None, :]) & same
    c[:, 6, :] = same
    for ci in range(2):
        for e in range(2):
            c[:, 7 + 2 * ci + e, :] = ((i[:, None] // 64) == ci) & ((i[None, :] // 64) == e)
    return c


def _cols2(inp):
    c = np.zeros((128, DEPTH, 64), np.float32)
    for l in range(DEPTH):
        cw = inp["a_conv_w"][l]
        for m in range(6):
            c[:, l, m * 4:m * 4 + 4] = cw[:, m * 128:(m + 1) * 128].T
        c[:, l, 24] = np.tile(inp["a_norm"][l], 2)
        c[:, l, 32:40] = inp["c_mu"][l].reshape(8, 128).T
        for nm, base in (("c_w0", 40), ("c_a0", 42), ("c_k_k", 44), ("c_k_a", 46), ("c_r_k", 48), ("c_ln_w", 50), ("c_ln_b", 52)):
            c[:, l, base:base + 2] = inp[nm][l].reshape(2, 128).T
    return c


def _cw3(inp):
    w = np.zeros((128, DEPTH, 3, 256), np.float32)
    for l in range(DEPTH):
        w[0:64, l, 0, :] = inp["c_w_up"][l]
        w[64:128, l, 1, :] = inp["c_a_up"][l]
        w[:, l, 2, :] = inp["c_g_up"][l]
    return w


def _rowp(inp):
    return np.concatenate([inp["a_A_log"], inp["a_dt_bias"]], axis=1).astype(np.float32)


def _amask():
    kp = np.arange(128)[:, None]
    q = np.arange(512)[None, :]
    m = np.zeros((128, 4, 512), np.float32)
    for j in range(4):
        m[:, j, :] = (2 * j + kp // 64) <= (q // 64)
    return m.astype(ml_dtypes.bfloat16)


def _rope_table(pos):
    half = 32
    inv = (10000.0 ** (-2.0 * np.arange(half, dtype=np.float32) / 64)).astype(np.float32)
    ang = pos.astype(np.float32)[:, None] * inv[None, :]
    return np.concatenate([np.cos(ang), np.sin(ang)], axis=1).astype(np.float32)


def _cols(inp):
    c = np.zeros((128, 64), np.float32)
    for l in range(DEPTH):
        for nm, base in (("norm_mix", 0), ("norm_ffn", 8), ("norm_ple", 16)):
            c[:, l * 24 + base:l * 24 + base + 8] = inp[nm][l].reshape(8, 128).T
        lam_init = 0.8 - 0.6 * math.exp(-0.3 * l)
        c[:, 48 + l] = inp["b_norm"][l]
    c[:, 50] = NORM_EPS
    c[:, 59] = C_LN_EPS
    c[:, 51:59] = inp["norm_final"].reshape(8, 128).T
    return c


_CACHE = {}


def run(inputs, seq, n_cores, stages=("prompt", "A", "C", "sample"), trace=False):
    key = (seq, stages)
    if key not in _CACHE:
        b = Builder(seq, stages=stages)
        b.build()
        _CACHE[key] = b
    b = _CACHE[key]
    cols = _cols(inputs)
    consts = _consts()
    amask = _amask()
    rope = _rope_table(np.arange(seq))
    lamrow = np.stack([np.stack([inputs[n][l] for n in ("b_lam_q1", "b_lam_k1", "b_lam_q2", "b_lam_k2")])
                       for l in range(DEPTH)]).astype(np.float32)
    shared = {"cols": cols, "consts": consts, "amask": amask, "rope": rope, "lamrow": lamrow,
              "cols2": _cols2(inputs), "rowp": _rowp(inputs), "cw3": _cw3(inputs)}
    for nm in ("w_in", "w_out", "w_ff1", "w_ff2", "w_ple_gate", "w_ple_proj"):
        shared[nm] = inputs[nm]
    if "sample" in stages:
        past = inputs["cache_b_k"].shape[2]
        dec = inputs["x_sample"].shape[1]
        shared["rope_s"] = _rope_table(np.arange(past, past + dec))
    in_maps = []
    for c in range(n_cores):
        m = dict(shared)
        m["x"] = np.ascontiguousarray(inputs["x_prompt"][c])
        m["p"] = np.ascontiguousarray(inputs["p_prompt"][:, c])
        if "sample" in stages:
            m["xs"] = np.ascontiguousarray(inputs["x_sample"][c])
            m["psm"] = np.ascontiguousarray(inputs["p_sample"][:, c])
            m["cache_k"] = np.ascontiguousarray(inputs["cache_b_k"][:, c]).reshape(DEPTH, past, 512)
            m["cache_v"] = np.ascontiguousarray(inputs["cache_b_v"][:, c]).reshape(DEPTH, past, 512)
            m["st_conv"] = np.ascontiguousarray(inputs["state_a_conv"][:, c])
            m["st_delta"] = np.ascontiguousarray(inputs["state_a_delta"][:, c])
            m["st_shift"] = np.ascontiguousarray(inputs["state_c_shift"][:, c])
            m["st_wkv"] = np.ascontiguousarray(inputs["state_c_wkv"][:, c])
        in_maps.append(m)
    res = run_bass_kernel_spmd(b.nc, in_maps, core_ids=list(range(n_cores)), trace=trace)
    return res


def kernel(**inputs):
    inputs = {k: np.asarray(v) for k, v in inputs.items()}
    n, seq = inputs["x_prompt"].shape[0], inputs["x_prompt"].shape[1]
    dec = inputs["x_sample"].shape[1]
    r = run(inputs, seq, n).results

    def st(name, axis):
        return np.stack([np.asarray(r[c][name]) for c in range(n)], axis=axis)
    return (st("y", 0), st("y_s", 0),
            st("a_conv", 1), st("a_delta", 1),
            st("b_k", 1).reshape(DEPTH, n, seq, 4, 128), st("b_v", 1).reshape(DEPTH, n, seq, 4, 128),
            st("c_shift", 1), st("c_wkv", 1),
            st("a_conv_s", 1), st("a_delta_s", 1),
            st("b_k_s", 1).reshape(DEPTH, n, dec, 4, 128), st("b_v_s", 1).reshape(DEPTH, n, dec, 4, 128),
            st("c_shift_s", 1), st("c_wkv_s", 1))
```

```python
import math
from contextlib import ExitStack

import numpy as np
import ml_dtypes
import concourse.bass as bass
import concourse.mybir as mybir
from concourse.bass_utils import run_bass_kernel_spmd

F32 = mybir.dt.float32
BF16 = mybir.dt.bfloat16
AF = mybir.ActivationFunctionType
ALU = mybir.AluOpType
AX = mybir.AxisListType

D_MODEL = 1024
DEPTH = 2
PLE_DIM = 256
D_FF = 4096
IN_WIDTH = 3592
NORM_EPS = 1e-6
L2_EPS = 1e-6
C_LN_EPS = 64e-5
O_AQKV, O_AZ, O_AA, O_AB, O_BQ, O_BK, O_BV, O_C = 0, 768, 1024, 1028, 1032, 1544, 2056, 2568

ENGS = ("pe", "act", "dve", "pool", "sp")
SEM_WRAP = 30000


class Buf:
    __slots__ = ("name", "writer", "readers", "chan", "multi")

    def __init__(self, name, multi=False):
        self.name = name
        self.writer = [] if multi else None
        self.readers = []
        self.chan = None
        self.multi = multi


class Chan:
    __slots__ = ("sem", "count", "name")

    def __init__(self, name):
        self.name = name
        self.sem = None
        self.count = 0


class Op:
    __slots__ = ("eng", "fn", "deps", "signal", "semidx", "semval", "is_dma", "chan", "chan_val")

    def __init__(self, eng, fn):
        self.eng = eng
        self.fn = fn
        self.deps = []
        self.signal = False
        self.semidx = 0
        self.semval = 0
        self.is_dma = False
        self.chan = None
        self.chan_val = 0


class Sched:
    def __init__(self, nc):
        self.nc = nc
        self.ops = {e: [] for e in ENGS}
        self.chans = []
        self.final_waits = []

    def _collect(self, op, reads, writes, waits=()):
        deps = []
        for b in waits:
            if b.multi:
                deps.extend(b.writer)
            elif b.writer is not None:
                deps.append(b.writer)
            deps.extend(b.readers)
        for b in reads:
            if b.multi:
                deps.extend(b.writer)
            elif b.writer is not None:
                deps.append(b.writer)
        for b in writes:
            if b.multi:
                deps.extend(b.writer)
            elif b.writer is not None:
                deps.append(b.writer)
            deps.extend(b.readers)
        seen = set()
        for d in deps:
            if d is op or id(d) in seen:
                continue
            seen.add(id(d))
            if d.eng == "pe" and op.eng == "pe" and not d.is_dma and not op.is_dma:
                continue
            op.deps.append(d)
        for b in writes:
            if b.multi:
                b.writer.append(op)
            else:
                b.writer = op
                b.readers = []
        for b in reads:
            b.readers.append(op)

    def op(self, eng, fn, reads=(), writes=(), waits=()):
        o = Op(eng, fn)
        self.ops[eng].append(o)
        self._collect(o, reads, writes, waits)
        return o

    def dma(self, eng, out, in_, reads=(), writes=(), chan_buf=None, final=False, waits=(), **kw):
        if chan_buf is None:
            chan_buf = (list(writes) + list(reads))[0]
        if chan_buf.chan is None:
            chan_buf.chan = {}
        if eng not in chan_buf.chan:
            chan_buf.chan[eng] = Chan(chan_buf.name + "_" + eng)
            self.chans.append(chan_buf.chan[eng])
        ch = chan_buf.chan[eng]
        o = Op(eng, None)
        o.is_dma = True
        o.chan = ch
        ch.count += 1
        o.chan_val = 16 * ch.count
        o.fn = lambda e, out=out, in_=in_, kw=kw: e.dma_start(out=out, in_=in_, **kw)
        self.ops[eng].append(o)
        self._collect(o, reads, writes, waits)
        if final:
            self.final_waits.append(o)
        return o

    def emit(self, stack):
        nc = self.nc
        for e in ENGS:
            for o in self.ops[e]:
                for d in o.deps:
                    if not d.is_dma:
                        d.signal = True
        nsem = {}
        for e in ENGS:
            cnt = 0
            for o in self.ops[e]:
                if o.signal and not o.is_dma:
                    o.semidx = cnt // SEM_WRAP
                    o.semval = cnt % SEM_WRAP + 1
                    cnt += 1
            nsem[e] = cnt // SEM_WRAP + 1
        esems = {e: [stack.enter_context(nc.semaphore(f"s_{e}{i}")) for i in range(nsem[e])] for e in ENGS}
        for i, ch in enumerate(self.chans):
            ch.sem = stack.enter_context(nc.semaphore(f"c{i}_{ch.name}"))
        block = stack.enter_context(nc.Block())

        def run(e, eng):
            seen = {}
            maxidx = {}
            for o in self.ops[e]:
                need = {}
                for d in o.deps:
                    if d.is_dma:
                        key = ("c", id(d.chan)); sem = d.chan.sem; val = d.chan_val
                    else:
                        key = (d.eng, d.semidx); sem = esems[d.eng][d.semidx]; val = d.semval
                    if key not in need or need[key][1] < val:
                        need[key] = (sem, val)
                for key, (sem, val) in need.items():
                    if key[0] != "c":
                        if maxidx.get(key[0], -1) > key[1]:
                            continue
                    if seen.get(key, 0) >= val:
                        continue
                    seen[key] = val
                    if key[0] != "c":
                        maxidx[key[0]] = max(maxidx.get(key[0], -1), key[1])
                    eng.wait_ge(sem, val)
                ins = o.fn(eng)
                if o.is_dma:
                    ins.then_inc(o.chan.sem, 16)
                elif o.signal:
                    ins.then_inc(esems[e][o.semidx], 1)
            if e == "sp":
                fin = {}
                for o in self.final_waits:
                    fin[id(o.chan)] = (o.chan, max(o.chan_val, fin.get(id(o.chan), (None, 0))[1]))
                for ch, val in fin.values():
                    eng.wait_ge(ch.sem, val)

        @block.tensor
        def _(eng):
            run("pe", eng)

        @block.scalar
        def _(eng):
            run("act", eng)

        @block.vector
        def _(eng):
            run("dve", eng)

        @block.gpsimd
        def _(eng):
            run("pool", eng)

        @block.sync
        def _(eng):
            run("sp", eng)


class V:
    __slots__ = ("ap", "bufs", "wb")

    def __init__(self, ap, bufs, wb=()):
        self.ap = ap
        self.bufs = bufs
        self.wb = wb

    def __getitem__(self, idx):
        return V(self.ap[idx], self.bufs, self.wb)

    def r(self, s, **kw):
        return V(self.ap.rearrange(s, **kw), self.bufs, self.wb)

    def bc(self, shape):
        return V(self.ap.to_broadcast(shape), self.bufs, self.wb)

    def unsq(self, ax):
        return V(self.ap.unsqueeze(ax), self.bufs, self.wb)


class T:
    def __init__(self, t, name, track=True, multi=False, bufs=None):
        self.t = t
        self.name = name
        self.buf = Buf(name, multi=multi) if track else None
        self.bufs = bufs
        self.wbufs = None
        self.extra = ()
        self.hb = None

    def __getitem__(self, idx):
        if self.wbufs is not None:
            own = (self.buf,) if self.buf is not None else tuple(self.bufs)
            return V(self.t[idx], own + tuple(self.extra), tuple(self.wbufs()))
        if self.bufs is not None:
            b = self.bufs() if callable(self.bufs) else self.bufs
            return V(self.t[idx], tuple(b) + tuple(self.extra))
        return V(self.t[idx], ((self.buf,) if self.buf is not None else ()) + tuple(self.extra))

    def v(self):
        return self[:]


class Builder:
    def __init__(self, seq, dec_seq=16, past=2048, stages=("prompt", "A", "C", "sample")):
        self.SEQ = seq
        self.DEC = dec_seq
        self.PAST = past
        self.TT = min(512, seq)
        self.CH = min(64, seq)
        self.stages = stages
        self.nc = bass.Bass("TRN2", target_bir_lowering=False)
        self.S = Sched(self.nc)
        self.st = ExitStack()
        self.in_names = []
        self.out_names = []
        import os
        self.bsub = int(os.environ.get('BSUB', '9'))
        self.poolsum = os.environ.get('POOLSUM', '0') == '1'
        self.wcache = os.environ.get('WCACHE', '1') == '1'
        self.asub = int(os.environ.get('ASUB', '9'))
        self.a1 = int(os.environ.get('A1', '9'))
        self.a3 = int(os.environ.get('A3', '9'))

    def sb(self, name, shape, dt=F32):
        import os
        if os.environ.get("ALLOCDBG"):
            print("ALLOC", name, shape, dt, int(np.prod(shape[1:])) * (4 if dt == F32 else 2))
        return T(self.st.enter_context(self.nc.sbuf_tensor("s_" + name, list(shape), dt)), name)

    def arena(self, name, nfloats):
        t = self.st.enter_context(self.nc.sbuf_tensor("s_" + name, [128, nfloats], F32))
        return {"t": t, "off": 0, "bufs": [], "n": nfloats, "parts": {}}

    def _arena_ap(self, ar, off, shape, dt):
        n = int(np.prod(shape[1:]))
        nfl = n if dt == F32 else n // 2
        assert off + nfl <= ar["n"], (off, nfl, ar["n"])
        ap = ar["t"][:, off:off + nfl]
        if dt != F32:
            ap = ap.bitcast(dt)
        if len(shape) == 3:
            ap = ap.rearrange("p (a b) -> p a b", a=shape[1])
        elif len(shape) == 4:
            ap = ap.rearrange("p (a b c) -> p a b c", a=shape[1], b=shape[2])
        return ap, nfl

    def sub(self, ar, name, shape, dt=F32, part=None):
        if part is None:
            ap, nfl = self._arena_ap(ar, ar["off"], shape, dt)
            ar["off"] += nfl
            tt = T(ap, name)
            ar["bufs"].append(tt.buf)
            return tt
        pd = ar["parts"].setdefault(part, {"off": 0, "bufs": []})
        ap, nfl = self._arena_ap(ar, pd["off"], shape, dt)
        pd["off"] += nfl
        tt = T(ap, name)
        pd["bufs"].append(tt.buf)
        tt.wbufs = lambda: [b for p, d in ar["parts"].items() if p != part for b in d["bufs"]]
        return tt

    def whole(self, ar, name, shape, dt=F32):
        ap, _ = self._arena_ap(ar, 0, shape, dt)
        return T(ap, name, track=False, bufs=ar["bufs"])

    def psum(self, name, shape, dt=F32):
        return T(self.st.enter_context(self.nc.psum_tensor("p_" + name, list(shape), dt)), name)

    def din(self, name, shape, dt=F32):
        self.in_names.append(name)
        return T(self.nc.dram_tensor(name, list(shape), dt, kind="ExternalInput").ap(), name, track=False)

    def dout(self, name, shape, dt=F32):
        self.out_names.append(name)
        return T(self.nc.dram_tensor(name, list(shape), dt, kind="ExternalOutput").ap(), name, track=False)

    def dscr(self, name, shape, dt):
        return T(self.nc.dram_tensor(name, list(shape), dt).ap(), name, multi=True)

    def _op(self, eng, meth, *args, _r=(), _w=(), **kw):
        reads, writes = list(_r), list(_w)
        waits = []
        cargs = []
        for i, a in enumerate(args):
            if isinstance(a, V):
                (writes if i == 0 else reads).extend(a.bufs)
                waits.extend(a.wb)
                cargs.append(a.ap)
            else:
                cargs.append(a)
        ckw = {}
        for k, a in kw.items():
            if isinstance(a, V):
                (writes if k in ("out", "accum_out") else reads).extend(a.bufs)
                waits.extend(a.wb)
                ckw[k] = a.ap
            else:
                ckw[k] = a
        return self.S.op(eng, lambda e: getattr(e, meth)(*cargs, **ckw), reads=reads, writes=writes, waits=waits)

    def pe(self, meth, *a, **k):
        return self._op("pe", meth, *a, **k)

    def act(self, meth, *a, **k):
        return self._op("act", meth, *a, **k)

    def dve(self, meth, *a, **k):
        return self._op("dve", meth, *a, **k)

    def dma(self, q, out, in_, final=False, **kw):
        import os
        if os.environ.get("NOSCR") and any(b.multi for b in list(out.bufs) + list(in_.bufs)):
            return
        if os.environ.get("NOOUT") and final and "b_" in str(out.ap):
            return
        reads = list(in_.bufs)
        writes = list(out.bufs)
        cands = [b for b in writes + reads if not b.multi]
        return self.S.dma(q, out.ap, in_.ap, reads=reads, writes=writes, chan_buf=cands[0], final=final,
                          waits=list(out.wb) + list(in_.wb), **kw)

    def build(self):
        nc = self.nc
        SEQ, TT = self.SEQ, self.TT
        NT = SEQ // TT
        NB = TT // 128
        st = self.st

        x_in = self.din("x", [SEQ, D_MODEL])
        p_in = self.din("p", [DEPTH, SEQ, PLE_DIM])
        W = {}
        for nm, shp in [("w_in", [DEPTH, D_MODEL, IN_WIDTH]), ("w_out", [DEPTH, D_MODEL, D_MODEL]),
                        ("w_ff1", [DEPTH, D_MODEL, D_FF]), ("w_ff2", [DEPTH, D_FF, D_MODEL]),
                        ("w_ple_gate", [DEPTH, D_MODEL, D_MODEL]), ("w_ple_proj", [DEPTH, PLE_DIM, D_MODEL])]:
            W[nm] = self.din(nm, shp)
        cols_in = self.din("cols", [128, 64])
        NCONST = 11
        consts_in = self.din("consts", [128, NCONST, 128])
        cols2_in = self.din("cols2", [128, DEPTH, 64])
        cw3_in = self.din("cw3", [128, DEPTH, 3, 256])
        cshift_out = self.dout("c_shift", [DEPTH, 1024])
        cwkv_out = self.dout("c_wkv", [DEPTH, 4, 64, 64])
        rowp_in = self.din("rowp", [DEPTH, 8])
        aconv_out = self.dout("a_conv", [DEPTH, 3, 768])
        adelta_out = self.dout("a_delta", [DEPTH, 4, 64, 64])
        rope_in = self.din("rope", [SEQ, 64])
        amask_in = self.din("amask", [128, 4, 512], BF16)
        lam_in = self.din("lamrow", [DEPTH, 4, 64])
        y_out = self.dout("y", [SEQ, D_MODEL])
        bk_out = self.dout("b_k", [DEPTH, SEQ, 512])
        bv_out = self.dout("b_v", [DEPTH, SEQ, 512])
        kT_scr = [self.dscr(f"kT_scr{l}", [512, SEQ], BF16) for l in range(DEPTH)]
        v_scr = [self.dscr(f"v_scr{l}", [SEQ, 512], BF16) for l in range(DEPTH)]

        consts = self.sb("consts", [128, NCONST, 128])
        cols2 = self.sb("cols2", [128, DEPTH, 64])
        rowp = self.sb("rowp", [128, DEPTH, 8])
        cU, cSU, cL, cLI, cSame = (consts[:, i, :] for i in (2, 3, 4, 5, 6))
        arA = self.arena("arA", 8192); arB = self.arena("arB", 4096); arC = self.arena("arC", 4096)
        aq = self.sub(arA, "aq", [128, 6, 3 + TT], part="A")
        acarry = [self.sb(f"acarry{l}", [128, 6, 3]) for l in range(DEPTH)]
        Sd = [self.sb(f"Sd{l}", [128, 2, 128]) for l in range(DEPTH)]
        ac = self.sub(arA, "ac", [128, 6, TT], part="A")
        qkn = self.sub(arB, "qkn", [128, 4, TT], part="A")
        zs = self.sub(arB, "zs", [128, 2, TT], part="A")
        abtm = self.sb("abtm", [128, NB, 8])
        astep = self.sb("astep", [128, NB, 4])
        beta = self.sb("beta", [128, NB, 4])
        arD = self.arena("arD", 4096)
        kvtm = self.sub(arD, "kvtm", [128, NB, 512], part="A")
        ontm = self.sub(arD, "ontm", [128, NB, 256], part="A")
        aL = self.sub(arD, "aL", [128, 4, 128], part="A"); aU = self.sub(arD, "aU", [128, 4, 128], part="A")
        decA = self.sub(arB, "decA", [128, 4, 128], part="A"); decT = self.sub(arB, "decT", [128, 4, 128], part="A")
        eg = self.sb("eg", [128, 16]); bkg = self.sb("bkg", [128, 4]); egt = self.sb("egt", [128, 2, 2])
        Am = self.sub(arA, "Am", [128, 4, 128], part="A")
        Xa = [self.sb("Xa0", [128, 4, 128])] * 2
        XTa = [self.sb("XTa0", [128, 4, 128])] * 2
        Rm = self.sub(arA, "Rm", [128, 4, 128], part="A")
        vb_ = self.sb("vb_", [128, 4, 64]); kbz = self.sb("kbz", [128, 4, 128]); ktz = self.sb("ktz", [128, 2, 4, 128]); kzb = self.sb("kzb", [128, 4, 128])
        unew = self.sb("unew", [128, 256]); wT = self.sb("wT", [128, 2, 128])
        qkT = self.sub(arA, "qkT", [128, 4, 128], part="A"); ssq = self.sb("ssq", [128, 4])
        ident = consts[:, 0, :]
        identb = self.sb("identb", [128, 128], BF16)
        onesb = self.sb("onesb", [128, 128], BF16)
        cols = self.sb("cols", [128, 64])
        amask = self.sb("amask", [128, 4, 512], BF16)
        lamt = self.sb("lamt", [128, 8])
        h = self.sb("h", [128, 8, TT])
        xb = self.sb("xb", [128, 8, TT], BF16)
        rstd = self.sb("rstd", [128, TT])
        sq = [self.sb(f"sq{i}", [128, TT], BF16) for i in range(2)]
        NSLOT = 3
        ring = [self.sb(f"wr{i}", [128, 4096], BF16) for i in range(NSLOT)]
        self.ring_i = 0
        mixT = self.sb("mixT", [128, 8, TT], BF16)
        pT = self.sb("pT", [128, 2, TT], BF16)
        tmpf = [self.sb(f"tmpf{i}", [128, TT]) for i in range(2)]
        ropet = self.sb("ropet", [128, NB, 64])
        kfs = [self.sb(f"kfs{i}", [128, 512]) for i in range(2)]
        vfs = [self.sb(f"vfs{i}", [128, 512]) for i in range(2)]
        qT = self.sb("qT", [128, 4, 2, TT], BF16)

        ptok = self.sub(arD, "ptok", [128, NB, PLE_DIM], part="P")
        kblk = [self.sub(arD, f"kblk{i}", [128, 512], BF16, part="B") for i in range(2)]
        vblk = [self.sub(arD, f"vblk{i}", [128, 4, 128], BF16, part="B") for i in range(2)]
        PT = [self.sub(arD, f"PT{i}", [128, TT], BF16, part="B") for i in range(4)]
        osb = [self.sub(arD, f"osb{i}", [128, TT], part="B") for i in range(3)]
        qtm = self.sub(arC, "qtm", [128, NB, 512], BF16, part="B")
        ktm = self.sub(arC, "ktm", [128, NB, 512], BF16, part="B")
        vtb = self.sub(arC, "vtb", [128, NB, 512], BF16, part="B")
        kTt = self.sub(arC, "kTt", [128, 4, TT], BF16, part="B")
        xtok = self.sub(arC, "xtok", [128, NB, D_MODEL], part="X")
        uT = self.sub(arA, "uT", [128, 32, TT], BF16, part="F")
        gate = self.sub(arB, "gate", [128, 8, TT], part="F")
        c_r = self.sub(arA, "c_r", [128, 2, TT], part="C"); c_v = self.sub(arA, "c_v", [128, 2, TT], part="C")
        c_wl = self.sub(arA, "c_wl", [128, 2, TT], part="C"); c_g = self.sub(arA, "c_g", [128, 2, TT], part="C")
        c_kk = self.sub(arA, "c_kk", [128, 2, TT], part="C"); c_km = self.sub(arA, "c_km", [128, 2, TT], part="C")
        c_b = self.sub(arA, "c_b", [128, 2, TT], part="C"); c_bon = self.sub(arA, "c_bon", [128, 2, TT], part="C")
        c_a = self.sub(arB, "c_a", [128, 2, TT], part="C"); c_k = self.sub(arB, "c_k", [128, 2, TT], part="C")
        c_6 = self.sub(arB, "c_6", [128, TT], part="C"); c_7 = self.sub(arB, "c_7", [128, TT], part="C")
        C1T = self.sub(arB, "C1T", [128, 4, 128], part="C"); C2T = self.sub(arB, "C2T", [128, 4, 128], part="C")
        cXb = T(c_6.t[:, 0:256].bitcast(BF16).rearrange("p (h s) -> p h s", h=4), "cXb", track=False, bufs=[c_6.buf])
        cXb.wbufs = c_6.wbufs
        cXTb = T(c_7.t[:, 0:256].bitcast(BF16).rearrange("p (h s) -> p h s", h=4), "cXTb", track=False, bufs=[c_7.buf])
        cXTb.wbufs = c_7.wbufs
        crawm = [self.sub(arD, f"crawm{i}", [128, 1 + TT], part="C") for i in range(2)]
        tm5 = self.sub(arD, "tm5", [128, 5, 256], part="C")
        rTt = self.sub(arD, "rTt", [128, 2, 128], part="C"); kapTt = self.sub(arD, "kapTt", [128, 2, 128], part="C")
        kTm = self.sub(arD, "kTm", [128, 4, 128], part="C"); bTm = self.sub(arD, "bTm", [128, 4, 128], part="C")
        uval = self.sub(arC, "uval", [128, 256], part="A"); oq = self.sub(arC, "oq", [128, 256], part="A")
        otm = self.sub(arC, "otm", [128, 256], part="A"); o2 = self.sub(arC, "o2", [128, 256], part="A")
        kapz = self.sub(arC, "kapz", [128, 4, 128], BF16, part="C")
        cktz = self.sub(arC, "cktz", [128, 2, 4, 128], part="C"); cbtz = self.sub(arC, "cbtz", [128, 2, 4, 128], part="C")
        cx0 = self.sub(arC, "cx0", [128, 256], part="C"); cx0b = self.sub(arC, "cx0b", [128, 256], BF16, part="C"); cU0 = self.sub(arC, "cU0", [128, 256], part="C")
        cU_ = self.sub(arC, "cU_", [128, 256], part="C"); cyq = self.sub(arC, "cyq", [128, 256], part="C")
        cytm = self.sub(arC, "cytm", [128, 256], part="C")
        cR2 = T(vfs[0].t[:, 0:256].bitcast(BF16).rearrange("p (h s) -> p h s", h=4), "cR2", track=False, bufs=[vfs[0].buf])
        cBmT = T(vfs[1].t[:, :].rearrange("p (h s) -> p h s", h=4), "cBmT", track=False, bufs=[vfs[1].buf])
        Tst = [self.sb(f"Tst{l}", [128, 2, 128]) for l in range(DEPTH)]
        ccarry = [self.sb(f"ccarry{l}", [128, 8]) for l in range(DEPTH)]
        acarry_s = [self.sb(f"acarry_s{l}", [128, 6, 3]) for l in range(DEPTH)]
        Sd_s = [self.sb(f"Sd_s{l}", [128, 2, 128]) for l in range(DEPTH)]
        Tst_s = [self.sb(f"Tst_s{l}", [128, 2, 128]) for l in range(DEPTH)]
        ccarry_s = [self.sb(f"ccarry_s{l}", [128, 8]) for l in range(DEPTH)]
        kvp = self.sb("kvp", [128, 2, 2, 64]); pcc = self.sb("pcc", [128, 2, 2])
        cw3 = self.sb("cw3", [128, DEPTH, 3, 256], BF16)
        cst = self.sb("cst", [128, 8])

        def cyn(blk):
            return kfs[blk // 2][:, (blk % 2) * 256:(blk % 2 + 1) * 256]
        cgt = self.sb("cgt", [128, 2, 256])
        ps = [self.psum(f"ps{i}", [128, 512]) for i in range(7)]
        psbT = self.psum("psb", [128, 1024], BF16)

        self.dma("sp", consts.v(), consts_in.v())
        self.dma("sp", cols.v(), cols_in.v())
        self.dma("sp", cols2.v(), cols2_in.v())
        self.dma("sp", rowp.v(), V(rowp_in.t.rearrange("l a -> (l a)").partition_broadcast(128)
                                  .rearrange("p (l a) -> p l a", l=DEPTH), ()))
        self.act("activation", rowp[:, :, 0:4], rowp[:, :, 0:4], AF.Exp)
        self.dve("tensor_scalar_mul", rowp[:, :, 0:4], rowp[:, :, 0:4], -1.0)
        for l in range(DEPTH):
            self.dve("memset", acarry[l].v(), 0.0)
            self.dve("memset", Sd[l].v(), 0.0)
        self.dma("pool", cw3.v(), cw3_in.v())
        for l in range(DEPTH):
            self.dve("memset", Tst[l].v(), 0.0)
            self.dve("memset", ccarry[l].v(), 0.0)
            self.dve("tensor_scalar", cols2[:, l, 54:56], cols2[:, l, 46:48], -1.0, 1.0, ALU.mult, ALU.add)
        self.dve("memset", kbz.v(), 0.0)
        self.dve("memset", ktz.v(), 0.0)
        self.dve("memset", unew.v(), 0.0)
        self.dma("sp", amask.v(), amask_in.v())
        self.dve("tensor_copy", identb.v(), consts[:, 0, :])
        self.dve("tensor_copy", onesb.v(), consts[:, 1, :])
        lrow = T(ptok.t[:, 0:2, :].rearrange("p a (b d) -> p a b d", b=4), "lrow", track=False, bufs=[ptok.buf])
        self.dma("sp", lrow.v(), V(lam_in.t.rearrange("l a d -> (l a d)").partition_broadcast(128)
                                  .rearrange("p (l a d) -> p l a d", l=DEPTH, a=4), ()))
        lsum = self.sb("lsum", [128, 4])
        for l in range(DEPTH):
            for m in range(2):
                self.dve("tensor_tensor", tmpf[0][:, 0:64], lrow[:, l, 2 * m, :], lrow[:, l, 2 * m + 1, :], ALU.mult)
                self.dve("reduce_sum", lsum[:, 2 * l + m:2 * l + m + 1], tmpf[0][:, 0:64], AX.X)
        self.act("activation", lsum.v(), lsum.v(), AF.Exp)
        for l in range(DEPTH):
            lam_init = 0.8 - 0.6 * math.exp(-0.3 * l)
            self.dve("scalar_tensor_tensor", lamt[:, l:l + 1], lsum[:, 2 * l + 1:2 * l + 2], -lam_init,
                     lsum[:, 2 * l:2 * l + 1], ALU.add, ALU.subtract)

        bns = self.sb("bns", [128, DEPTH])
        for l in range(DEPTH):
            self.dve("tensor_scalar_mul", bns[:, l:l + 1], cols[:, 48 + l:49 + l], float(1.0 - (0.8 - 0.6 * math.exp(-0.3 * l))))
        def col(name, l, k=0):
            base = {"norm_mix": 0, "norm_ffn": 8, "norm_ple": 16, "b_norm": 24}[name]
            if name == "b_norm":
                return bns[:, l:l + 1]
            return cols[:, l * 24 + base + k: l * 24 + base + k + 1]

        def colf(k):
            return cols[:, 50 + k:51 + k]

        def rmsnorm_to_xb(ntok, gcol):
            for k in range(8):
                s = sq[k % 2]
                self.act("activation", s[:, :ntok], h[:, k, :ntok], AF.Square)
                self.pe("matmul", ps[2][:, :ntok], onesb.v(), s[:, :ntok], start=(k == 0), stop=(k == 7))
            self.act("activation", rstd[:, :ntok], ps[2][:, :ntok], AF.Ln, bias=colf(0), scale=1.0 / D_MODEL)
            self.act("activation", rstd[:, :ntok], rstd[:, :ntok], AF.Exp, scale=-0.5)
            for k in range(8):
                self.dve("scalar_tensor_tensor", xb[:, k, :ntok], h[:, k, :ntok], gcol(k), rstd[:, :ntok],
                         ALU.mult, ALU.mult)

        wscr = {}

        def load_piece(wref, KC, c0, ncols):
            wname, l = wref
            wv = W[wname].t[l]
            slot = ring[self.ring_i % NSLOT]
            self.ring_i += 1
            flat = slot[:, 0:KC * ncols]
            sv = flat.r("p (k n) -> p k n", k=KC)
            key = (wname, l, KC, c0, ncols)
            if key in wscr and self.wcache:
                self.dma("sp", flat, wscr[key].v())
                return sv
            for k0 in range(0, KC, 8):
                k1 = min(KC, k0 + 8)
                src = V(wv.rearrange("(kc p) n -> p kc n", p=128)[:, k0:k1, c0:c0 + ncols], ())
                self.dma("pool", sv[:, k0:k1, :], src)
            if self.wcache:
                scr = self.dscr(f"wb_{wname}_{l}_{c0}_{ncols}", [128, KC * ncols], BF16)
                wscr[key] = scr
                self.dma("sp", scr.v(), flat)
            return sv

        self.psd = 0

        def dense_fm(wv, KC, c0, ncols_total, rhs, ntok, epi, mchunk=128):
            per = max(128, min(512, 4096 // KC))
            m = 0
            for pc0 in range(0, ncols_total, per):
                pn = min(per, ncols_total - pc0)
                sv = load_piece(wv, KC, c0 + pc0, pn)
                for cc in range(0, pn, mchunk):
                    mc = min(mchunk, pn - cc)
                    pst = ps[self.psd % 2]
                    self.psd += 1
                    for k in range(KC):
                        self.pe("matmul", pst[:mc, :ntok], sv[:, k, cc:cc + mc], rhs[:, k, :ntok],
                                start=(k == 0), stop=(k == KC - 1))
                    epi(m, pst[:mc, :ntok])
                    m += 1

        def dense_tm(wv, c0, ncols, ntok, epi):
            sv = load_piece(wv, 8, c0, ncols)
            for b0 in range(0, ntok, 128):
                bt = min(128, ntok - b0)
                pst = ps[self.psd % 2]
                self.psd += 1
                for k in range(8):
                    self.pe("matmul", pst[:bt, :ncols], xb[:, k, b0:b0 + bt], sv[:, k, :], start=(k == 0), stop=(k == 7))
                epi(b0 // 128, bt, pst[:bt, :ncols])

        def rope_tm(dst, src_sb, bt, blk):
            import os
            if os.environ.get("NOROPE"):
                self.dve("tensor_copy", dst, src_sb[:bt, :])
                return
            s4 = src_sb[:bt, :].r("p (g t f) -> p g t f", g=8, t=2)
            d4 = dst.r("p (g t f) -> p g t f", g=8, t=2)
            cos = ropet[:bt, blk, 0:32].unsq(1).bc([bt, 8, 32])
            sin = ropet[:bt, blk, 32:64].unsq(1).bc([bt, 8, 32])
            ta = tmpf[0][:bt, 0:256].r("p (g f) -> p g f", g=8)
            tb = tmpf[1][:bt, 0:256].r("p (g f) -> p g f", g=8)
            self.dve("tensor_tensor", ta, s4[:, :, 0, :], cos, ALU.mult)
            self.dve("tensor_tensor", tb, s4[:, :, 1, :], sin, ALU.mult)
            self.dve("tensor_tensor", d4[:, :, 0, :], ta, tb, ALU.subtract)
            self.dve("tensor_tensor", ta, s4[:, :, 1, :], cos, ALU.mult)
            self.dve("tensor_tensor", tb, s4[:, :, 0, :], sin, ALU.mult)
            self.dve("tensor_tensor", d4[:, :, 1, :], ta, tb, ALU.add)

        def split2(tt):
            if tt.hb is None:
                tt.hb = [Buf(tt.name + "_h0"), Buf(tt.name + "_h1")]
                tt.extra = tuple(tt.hb)
            return tt

        for tt in (Xa[0], XTa[0], Rm, cXb, cXTb, cR2):
            split2(tt)

        def neumann(bt, X, XT, R, CH):
            nlev = 5 if CH > 16 else 3

            def sh(tt, hh):
                return V(tt.t[:bt, hh, :bt], (tt.hb[hh // 2],))

            def sg(tt, g):
                return V(tt.t[:bt, 2 * g:2 * g + 2, :bt], (tt.hb[g],))

            bX, bXT, bR = (ps[0], ps[4]), (ps[1], ps[5]), (ps[3], ps[6])

            def ph(pts, hh):
                return pts[hh // 2][:bt, (hh % 2) * 128:(hh % 2) * 128 + bt]

            def pg(pts, g):
                return pts[g][:bt, 0:256].r("p (h s) -> p h s", h=2)[:, :, :bt]

            for lev in range(nlev):
                last = lev == nlev - 1
                for g in range(2):
                    for hh in (2 * g, 2 * g + 1):
                        self.pe("matmul", ph(bXT, hh), sh(X, hh), sh(XT, hh), start=True, stop=True)
                    if not last:
                        for hh in (2 * g, 2 * g + 1):
                            self.pe("matmul", ph(bX, hh), sh(XT, hh), sh(X, hh), start=True, stop=True)
                for g in range(2):
                    self.dve("tensor_copy", sg(XT, g), pg(bXT, g))
                    if not last:
                        self.act("copy", sg(X, g), pg(bX, g))
                for g in range(2):
                    for hh in (2 * g, 2 * g + 1):
                        self.pe("matmul", ph(bR, hh), sh(XT, hh), sh(R, hh), start=True, stop=True)
                for g in range(2):
                    self.dve("tensor_tensor", sg(R, g), sg(R, g), pg(bR, g), ALU.add)

        def tile_body(cx):
            ntok = cx.ntok
            nbk = (ntok + 127) // 128
            bl = min(128, ntok)
            kb0 = cx.key_base
            self.dma("sp", xtok[:bl, :nbk, :], V(cx.x_src.rearrange("(b p) d -> p b d", p=bl), ()))
            for k in range(8):
                for b in range(nbk):
                    self.pe("transpose", ps[3][:, b * 128:b * 128 + bl], xtok[:bl, b, k * 128:(k + 1) * 128], ident[:bl, :bl])
                self.act("copy", h[:, k, :ntok], ps[3][:, :ntok])
            self.dma("sp", ropet[:bl, :nbk, :], V(cx.rope_src.rearrange("(b p) d -> p b d", p=bl), ()))

            for l in range(DEPTH):
                lam_init = 0.8 - 0.6 * math.exp(-0.3 * l)
                rmsnorm_to_xb(ntok, lambda k: col("norm_mix", l, k))
                w_in = ("w_in", l)
                kT_s, v_s = cx.kT_scr[l], cx.v_scr[l]

                if True:
                    def epi_q(blk, bt, pv):
                        self.act("copy", osb[0][:bt, :], pv)
                        rope_tm(qtm[:bt, blk, :], osb[0], bt, blk)
                    dense_tm(w_in, O_BQ, 512, ntok, epi_q)

                    def epi_k(blk, bt, pv):
                        kf = kfs[blk % 2]
                        self.act("copy", osb[1][:bt, :], pv)
                        rope_tm(kf[:bt, :], osb[1], bt, blk)
                        self.act("copy", ktm[:bt, blk, :], kf[:bt, :])
                        self.dma("sp", V(cx.bk_dst(l)[blk * 128:blk * 128 + bt, :], ()), kf[:bt, :], final=True)
                    dense_tm(w_in, O_BK, 512, ntok, epi_k)

                    def epi_v(blk, bt, pv):
                        vf = vfs[blk % 2]
                        self.act("copy", vf[:bt, :], pv)
                        self.dve("tensor_copy", vtb[:bt, blk, :], vf[:bt, :])
                        self.dma("sp", V(cx.bv_dst(l)[blk * 128:blk * 128 + bt, :], ()), vf[:bt, :], final=True)
                    dense_tm(w_in, O_BV, 512, ntok, epi_v)
                    self.dma("sp", V(v_s.t[kb0:kb0 + ntok, :].rearrange("(b p) d -> p b d", p=bl), (v_s.buf,)), vtb[:bl, :nbk, :])
                    psb = psbT.v()
                    for c in range(4):
                        for b in range(nbk):
                            self.pe("transpose", psb[:, b * 128:b * 128 + bl], qtm[:bl, b, c * 128:(c + 1) * 128], identb[:bl, :bl])
                        for m in range(2):
                            self.act("mul", qT[:, c, m, :ntok], psb[:, 0:ntok], cSame[:, m * 64:m * 64 + 1])
                        for b in range(nbk):
                            self.pe("transpose", psb[:, 512 + b * 128:512 + b * 128 + bl], ktm[:bl, b, c * 128:(c + 1) * 128], identb[:bl, :bl])
                        self.dve("tensor_copy", kTt[:, c, :ntok], psb[:, 512:512 + ntok])
                    self.dma("sp", V(kT_s.t[:, kb0:kb0 + ntok].rearrange("(c p) s -> p c s", p=128), (kT_s.buf,)), kTt[:, :, :ntok])

                    nk = kb0 + ntok
                    nkb = (nk + 127) // 128
                    sbank = (ps[0], ps[1], ps[2], ps[5]) if self.poolsum else (ps[0], ps[1], ps[2], ps[0])
                    for hh in range(4):
                        units = []
                        for ks in range(0, nk, 512):
                            kn = min(512, nk - ks)
                            for j in range((kn + 127) // 128):
                                units.append((ks, kn, j, min(128, kn - j * 128), (ks + j * 128) // 128))

                        def stage1(ui):
                            ks, kn, j, kr, kb = units[ui]
                            sbi = (ks // 512) % 2
                            cur_k, cur_v = kblk[sbi], vblk[sbi]
                            if j == 0:
                                self.dma("sp", cur_k[:, :kn], V(kT_s.t[hh * 128:(hh + 1) * 128, ks:ks + kn], (kT_s.buf,)))
                                if kn == 512:
                                    self.dma("sp", cur_v.v(), V(v_s.t[ks:ks + 512, hh * 128:(hh + 1) * 128]
                                                                .rearrange("(j p) d -> p j d", p=128), (v_s.buf,)))
                                else:
                                    for jj in range((kn + 127) // 128):
                                        krr = min(128, kn - jj * 128)
                                        self.dma("sp", cur_v[:krr, jj, :], V(v_s.t[ks + jj * 128:ks + jj * 128 + krr, hh * 128:(hh + 1) * 128], (v_s.buf,)))
                            diag = cx.masked and kb * 128 >= kb0
                            for m in range(2):
                                pss = sbank[(ui % 2) * 2 + m] if self.poolsum else ps[(2 * ui + m) % 3]
                                pt = PT[(ui % 2) * 2 + m]
                                self.pe("matmul", pss[:kr, :ntok], cur_k[:, j * 128:j * 128 + kr],
                                        qT[:, hh, m, :ntok], start=True, stop=True)
                                self.act("activation", pt[:kr, :ntok], pss[:kr, :ntok], AF.Exp, scale=0.125)
                                if diag:
                                    self.dve("tensor_tensor", pt[:kr, :ntok], pt[:kr, :ntok], amask[:kr, kb - kb0 // 128, :ntok], ALU.mult)

                        def stage2(ui):
                            ks, kn, j, kr, kb = units[ui]
                            cur_v = vblk[(ks // 512) % 2]
                            first, last = ui == 0, ui == len(units) - 1
                            for m in range(2):
                                pt = PT[(ui % 2) * 2 + m]
                                self.pe("matmul", ps[3 + m][:, :ntok], cur_v[:kr, j, :], pt[:kr, :ntok], start=first, stop=last)
                                if not self.poolsum:
                                    self.pe("matmul", ps[5 + m][:, :ntok], onesb[:kr, :], pt[:kr, :ntok], start=first, stop=last)
                                elif first:
                                    self._op("pool", "tensor_copy", osb[m][:, :ntok], pt[:, :ntok])
                                else:
                                    self._op("pool", "tensor_tensor", osb[m][:kr, :ntok], osb[m][:kr, :ntok], pt[:kr, :ntok], ALU.add)

                        stage1(0)
                        for ui in range(1, len(units)):
                            stage1(ui)
                            stage2(ui - 1)
                        stage2(len(units) - 1)
                        n_ = slice(0, ntok)
                        if self.poolsum:
                            for m in range(2):
                                self.pe("matmul", ps[m][:, n_], consts[:, 1, :], osb[m][:, n_], start=True, stop=True)
                            self.dve("reciprocal", osb[0][:, n_], ps[0][:, n_])
                            self.dve("reciprocal", osb[1][:, n_], ps[1][:, n_])
                        else:
                            for m in range(2):
                                self.act("activation", osb[m][:, n_], ps[5 + m][:, n_], AF.Ln)
                                self.act("activation", osb[m][:, n_], osb[m][:, n_], AF.Exp, scale=-1.0)
                        self.dve("tensor_tensor", osb[0][:, n_], osb[0][:, n_], ps[3][:, n_], ALU.mult)
                        self.dve("tensor_tensor", osb[1][:, n_], osb[1][:, n_], ps[4][:, n_], ALU.mult)
                        self.dve("scalar_tensor_tensor", osb[2][:, n_], osb[1][:, n_], lamt[:, l:l + 1], osb[0][:, n_], ALU.mult, ALU.add)
                        self.act("activation", sq[0][:, n_], osb[2][:, n_], AF.Square)
                        self.pe("matmul", ps[2][:, n_], onesb.v(), sq[0][:, n_], start=True, stop=True)
                        self.act("activation", osb[0][:, n_], ps[2][:, n_], AF.Ln, bias=colf(0), scale=1.0 / 128)
                        self.act("activation", osb[0][:, n_], osb[0][:, n_], AF.Exp, scale=-0.5)
                        self.dve("scalar_tensor_tensor", mixT[:, 2 + hh, n_], osb[2][:, n_], col("b_norm", l), osb[0][:, n_],
                                 ALU.mult, ALU.mult)

                if "A" not in self.stages:
                    for c in (0, 1):
                        self.dve("memset", mixT[:, c, :], 0.0)
                else:
                    one_col = consts[:, 1, 0:1]
                    self.dve("tensor_copy", aq[:, :, 0:3], cx.acarry[l].v())

                    def epi_aqkv(m, pv):
                        self.act("copy", aq[:, m, 3:3 + ntok], pv)
                    dense_fm(w_in, 8, O_AQKV, 768, xb, ntok, epi_aqkv)
                    self.dve("tensor_copy", cx.acarry[l].v(), aq[:, :, ntok:ntok + 3])
                    if cx.is_last:
                        for m in range(6):
                            self.dma("sp", V(cx.aconv_out.t[l, :, m * 128:(m + 1) * 128].rearrange("j p -> p j"), ()),
                                     cx.acarry[l][:, m, :], final=True, allow_slow_non_contiguous=True)

                    def epi_az(m, pv):
                        self.act("activation", zs[:, m, :ntok], pv, AF.Silu)
                    if self.a1 >= 2: dense_fm(w_in, 8, O_AZ, 256, xb, ntok, epi_az)

                    def epi_ab(blk, bt, pv):
                        self.act("copy", abtm[:bt, blk, :], pv)
                    if self.a1 >= 3: dense_tm(w_in, O_AA, 8, ntok, epi_ab)
                    for m in (range(6) if self.a1 >= 4 else ()):
                        tf = tmpf[m % 2]
                        self.dve("tensor_scalar_mul", tf[:, :ntok], aq[:, m, 0:ntok], cols2[:, l, m * 4:m * 4 + 1])
                        for j in (1, 2, 3):
                            self.dve("scalar_tensor_tensor", tf[:, :ntok], aq[:, m, j:j + ntok],
                                     cols2[:, l, m * 4 + j:m * 4 + j + 1], tf[:, :ntok], ALU.mult, ALU.add)
                        self.act("activation", ac[:, m, :ntok], tf[:, :ntok], AF.Silu)
                    for m in (range(4) if self.a1 >= 5 else ()):
                        tf = tmpf[m % 2]
                        self.act("activation", tf[:, :ntok], ac[:, m, :ntok], AF.Square)
                        self.pe("matmul", ps[2][:, :ntok], cSame, tf[:, :ntok], start=True, stop=True)
                        self.act("activation", tf[:, :ntok], ps[2][:, :ntok], AF.Ln, bias=colf(0), scale=1.0)
                        self.act("activation", tf[:, :ntok], tf[:, :ntok], AF.Exp, scale=-0.5)
                        self.dve("scalar_tensor_tensor", qkn[:, m, :ntok], ac[:, m, :ntok], 0.125 if m < 2 else 1.0,
                                 tf[:, :ntok], ALU.mult, ALU.mult)
                    nbk = (ntok + 127) // 128
                    btl = min(128, ntok)
                    if self.a1 >= 6: self.dve("tensor_tensor", astep[:btl, :nbk, :], abtm[:btl, :nbk, 0:4],
                             rowp[:btl, l, 4:8].unsq(1).bc([btl, nbk, 4]), ALU.add)
                    if self.a1 >= 7: self.act("activation", astep[:btl, :nbk, :], astep[:btl, :nbk, :], AF.Exp)
                    if self.a1 >= 8: self.act("activation", astep[:btl, :nbk, :], astep[:btl, :nbk, :], AF.Ln, bias=one_col[:btl, :], scale=1.0)
                    if self.a1 >= 9: self.dve("tensor_tensor", astep[:btl, :nbk, :], astep[:btl, :nbk, :],
                             rowp[:btl, l, 0:4].unsq(1).bc([btl, nbk, 4]), ALU.mult)
                    if self.a1 >= 9: self.act("activation", beta[:btl, :nbk, :], abtm[:btl, :nbk, 4:8], AF.Sigmoid)

                    for blk in (range(nbk) if self.asub >= 2 else ()):
                        bt = min(128, ntok - blk * 128)
                        c0 = blk * 128
                        chunks = [(r0, min(r0 + cx.CH, bt)) for r0 in range(0, bt, cx.CH)]
                        for i, (src, m) in enumerate(((qkn, 2), (qkn, 3), (ac, 4), (ac, 5))):
                            self.pe("transpose", ps[3][:bt, i * 128:(i + 1) * 128], src[:, m, c0:c0 + bt], ident)
                        self.act("copy", kvtm[:bt, blk, :], ps[3][:bt, :])
                        ktm_ = kvtm[:bt, blk, 0:256].r("p (h d) -> p h d", h=4)
                        vtm_ = kvtm[:bt, blk, 256:512].r("p (h d) -> p h d", h=4)
                        a_b = astep[:bt, blk, :]
                        self.dve("tensor_tensor", aL[:bt, :, :bt], cL[:bt, :bt].unsq(1).bc([bt, 4, bt]),
                                 a_b.unsq(2).bc([bt, 4, bt]), ALU.mult)
                        self.dve("tensor_tensor", aU[:bt, :, :bt], cU[:bt, :bt].unsq(1).bc([bt, 4, bt]),
                                 a_b.unsq(2).bc([bt, 4, bt]), ALU.mult)
                        p0 = ps[0][:bt, :].r("p (h s) -> p h s", h=4)[:, :, :bt]
                        p1 = ps[1][:bt, :].r("p (h s) -> p h s", h=4)[:, :, :bt]
                        for hh in range(4):
                            self.pe("matmul", p0[:, hh, :], cU[:bt, :bt], aL[:bt, hh, :bt], start=True, stop=True)
                        for hh in range(4):
                            self.pe("matmul", p1[:, hh, :], cL[:bt, :bt], aU[:bt, hh, :bt], start=True, stop=True)
                        self.act("activation", decA[:bt, :, :bt], p0, AF.Exp)
                        self.act("activation", decT[:bt, :, :bt], p1, AF.Exp)
                        self.dve("tensor_tensor", decA[:bt, :, :bt], decA[:bt, :, :bt], cL[:bt, :bt].unsq(1).bc([bt, 4, bt]), ALU.mult)
                        self.dve("tensor_tensor", decT[:bt, :, :bt], decT[:bt, :, :bt], cU[:bt, :bt].unsq(1).bc([bt, 4, bt]), ALU.mult)
                        self.pe("matmul", ps[2][:bt, 0:4], cU[:bt, :bt], a_b, start=True, stop=True)
                        self.pe("matmul", ps[2][:bt, 4:8], cL[:bt, :bt], a_b, start=True, stop=True)
                        for ci in range(len(chunks)):
                            for e in range(2):
                                self.pe("matmul", ps[2][:, 8 + 2 * ci:10 + 2 * ci], consts[:bt, 7 + 2 * ci + e, :],
                                        astep[:bt, blk, e::2], start=(e == 0), stop=(e == 1))
                        self.act("activation", eg[:bt, 0:8], ps[2][:bt, 0:8], AF.Exp)
                        self.act("activation", egt.v().r("p c r -> p (c r)")[:, 0:2 * len(chunks)],
                                 ps[2][:, 8:8 + 2 * len(chunks)], AF.Exp)
                        self.dve("tensor_tensor", bkg[:bt, :], beta[:bt, blk, :], eg[:bt, 0:4], ALU.mult)
                        if self.asub < 3: continue
                        pk = ps[3][:bt, :].r("p (h s) -> p h s", h=4)[:, :, :bt]
                        import os
                        kkv = os.environ.get("KKV", "")
                        for hh in range(4):
                            e, pr = hh % 2, hh // 2
                            if kkv == "even" and e == 1:
                                continue
                            if kkv == "odd" and e == 0:
                                continue
                            self.dve("tensor_scalar_mul", kzb[:, hh, :bt], qkn[:, 2 + pr, c0:c0 + bt], cSame[:, e * 64:e * 64 + 1])
                            self.pe("matmul", pk[:, hh, :], kzb[:, hh, :bt], qkn[:, 2 + pr, c0:c0 + bt], start=True, stop=True)
                        import os
                        av = os.environ.get("AV", "")
                        for hh in range(4):
                            if av == "nodve":
                                continue
                            if av == "fsc":
                                self.dve("scalar_tensor_tensor", Am[:bt, hh, :bt], pk[:, hh, :], 1.0,
                                         decA[:bt, hh, :bt], ALU.mult, ALU.mult)
                                continue
                            if av == "tt":
                                self.dve("tensor_tensor", Am[:bt, hh, :bt], pk[:, hh, :], decA[:bt, hh, :bt], ALU.mult)
                                continue
                            self.dve("scalar_tensor_tensor", Am[:bt, hh, :bt], pk[:, hh, :], beta[:bt, blk, hh:hh + 1],
                                     decA[:bt, hh, :bt], ALU.mult, ALU.mult)
                        if self.a3 < 2: continue
                        pB = ps[0][:bt, :].r("p (h s) -> p h s", h=4)[:, :, :bt]
                        for hh in range(4):
                            self.pe("transpose", pB[:, hh, :], Am[:bt, hh, :bt], ident[:bt, :bt])
                        self.act("copy", Xa[0][:bt, :, :bt], pB)
                        self.dve("tensor_tensor", Rm[:bt, :, :bt], ident[:bt, :bt].unsq(1).bc([bt, 4, bt]), Xa[0][:bt, :, :bt], ALU.subtract)
                        self.dve("tensor_copy", XTa[0][:bt, :, :bt], Am[:bt, :, :bt])
                        neumann(bt, Xa[0], XTa[0], Rm, cx.CH)
                        self.dve("tensor_tensor", vb_[:bt, :, :], vtm_, beta[:bt, blk, :].unsq(2).bc([bt, 4, 64]), ALU.mult)
                        for hh in range(4):
                            e = hh % 2
                            self.dve("tensor_tensor", kbz[:bt, hh, e * 64:(e + 1) * 64], ktm_[:, hh, :],
                                     bkg[:bt, hh:hh + 1].bc([bt, 64]), ALU.mult)
                            for ci, (r0, r1) in enumerate(chunks):
                                self.dve("tensor_tensor", ktz[r0:r1, ci, hh, e * 64:(e + 1) * 64], ktm_[r0:r1, hh, :],
                                         eg[r0:r1, 4 + hh:5 + hh].bc([r1 - r0, 64]), ALU.mult)
                        for hh in range(4):
                            self.pe("matmul", ps[4][:bt, hh * 64:(hh + 1) * 64], Rm[:bt, hh, :bt], vb_[:bt, hh, :], start=True, stop=True)
                        for pr in range(2):
                            for e in range(2):
                                self.pe("matmul", ps[4][:, 256 + pr * 128:256 + pr * 128 + bt], kbz[:bt, 2 * pr + e, :],
                                        Rm[:bt, 2 * pr + e, :bt], start=(e == 0), stop=(e == 1))
                        self.act("copy", uval[:bt, :], ps[4][:bt, 0:256])
                        self.act("copy", wT[:, :, :bt], ps[4][:, 256:512].r("p (a t) -> p a t", a=2)[:, :, :bt])
                        pq = ps[5][:bt, :].r("p (h s) -> p h s", h=4)[:, :, :bt]
                        for hh in range(4):
                            e, pr = hh % 2, hh // 2
                            self.pe("matmul", pq[:, hh, :], kzb[:, hh, :bt], qkn[:, pr, c0:c0 + bt], start=True, stop=True)
                        self.dve("tensor_tensor", qkT[:bt, :, :bt], pq, decT[:bt, :, :bt], ALU.mult)
                        if self.asub < 5: continue
                        for ci, (r0, r1) in enumerate(chunks):
                            for pr in range(2):
                                self.pe("matmul", ps[6][:bt, pr * 128:(pr + 1) * 128], wT[:, pr, :bt], cx.Sd[l][:, pr, :], start=True, stop=True)
                            for pr in range(2):
                                self.pe("matmul", ps[6][:bt, 256 + pr * 128:256 + (pr + 1) * 128], qkn[:, pr, c0:c0 + bt],
                                        cx.Sd[l][:, pr, :], start=True, stop=True)
                            self.dve("tensor_tensor", unew[r0:r1, :], uval[r0:r1, :], ps[6][r0:r1, 0:256], ALU.subtract)
                            self.dve("tensor_tensor", oq[r0:r1, :].r("p (h d) -> p h d", h=4),
                                     ps[6][r0:r1, 256:512].r("p (h d) -> p h d", h=4),
                                     eg[r0:r1, 0:4].unsq(2).bc([r1 - r0, 4, 64]), ALU.mult)
                            for pr in range(2):
                                for e in range(2):
                                    hh = 2 * pr + e
                                    self.pe("matmul", ps[2][:, 64 + pr * 64:128 + pr * 64], ktz[:bt, ci, hh, :],
                                            unew[:bt, hh * 64:(hh + 1) * 64], start=(e == 0), stop=(e == 1))
                            for e in range(2):
                                sdv = cx.Sd[l][e * 64:(e + 1) * 64, :, e * 64:(e + 1) * 64]
                                self.dve("tensor_tensor", sdv, sdv, egt[e * 64:(e + 1) * 64, ci, :].unsq(2).bc([64, 2, 64]), ALU.mult)
                                self.dve("tensor_tensor", sdv, sdv, ps[2][e * 64:(e + 1) * 64, 64:192].r("p (a d) -> p a d", a=2), ALU.add)
                        if self.asub < 6: continue
                        for hh in range(4):
                            self.pe("matmul", ps[5][:bt, hh * 64:(hh + 1) * 64], qkT[:bt, hh, :bt], unew[:bt, hh * 64:(hh + 1) * 64],
                                    start=True, stop=True)
                        self.dve("tensor_tensor", otm[:bt, :], oq[:bt, :], ps[5][:bt, 0:256], ALU.add)
                        self.dve("tensor_tensor", o2[:bt, :], otm[:bt, :], otm[:bt, :], ALU.mult)
                        self.dve("tensor_reduce", ssq[:bt, :], o2[:bt, :].r("p (h d) -> p h d", h=4), AX.X, ALU.add)
                        self.act("activation", ssq[:bt, :], ssq[:bt, :], AF.Ln, bias=colf(0)[:bt, :], scale=1.0 / 64)
                        self.act("activation", ssq[:bt, :], ssq[:bt, :], AF.Exp, scale=-0.5)
                        self.dve("tensor_tensor", ontm[:bt, blk, :].r("p (h d) -> p h d", h=4),
                                 otm[:bt, :].r("p (h d) -> p h d", h=4), ssq[:bt, :].unsq(2).bc([bt, 4, 64]), ALU.mult)
                    if self.asub < 9:
                        for c in (0, 1):
                            self.dve("memset", mixT[:, c, :], 0.0)
                    for pr in (range(2) if self.asub >= 9 else ()):
                        for blk in range(nbk):
                            bt = min(128, ntok - blk * 128)
                            self.pe("transpose", ps[3][:, blk * 128:blk * 128 + bt], ontm[:bt, blk, pr * 128:(pr + 1) * 128], ident[:bt, :bt])
                        self.dve("scalar_tensor_tensor", mixT[:, pr, :ntok], ps[3][:, :ntok], cols2[:, l, 24:25], zs[:, pr, :ntok],
                                 ALU.mult, ALU.mult)
                    if cx.is_last:
                        for e in range(2):
                            self.dma("sp", V(cx.adelta_out.t[l].rearrange("(pr e) k v -> e k pr v", e=2)[e], ()),
                                     cx.Sd[l][e * 64:(e + 1) * 64, :, e * 64:(e + 1) * 64], final=True)

                if "C" not in self.stages:
                    for c in (6, 7):
                        self.dve("memset", mixT[:, c, :], 0.0)
                else:
                    nbk = (ntok + 127) // 128
                    self.dve("memset", kapz.v(), 0.0)
                    self.dve("memset", cktz.v(), 0.0)
                    self.dve("memset", cbtz.v(), 0.0)
                    c2 = lambda a, b=None: cols2[:, l, a:(a + 1 if b is None else b)]
                    dests = [c_r[:, 0, :ntok], c_r[:, 1, :ntok], c_k[:, 0, :ntok], c_k[:, 1, :ntok],
                             c_v[:, 0, :ntok], c_v[:, 1, :ntok], c_6[:, :ntok], c_7[:, :ntok]]

                    def epi_c(m, pv):
                        cm = crawm[m % 2]
                        self.act("copy", cm[:, 1:1 + ntok], pv)
                        self.act("copy", cm[:, 0:1], cx.ccarry[l][:, m:m + 1])
                        self.act("copy", cx.ccarry[l][:, m:m + 1], cm[:, ntok:ntok + 1])
                        tf = tmpf[m % 2]
                        self.dve("tensor_tensor", tf[:, :ntok], cm[:, 0:ntok], cm[:, 1:1 + ntok], ALU.subtract)
                        self.dve("scalar_tensor_tensor", dests[m], tf[:, :ntok], c2(32 + m), cm[:, 1:1 + ntok], ALU.mult, ALU.add)
                    dense_fm(w_in, 8, O_C, 1024, xb, ntok, epi_c)
                    if cx.is_last:
                        self.dma("sp", V(cx.cshift_out.t[l].rearrange("(c p) -> p c", p=128), ()), cx.ccarry[l].v(), final=True,
                                 allow_slow_non_contiguous=True)
                    self.act("activation", sq[0][:, :ntok], c_6[:, :ntok], AF.Tanh)
                    self.act("copy", sq[1][:, :ntok], c_6[:, :ntok])
                    for m in range(2):
                        self.pe("matmul", ps[0][:, :ntok], cw3[:, l, 0, m * 128:(m + 1) * 128], sq[0][:, :ntok], start=True, stop=True)
                        self.act("activation", c_wl[:, m, :ntok], ps[0][:, :ntok], AF.Sigmoid, bias=c2(40 + m), scale=1.0)
                        self.dve("tensor_scalar_mul", c_wl[:, m, :ntok], c_wl[:, m, :ntok], -math.exp(-0.5))
                        self.pe("matmul", ps[1][:, :ntok], cw3[:, l, 1, m * 128:(m + 1) * 128], sq[1][:, :ntok], start=True, stop=True)
                        self.act("activation", c_a[:, m, :ntok], ps[1][:, :ntok], AF.Sigmoid, bias=c2(42 + m), scale=1.0)
                    self.act("activation", sq[0][:, :ntok], c_7[:, :ntok], AF.Sigmoid)
                    for m in range(2):
                        self.pe("matmul", ps[0][:, :ntok], cw3[:, l, 2, m * 128:(m + 1) * 128], sq[0][:, :ntok], start=True, stop=True)
                        self.act("copy", c_g[:, m, :ntok], ps[0][:, :ntok])
                    for m in range(2):
                        tf = tmpf[m % 2]
                        self.dve("tensor_scalar_mul", c_kk[:, m, :ntok], c_k[:, m, :ntok], c2(44 + m))
                        self.act("activation", tf[:, :ntok], c_kk[:, m, :ntok], AF.Square)
                        self.pe("matmul", ps[2][:, :ntok], cSame, tf[:, :ntok], start=True, stop=True)
                        self.act("activation", tf[:, :ntok], ps[2][:, :ntok], AF.Ln, bias=colf(0), scale=1.0)
                        self.act("activation", tf[:, :ntok], tf[:, :ntok], AF.Exp, scale=-0.5)
                        self.dve("tensor_tensor", c_kk[:, m, :ntok], c_kk[:, m, :ntok], tf[:, :ntok], ALU.mult)
                        self.dve("tensor_scalar", tf[:, :ntok], c_a[:, m, :ntok], c2(46 + m), c2(54 + m), ALU.mult, ALU.add)
                        self.dve("tensor_tensor", c_km[:, m, :ntok], c_k[:, m, :ntok], tf[:, :ntok], ALU.mult)
                        self.dve("tensor_tensor", c_b[:, m, :ntok], c_a[:, m, :ntok], c_kk[:, m, :ntok], ALU.mult)
                        self.dve("scalar_tensor_tensor", tf[:, :ntok], c_r[:, m, :ntok], c2(48 + m), c_km[:, m, :ntok], ALU.mult, ALU.mult)
                        self.pe("matmul", ps[2][:, :ntok], cSame, tf[:, :ntok], start=True, stop=True)
                        self.dve("tensor_tensor", c_bon[:, m, :ntok], ps[2][:, :ntok], c_v[:, m, :ntok], ALU.mult)

                    for blk in range(nbk):
                        bt = min(128, ntok - blk * 128)
                        c0 = blk * 128
                        chunks = [(r0, min(r0 + cx.CH, bt)) for r0 in range(0, bt, cx.CH)]
                        for i, src in enumerate((c_wl, c_km, c_b, c_kk, c_v)):
                            pst = ps[3] if i % 2 == 0 else ps[4]
                            for m in range(2):
                                self.pe("transpose", pst[:bt, m * 128:(m + 1) * 128], src[:, m, c0:c0 + bt], ident)
                            self.act("copy", tm5[:bt, i, :], pst[:bt, 0:256])
                        wl_tm, km_tm, b_tm, kk_tm, v_tm = (tm5[:bt, i, :] for i in range(5))
                        self.pe("matmul", ps[0][:bt, 0:256], cU[:bt, :bt], wl_tm, start=True, stop=True)
                        for m in range(2):
                            self.pe("matmul", ps[1][:, m * 128:m * 128 + bt], tm5[:bt, 0, m * 128:(m + 1) * 128], cU[:bt, :bt],
                                    start=True, stop=True)
                        eng = cgt[:bt, 0, :]
                        egm = cgt[:bt, 1, :]
                        self.act("activation", eng, ps[0][:bt, 0:256], AF.Exp, scale=-1.0)
                        self.act("activation", egm, ps[0][:bt, 0:256], AF.Exp)
                        self.act("activation", cx0[:bt, :], wl_tm, AF.Exp, scale=-1.0)
                        self.dve("tensor_tensor", egm, egm, cx0[:bt, :], ALU.mult)
                        for hh in range(4):
                            e = hh % 2
                            hs = slice(hh * 64, (hh + 1) * 64)
                            self.dve("tensor_tensor", kapz[:bt, hh, e * 64:(e + 1) * 64], kk_tm[:, hs], egm[:, hs], ALU.mult)
                            for ci, (r0, r1) in enumerate(chunks):
                                self.dve("tensor_tensor", cktz[r0:r1, ci, hh, e * 64:(e + 1) * 64], km_tm[r0:r1, hs], eng[r0:r1, hs], ALU.mult)
                                self.dve("tensor_tensor", cbtz[r0:r1, ci, hh, e * 64:(e + 1) * 64], b_tm[r0:r1, hs], eng[r0:r1, hs], ALU.mult)
                        pgT = ps[1][:, 0:256].r("p (m t) -> p m t", m=2)[:, :, :bt]
                        egT = C1T[:, 0:2, :bt]
                        engT = C1T[:, 2:4, :bt]
                        egmT = C2T[:, 0:2, :bt]
                        self.act("activation", egT, pgT, AF.Exp)
                        self.act("activation", engT, pgT, AF.Exp, scale=-1.0)
                        self.dve("tensor_tensor", egmT, pgT, c_wl[:, :, c0:c0 + bt], ALU.subtract)
                        self.act("activation", egmT, egmT, AF.Exp)
                        self.dve("tensor_tensor", rTt[:, :, :bt], c_r[:, :, c0:c0 + bt], egT, ALU.mult)
                        self.dve("tensor_tensor", kapTt[:, :, :bt], c_kk[:, :, c0:c0 + bt], egmT, ALU.mult)
                        for ci, (r0, r1) in enumerate(chunks):
                            self.act("copy", pcc[:, ci, :], egT[:, :, r1 - 1])
                        for hh in range(4):
                            e, pr = hh % 2, hh // 2
                            self.dve("scalar_tensor_tensor", kTm[:, hh, :bt], c_km[:, pr, c0:c0 + bt], cSame[:, e * 64:e * 64 + 1],
                                     engT[:, pr, :], ALU.mult, ALU.mult)
                            self.dve("scalar_tensor_tensor", bTm[:, hh, :bt], c_b[:, pr, c0:c0 + bt], cSame[:, e * 64:e * 64 + 1],
                                     engT[:, pr, :], ALU.mult, ALU.mult)
                        def hview(pt):
                            return pt[:bt, :].r("p (h s) -> p h s", h=4)[:, :, :bt]
                        pB, pBm, pC1, pC2 = hview(ps[0]), hview(ps[3]), hview(ps[4]), hview(ps[5])
                        for hh in range(4):
                            pr = hh // 2
                            self.pe("matmul", pB[:, hh, :], bTm[:, hh, :bt], kapTt[:, pr, :bt], start=True, stop=True)
                        for hh in range(4):
                            pr = hh // 2
                            self.pe("matmul", pBm[:, hh, :], kTm[:, hh, :bt], kapTt[:, pr, :bt], start=True, stop=True)
                        for hh in range(4):
                            pr = hh // 2
                            self.pe("matmul", pC1[:, hh, :], kTm[:, hh, :bt], rTt[:, pr, :bt], start=True, stop=True)
                        for hh in range(4):
                            pr = hh // 2
                            self.pe("matmul", pC2[:, hh, :], bTm[:, hh, :bt], rTt[:, pr, :bt], start=True, stop=True)
                        msu = cSU[:bt, :bt].unsq(1).bc([bt, 4, bt])
                        mu_ = cU[:bt, :bt].unsq(1).bc([bt, 4, bt])
                        X, XT, R = cXb, cXTb, cR2
                        self.dve("tensor_tensor", X[:bt, :, :bt], pB, msu, ALU.mult)
                        self.dve("tensor_tensor", cBmT[:bt, :, :bt], pBm, msu, ALU.mult)
                        self.dve("tensor_tensor", C1T[:bt, :, :bt], pC1, mu_, ALU.mult)
                        self.dve("scalar_tensor_tensor", C2T[:bt, :, :bt], pC2, -1.0, mu_, ALU.mult, ALU.mult)
                        pA = psbT[:bt, 0:512].r("p (h s) -> p h s", h=4)[:, :, :bt]
                        for hh in range(4):
                            self.pe("transpose", pA[:, hh, :], X[:bt, hh, :bt], identb[:bt, :bt])
                        self.act("copy", XT[:bt, :, :bt], pA)
                        self.dve("tensor_tensor", R[:bt, :, :bt], ident[:bt, :bt].unsq(1).bc([bt, 4, bt]), X[:bt, :, :bt], ALU.subtract)
                        neumann(bt, X, XT, R, cx.CH)
                        for hh in range(4):
                            hs = slice(hh * 64, (hh + 1) * 64)
                            self.pe("matmul", ps[4][:bt, hs], cBmT[:bt, hh, :bt], v_tm[:, hs], start=True, stop=True)
                        self.act("copy", cx0b[:bt, :], ps[4][:bt, 0:256])
                        for hh in range(4):
                            hs = slice(hh * 64, (hh + 1) * 64)
                            self.pe("matmul", ps[4][:bt, 256 + hh * 64:256 + (hh + 1) * 64], R[:bt, hh, :bt], cx0b[:bt, hs], start=True, stop=True)
                        self.act("copy", cU0[:bt, :], ps[4][:bt, 256:512])
                        for pr in range(2):
                            for e in range(2):
                                self.pe("matmul", ps[5][:, pr * 128:pr * 128 + bt], kapz[:bt, 2 * pr + e, :], R[:bt, 2 * pr + e, :bt],
                                        start=(e == 0), stop=(e == 1))
                        self.act("copy", wT[:, :, :bt], ps[5][:, 0:256].r("p (a t) -> p a t", a=2)[:, :, :bt])
                        for ci in range(len(chunks)):
                            for pr in range(2):
                                for e in range(2):
                                    hh = 2 * pr + e
                                    self.pe("matmul", ps[5][:, 256 + ci * 128 + pr * 64:256 + ci * 128 + (pr + 1) * 64],
                                            cktz[:bt, ci, hh, :], v_tm[:, hh * 64:(hh + 1) * 64], start=(e == 0), stop=(e == 1))
                        self.act("copy", kvp[:, 0:len(chunks), :, :], ps[5][:, 256:256 + 128 * len(chunks)].r("p (c a d) -> p c a d", c=len(chunks), a=2))
                        for ci, (r0, r1) in enumerate(chunks):
                            for pr in range(2):
                                self.pe("matmul", ps[6][:bt, pr * 128:(pr + 1) * 128], wT[:, pr, :bt], cx.Tst[l][:, pr, :], start=True, stop=True)
                            for pr in range(2):
                                self.pe("matmul", ps[6][:bt, 256 + pr * 128:256 + (pr + 1) * 128], rTt[:, pr, :bt], cx.Tst[l][:, pr, :],
                                        start=True, stop=True)
                            self.dve("tensor_tensor", cU_[r0:r1, :], cU0[r0:r1, :], ps[6][r0:r1, 0:256], ALU.add)
                            self.dve("tensor_copy", cyq[r0:r1, :], ps[6][r0:r1, 256:512])
                            for e in range(2):
                                tdv = cx.Tst[l][e * 64:(e + 1) * 64, :, e * 64:(e + 1) * 64]
                                self.dve("tensor_tensor", tdv, tdv, kvp[e * 64:(e + 1) * 64, ci, :, :], ALU.add)
                            for pr in range(2):
                                for e in range(2):
                                    hh = 2 * pr + e
                                    self.pe("matmul", ps[2][:, 64 + pr * 64:128 + pr * 64], cbtz[:bt, ci, hh, :],
                                            cU_[:bt, hh * 64:(hh + 1) * 64], start=(e == 0), stop=(e == 1))
                            for e in range(2):
                                tdv = cx.Tst[l][e * 64:(e + 1) * 64, :, e * 64:(e + 1) * 64]
                                self.dve("tensor_tensor", tdv, tdv, ps[2][e * 64:(e + 1) * 64, 64:192].r("p (a d) -> p a d", a=2), ALU.subtract)
                                self.dve("tensor_tensor", tdv, tdv, pcc[e * 64:(e + 1) * 64, ci, :].unsq(2).bc([64, 2, 64]), ALU.mult)
                        for hh in range(4):
                            hs = slice(hh * 64, (hh + 1) * 64)
                            self.pe("matmul", ps[4][:bt, hs], C1T[:bt, hh, :bt], v_tm[:, hs], start=True, stop=False)
                            self.pe("matmul", ps[4][:bt, hs], C2T[:bt, hh, :bt], cU_[:bt, hs], start=False, stop=True)
                        self.dve("tensor_tensor", cytm[:bt, :], cyq[:bt, :], ps[4][:bt, 0:256], ALU.add)
                        y3 = cytm[:bt, :].r("p (h d) -> p h d", h=4)
                        self.dve("tensor_reduce", cst[:bt, 0:4], y3, AX.X, ALU.add)
                        self.dve("tensor_scalar_mul", cst[:bt, 0:4], cst[:bt, 0:4], 1.0 / 64)
                        self.dve("tensor_tensor", y3, y3, cst[:bt, 0:4].unsq(2).bc([bt, 4, 64]), ALU.subtract)
                        self.dve("tensor_tensor", cyq[:bt, :], cytm[:bt, :], cytm[:bt, :], ALU.mult)
                        self.dve("tensor_reduce", cst[:bt, 4:8], cyq[:bt, :].r("p (h d) -> p h d", h=4), AX.X, ALU.add)
                        self.act("activation", cst[:bt, 4:8], cst[:bt, 4:8], AF.Ln, bias=colf(9)[:bt, :], scale=1.0 / 64)
                        self.act("activation", cst[:bt, 4:8], cst[:bt, 4:8], AF.Exp, scale=-0.5)
                        self.dve("tensor_tensor", cyn(blk)[:bt, :].r("p (h d) -> p h d", h=4), y3,
                                 cst[:bt, 4:8].unsq(2).bc([bt, 4, 64]), ALU.mult)
                    for pr in range(2):
                        for blk in range(nbk):
                            bt = min(128, ntok - blk * 128)
                            self.pe("transpose", ps[3][:, blk * 128:blk * 128 + bt], cyn(blk)[:bt, pr * 128:(pr + 1) * 128], ident[:bt, :bt])
                        tf = tmpf[pr]
                        self.dve("scalar_tensor_tensor", tf[:, :ntok], ps[3][:, :ntok], c2(50 + pr), c_bon[:, pr, :ntok], ALU.mult, ALU.add)
                        self.dve("scalar_tensor_tensor", mixT[:, 6 + pr, :ntok], tf[:, :ntok], c2(52 + pr), c_g[:, pr, :ntok], ALU.add, ALU.mult)
                    if cx.is_last:
                        for pr in range(2):
                            self.pe("transpose", ps[3][:, pr * 128:(pr + 1) * 128], cx.Tst[l][:, pr, :], ident)
                        self.act("copy", wT.v(), ps[3][:, 0:256].r("p (a t) -> p a t", a=2))
                        for e in range(2):
                            self.dma("sp", V(cx.cwkv_out.t[l].rearrange("(pr e) v k -> e v pr k", e=2)[e], ()),
                                     wT[e * 64:(e + 1) * 64, :, e * 64:(e + 1) * 64], final=True)

                def epi_res(m, pv):
                    self.dve("tensor_tensor", h[:, m, :ntok], h[:, m, :ntok], pv, ALU.add)
                dense_fm(("w_out", l), 8, 0, D_MODEL, mixT, ntok, epi_res)

                rmsnorm_to_xb(ntok, lambda k: col("norm_ffn", l, k))

                def epi_ff1(m, pv):
                    tf = tmpf[m % 2]
                    self.act("activation", tf[:, :ntok], pv, AF.Relu)
                    self.dve("tensor_tensor", uT[:, m, :ntok], tf[:, :ntok], tf[:, :ntok], ALU.mult)
                dense_fm(("w_ff1", l), 8, 0, D_FF, xb, ntok, epi_ff1)
                dense_fm(("w_ff2", l), 32, 0, D_MODEL, uT, ntok, epi_res)

                rmsnorm_to_xb(ntok, lambda k: col("norm_ple", l, k))

                def epi_gate(m, pv):
                    self.act("activation", gate[:, m, :ntok], pv, AF.Sigmoid)
                dense_fm(("w_ple_gate", l), 8, 0, D_MODEL, xb, ntok, epi_gate)
                self.dma("sp", ptok[:bl, :nbk, :], V(cx.p_src(l).rearrange("(b p) d -> p b d", p=bl), ()))
                for k in range(2):
                    for b in range(nbk):
                        self.pe("transpose", ps[3][:, b * 128:b * 128 + bl], ptok[:bl, b, k * 128:(k + 1) * 128], ident[:bl, :bl])
                    self.act("copy", pT[:, k, :ntok], ps[3][:, :ntok])

                def epi_ple(m, pv):
                    self.dve("tensor_tensor", tmpf[m % 2][:, :ntok], gate[:, m, :ntok], pv, ALU.mult)
                    self.dve("tensor_tensor", h[:, m, :ntok], h[:, m, :ntok], tmpf[m % 2][:, :ntok], ALU.add)
                dense_fm(("w_ple_proj", l), 2, 0, D_MODEL, pT, ntok, epi_ple)

            for k in range(8):
                s_ = sq[k % 2]
                self.act("activation", s_[:, :ntok], h[:, k, :ntok], AF.Square)
                self.pe("matmul", ps[2][:, :ntok], onesb.v(), s_[:, :ntok], start=(k == 0), stop=(k == 7))
            self.act("activation", rstd[:, :ntok], ps[2][:, :ntok], AF.Ln, bias=colf(0), scale=1.0 / D_MODEL)
            self.act("activation", rstd[:, :ntok], rstd[:, :ntok], AF.Exp, scale=-0.5)
            for k in range(8):
                tf = tmpf[k % 2]
                self.dve("scalar_tensor_tensor", tf[:, :ntok], h[:, k, :ntok], cols[:, 51 + k:52 + k], rstd[:, :ntok], ALU.mult, ALU.mult)
                for b in range(nbk):
                    self.pe("transpose", ps[3][:bl, b * 128:(b + 1) * 128], tf[:, b * 128:b * 128 + bl], ident)
                self.act("copy", xtok[:bl, :nbk, k * 128:(k + 1) * 128], ps[3][:bl, :nbk * 128].r("p (b f) -> p b f", b=nbk))
            self.dma("sp", V(cx.y_dst.rearrange("(b p) d -> p b d", p=bl), ()), xtok[:bl, :nbk, :], final=True)

        class Cx:
            pass

        if "prompt" in self.stages:
            for t in range(NT):
                cx = Cx()
                t0 = t * TT
                cx.ntok = TT; cx.CH = min(64, SEQ); cx.key_base = t0; cx.masked = True; cx.is_last = (t == NT - 1)
                cx.x_src = x_in.t[t0:t0 + TT, :]
                cx.rope_src = rope_in.t[t0:t0 + TT, :]
                cx.p_src = lambda l, t0=t0: p_in.t[l, t0:t0 + TT, :]
                cx.y_dst = y_out.t[t0:t0 + TT, :]
                cx.bk_dst = lambda l, t0=t0: bk_out.t[l, t0:t0 + TT, :]
                cx.bv_dst = lambda l, t0=t0: bv_out.t[l, t0:t0 + TT, :]
                cx.kT_scr, cx.v_scr = kT_scr, v_scr
                cx.acarry, cx.Sd, cx.Tst, cx.ccarry = acarry, Sd, Tst, ccarry
                cx.aconv_out, cx.adelta_out, cx.cshift_out, cx.cwkv_out = aconv_out, adelta_out, cshift_out, cwkv_out
                tile_body(cx)

        if "sample" in self.stages:
            DEC, PAST = self.DEC, self.PAST
            xs_in = self.din("xs", [DEC, D_MODEL])
            ps_in = self.din("psm", [DEPTH, DEC, PLE_DIM])
            ropes_in = self.din("rope_s", [DEC, 64])
            ck_in = self.din("cache_k", [DEPTH, PAST, 512])
            cv_in = self.din("cache_v", [DEPTH, PAST, 512])
            sconv_in = self.din("st_conv", [DEPTH, 3, 768])
            sdelta_in = self.din("st_delta", [DEPTH, 4, 64, 64])
            sshift_in = self.din("st_shift", [DEPTH, 1024])
            swkv_in = self.din("st_wkv", [DEPTH, 4, 64, 64])
            ys_out = self.dout("y_s", [DEC, D_MODEL])
            bks_out = self.dout("b_k_s", [DEPTH, DEC, 512])
            bvs_out = self.dout("b_v_s", [DEPTH, DEC, 512])
            aconvs_out = self.dout("a_conv_s", [DEPTH, 3, 768])
            adeltas_out = self.dout("a_delta_s", [DEPTH, 4, 64, 64])
            cshifts_out = self.dout("c_shift_s", [DEPTH, 1024])
            cwkvs_out = self.dout("c_wkv_s", [DEPTH, 4, 64, 64])
            kT_scr_s = [self.dscr(f"kT_scr_s{l}", [512, PAST + DEC], BF16) for l in range(DEPTH)]
            v_scr_s = [self.dscr(f"v_scr_s{l}", [PAST + DEC, 512], BF16) for l in range(DEPTH)]
            psb = psbT.v()
            for l in range(DEPTH):
                for m in range(6):
                    self.dma("sp", acarry_s[l][:, m, :], V(sconv_in.t[l, :, m * 128:(m + 1) * 128].rearrange("j p -> p j"), ()),
                             allow_slow_non_contiguous=True)
                self.dma("sp", ccarry_s[l].v(), V(sshift_in.t[l].rearrange("(c p) -> p c", p=128), ()), allow_slow_non_contiguous=True)
                self.dve("memset", Sd_s[l].v(), 0.0)
                self.dve("memset", wT.v(), 0.0)
                for e in range(2):
                    self.dma("sp", Sd_s[l][e * 64:(e + 1) * 64, :, e * 64:(e + 1) * 64],
                             V(sdelta_in.t[l].rearrange("(pr e) k v -> e k pr v", e=2)[e], ()))
                    self.dma("sp", wT[e * 64:(e + 1) * 64, :, e * 64:(e + 1) * 64],
                             V(swkv_in.t[l].rearrange("(pr e) v k -> e v pr k", e=2)[e], ()))
                for pr in range(2):
                    self.pe("transpose", ps[3][:, pr * 128:(pr + 1) * 128], wT[:, pr, :], ident)
                self.act("copy", Tst_s[l].v(), ps[3][:, 0:256].r("p (a t) -> p a t", a=2))
                for g in range(PAST // 512):
                    for b in range(4):
                        r0 = g * 512 + b * 128
                        kf, vf = kfs[b % 2], vfs[b % 2]
                        self.dma("sp", kf.v(), V(ck_in.t[l, r0:r0 + 128, :], ()))
                        self.act("copy", ktm[:, b, :], kf.v())
                        self.dma("sp", vf.v(), V(cv_in.t[l, r0:r0 + 128, :], ()))
                        self.dve("tensor_copy", vtb[:, b, :], vf.v())
                    for c in range(4):
                        for b in range(4):
                            self.pe("transpose", psb[:, 512 + b * 128:512 + (b + 1) * 128], ktm[:, b, c * 128:(c + 1) * 128], identb.v())
                        self.dve("tensor_copy", kTt[:, c, :], psb[:, 512:1024])
                    self.dma("sp", V(kT_scr_s[l].t[:, g * 512:(g + 1) * 512].rearrange("(c p) s -> p c s", p=128), (kT_scr_s[l].buf,)), kTt.v())
                    self.dma("sp", V(v_scr_s[l].t[g * 512:(g + 1) * 512, :].rearrange("(b p) d -> p b d", p=128), (v_scr_s[l].buf,)), vtb.v())
            cx = Cx()
            cx.ntok = DEC; cx.CH = min(64, DEC); cx.key_base = PAST; cx.masked = False; cx.is_last = True
            cx.x_src = xs_in.t[:, :]
            cx.rope_src = ropes_in.t[:, :]
            cx.p_src = lambda l: ps_in.t[l, :, :]
            cx.y_dst = ys_out.t[:, :]
            cx.bk_dst = lambda l: bks_out.t[l, :, :]
            cx.bv_dst = lambda l: bvs_out.t[l, :, :]
            cx.kT_scr, cx.v_scr = kT_scr_s, v_scr_s
            cx.acarry, cx.Sd, cx.Tst, cx.ccarry = acarry_s, Sd_s, Tst_s, ccarry_s
            cx.aconv_out, cx.adelta_out, cx.cshift_out, cx.cwkv_out = aconvs_out, adeltas_out, cshifts_out, cwkvs_out
            tile_body(cx)

        self.S.emit(st)
        st.close()
        return nc


def _consts():
    c = np.zeros((128, 11, 128), np.float32)
    i = np.arange(128)
    same = (i[:, None] // 64) == (i[None, :] // 64)
    c[:, 0, :] = np.eye(128)
    c[:, 1, :] = 1.0
    c[:, 2, :] = (i[:, None] <= i[None, :]) & same
    c[:, 3, :] = (i[:, None] < i[None, :]) & same
    c[:, 4, :] = (i[:, None] > i[None, :]) & same
    c[:, 5, :] = (i[:, None] >= i[None, :]) & same
    c[:, 6, :] = same
    for ci in range(2):
        for e in range(2):
            c[:, 7 + 2 * ci + e, :] = ((i[:, None] // 64) == ci) & ((i[None, :] // 64) == e)
    return c


def _cols2(inp):
    c = np.zeros((128, DEPTH, 64), np.float32)
    for l in range(DEPTH):
        cw = inp["a_conv_w"][l]
        for m in range(6):
            c[:, l, m * 4:m * 4 + 4] = cw[:, m * 128:(m + 1) * 128].T
        c[:, l, 24] = np.tile(inp["a_norm"][l], 2)
        c[:, l, 32:40] = inp["c_mu"][l].reshape(8, 128).T
        for nm, base in (("c_w0", 40), ("c_a0", 42), ("c_k_k", 44), ("c_k_a", 46), ("c_r_k", 48), ("c_ln_w", 50), ("c_ln_b", 52)):
            c[:, l, base:base + 2] = inp[nm][l].reshape(2, 128).T
    return c


def _cw3(inp):
    w = np.zeros((128, DEPTH, 3, 256), np.float32)
    for l in range(DEPTH):
        w[0:64, l, 0, :] = inp["c_w_up"][l]
        w[64:128, l, 1, :] = inp["c_a_up"][l]
        w[:, l, 2, :] = inp["c_g_up"][l]
    return w


def _rowp(inp):
    return np.concatenate([inp["a_A_log"], inp["a_dt_bias"]], axis=1).astype(np.float32)


def _amask():
    kp = np.arange(128)[:, None]
    q = np.arange(512)[None, :]
    m = np.zeros((128, 4, 512), np.float32)
    for j in range(4):
        m[:, j, :] = (2 * j + kp // 64) <= (q // 64)
    return m.astype(ml_dtypes.bfloat16)


def _rope_table(pos):
    half = 32
    inv = (10000.0 ** (-2.0 * np.arange(half, dtype=np.float32) / 64)).astype(np.float32)
    ang = pos.astype(np.float32)[:, None] * inv[None, :]
    return np.concatenate([np.cos(ang), np.sin(ang)], axis=1).astype(np.float32)


def _cols(inp):
    c = np.zeros((128, 64), np.float32)
    for l in range(DEPTH):
        for nm, base in (("norm_mix", 0), ("norm_ffn", 8), ("norm_ple", 16)):
            c[:, l * 24 + base:l * 24 + base + 8] = inp[nm][l].reshape(8, 128).T
        lam_init = 0.8 - 0.6 * math.exp(-0.3 * l)
        c[:, 48 + l] = inp["b_norm"][l]
    c[:, 50] = NORM_EPS
    c[:, 59] = C_LN_EPS
    c[:, 51:59] = inp["norm_final"].reshape(8, 128).T
    return c


_CACHE = {}


def run(inputs, seq, n_cores, stages=("prompt", "A", "C", "sample"), trace=False):
    key = (seq, stages)
    if key not in _CACHE:
        b = Builder(seq, stages=stages)
        b.build()
        _CACHE[key] = b
    b = _CACHE[key]
    cols = _cols(inputs)
    consts = _consts()
    amask = _amask()
    rope = _rope_table(np.arange(seq))
    lamrow = np.stack([np.stack([inputs[n][l] for n in ("b_lam_q1", "b_lam_k1", "b_lam_q2", "b_lam_k2")])
                       for l in range(DEPTH)]).astype(np.float32)
    shared = {"cols": cols, "consts": consts, "amask": amask, "rope": rope, "lamrow": lamrow,
              "cols2": _cols2(inputs), "rowp": _rowp(inputs), "cw3": _cw3(inputs)}
    for nm in ("w_in", "w_out", "w_ff1", "w_ff2", "w_ple_gate", "w_ple_proj"):
        shared[nm] = inputs[nm]
    if "sample" in stages:
        past = inputs["cache_b_k"].shape[2]
        dec = inputs["x_sample"].shape[1]
        shared["rope_s"] = _rope_table(np.arange(past, past + dec))
    in_maps = []
    for c in range(n_cores):
        m = dict(shared)
        m["x"] = np.ascontiguousarray(inputs["x_prompt"][c])
        m["p"] = np.ascontiguousarray(inputs["p_prompt"][:, c])
        if "sample" in stages:
            m["xs"] = np.ascontiguousarray(inputs["x_sample"][c])
            m["psm"] = np.ascontiguousarray(inputs["p_sample"][:, c])
            m["cache_k"] = np.ascontiguousarray(inputs["cache_b_k"][:, c]).reshape(DEPTH, past, 512)
            m["cache_v"] = np.ascontiguousarray(inputs["cache_b_v"][:, c]).reshape(DEPTH, past, 512)
            m["st_conv"] = np.ascontiguousarray(inputs["state_a_conv"][:, c])
            m["st_delta"] = np.ascontiguousarray(inputs["state_a_delta"][:, c])
            m["st_shift"] = np.ascontiguousarray(inputs["state_c_shift"][:, c])
            m["st_wkv"] = np.ascontiguousarray(inputs["state_c_wkv"][:, c])
        in_maps.append(m)
    res = run_bass_kernel_spmd(b.nc, in_maps, core_ids=list(range(n_cores)), trace=trace)
    return res


def kernel(**inputs):
    inputs = {k: np.asarray(v) for k, v in inputs.items()}
    n, seq = inputs["x_prompt"].shape[0], inputs["x_prompt"].shape[1]
    dec = inputs["x_sample"].shape[1]
    r = run(inputs, seq, n).results

    def st(name, axis):
        return np.stack([np.asarray(r[c][name]) for c in range(n)], axis=axis)
    return (st("y", 0), st("y_s", 0),
            st("a_conv", 1), st("a_delta", 1),
            st("b_k", 1).reshape(DEPTH, n, seq, 4, 128), st("b_v", 1).reshape(DEPTH, n, seq, 4, 128),
            st("c_shift", 1), st("c_wkv", 1),
            st("a_conv_s", 1), st("a_delta_s", 1),
            st("b_k_s", 1).reshape(DEPTH, n, dec, 4, 128), st("b_v_s", 1).reshape(DEPTH, n, dec, 4, 128),
            st("c_shift_s", 1), st("c_wkv_s", 1))
```

```python
import math
from contextlib import ExitStack

import numpy as np
import ml_dtypes
import concourse.bass as bass
import concourse.mybir as mybir
from concourse.bass_utils import run_bass_kernel_spmd

F32 = mybir.dt.float32
BF16 = mybir.dt.bfloat16
AF = mybir.ActivationFunctionType
ALU = mybir.AluOpType
AX = mybir.AxisListType

D_MODEL = 1024
DEPTH = 2
PLE_DIM = 256
D_FF = 4096
IN_WIDTH = 3592
NORM_EPS = 1e-6
L2_EPS = 1e-6
C_LN_EPS = 64e-5
O_AQKV, O_AZ, O_AA, O_AB, O_BQ, O_BK, O_BV, O_C = 0, 768, 1024, 1028, 1032, 1544, 2056, 2568

ENGS = ("pe", "act", "dve", "pool", "sp")
SEM_WRAP = 30000


class Buf:
    __slots__ = ("name", "writer", "readers", "chan", "multi")

    def __init__(self, name, multi=False):
        self.name = name
        self.writer = [] if multi else None
        self.readers = []
        self.chan = None
        self.multi = multi


class Chan:
    __slots__ = ("sem", "count", "name")

    def __init__(self, name):
        self.name = name
        self.sem = None
        self.count = 0


class Op:
    __slots__ = ("eng", "fn", "deps", "signal", "semidx", "semval", "is_dma", "chan", "chan_val")

    def __init__(self, eng, fn):
        self.eng = eng
        self.fn = fn
        self.deps = []
        self.signal = False
        self.semidx = 0
        self.semval = 0
        self.is_dma = False
        self.chan = None
        self.chan_val = 0


class Sched:
    def __init__(self, nc):
        self.nc = nc
        self.ops = {e: [] for e in ENGS}
        self.chans = []
        self.final_waits = []

    def _collect(self, op, reads, writes, waits=()):
        deps = []
        for b in waits:
            if b.multi:
                deps.extend(b.writer)
            elif b.writer is not None:
                deps.append(b.writer)
            deps.extend(b.readers)
        for b in reads:
            if b.multi:
                deps.extend(b.writer)
            elif b.writer is not None:
                deps.append(b.writer)
        for b in writes:
            if b.multi:
                deps.extend(b.writer)
            elif b.writer is not None:
                deps.append(b.writer)
            deps.extend(b.readers)
        seen = set()
        for d in deps:
            if d is op or id(d) in seen:
                continue
            seen.add(id(d))
            if d.eng == "pe" and op.eng == "pe" and not d.is_dma and not op.is_dma:
                continue
            op.deps.append(d)
        for b in writes:
            if b.multi:
                b.writer.append(op)
            else:
                b.writer = op
                b.readers = []
        for b in reads:
            b.readers.append(op)

    def op(self, eng, fn, reads=(), writes=(), waits=()):
        o = Op(eng, fn)
        self.ops[eng].append(o)
        self._collect(o, reads, writes, waits)
        return o

    def dma(self, eng, out, in_, reads=(), writes=(), chan_buf=None, final=False, waits=(), **kw):
        if chan_buf is None:
            chan_buf = (list(writes) + list(reads))[0]
        if chan_buf.chan is None:
            chan_buf.chan = {}
        if eng not in chan_buf.chan:
            chan_buf.chan[eng] = Chan(chan_buf.name + "_" + eng)
            self.chans.append(chan_buf.chan[eng])
        ch = chan_buf.chan[eng]
        o = Op(eng, None)
        o.is_dma = True
        o.chan = ch
        ch.count += 1
        o.chan_val = 16 * ch.count
        o.fn = lambda e, out=out, in_=in_, kw=kw: e.dma_start(out=out, in_=in_, **kw)
        self.ops[eng].append(o)
        self._collect(o, reads, writes, waits)
        if final:
            self.final_waits.append(o)
        return o

    def emit(self, stack):
        nc = self.nc
        for e in ENGS:
            for o in self.ops[e]:
                for d in o.deps:
                    if not d.is_dma:
                        d.signal = True
        nsem = {}
        for e in ENGS:
            cnt = 0
            for o in self.ops[e]:
                if o.signal and not o.is_dma:
                    o.semidx = cnt // SEM_WRAP
                    o.semval = cnt % SEM_WRAP + 1
                    cnt += 1
            nsem[e] = cnt // SEM_WRAP + 1
        esems = {e: [stack.enter_context(nc.semaphore(f"s_{e}{i}")) for i in range(nsem[e])] for e in ENGS}
        for i, ch in enumerate(self.chans):
            ch.sem = stack.enter_context(nc.semaphore(f"c{i}_{ch.name}"))
        block = stack.enter_context(nc.Block())

        def run(e, eng):
            seen = {}
            maxidx = {}
            for o in self.ops[e]:
                need = {}
                for d in o.deps:
                    if d.is_dma:
                        key = ("c", id(d.chan)); sem = d.chan.sem; val = d.chan_val
                    else:
                        key = (d.eng, d.semidx); sem = esems[d.eng][d.semidx]; val = d.semval
                    if key not in need or need[key][1] < val:
                        need[key] = (sem, val)
                for key, (sem, val) in need.items():
                    if key[0] != "c":
                        if maxidx.get(key[0], -1) > key[1]:
                            continue
                    if seen.get(key, 0) >= val:
                        continue
                    seen[key] = val
                    if key[0] != "c":
                        maxidx[key[0]] = max(maxidx.get(key[0], -1), key[1])
                    eng.wait_ge(sem, val)
                ins = o.fn(eng)
                if o.is_dma:
                    ins.then_inc(o.chan.sem, 16)
                elif o.signal:
                    ins.then_inc(esems[e][o.semidx], 1)
            if e == "sp":
                fin = {}
                for o in self.final_waits:
                    fin[id(o.chan)] = (o.chan, max(o.chan_val, fin.get(id(o.chan), (None, 0))[1]))
                for ch, val in fin.values():
                    eng.wait_ge(ch.sem, val)

        @block.tensor
        def _(eng):
            run("pe", eng)

        @block.scalar
        def _(eng):
            run("act", eng)

        @block.vector
        def _(eng):
            run("dve", eng)

        @block.gpsimd
        def _(eng):
            run("pool", eng)

        @block.sync
        def _(eng):
            run("sp", eng)


class V:
    __slots__ = ("ap", "bufs", "wb")

    def __init__(self, ap, bufs, wb=()):
        self.ap = ap
        self.bufs = bufs
        self.wb = wb

    def __getitem__(self, idx):
        return V(self.ap[idx], self.bufs, self.wb)

    def r(self, s, **kw):
        return V(self.ap.rearrange(s, **kw), self.bufs, self.wb)

    def bc(self, shape):
        return V(self.ap.to_broadcast(shape), self.bufs, self.wb)

    def unsq(self, ax):
        return V(self.ap.unsqueeze(ax), self.bufs, self.wb)


class T:
    def __init__(self, t, name, track=True, multi=False, bufs=None):
        self.t = t
        self.name = name
        self.buf = Buf(name, multi=multi) if track else None
        self.bufs = bufs
        self.wbufs = None
        self.extra = ()
        self.hb = None

    def __getitem__(self, idx):
        if self.wbufs is not None:
            own = (self.buf,) if self.buf is not None else tuple(self.bufs)
            return V(self.t[idx], own + tuple(self.extra), tuple(self.wbufs()))
        if self.bufs is not None:
            b = self.bufs() if callable(self.bufs) else self.bufs
            return V(self.t[idx], tuple(b) + tuple(self.extra))
        return V(self.t[idx], ((self.buf,) if self.buf is not None else ()) + tuple(self.extra))

    def v(self):
        return self[:]


class Builder:
    def __init__(self, seq, dec_seq=16, past=2048, stages=("prompt", "A", "C", "sample")):
        self.SEQ = seq
        self.DEC = dec_seq
        self.PAST = past
        self.TT = min(512, seq)
        self.CH = min(64, seq)
        self.stages = stages
        self.nc = bass.Bass("TRN2", target_bir_lowering=False)
        self.S = Sched(self.nc)
        self.st = ExitStack()
        self.in_names = []
        self.out_names = []
        import os
        self.bsub = int(os.environ.get('BSUB', '9'))
        self.poolsum = os.environ.get('POOLSUM', '0') == '1'
        self.wcache = os.environ.get('WCACHE', '1') == '1'
        self.asub = int(os.environ.get('ASUB', '9'))
        self.a1 = int(os.environ.get('A1', '9'))
        self.a3 = int(os.environ.get('A3', '9'))

    def sb(self, name, shape, dt=F32):
        import os
        if os.environ.get("ALLOCDBG"):
            print("ALLOC", name, shape, dt, int(np.prod(shape[1:])) * (4 if dt == F32 else 2))
        return T(self.st.enter_context(self.nc.sbuf_tensor("s_" + name, list(shape), dt)), name)

    def arena(self, name, nfloats):
        t = self.st.enter_context(self.nc.sbuf_tensor("s_" + name, [128, nfloats], F32))
        return {"t": t, "off": 0, "bufs": [], "n": nfloats, "parts": {}}

    def _arena_ap(self, ar, off, shape, dt):
        n = int(np.prod(shape[1:]))
        nfl = n if dt == F32 else n // 2
        assert off + nfl <= ar["n"], (off, nfl, ar["n"])
        ap = ar["t"][:, off:off + nfl]
        if dt != F32:
            ap = ap.bitcast(dt)
        if len(shape) == 3:
            ap = ap.rearrange("p (a b) -> p a b", a=shape[1])
        elif len(shape) == 4:
            ap = ap.rearrange("p (a b c) -> p a b c", a=shape[1], b=shape[2])
        return ap, nfl

    def sub(self, ar, name, shape, dt=F32, part=None):
        if part is None:
            ap, nfl = self._arena_ap(ar, ar["off"], shape, dt)
            ar["off"] += nfl
            tt = T(ap, name)
            ar["bufs"].append(tt.buf)
            return tt
        pd = ar["parts"].setdefault(part, {"off": 0, "bufs": []})
        ap, nfl = self._arena_ap(ar, pd["off"], shape, dt)
        pd["off"] += nfl
        tt = T(ap, name)
        pd["bufs"].append(tt.buf)
        tt.wbufs = lambda: [b for p, d in ar["parts"].items() if p != part for b in d["bufs"]]
        return tt

    def whole(self, ar, name, shape, dt=F32):
        ap, _ = self._arena_ap(ar, 0, shape, dt)
        return T(ap, name, track=False, bufs=ar["bufs"])

    def psum(self, name, shape, dt=F32):
        return T(self.st.enter_context(self.nc.psum_tensor("p_" + name, list(shape), dt)), name)

    def din(self, name, shape, dt=F32):
        self.in_names.append(name)
        return T(self.nc.dram_tensor(name, list(shape), dt, kind="ExternalInput").ap(), name, track=False)

    def dout(self, name, shape, dt=F32):
        self.out_names.append(name)
        return T(self.nc.dram_tensor(name, list(shape), dt, kind="ExternalOutput").ap(), name, track=False)

    def dscr(self, name, shape, dt):
        return T(self.nc.dram_tensor(name, list(shape), dt).ap(), name, multi=True)

    def _op(self, eng, meth, *args, _r=(), _w=(), **kw):
        reads, writes = list(_r), list(_w)
        waits = []
        cargs = []
        for i, a in enumerate(args):
            if isinstance(a, V):
                (writes if i == 0 else reads).extend(a.bufs)
                waits.extend(a.wb)
                cargs.append(a.ap)
            else:
                cargs.append(a)
        ckw = {}
        for k, a in kw.items():
            if isinstance(a, V):
                (writes if k in ("out", "accum_out") else reads).extend(a.bufs)
                waits.extend(a.wb)
                ckw[k] = a.ap
            else:
                ckw[k] = a
        return self.S.op(eng, lambda e: getattr(e, meth)(*cargs, **ckw), reads=reads, writes=writes, waits=waits)

    def pe(self, meth, *a, **k):
        return self._op("pe", meth, *a, **k)

    def act(self, meth, *a, **k):
        return self._op("act", meth, *a, **k)

    def dve(self, meth, *a, **k):
        return self._op("dve", meth, *a, **k)

    def dma(self, q, out, in_, final=False, **kw):
        import os
        if os.environ.get("NOSCR") and any(b.multi for b in list(out.bufs) + list(in_.bufs)):
            return
        if os.environ.get("NOOUT") and final and "b_" in str(out.ap):
            return
        reads = list(in_.bufs)
        writes = list(out.bufs)
        cands = [b for b in writes + reads if not b.multi]
        return self.S.dma(q, out.ap, in_.ap, reads=reads, writes=writes, chan_buf=cands[0], final=final,
                          waits=list(out.wb) + list(in_.wb), **kw)

    def build(self):
        nc = self.nc
        SEQ, TT = self.SEQ, self.TT
        NT = SEQ // TT
        NB = TT // 128
        st = self.st

        x_in = self.din("x", [SEQ, D_MODEL])
        p_in = self.din("p", [DEPTH, SEQ, PLE_DIM])
        W = {}
        for nm, shp in [("w_in", [DEPTH, D_MODEL, IN_WIDTH]), ("w_out", [DEPTH, D_MODEL, D_MODEL]),
                        ("w_ff1", [DEPTH, D_MODEL, D_FF]), ("w_ff2", [DEPTH, D_FF, D_MODEL]),
                        ("w_ple_gate", [DEPTH, D_MODEL, D_MODEL]), ("w_ple_proj", [DEPTH, PLE_DIM, D_MODEL])]:
            W[nm] = self.din(nm, shp)
        cols_in = self.din("cols", [128, 64])
        NCONST = 11
        consts_in = self.din("consts", [128, NCONST, 128])
        cols2_in = self.din("cols2", [128, DEPTH, 64])
        cw3_in = self.din("cw3", [128, DEPTH, 3, 256])
        cshift_out = self.dout("c_shift", [DEPTH, 1024])
        cwkv_out = self.dout("c_wkv", [DEPTH, 4, 64, 64])
        rowp_in = self.din("rowp", [DEPTH, 8])
        aconv_out = self.dout("a_conv", [DEPTH, 3, 768])
        adelta_out = self.dout("a_delta", [DEPTH, 4, 64, 64])
        rope_in = self.din("rope", [SEQ, 64])
        amask_in = self.din("amask", [128, 4, 512], BF16)
        lam_in = self.din("lamrow", [DEPTH, 4, 64])
        y_out = self.dout("y", [SEQ, D_MODEL])
        bk_out = self.dout("b_k", [DEPTH, SEQ, 512])
        bv_out = self.dout("b_v", [DEPTH, SEQ, 512])
        kT_scr = [self.dscr(f"kT_scr{l}", [512, SEQ], BF16) for l in range(DEPTH)]
        v_scr = [self.dscr(f"v_scr{l}", [SEQ, 512], BF16) for l in range(DEPTH)]

        consts = self.sb("consts", [128, NCONST, 128])
        cols2 = self.sb("cols2", [128, DEPTH, 64])
        rowp = self.sb("rowp", [128, DEPTH, 8])
        cU, cSU, cL, cLI, cSame = (consts[:, i, :] for i in (2, 3, 4, 5, 6))
        arA = self.arena("arA", 8192); arB = self.arena("arB", 4096); arC = self.arena("arC", 4096)
        aq = self.sub(arA, "aq", [128, 6, 3 + TT], part="A")
        acarry = [self.sb(f"acarry{l}", [128, 6, 3]) for l in range(DEPTH)]
        Sd = [self.sb(f"Sd{l}", [128, 2, 128]) for l in range(DEPTH)]
        ac = self.sub(arA, "ac", [128, 6, TT], part="A")
        qkn = self.sub(arB, "qkn", [128, 4, TT], part="A")
        zs = self.sub(arB, "zs", [128, 2, TT], part="A")
        abtm = self.sb("abtm", [128, NB, 8])
        astep = self.sb("astep", [128, NB, 4])
        beta = self.sb("beta", [128, NB, 4])
        arD = self.arena("arD", 4096)
        kvtm = self.sub(arD, "kvtm", [128, NB, 512], part="A")
        ontm = self.sub(arD, "ontm", [128, NB, 256], part="A")
        aL = self.sub(arD, "aL", [128, 4, 128], part="A"); aU = self.sub(arD, "aU", [128, 4, 128], part="A")
        decA = self.sub(arB, "decA", [128, 4, 128], part="A"); decT = self.sub(arB, "decT", [128, 4, 128], part="A")
        eg = self.sb("eg", [128, 16]); bkg = self.sb("bkg", [128, 4]); egt = self.sb("egt", [128, 2, 2])
        Am = self.sub(arA, "Am", [128, 4, 128], part="A")
        Xa = [self.sb("Xa0", [128, 4, 128])] * 2
        XTa = [self.sb("XTa0", [128, 4, 128])] * 2
        Rm = self.sub(arA, "Rm", [128, 4, 128], part="A")
        vb_ = self.sb("vb_", [128, 4, 64]); kbz = self.sb("kbz", [128, 4, 128]); ktz = self.sb("ktz", [128, 2, 4, 128]); kzb = self.sb("kzb", [128, 4, 128])
        unew = self.sb("unew", [128, 256]); wT = self.sb("wT", [128, 2, 128])
        qkT = self.sub(arA, "qkT", [128, 4, 128], part="A"); ssq = self.sb("ssq", [128, 4])
        ident = consts[:, 0, :]
        identb = self.sb("identb", [128, 128], BF16)
        onesb = self.sb("onesb", [128, 128], BF16)
        cols = self.sb("cols", [128, 64])
        amask = self.sb("amask", [128, 4, 512], BF16)
        lamt = self.sb("lamt", [128, 8])
        h = self.sb("h", [128, 8, TT])
        xb = self.sb("xb", [128, 8, TT], BF16)
        rstd = self.sb("rstd", [128, TT])
        sq = [self.sb(f"sq{i}", [128, TT], BF16) for i in range(2)]
        NSLOT = 3
        ring = [self.sb(f"wr{i}", [128, 4096], BF16) for i in range(NSLOT)]
        self.ring_i = 0
        mixT = self.sb("mixT", [128, 8, TT], BF16)
        pT = self.sb("pT", [128, 2, TT], BF16)
        tmpf = [self.sb(f"tmpf{i}", [128, TT]) for i in range(2)]
        ropet = self.sb("ropet", [128, NB, 64])
        kfs = [self.sb(f"kfs{i}", [128, 512]) for i in range(2)]
        vfs = [self.sb(f"vfs{i}", [128, 512]) for i in range(2)]
        qT = self.sb("qT", [128, 4, 2, TT], BF16)

        ptok = self.sub(arD, "ptok", [128, NB, PLE_DIM], part="P")
        kblk = [self.sub(arD, f"kblk{i}", [128, 512], BF16, part="B") for i in range(2)]
        vblk = [self.sub(arD, f"vblk{i}", [128, 4, 128], BF16, part="B") for i in range(2)]
        PT = [self.sub(arD, f"PT{i}", [128, TT], BF16, part="B") for i in range(4)]
        osb = [self.sub(arD, f"osb{i}", [128, TT], part="B") for i in range(3)]
        qtm = self.sub(arC, "qtm", [128, NB, 512], BF16, part="B")
        ktm = self.sub(arC, "ktm", [128, NB, 512], BF16, part="B")
        vtb = self.sub(arC, "vtb", [128, NB, 512], BF16, part="B")
        kTt = self.sub(arC, "kTt", [128, 4, TT], BF16, part="B")
        xtok = self.sub(arC, "xtok", [128, NB, D_MODEL], part="X")
        uT = self.sub(arA, "uT", [128, 32, TT], BF16, part="F")
        gate = self.sub(arB, "gate", [128, 8, TT], part="F")
        c_r = self.sub(arA, "c_r", [128, 2, TT], part="C"); c_v = self.sub(arA, "c_v", [128, 2, TT], part="C")
        c_wl = self.sub(arA, "c_wl", [128, 2, TT], part="C"); c_g = self.sub(arA, "c_g", [128, 2, TT], part="C")
        c_kk = self.sub(arA, "c_kk", [128, 2, TT], part="C"); c_km = self.sub(arA, "c_km", [128, 2, TT], part="C")
        c_b = self.sub(arA, "c_b", [128, 2, TT], part="C"); c_bon = self.sub(arA, "c_bon", [128, 2, TT], part="C")
        c_a = self.sub(arB, "c_a", [128, 2, TT], part="C"); c_k = self.sub(arB, "c_k", [128, 2, TT], part="C")
        c_6 = self.sub(arB, "c_6", [128, TT], part="C"); c_7 = self.sub(arB, "c_7", [128, TT], part="C")
        C1T = self.sub(arB, "C1T", [128, 4, 128], part="C"); C2T = self.sub(arB, "C2T", [128, 4, 128], part="C")
        cXb = T(c_6.t[:, 0:256].bitcast(BF16).rearrange("p (h s) -> p h s", h=4), "cXb", track=False, bufs=[c_6.buf])
        cXb.wbufs = c_6.wbufs
        cXTb = T(c_7.t[:, 0:256].bitcast(BF16).rearrange("p (h s) -> p h s", h=4), "cXTb", track=False, bufs=[c_7.buf])
        cXTb.wbufs = c_7.wbufs
        crawm = [self.sub(arD, f"crawm{i}", [128, 1 + TT], part="C") for i in range(2)]
        tm5 = self.sub(arD, "tm5", [128, 5, 256], part="C")
        rTt = self.sub(arD, "rTt", [128, 2, 128], part="C"); kapTt = self.sub(arD, "kapTt", [128, 2, 128], part="C")
        rTb = self.sub(arD, "rTb", [128, 2, 128], BF16, part="C"); kapTb = self.sub(arD, "kapTb", [128, 2, 128], BF16, part="C")
        kTm = self.sub(arD, "kTm", [128, 4, 128], BF16, part="C"); bTm = self.sub(arD, "bTm", [128, 4, 128], BF16, part="C")
        uval = self.sub(arC, "uval", [128, 256], part="A"); oq = self.sub(arC, "oq", [128, 256], part="A")
        otm = self.sub(arC, "otm", [128, 256], part="A"); o2 = self.sub(arC, "o2", [128, 256], part="A")
        kapz = self.sub(arC, "kapz", [128, 4, 128], BF16, part="C")
        cktz = self.sub(arC, "cktz", [128, 2, 4, 128], part="C"); cbtz = self.sub(arC, "cbtz", [128, 2, 4, 128], part="C")
        cx0 = self.sub(arC, "cx0", [128, 256], part="C"); cx0b = self.sub(arC, "cx0b", [128, 256], BF16, part="C"); cU0 = self.sub(arC, "cU0", [128, 256], part="C")
        cU_ = self.sub(arC, "cU_", [128, 256], part="C"); cyq = self.sub(arC, "cyq", [128, 256], part="C")
        cytm = self.sub(arC, "cytm", [128, 256], part="C")
        cR2 = T(vfs[0].t[:, 0:256].bitcast(BF16).rearrange("p (h s) -> p h s", h=4), "cR2", track=False, bufs=[vfs[0].buf])
        cBmT = T(vfs[1].t[:, :].rearrange("p (h s) -> p h s", h=4), "cBmT", track=False, bufs=[vfs[1].buf])
        Tst = [self.sb(f"Tst{l}", [128, 2, 128]) for l in range(DEPTH)]
        ccarry = [self.sb(f"ccarry{l}", [128, 8]) for l in range(DEPTH)]
        acarry_s = [self.sb(f"acarry_s{l}", [128, 6, 3]) for l in range(DEPTH)]
        Sd_s = [self.sb(f"Sd_s{l}", [128, 2, 128]) for l in range(DEPTH)]
        Tst_s = [self.sb(f"Tst_s{l}", [128, 2, 128]) for l in range(DEPTH)]
        ccarry_s = [self.sb(f"ccarry_s{l}", [128, 8]) for l in range(DEPTH)]
        kvp = self.sb("kvp", [128, 2, 2, 64]); pcc = self.sb("pcc", [128, 2, 2])
        cw3 = self.sb("cw3", [128, DEPTH, 3, 256], BF16)
        cst = self.sb("cst", [128, 8])

        def cyn(blk):
            return kfs[blk // 2][:, (blk % 2) * 256:(blk % 2 + 1) * 256]
        cgt = self.sb("cgt", [128, 2, 256])
        ps = [self.psum(f"ps{i}", [128, 512]) for i in range(7)]
        psbT = self.psum("psb", [128, 1024], BF16)

        self.dma("sp", consts.v(), consts_in.v())
        self.dma("sp", cols.v(), cols_in.v())
        self.dma("sp", cols2.v(), cols2_in.v())
        self.dma("sp", rowp.v(), V(rowp_in.t.rearrange("l a -> (l a)").partition_broadcast(128)
                                  .rearrange("p (l a) -> p l a", l=DEPTH), ()))
        self.act("activation", rowp[:, :, 0:4], rowp[:, :, 0:4], AF.Exp)
        self.dve("tensor_scalar_mul", rowp[:, :, 0:4], rowp[:, :, 0:4], -1.0)
        for l in range(DEPTH):
            self.dve("memset", acarry[l].v(), 0.0)
            self.dve("memset", Sd[l].v(), 0.0)
        self.dma("pool", cw3.v(), cw3_in.v())
        for l in range(DEPTH):
            self.dve("memset", Tst[l].v(), 0.0)
            self.dve("memset", ccarry[l].v(), 0.0)
            self.dve("tensor_scalar", cols2[:, l, 54:56], cols2[:, l, 46:48], -1.0, 1.0, ALU.mult, ALU.add)
        self.dve("memset", kbz.v(), 0.0)
        self.dve("memset", ktz.v(), 0.0)
        self.dve("memset", unew.v(), 0.0)
        self.dma("sp", amask.v(), amask_in.v())
        self.dve("tensor_copy", identb.v(), consts[:, 0, :])
        self.dve("tensor_copy", onesb.v(), consts[:, 1, :])
        lrow = T(ptok.t[:, 0:2, :].rearrange("p a (b d) -> p a b d", b=4), "lrow", track=False, bufs=[ptok.buf])
        self.dma("sp", lrow.v(), V(lam_in.t.rearrange("l a d -> (l a d)").partition_broadcast(128)
                                  .rearrange("p (l a d) -> p l a d", l=DEPTH, a=4), ()))
        lsum = self.sb("lsum", [128, 4])
        for l in range(DEPTH):
            for m in range(2):
                self.dve("tensor_tensor", tmpf[0][:, 0:64], lrow[:, l, 2 * m, :], lrow[:, l, 2 * m + 1, :], ALU.mult)
                self.dve("reduce_sum", lsum[:, 2 * l + m:2 * l + m + 1], tmpf[0][:, 0:64], AX.X)
        self.act("activation", lsum.v(), lsum.v(), AF.Exp)
        for l in range(DEPTH):
            lam_init = 0.8 - 0.6 * math.exp(-0.3 * l)
            self.dve("scalar_tensor_tensor", lamt[:, l:l + 1], lsum[:, 2 * l + 1:2 * l + 2], -lam_init,
                     lsum[:, 2 * l:2 * l + 1], ALU.add, ALU.subtract)

        bns = self.sb("bns", [128, DEPTH])
        for l in range(DEPTH):
            self.dve("tensor_scalar_mul", bns[:, l:l + 1], cols[:, 48 + l:49 + l], float(1.0 - (0.8 - 0.6 * math.exp(-0.3 * l))))
        def col(name, l, k=0):
            base = {"norm_mix": 0, "norm_ffn": 8, "norm_ple": 16, "b_norm": 24}[name]
            if name == "b_norm":
                return bns[:, l:l + 1]
            return cols[:, l * 24 + base + k: l * 24 + base + k + 1]

        def colf(k):
            return cols[:, 50 + k:51 + k]

        def rmsnorm_to_xb(ntok, gcol):
            for k in range(8):
                s = sq[k % 2]
                self.act("activation", s[:, :ntok], h[:, k, :ntok], AF.Square)
                self.pe("matmul", ps[2][:, :ntok], onesb.v(), s[:, :ntok], start=(k == 0), stop=(k == 7))
            self.act("activation", rstd[:, :ntok], ps[2][:, :ntok], AF.Ln, bias=colf(0), scale=1.0 / D_MODEL)
            self.act("activation", rstd[:, :ntok], rstd[:, :ntok], AF.Exp, scale=-0.5)
            for k in range(8):
                self.dve("scalar_tensor_tensor", xb[:, k, :ntok], h[:, k, :ntok], gcol(k), rstd[:, :ntok],
                         ALU.mult, ALU.mult)

        wscr = {}

        def load_piece(wref, KC, c0, ncols):
            wname, l = wref
            wv = W[wname].t[l]
            slot = ring[self.ring_i % NSLOT]
            self.ring_i += 1
            flat = slot[:, 0:KC * ncols]
            sv = flat.r("p (k n) -> p k n", k=KC)
            key = (wname, l, KC, c0, ncols)
            if key in wscr and self.wcache:
                self.dma("sp", flat, wscr[key].v())
                return sv
            for k0 in range(0, KC, 8):
                k1 = min(KC, k0 + 8)
                src = V(wv.rearrange("(kc p) n -> p kc n", p=128)[:, k0:k1, c0:c0 + ncols], ())
                self.dma("pool", sv[:, k0:k1, :], src)
            if self.wcache:
                scr = self.dscr(f"wb_{wname}_{l}_{c0}_{ncols}", [128, KC * ncols], BF16)
                wscr[key] = scr
                self.dma("sp", scr.v(), flat)
            return sv

        self.psd = 0

        def dense_fm(wv, KC, c0, ncols_total, rhs, ntok, epi, mchunk=128):
            per = max(128, min(512, 4096 // KC))
            m = 0
            for pc0 in range(0, ncols_total, per):
                pn = min(per, ncols_total - pc0)
                sv = load_piece(wv, KC, c0 + pc0, pn)
                for cc in range(0, pn, mchunk):
                    mc = min(mchunk, pn - cc)
                    pst = ps[self.psd % 2]
                    self.psd += 1
                    for k in range(KC):
                        self.pe("matmul", pst[:mc, :ntok], sv[:, k, cc:cc + mc], rhs[:, k, :ntok],
                                start=(k == 0), stop=(k == KC - 1))
                    epi(m, pst[:mc, :ntok])
                    m += 1

        def dense_tm(wv, c0, ncols, ntok, epi):
            sv = load_piece(wv, 8, c0, ncols)
            for b0 in range(0, ntok, 128):
                bt = min(128, ntok - b0)
                pst = ps[self.psd % 2]
                self.psd += 1
                for k in range(8):
                    self.pe("matmul", pst[:bt, :ncols], xb[:, k, b0:b0 + bt], sv[:, k, :], start=(k == 0), stop=(k == 7))
                epi(b0 // 128, bt, pst[:bt, :ncols])

        def rope_tm(dst, src_sb, bt, blk):
            import os
            if os.environ.get("NOROPE"):
                self.dve("tensor_copy", dst, src_sb[:bt, :])
                return
            s4 = src_sb[:bt, :].r("p (g t f) -> p g t f", g=8, t=2)
            d4 = dst.r("p (g t f) -> p g t f", g=8, t=2)
            cos = ropet[:bt, blk, 0:32].unsq(1).bc([bt, 8, 32])
            sin = ropet[:bt, blk, 32:64].unsq(1).bc([bt, 8, 32])
            ta = tmpf[0][:bt, 0:256].r("p (g f) -> p g f", g=8)
            tb = tmpf[1][:bt, 0:256].r("p (g f) -> p g f", g=8)
            self.dve("tensor_tensor", ta, s4[:, :, 0, :], cos, ALU.mult)
            self.dve("tensor_tensor", tb, s4[:, :, 1, :], sin, ALU.mult)
            self.dve("tensor_tensor", d4[:, :, 0, :], ta, tb, ALU.subtract)
            self.dve("tensor_tensor", ta, s4[:, :, 1, :], cos, ALU.mult)
            self.dve("tensor_tensor", tb, s4[:, :, 0, :], sin, ALU.mult)
            self.dve("tensor_tensor", d4[:, :, 1, :], ta, tb, ALU.add)

        def split2(tt):
            if tt.hb is None:
                tt.hb = [Buf(tt.name + "_h0"), Buf(tt.name + "_h1")]
                tt.extra = tuple(tt.hb)
            return tt

        for tt in (Xa[0], XTa[0], Rm, cXb, cXTb, cR2):
            split2(tt)

        def neumann(bt, X, XT, R, CH):
            nlev = 5 if CH > 16 else 3

            def sh(tt, hh):
                return V(tt.t[:bt, hh, :bt], (tt.hb[hh // 2],))

            def sg(tt, g):
                return V(tt.t[:bt, 2 * g:2 * g + 2, :bt], (tt.hb[g],))

            bX, bXT, bR = (ps[0], ps[4]), (ps[1], ps[5]), (ps[3], ps[6])

            def ph(pts, hh):
                return pts[hh // 2][:bt, (hh % 2) * 128:(hh % 2) * 128 + bt]

            def pg(pts, g):
                return pts[g][:bt, 0:256].r("p (h s) -> p h s", h=2)[:, :, :bt]

            for lev in range(nlev):
                last = lev == nlev - 1
                for g in range(2):
                    for hh in (2 * g, 2 * g + 1):
                        self.pe("matmul", ph(bXT, hh), sh(X, hh), sh(XT, hh), start=True, stop=True)
                    if not last:
                        for hh in (2 * g, 2 * g + 1):
                            self.pe("matmul", ph(bX, hh), sh(XT, hh), sh(X, hh), start=True, stop=True)
                for g in range(2):
                    self.dve("tensor_copy", sg(XT, g), pg(bXT, g))
                    if not last:
                        self.act("copy", sg(X, g), pg(bX, g))
                for g in range(2):
                    for hh in (2 * g, 2 * g + 1):
                        self.pe("matmul", ph(bR, hh), sh(XT, hh), sh(R, hh), start=True, stop=True)
                for g in range(2):
                    self.dve("tensor_tensor", sg(R, g), sg(R, g), pg(bR, g), ALU.add)

        def tile_body(cx):
            ntok = cx.ntok
            nbk = (ntok + 127) // 128
            bl = min(128, ntok)
            kb0 = cx.key_base
            self.dma("sp", xtok[:bl, :nbk, :], V(cx.x_src.rearrange("(b p) d -> p b d", p=bl), ()))
            for k in range(8):
                for b in range(nbk):
                    self.pe("transpose", ps[3][:, b * 128:b * 128 + bl], xtok[:bl, b, k * 128:(k + 1) * 128], ident[:bl, :bl])
                self.act("copy", h[:, k, :ntok], ps[3][:, :ntok])
            self.dma("sp", ropet[:bl, :nbk, :], V(cx.rope_src.rearrange("(b p) d -> p b d", p=bl), ()))

            for l in range(DEPTH):
                lam_init = 0.8 - 0.6 * math.exp(-0.3 * l)
                rmsnorm_to_xb(ntok, lambda k: col("norm_mix", l, k))
                w_in = ("w_in", l)
                kT_s, v_s = cx.kT_scr[l], cx.v_scr[l]

                if True:
                    def epi_q(blk, bt, pv):
                        self.act("copy", osb[0][:bt, :], pv)
                        rope_tm(qtm[:bt, blk, :], osb[0], bt, blk)
                    dense_tm(w_in, O_BQ, 512, ntok, epi_q)

                    def epi_k(blk, bt, pv):
                        kf = kfs[blk % 2]
                        self.act("copy", osb[1][:bt, :], pv)
                        rope_tm(kf[:bt, :], osb[1], bt, blk)
                        self.act("copy", ktm[:bt, blk, :], kf[:bt, :])
                        self.dma("sp", V(cx.bk_dst(l)[blk * 128:blk * 128 + bt, :], ()), kf[:bt, :], final=True)
                    dense_tm(w_in, O_BK, 512, ntok, epi_k)

                    def epi_v(blk, bt, pv):
                        vf = vfs[blk % 2]
                        self.act("copy", vf[:bt, :], pv)
                        self.dve("tensor_copy", vtb[:bt, blk, :], vf[:bt, :])
                        self.dma("sp", V(cx.bv_dst(l)[blk * 128:blk * 128 + bt, :], ()), vf[:bt, :], final=True)
                    dense_tm(w_in, O_BV, 512, ntok, epi_v)
                    self.dma("sp", V(v_s.t[kb0:kb0 + ntok, :].rearrange("(b p) d -> p b d", p=bl), (v_s.buf,)), vtb[:bl, :nbk, :])
                    psb = psbT.v()
                    for c in range(4):
                        for b in range(nbk):
                            self.pe("transpose", psb[:, b * 128:b * 128 + bl], qtm[:bl, b, c * 128:(c + 1) * 128], identb[:bl, :bl])
                        for m in range(2):
                            self.act("mul", qT[:, c, m, :ntok], psb[:, 0:ntok], cSame[:, m * 64:m * 64 + 1])
                        for b in range(nbk):
                            self.pe("transpose", psb[:, 512 + b * 128:512 + b * 128 + bl], ktm[:bl, b, c * 128:(c + 1) * 128], identb[:bl, :bl])
                        self.dve("tensor_copy", kTt[:, c, :ntok], psb[:, 512:512 + ntok])
                    self.dma("sp", V(kT_s.t[:, kb0:kb0 + ntok].rearrange("(c p) s -> p c s", p=128), (kT_s.buf,)), kTt[:, :, :ntok])

                    nk = kb0 + ntok
                    nkb = (nk + 127) // 128
                    sbank = (ps[0], ps[1], ps[2], ps[5]) if self.poolsum else (ps[0], ps[1], ps[2], ps[0])
                    for hh in range(4):
                        units = []
                        for ks in range(0, nk, 512):
                            kn = min(512, nk - ks)
                            for j in range((kn + 127) // 128):
                                units.append((ks, kn, j, min(128, kn - j * 128), (ks + j * 128) // 128))

                        def stage1(ui):
                            ks, kn, j, kr, kb = units[ui]
                            sbi = (ks // 512) % 2
                            cur_k, cur_v = kblk[sbi], vblk[sbi]
                            if j == 0:
                                self.dma("sp", cur_k[:, :kn], V(kT_s.t[hh * 128:(hh + 1) * 128, ks:ks + kn], (kT_s.buf,)))
                                if kn == 512:
                                    self.dma("sp", cur_v.v(), V(v_s.t[ks:ks + 512, hh * 128:(hh + 1) * 128]
                                                                .rearrange("(j p) d -> p j d", p=128), (v_s.buf,)))
                                else:
                                    for jj in range((kn + 127) // 128):
                                        krr = min(128, kn - jj * 128)
                                        self.dma("sp", cur_v[:krr, jj, :], V(v_s.t[ks + jj * 128:ks + jj * 128 + krr, hh * 128:(hh + 1) * 128], (v_s.buf,)))
                            diag = cx.masked and kb * 128 >= kb0
                            for m in range(2):
                                pss = sbank[(ui % 2) * 2 + m] if self.poolsum else ps[(2 * ui + m) % 3]
                                pt = PT[(ui % 2) * 2 + m]
                                self.pe("matmul", pss[:kr, :ntok], cur_k[:, j * 128:j * 128 + kr],
                                        qT[:, hh, m, :ntok], start=True, stop=True)
                                self.act("activation", pt[:kr, :ntok], pss[:kr, :ntok], AF.Exp, scale=0.125)
                                if diag:
                                    self.dve("tensor_tensor", pt[:kr, :ntok], pt[:kr, :ntok], amask[:kr, kb - kb0 // 128, :ntok], ALU.mult)

                        def stage2(ui):
                            ks, kn, j, kr, kb = units[ui]
                            cur_v = vblk[(ks // 512) % 2]
                            first, last = ui == 0, ui == len(units) - 1
                            for m in range(2):
                                pt = PT[(ui % 2) * 2 + m]
                                self.pe("matmul", ps[3 + m][:, :ntok], cur_v[:kr, j, :], pt[:kr, :ntok], start=first, stop=last)
                                if not self.poolsum:
                                    self.pe("matmul", ps[5 + m][:, :ntok], onesb[:kr, :], pt[:kr, :ntok], start=first, stop=last)
                                elif first:
                                    self._op("pool", "tensor_copy", osb[m][:, :ntok], pt[:, :ntok])
                                else:
                                    self._op("pool", "tensor_tensor", osb[m][:kr, :ntok], osb[m][:kr, :ntok], pt[:kr, :ntok], ALU.add)

                        stage1(0)
                        for ui in range(1, len(units)):
                            stage1(ui)
                            stage2(ui - 1)
                        stage2(len(units) - 1)
                        n_ = slice(0, ntok)
                        if self.poolsum:
                            for m in range(2):
                                self.pe("matmul", ps[m][:, n_], consts[:, 1, :], osb[m][:, n_], start=True, stop=True)
                            self.dve("reciprocal", osb[0][:, n_], ps[0][:, n_])
                            self.dve("reciprocal", osb[1][:, n_], ps[1][:, n_])
                        else:
                            for m in range(2):
                                self.act("activation", osb[m][:, n_], ps[5 + m][:, n_], AF.Ln)
                                self.act("activation", osb[m][:, n_], osb[m][:, n_], AF.Exp, scale=-1.0)
                        self.dve("tensor_tensor", osb[0][:, n_], osb[0][:, n_], ps[3][:, n_], ALU.mult)
                        self.dve("tensor_tensor", osb[1][:, n_], osb[1][:, n_], ps[4][:, n_], ALU.mult)
                        self.dve("scalar_tensor_tensor", osb[2][:, n_], osb[1][:, n_], lamt[:, l:l + 1], osb[0][:, n_], ALU.mult, ALU.add)
                        self.act("activation", sq[0][:, n_], osb[2][:, n_], AF.Square)
                        self.pe("matmul", ps[2][:, n_], onesb.v(), sq[0][:, n_], start=True, stop=True)
                        self.act("activation", osb[0][:, n_], ps[2][:, n_], AF.Ln, bias=colf(0), scale=1.0 / 128)
                        self.act("activation", osb[0][:, n_], osb[0][:, n_], AF.Exp, scale=-0.5)
                        self.dve("scalar_tensor_tensor", mixT[:, 2 + hh, n_], osb[2][:, n_], col("b_norm", l), osb[0][:, n_],
                                 ALU.mult, ALU.mult)

                if "A" not in self.stages:
                    for c in (0, 1):
                        self.dve("memset", mixT[:, c, :], 0.0)
                else:
                    one_col = consts[:, 1, 0:1]
                    self.dve("tensor_copy", aq[:, :, 0:3], cx.acarry[l].v())

                    def epi_aqkv(m, pv):
                        self.act("copy", aq[:, m, 3:3 + ntok], pv)
                    dense_fm(w_in, 8, O_AQKV, 768, xb, ntok, epi_aqkv)
                    self.dve("tensor_copy", cx.acarry[l].v(), aq[:, :, ntok:ntok + 3])
                    if cx.is_last:
                        for m in range(6):
                            self.dma("sp", V(cx.aconv_out.t[l, :, m * 128:(m + 1) * 128].rearrange("j p -> p j"), ()),
                                     cx.acarry[l][:, m, :], final=True, allow_slow_non_contiguous=True)

                    def epi_az(m, pv):
                        self.act("activation", zs[:, m, :ntok], pv, AF.Silu)
                    if self.a1 >= 2: dense_fm(w_in, 8, O_AZ, 256, xb, ntok, epi_az)

                    def epi_ab(blk, bt, pv):
                        self.act("copy", abtm[:bt, blk, :], pv)
                    if self.a1 >= 3: dense_tm(w_in, O_AA, 8, ntok, epi_ab)
                    for m in (range(6) if self.a1 >= 4 else ()):
                        tf = tmpf[m % 2]
                        self.dve("tensor_scalar_mul", tf[:, :ntok], aq[:, m, 0:ntok], cols2[:, l, m * 4:m * 4 + 1])
                        for j in (1, 2, 3):
                            self.dve("scalar_tensor_tensor", tf[:, :ntok], aq[:, m, j:j + ntok],
                                     cols2[:, l, m * 4 + j:m * 4 + j + 1], tf[:, :ntok], ALU.mult, ALU.add)
                        self.act("activation", ac[:, m, :ntok], tf[:, :ntok], AF.Silu)
                    for m in (range(4) if self.a1 >= 5 else ()):
                        tf = tmpf[m % 2]
                        self.act("activation", tf[:, :ntok], ac[:, m, :ntok], AF.Square)
                        self.pe("matmul", ps[2][:, :ntok], cSame, tf[:, :ntok], start=True, stop=True)
                        self.act("activation", tf[:, :ntok], ps[2][:, :ntok], AF.Ln, bias=colf(0), scale=1.0)
                        self.act("activation", tf[:, :ntok], tf[:, :ntok], AF.Exp, scale=-0.5)
                        self.dve("scalar_tensor_tensor", qkn[:, m, :ntok], ac[:, m, :ntok], 0.125 if m < 2 else 1.0,
                                 tf[:, :ntok], ALU.mult, ALU.mult)
                    nbk = (ntok + 127) // 128
                    btl = min(128, ntok)
                    if self.a1 >= 6: self.dve("tensor_tensor", astep[:btl, :nbk, :], abtm[:btl, :nbk, 0:4],
                             rowp[:btl, l, 4:8].unsq(1).bc([btl, nbk, 4]), ALU.add)
                    if self.a1 >= 7: self.act("activation", astep[:btl, :nbk, :], astep[:btl, :nbk, :], AF.Exp)
                    if self.a1 >= 8: self.act("activation", astep[:btl, :nbk, :], astep[:btl, :nbk, :], AF.Ln, bias=one_col[:btl, :], scale=1.0)
                    if self.a1 >= 9: self.dve("tensor_tensor", astep[:btl, :nbk, :], astep[:btl, :nbk, :],
                             rowp[:btl, l, 0:4].unsq(1).bc([btl, nbk, 4]), ALU.mult)
                    if self.a1 >= 9: self.act("activation", beta[:btl, :nbk, :], abtm[:btl, :nbk, 4:8], AF.Sigmoid)

                    for blk in (range(nbk) if self.asub >= 2 else ()):
                        bt = min(128, ntok - blk * 128)
                        c0 = blk * 128
                        chunks = [(r0, min(r0 + cx.CH, bt)) for r0 in range(0, bt, cx.CH)]
                        for i, (src, m) in enumerate(((qkn, 2), (qkn, 3), (ac, 4), (ac, 5))):
                            self.pe("transpose", ps[3][:bt, i * 128:(i + 1) * 128], src[:, m, c0:c0 + bt], ident)
                        self.act("copy", kvtm[:bt, blk, :], ps[3][:bt, :])
                        ktm_ = kvtm[:bt, blk, 0:256].r("p (h d) -> p h d", h=4)
                        vtm_ = kvtm[:bt, blk, 256:512].r("p (h d) -> p h d", h=4)
                        a_b = astep[:bt, blk, :]
                        self.dve("tensor_tensor", aL[:bt, :, :bt], cL[:bt, :bt].unsq(1).bc([bt, 4, bt]),
                                 a_b.unsq(2).bc([bt, 4, bt]), ALU.mult)
                        self.dve("tensor_tensor", aU[:bt, :, :bt], cU[:bt, :bt].unsq(1).bc([bt, 4, bt]),
                                 a_b.unsq(2).bc([bt, 4, bt]), ALU.mult)
                        p0 = ps[0][:bt, :].r("p (h s) -> p h s", h=4)[:, :, :bt]
                        p1 = ps[1][:bt, :].r("p (h s) -> p h s", h=4)[:, :, :bt]
                        for hh in range(4):
                            self.pe("matmul", p0[:, hh, :], cU[:bt, :bt], aL[:bt, hh, :bt], start=True, stop=True)
                        for hh in range(4):
                            self.pe("matmul", p1[:, hh, :], cL[:bt, :bt], aU[:bt, hh, :bt], start=True, stop=True)
                        self.act("activation", decA[:bt, :, :bt], p0, AF.Exp)
                        self.act("activation", decT[:bt, :, :bt], p1, AF.Exp)
                        self.dve("tensor_tensor", decA[:bt, :, :bt], decA[:bt, :, :bt], cL[:bt, :bt].unsq(1).bc([bt, 4, bt]), ALU.mult)
                        self.dve("tensor_tensor", decT[:bt, :, :bt], decT[:bt, :, :bt], cU[:bt, :bt].unsq(1).bc([bt, 4, bt]), ALU.mult)
                        self.pe("matmul", ps[2][:bt, 0:4], cU[:bt, :bt], a_b, start=True, stop=True)
                        self.pe("matmul", ps[2][:bt, 4:8], cL[:bt, :bt], a_b, start=True, stop=True)
                        for ci in range(len(chunks)):
                            for e in range(2):
                                self.pe("matmul", ps[2][:, 8 + 2 * ci:10 + 2 * ci], consts[:bt, 7 + 2 * ci + e, :],
                                        astep[:bt, blk, e::2], start=(e == 0), stop=(e == 1))
                        self.act("activation", eg[:bt, 0:8], ps[2][:bt, 0:8], AF.Exp)
                        self.act("activation", egt.v().r("p c r -> p (c r)")[:, 0:2 * len(chunks)],
                                 ps[2][:, 8:8 + 2 * len(chunks)], AF.Exp)
                        self.dve("tensor_tensor", bkg[:bt, :], beta[:bt, blk, :], eg[:bt, 0:4], ALU.mult)
                        if self.asub < 3: continue
                        pk = ps[3][:bt, :].r("p (h s) -> p h s", h=4)[:, :, :bt]
                        import os
                        kkv = os.environ.get("KKV", "")
                        for hh in range(4):
                            e, pr = hh % 2, hh // 2
                            if kkv == "even" and e == 1:
                                continue
                            if kkv == "odd" and e == 0:
                                continue
                            self.dve("tensor_scalar_mul", kzb[:, hh, :bt], qkn[:, 2 + pr, c0:c0 + bt], cSame[:, e * 64:e * 64 + 1])
                            self.pe("matmul", pk[:, hh, :], kzb[:, hh, :bt], qkn[:, 2 + pr, c0:c0 + bt], start=True, stop=True)
                        import os
                        av = os.environ.get("AV", "")
                        for hh in range(4):
                            if av == "nodve":
                                continue
                            if av == "fsc":
                                self.dve("scalar_tensor_tensor", Am[:bt, hh, :bt], pk[:, hh, :], 1.0,
                                         decA[:bt, hh, :bt], ALU.mult, ALU.mult)
                                continue
                            if av == "tt":
                                self.dve("tensor_tensor", Am[:bt, hh, :bt], pk[:, hh, :], decA[:bt, hh, :bt], ALU.mult)
                                continue
                            self.dve("scalar_tensor_tensor", Am[:bt, hh, :bt], pk[:, hh, :], beta[:bt, blk, hh:hh + 1],
                                     decA[:bt, hh, :bt], ALU.mult, ALU.mult)
                        if self.a3 < 2: continue
                        pB = ps[0][:bt, :].r("p (h s) -> p h s", h=4)[:, :, :bt]
                        for hh in range(4):
                            self.pe("transpose", pB[:, hh, :], Am[:bt, hh, :bt], ident[:bt, :bt])
                        self.act("copy", Xa[0][:bt, :, :bt], pB)
                        self.dve("tensor_tensor", Rm[:bt, :, :bt], ident[:bt, :bt].unsq(1).bc([bt, 4, bt]), Xa[0][:bt, :, :bt], ALU.subtract)
                        self.dve("tensor_copy", XTa[0][:bt, :, :bt], Am[:bt, :, :bt])
                        neumann(bt, Xa[0], XTa[0], Rm, cx.CH)
                        self.dve("tensor_tensor", vb_[:bt, :, :], vtm_, beta[:bt, blk, :].unsq(2).bc([bt, 4, 64]), ALU.mult)
                        for hh in range(4):
                            e = hh % 2
                            self.dve("tensor_tensor", kbz[:bt, hh, e * 64:(e + 1) * 64], ktm_[:, hh, :],
                                     bkg[:bt, hh:hh + 1].bc([bt, 64]), ALU.mult)
                            for ci, (r0, r1) in enumerate(chunks):
                                self.dve("tensor_tensor", ktz[r0:r1, ci, hh, e * 64:(e + 1) * 64], ktm_[r0:r1, hh, :],
                                         eg[r0:r1, 4 + hh:5 + hh].bc([r1 - r0, 64]), ALU.mult)
                        for hh in range(4):
                            self.pe("matmul", ps[4][:bt, hh * 64:(hh + 1) * 64], Rm[:bt, hh, :bt], vb_[:bt, hh, :], start=True, stop=True)
                        for pr in range(2):
                            for e in range(2):
                                self.pe("matmul", ps[4][:, 256 + pr * 128:256 + pr * 128 + bt], kbz[:bt, 2 * pr + e, :],
                                        Rm[:bt, 2 * pr + e, :bt], start=(e == 0), stop=(e == 1))
                        self.act("copy", uval[:bt, :], ps[4][:bt, 0:256])
                        self.act("copy", wT[:, :, :bt], ps[4][:, 256:512].r("p (a t) -> p a t", a=2)[:, :, :bt])
                        pq = ps[5][:bt, :].r("p (h s) -> p h s", h=4)[:, :, :bt]
                        for hh in range(4):
                            e, pr = hh % 2, hh // 2
                            self.pe("matmul", pq[:, hh, :], kzb[:, hh, :bt], qkn[:, pr, c0:c0 + bt], start=True, stop=True)
                        self.dve("tensor_tensor", qkT[:bt, :, :bt], pq, decT[:bt, :, :bt], ALU.mult)
                        if self.asub < 5: continue
                        for ci, (r0, r1) in enumerate(chunks):
                            for pr in range(2):
                                self.pe("matmul", ps[6][:bt, pr * 128:(pr + 1) * 128], wT[:, pr, :bt], cx.Sd[l][:, pr, :], start=True, stop=True)
                            for pr in range(2):
                                self.pe("matmul", ps[6][:bt, 256 + pr * 128:256 + (pr + 1) * 128], qkn[:, pr, c0:c0 + bt],
                                        cx.Sd[l][:, pr, :], start=True, stop=True)
                            self.dve("tensor_tensor", unew[r0:r1, :], uval[r0:r1, :], ps[6][r0:r1, 0:256], ALU.subtract)
                            self.dve("tensor_tensor", oq[r0:r1, :].r("p (h d) -> p h d", h=4),
                                     ps[6][r0:r1, 256:512].r("p (h d) -> p h d", h=4),
                                     eg[r0:r1, 0:4].unsq(2).bc([r1 - r0, 4, 64]), ALU.mult)
                            for pr in range(2):
                                for e in range(2):
                                    hh = 2 * pr + e
                                    self.pe("matmul", ps[2][:, 64 + pr * 64:128 + pr * 64], ktz[:bt, ci, hh, :],
                                            unew[:bt, hh * 64:(hh + 1) * 64], start=(e == 0), stop=(e == 1))
                            for e in range(2):
                                sdv = cx.Sd[l][e * 64:(e + 1) * 64, :, e * 64:(e + 1) * 64]
                                self.dve("tensor_tensor", sdv, sdv, egt[e * 64:(e + 1) * 64, ci, :].unsq(2).bc([64, 2, 64]), ALU.mult)
                                self.dve("tensor_tensor", sdv, sdv, ps[2][e * 64:(e + 1) * 64, 64:192].r("p (a d) -> p a d", a=2), ALU.add)
                        if self.asub < 6: continue
                        for hh in range(4):
                            self.pe("matmul", ps[5][:bt, hh * 64:(hh + 1) * 64], qkT[:bt, hh, :bt], unew[:bt, hh * 64:(hh + 1) * 64],
                                    start=True, stop=True)
                        self.dve("tensor_tensor", otm[:bt, :], oq[:bt, :], ps[5][:bt, 0:256], ALU.add)
                        self.dve("tensor_tensor", o2[:bt, :], otm[:bt, :], otm[:bt, :], ALU.mult)
                        self.dve("tensor_reduce", ssq[:bt, :], o2[:bt, :].r("p (h d) -> p h d", h=4), AX.X, ALU.add)
                        self.act("activation", ssq[:bt, :], ssq[:bt, :], AF.Ln, bias=colf(0)[:bt, :], scale=1.0 / 64)
                        self.act("activation", ssq[:bt, :], ssq[:bt, :], AF.Exp, scale=-0.5)
                        self.dve("tensor_tensor", ontm[:bt, blk, :].r("p (h d) -> p h d", h=4),
                                 otm[:bt, :].r("p (h d) -> p h d", h=4), ssq[:bt, :].unsq(2).bc([bt, 4, 64]), ALU.mult)
                    if self.asub < 9:
                        for c in (0, 1):
                            self.dve("memset", mixT[:, c, :], 0.0)
                    for pr in (range(2) if self.asub >= 9 else ()):
                        for blk in range(nbk):
                            bt = min(128, ntok - blk * 128)
                            self.pe("transpose", ps[3][:, blk * 128:blk * 128 + bt], ontm[:bt, blk, pr * 128:(pr + 1) * 128], ident[:bt, :bt])
                        self.dve("scalar_tensor_tensor", mixT[:, pr, :ntok], ps[3][:, :ntok], cols2[:, l, 24:25], zs[:, pr, :ntok],
                                 ALU.mult, ALU.mult)
                    if cx.is_last:
                        for e in range(2):
                            self.dma("sp", V(cx.adelta_out.t[l].rearrange("(pr e) k v -> e k pr v", e=2)[e], ()),
                                     cx.Sd[l][e * 64:(e + 1) * 64, :, e * 64:(e + 1) * 64], final=True)

                if "C" not in self.stages:
                    for c in (6, 7):
                        self.dve("memset", mixT[:, c, :], 0.0)
                else:
                    nbk = (ntok + 127) // 128
                    self.dve("memset", kapz.v(), 0.0)
                    self.dve("memset", cktz.v(), 0.0)
                    self.dve("memset", cbtz.v(), 0.0)
                    c2 = lambda a, b=None: cols2[:, l, a:(a + 1 if b is None else b)]
                    dests = [c_r[:, 0, :ntok], c_r[:, 1, :ntok], c_k[:, 0, :ntok], c_k[:, 1, :ntok],
                             c_v[:, 0, :ntok], c_v[:, 1, :ntok], c_6[:, :ntok], c_7[:, :ntok]]

                    def epi_c(m, pv):
                        cm = crawm[m % 2]
                        self.act("copy", cm[:, 1:1 + ntok], pv)
                        self.act("copy", cm[:, 0:1], cx.ccarry[l][:, m:m + 1])
                        self.act("copy", cx.ccarry[l][:, m:m + 1], cm[:, ntok:ntok + 1])
                        tf = tmpf[m % 2]
                        self.dve("tensor_tensor", tf[:, :ntok], cm[:, 0:ntok], cm[:, 1:1 + ntok], ALU.subtract)
                        self.dve("scalar_tensor_tensor", dests[m], tf[:, :ntok], c2(32 + m), cm[:, 1:1 + ntok], ALU.mult, ALU.add)
                    dense_fm(w_in, 8, O_C, 1024, xb, ntok, epi_c)
                    if cx.is_last:
                        self.dma("sp", V(cx.cshift_out.t[l].rearrange("(c p) -> p c", p=128), ()), cx.ccarry[l].v(), final=True,
                                 allow_slow_non_contiguous=True)
                    self.act("activation", sq[0][:, :ntok], c_6[:, :ntok], AF.Tanh)
                    self.act("copy", sq[1][:, :ntok], c_6[:, :ntok])
                    for m in range(2):
                        self.pe("matmul", ps[0][:, :ntok], cw3[:, l, 0, m * 128:(m + 1) * 128], sq[0][:, :ntok], start=True, stop=True)
                        self.act("activation", c_wl[:, m, :ntok], ps[0][:, :ntok], AF.Sigmoid, bias=c2(40 + m), scale=1.0)
                        self.dve("tensor_scalar_mul", c_wl[:, m, :ntok], c_wl[:, m, :ntok], -math.exp(-0.5))
                        self.pe("matmul", ps[1][:, :ntok], cw3[:, l, 1, m * 128:(m + 1) * 128], sq[1][:, :ntok], start=True, stop=True)
                        self.act("activation", c_a[:, m, :ntok], ps[1][:, :ntok], AF.Sigmoid, bias=c2(42 + m), scale=1.0)
                    self.act("activation", sq[0][:, :ntok], c_7[:, :ntok], AF.Sigmoid)
                    for m in range(2):
                        self.pe("matmul", ps[0][:, :ntok], cw3[:, l, 2, m * 128:(m + 1) * 128], sq[0][:, :ntok], start=True, stop=True)
                        self.act("copy", c_g[:, m, :ntok], ps[0][:, :ntok])
                    for m in range(2):
                        tf = tmpf[m % 2]
                        self.dve("tensor_scalar_mul", c_kk[:, m, :ntok], c_k[:, m, :ntok], c2(44 + m))
                        self.act("activation", tf[:, :ntok], c_kk[:, m, :ntok], AF.Square)
                        self.pe("matmul", ps[2][:, :ntok], cSame, tf[:, :ntok], start=True, stop=True)
                        self.act("activation", tf[:, :ntok], ps[2][:, :ntok], AF.Ln, bias=colf(0), scale=1.0)
                        self.act("activation", tf[:, :ntok], tf[:, :ntok], AF.Exp, scale=-0.5)
                        self.dve("tensor_tensor", c_kk[:, m, :ntok], c_kk[:, m, :ntok], tf[:, :ntok], ALU.mult)
                        self.dve("tensor_scalar", tf[:, :ntok], c_a[:, m, :ntok], c2(46 + m), c2(54 + m), ALU.mult, ALU.add)
                        self.dve("tensor_tensor", c_km[:, m, :ntok], c_k[:, m, :ntok], tf[:, :ntok], ALU.mult)
                        self.dve("tensor_tensor", c_b[:, m, :ntok], c_a[:, m, :ntok], c_kk[:, m, :ntok], ALU.mult)
                        self.dve("scalar_tensor_tensor", tf[:, :ntok], c_r[:, m, :ntok], c2(48 + m), c_km[:, m, :ntok], ALU.mult, ALU.mult)
                        self.pe("matmul", ps[2][:, :ntok], cSame, tf[:, :ntok], start=True, stop=True)
                        self.dve("tensor_tensor", c_bon[:, m, :ntok], ps[2][:, :ntok], c_v[:, m, :ntok], ALU.mult)

                    for blk in range(nbk):
                        bt = min(128, ntok - blk * 128)
                        c0 = blk * 128
                        chunks = [(r0, min(r0 + cx.CH, bt)) for r0 in range(0, bt, cx.CH)]
                        for i, src in enumerate((c_wl, c_km, c_b, c_kk, c_v)):
                            pst = ps[3] if i % 2 == 0 else ps[4]
                            for m in range(2):
                                self.pe("transpose", pst[:bt, m * 128:(m + 1) * 128], src[:, m, c0:c0 + bt], ident)
                            self.act("copy", tm5[:bt, i, :], pst[:bt, 0:256])
                        wl_tm, km_tm, b_tm, kk_tm, v_tm = (tm5[:bt, i, :] for i in range(5))
                        self.pe("matmul", ps[0][:bt, 0:256], cU[:bt, :bt], wl_tm, start=True, stop=True)
                        for m in range(2):
                            self.pe("matmul", ps[1][:, m * 128:m * 128 + bt], tm5[:bt, 0, m * 128:(m + 1) * 128], cU[:bt, :bt],
                                    start=True, stop=True)
                        eng = cgt[:bt, 0, :]
                        egm = cgt[:bt, 1, :]
                        self.act("activation", eng, ps[0][:bt, 0:256], AF.Exp, scale=-1.0)
                        self.act("activation", egm, ps[0][:bt, 0:256], AF.Exp)
                        self.act("activation", cx0[:bt, :], wl_tm, AF.Exp, scale=-1.0)
                        self.dve("tensor_tensor", egm, egm, cx0[:bt, :], ALU.mult)
                        for hh in range(4):
                            e = hh % 2
                            hs = slice(hh * 64, (hh + 1) * 64)
                            self.dve("tensor_tensor", kapz[:bt, hh, e * 64:(e + 1) * 64], kk_tm[:, hs], egm[:, hs], ALU.mult)
                            for ci, (r0, r1) in enumerate(chunks):
                                self.dve("tensor_tensor", cktz[r0:r1, ci, hh, e * 64:(e + 1) * 64], km_tm[r0:r1, hs], eng[r0:r1, hs], ALU.mult)
                                self.dve("tensor_tensor", cbtz[r0:r1, ci, hh, e * 64:(e + 1) * 64], b_tm[r0:r1, hs], eng[r0:r1, hs], ALU.mult)
                        pgT = ps[1][:, 0:256].r("p (m t) -> p m t", m=2)[:, :, :bt]
                        egT = C1T[:, 0:2, :bt]
                        engT = C1T[:, 2:4, :bt]
                        egmT = C2T[:, 0:2, :bt]
                        self.act("activation", egT, pgT, AF.Exp)
                        self.act("activation", engT, pgT, AF.Exp, scale=-1.0)
                        self.dve("tensor_tensor", egmT, pgT, c_wl[:, :, c0:c0 + bt], ALU.subtract)
                        self.act("activation", egmT, egmT, AF.Exp)
                        self.dve("tensor_tensor", rTt[:, :, :bt], c_r[:, :, c0:c0 + bt], egT, ALU.mult)
                        self.act("copy", rTb[:, :, :bt], rTt[:, :, :bt])
                        self.dve("tensor_tensor", kapTb[:, :, :bt], c_kk[:, :, c0:c0 + bt], egmT, ALU.mult)
                        for ci, (r0, r1) in enumerate(chunks):
                            self.act("copy", pcc[:, ci, :], egT[:, :, r1 - 1])
                        for hh in range(4):
                            e, pr = hh % 2, hh // 2
                            self.dve("scalar_tensor_tensor", kTm[:, hh, :bt], c_km[:, pr, c0:c0 + bt], cSame[:, e * 64:e * 64 + 1],
                                     engT[:, pr, :], ALU.mult, ALU.mult)
                            self.dve("scalar_tensor_tensor", bTm[:, hh, :bt], c_b[:, pr, c0:c0 + bt], cSame[:, e * 64:e * 64 + 1],
                                     engT[:, pr, :], ALU.mult, ALU.mult)
                        def hview(pt):
                            return pt[:bt, :].r("p (h s) -> p h s", h=4)[:, :, :bt]
                        pB, pBm, pC1, pC2 = hview(ps[0]), hview(ps[3]), hview(ps[4]), hview(ps[5])
                        for hh in range(4):
                            pr = hh // 2
                            self.pe("matmul", pB[:, hh, :], bTm[:, hh, :bt], kapTb[:, pr, :bt], start=True, stop=True)
                        for hh in range(4):
                            pr = hh // 2
                            self.pe("matmul", pBm[:, hh, :], kTm[:, hh, :bt], kapTb[:, pr, :bt], start=True, stop=True)
                        for hh in range(4):
                            pr = hh // 2
                            self.pe("matmul", pC1[:, hh, :], kTm[:, hh, :bt], rTb[:, pr, :bt], start=True, stop=True)
                        for hh in range(4):
                            pr = hh // 2
                            self.pe("matmul", pC2[:, hh, :], bTm[:, hh, :bt], rTb[:, pr, :bt], start=True, stop=True)
                        msu = cSU[:bt, :bt].unsq(1).bc([bt, 4, bt])
                        mu_ = cU[:bt, :bt].unsq(1).bc([bt, 4, bt])
                        X, XT, R = cXb, cXTb, cR2
                        self.dve("tensor_tensor", X[:bt, :, :bt], pB, msu, ALU.mult)
                        self.dve("tensor_tensor", cBmT[:bt, :, :bt], pBm, msu, ALU.mult)
                        self.dve("tensor_tensor", C1T[:bt, :, :bt], pC1, mu_, ALU.mult)
                        self.dve("scalar_tensor_tensor", C2T[:bt, :, :bt], pC2, -1.0, mu_, ALU.mult, ALU.mult)
                        pA = psbT[:bt, 0:512].r("p (h s) -> p h s", h=4)[:, :, :bt]
                        for hh in range(4):
                            self.pe("transpose", pA[:, hh, :], X[:bt, hh, :bt], identb[:bt, :bt])
                        self.act("copy", XT[:bt, :, :bt], pA)
                        self.dve("tensor_tensor", R[:bt, :, :bt], ident[:bt, :bt].unsq(1).bc([bt, 4, bt]), X[:bt, :, :bt], ALU.subtract)
                        neumann(bt, X, XT, R, cx.CH)
                        for hh in range(4):
                            hs = slice(hh * 64, (hh + 1) * 64)
                            self.pe("matmul", ps[4][:bt, hs], cBmT[:bt, hh, :bt], v_tm[:, hs], start=True, stop=True)
                        self.act("copy", cx0b[:bt, :], ps[4][:bt, 0:256])
                        for hh in range(4):
                            hs = slice(hh * 64, (hh + 1) * 64)
                            self.pe("matmul", ps[4][:bt, 256 + hh * 64:256 + (hh + 1) * 64], R[:bt, hh, :bt], cx0b[:bt, hs], start=True, stop=True)
                        self.act("copy", cU0[:bt, :], ps[4][:bt, 256:512])
                        for pr in range(2):
                            for e in range(2):
                                self.pe("matmul", ps[5][:, pr * 128:pr * 128 + bt], kapz[:bt, 2 * pr + e, :], R[:bt, 2 * pr + e, :bt],
                                        start=(e == 0), stop=(e == 1))
                        self.act("copy", wT[:, :, :bt], ps[5][:, 0:256].r("p (a t) -> p a t", a=2)[:, :, :bt])
                        for ci in range(len(chunks)):
                            for pr in range(2):
                                for e in range(2):
                                    hh = 2 * pr + e
                                    self.pe("matmul", ps[5][:, 256 + ci * 128 + pr * 64:256 + ci * 128 + (pr + 1) * 64],
                                            cktz[:bt, ci, hh, :], v_tm[:, hh * 64:(hh + 1) * 64], start=(e == 0), stop=(e == 1))
                        self.act("copy", kvp[:, 0:len(chunks), :, :], ps[5][:, 256:256 + 128 * len(chunks)].r("p (c a d) -> p c a d", c=len(chunks), a=2))
                        for ci, (r0, r1) in enumerate(chunks):
                            for pr in range(2):
                                self.pe("matmul", ps[6][:bt, pr * 128:(pr + 1) * 128], wT[:, pr, :bt], cx.Tst[l][:, pr, :], start=True, stop=True)
                            for pr in range(2):
                                self.pe("matmul", ps[6][:bt, 256 + pr * 128:256 + (pr + 1) * 128], rTt[:, pr, :bt], cx.Tst[l][:, pr, :],
                                        start=True, stop=True)
                            self.dve("tensor_tensor", cU_[r0:r1, :], cU0[r0:r1, :], ps[6][r0:r1, 0:256], ALU.add)
                            self.dve("tensor_copy", cyq[r0:r1, :], ps[6][r0:r1, 256:512])
                            for e in range(2):
                                tdv = cx.Tst[l][e * 64:(e + 1) * 64, :, e * 64:(e + 1) * 64]
                                self.dve("tensor_tensor", tdv, tdv, kvp[e * 64:(e + 1) * 64, ci, :, :], ALU.add)
                            for pr in range(2):
                                for e in range(2):
                                    hh = 2 * pr + e
                                    self.pe("matmul", ps[2][:, 64 + pr * 64:128 + pr * 64], cbtz[:bt, ci, hh, :],
                                            cU_[:bt, hh * 64:(hh + 1) * 64], start=(e == 0), stop=(e == 1))
                            for e in range(2):
                                tdv = cx.Tst[l][e * 64:(e + 1) * 64, :, e * 64:(e + 1) * 64]
                                self.dve("tensor_tensor", tdv, tdv, ps[2][e * 64:(e + 1) * 64, 64:192].r("p (a d) -> p a d", a=2), ALU.subtract)
                                self.dve("tensor_tensor", tdv, tdv, pcc[e * 64:(e + 1) * 64, ci, :].unsq(2).bc([64, 2, 64]), ALU.mult)
                        for hh in range(4):
                            hs = slice(hh * 64, (hh + 1) * 64)
                            self.pe("matmul", ps[4][:bt, hs], C1T[:bt, hh, :bt], v_tm[:, hs], start=True, stop=False)
                            self.pe("matmul", ps[4][:bt, hs], C2T[:bt, hh, :bt], cU_[:bt, hs], start=False, stop=True)
                        self.dve("tensor_tensor", cytm[:bt, :], cyq[:bt, :], ps[4][:bt, 0:256], ALU.add)
                        y3 = cytm[:bt, :].r("p (h d) -> p h d", h=4)
                        self.dve("tensor_reduce", cst[:bt, 0:4], y3, AX.X, ALU.add)
                        self.dve("tensor_scalar_mul", cst[:bt, 0:4], cst[:bt, 0:4], 1.0 / 64)
                        self.dve("tensor_tensor", y3, y3, cst[:bt, 0:4].unsq(2).bc([bt, 4, 64]), ALU.subtract)
                        self.dve("tensor_tensor", cyq[:bt, :], cytm[:bt, :], cytm[:bt, :], ALU.mult)
                        self.dve("tensor_reduce", cst[:bt, 4:8], cyq[:bt, :].r("p (h d) -> p h d", h=4), AX.X, ALU.add)
                        self.act("activation", cst[:bt, 4:8], cst[:bt, 4:8], AF.Ln, bias=colf(9)[:bt, :], scale=1.0 / 64)
                        self.act("activation", cst[:bt, 4:8], cst[:bt, 4:8], AF.Exp, scale=-0.5)
                        self.dve("tensor_tensor", cyn(blk)[:bt, :].r("p (h d) -> p h d", h=4), y3,
                                 cst[:bt, 4:8].unsq(2).bc([bt, 4, 64]), ALU.mult)
                    for pr in range(2):
                        for blk in range(nbk):
                            bt = min(128, ntok - blk * 128)
                            self.pe("transpose", ps[3][:, blk * 128:blk * 128 + bt], cyn(blk)[:bt, pr * 128:(pr + 1) * 128], ident[:bt, :bt])
                        tf = tmpf[pr]
                        self.dve("scalar_tensor_tensor", tf[:, :ntok], ps[3][:, :ntok], c2(50 + pr), c_bon[:, pr, :ntok], ALU.mult, ALU.add)
                        self.dve("scalar_tensor_tensor", mixT[:, 6 + pr, :ntok], tf[:, :ntok], c2(52 + pr), c_g[:, pr, :ntok], ALU.add, ALU.mult)
                    if cx.is_last:
                        for pr in range(2):
                            self.pe("transpose", ps[3][:, pr * 128:(pr + 1) * 128], cx.Tst[l][:, pr, :], ident)
                        self.act("copy", wT.v(), ps[3][:, 0:256].r("p (a t) -> p a t", a=2))
                        for e in range(2):
                            self.dma("sp", V(cx.cwkv_out.t[l].rearrange("(pr e) v k -> e v pr k", e=2)[e], ()),
                                     wT[e * 64:(e + 1) * 64, :, e * 64:(e + 1) * 64], final=True)

                def epi_res(m, pv):
                    self.dve("tensor_tensor", h[:, m, :ntok], h[:, m, :ntok], pv, ALU.add)
                dense_fm(("w_out", l), 8, 0, D_MODEL, mixT, ntok, epi_res)

                rmsnorm_to_xb(ntok, lambda k: col("norm_ffn", l, k))

                def epi_ff1(m, pv):
                    tf = tmpf[m % 2]
                    self.act("activation", tf[:, :ntok], pv, AF.Relu)
                    self.dve("tensor_tensor", uT[:, m, :ntok], tf[:, :ntok], tf[:, :ntok], ALU.mult)
                dense_fm(("w_ff1", l), 8, 0, D_FF, xb, ntok, epi_ff1)
                dense_fm(("w_ff2", l), 32, 0, D_MODEL, uT, ntok, epi_res)

                rmsnorm_to_xb(ntok, lambda k: col("norm_ple", l, k))

                def epi_gate(m, pv):
                    self.act("activation", gate[:, m, :ntok], pv, AF.Sigmoid)
                dense_fm(("w_ple_gate", l), 8, 0, D_MODEL, xb, ntok, epi_gate)
                self.dma("sp", ptok[:bl, :nbk, :], V(cx.p_src(l).rearrange("(b p) d -> p b d", p=bl), ()))
                for k in range(2):
                    for b in range(nbk):
                        self.pe("transpose", ps[3][:, b * 128:b * 128 + bl], ptok[:bl, b, k * 128:(k + 1) * 128], ident[:bl, :bl])
                    self.act("copy", pT[:, k, :ntok], ps[3][:, :ntok])

                def epi_ple(m, pv):
                    self.dve("tensor_tensor", tmpf[m % 2][:, :ntok], gate[:, m, :ntok], pv, ALU.mult)
                    self.dve("tensor_tensor", h[:, m, :ntok], h[:, m, :ntok], tmpf[m % 2][:, :ntok], ALU.add)
                dense_fm(("w_ple_proj", l), 2, 0, D_MODEL, pT, ntok, epi_ple)

            for k in range(8):
                s_ = sq[k % 2]
                self.act("activation", s_[:, :ntok], h[:, k, :ntok], AF.Square)
                self.pe("matmul", ps[2][:, :ntok], onesb.v(), s_[:, :ntok], start=(k == 0), stop=(k == 7))
            self.act("activation", rstd[:, :ntok], ps[2][:, :ntok], AF.Ln, bias=colf(0), scale=1.0 / D_MODEL)
            self.act("activation", rstd[:, :ntok], rstd[:, :ntok], AF.Exp, scale=-0.5)
            for k in range(8):
                tf = tmpf[k % 2]
                self.dve("scalar_tensor_tensor", tf[:, :ntok], h[:, k, :ntok], cols[:, 51 + k:52 + k], rstd[:, :ntok], ALU.mult, ALU.mult)
                for b in range(nbk):
                    self.pe("transpose", ps[3][:bl, b * 128:(b + 1) * 128], tf[:, b * 128:b * 128 + bl], ident)
                self.act("copy", xtok[:bl, :nbk, k * 128:(k + 1) * 128], ps[3][:bl, :nbk * 128].r("p (b f) -> p b f", b=nbk))
            self.dma("sp", V(cx.y_dst.rearrange("(b p) d -> p b d", p=bl), ()), xtok[:bl, :nbk, :], final=True)

        class Cx:
            pass

        if "prompt" in self.stages:
            for t in range(NT):
                cx = Cx()
                t0 = t * TT
                cx.ntok = TT; cx.CH = min(64, SEQ); cx.key_base = t0; cx.masked = True; cx.is_last = (t == NT - 1)
                cx.x_src = x_in.t[t0:t0 + TT, :]
                cx.rope_src = rope_in.t[t0:t0 + TT, :]
                cx.p_src = lambda l, t0=t0: p_in.t[l, t0:t0 + TT, :]
                cx.y_dst = y_out.t[t0:t0 + TT, :]
                cx.bk_dst = lambda l, t0=t0: bk_out.t[l, t0:t0 + TT, :]
                cx.bv_dst = lambda l, t0=t0: bv_out.t[l, t0:t0 + TT, :]
                cx.kT_scr, cx.v_scr = kT_scr, v_scr
                cx.acarry, cx.Sd, cx.Tst, cx.ccarry = acarry, Sd, Tst, ccarry
                cx.aconv_out, cx.adelta_out, cx.cshift_out, cx.cwkv_out = aconv_out, adelta_out, cshift_out, cwkv_out
                tile_body(cx)

        if "sample" in self.stages:
            DEC, PAST = self.DEC, self.PAST
            xs_in = self.din("xs", [DEC, D_MODEL])
            ps_in = self.din("psm", [DEPTH, DEC, PLE_DIM])
            ropes_in = self.din("rope_s", [DEC, 64])
            ck_in = self.din("cache_k", [DEPTH, PAST, 512])
            cv_in = self.din("cache_v", [DEPTH, PAST, 512])
            sconv_in = self.din("st_conv", [DEPTH, 3, 768])
            sdelta_in = self.din("st_delta", [DEPTH, 4, 64, 64])
            sshift_in = self.din("st_shift", [DEPTH, 1024])
            swkv_in = self.din("st_wkv", [DEPTH, 4, 64, 64])
            ys_out = self.dout("y_s", [DEC, D_MODEL])
            bks_out = self.dout("b_k_s", [DEPTH, DEC, 512])
            bvs_out = self.dout("b_v_s", [DEPTH, DEC, 512])
            aconvs_out = self.dout("a_conv_s", [DEPTH, 3, 768])
            adeltas_out = self.dout("a_delta_s", [DEPTH, 4, 64, 64])
            cshifts_out = self.dout("c_shift_s", [DEPTH, 1024])
            cwkvs_out = self.dout("c_wkv_s", [DEPTH, 4, 64, 64])
            kT_scr_s = [self.dscr(f"kT_scr_s{l}", [512, PAST + DEC], BF16) for l in range(DEPTH)]
            v_scr_s = [self.dscr(f"v_scr_s{l}", [PAST + DEC, 512], BF16) for l in range(DEPTH)]
            psb = psbT.v()
            for l in range(DEPTH):
                for m in range(6):
                    self.dma("sp", acarry_s[l][:, m, :], V(sconv_in.t[l, :, m * 128:(m + 1) * 128].rearrange("j p -> p j"), ()),
                             allow_slow_non_contiguous=True)
                self.dma("sp", ccarry_s[l].v(), V(sshift_in.t[l].rearrange("(c p) -> p c", p=128), ()), allow_slow_non_contiguous=True)
                self.dve("memset", Sd_s[l].v(), 0.0)
                self.dve("memset", wT.v(), 0.0)
                for e in range(2):
                    self.dma("sp", Sd_s[l][e * 64:(e + 1) * 64, :, e * 64:(e + 1) * 64],
                             V(sdelta_in.t[l].rearrange("(pr e) k v -> e k pr v", e=2)[e], ()))
                    self.dma("sp", wT[e * 64:(e + 1) * 64, :, e * 64:(e + 1) * 64],
                             V(swkv_in.t[l].rearrange("(pr e) v k -> e v pr k", e=2)[e], ()))
                for pr in range(2):
                    self.pe("transpose", ps[3][:, pr * 128:(pr + 1) * 128], wT[:, pr, :], ident)
                self.act("copy", Tst_s[l].v(), ps[3][:, 0:256].r("p (a t) -> p a t", a=2))
                for g in range(PAST // 512):
                    for b in range(4):
                        r0 = g * 512 + b * 128
                        kf, vf = kfs[b % 2], vfs[b % 2]
                        self.dma("sp", kf.v(), V(ck_in.t[l, r0:r0 + 128, :], ()))
                        self.act("copy", ktm[:, b, :], kf.v())
                        self.dma("sp", vf.v(), V(cv_in.t[l, r0:r0 + 128, :], ()))
                        self.dve("tensor_copy", vtb[:, b, :], vf.v())
                    for c in range(4):
                        for b in range(4):
                            self.pe("transpose", psb[:, 512 + b * 128:512 + (b + 1) * 128], ktm[:, b, c * 128:(c + 1) * 128], identb.v())
                        self.dve("tensor_copy", kTt[:, c, :], psb[:, 512:1024])
                    self.dma("sp", V(kT_scr_s[l].t[:, g * 512:(g + 1) * 512].rearrange("(c p) s -> p c s", p=128), (kT_scr_s[l].buf,)), kTt.v())
                    self.dma("sp", V(v_scr_s[l].t[g * 512:(g + 1) * 512, :].rearrange("(b p) d -> p b d", p=128), (v_scr_s[l].buf,)), vtb.v())
            cx = Cx()
            cx.ntok = DEC; cx.CH = min(64, DEC); cx.key_base = PAST; cx.masked = False; cx.is_last = True
            cx.x_src = xs_in.t[:, :]
            cx.rope_src = ropes_in.t[:, :]
            cx.p_src = lambda l: ps_in.t[l, :, :]
            cx.y_dst = ys_out.t[:, :]
            cx.bk_dst = lambda l: bks_out.t[l, :, :]
            cx.bv_dst = lambda l: bvs_out.t[l, :, :]
            cx.kT_scr, cx.v_scr = kT_scr_s, v_scr_s
            cx.acarry, cx.Sd, cx.Tst, cx.ccarry = acarry_s, Sd_s, Tst_s, ccarry_s
            cx.aconv_out, cx.adelta_out, cx.cshift_out, cx.cwkv_out = aconvs_out, adeltas_out, cshifts_out, cwkvs_out
            tile_body(cx)

        self.S.emit(st)
        st.close()
        return nc


def _consts():
    c = np.zeros((128, 11, 128), np.float32)
    i = np.arange(128)
    same = (i[:, None] // 64) == (i[None, :] // 64)
    c[:, 0, :] = np.eye(128)
    c[:, 1, :] = 1.0
    c[:, 2, :] = (i[:, None] <= i[None, :]) & same
    c[:, 3, :] = (i[:, None] < i[None, :]) & same
    c[:, 4, :] = (i[:, None] > i[None, :]) & same
    c[:, 5, :] = (i[:, None] >= i[None, :]) & same
    c[:, 6, :] = same
    for ci in range(2):
        for e in range(2):
            c[:, 7 + 2 * ci + e, :] = ((i[:, None] // 64) == ci) & ((i[None, :] // 64) == e)
    return c


def _cols2(inp):
    c = np.zeros((128, DEPTH, 64), np.float32)
    for l in range(DEPTH):
        cw = inp["a_conv_w"][l]
        for m in range(6):
            c[:, l, m * 4:m * 4 + 4] = cw[:, m * 128:(m + 1) * 128].T
        c[:, l, 24] = np.tile(inp["a_norm"][l], 2)
        c[:, l, 32:40] = inp["c_mu"][l].reshape(8, 128).T
        for nm, base in (("c_w0", 40), ("c_a0", 42), ("c_k_k", 44), ("c_k_a", 46), ("c_r_k", 48), ("c_ln_w", 50), ("c_ln_b", 52)):
            c[:, l, base:base + 2] = inp[nm][l].reshape(2, 128).T
    return c


def _cw3(inp):
    w = np.zeros((128, DEPTH, 3, 256), np.float32)
    for l in range(DEPTH):
        w[0:64, l, 0, :] = inp["c_w_up"][l]
        w[64:128, l, 1, :] = inp["c_a_up"][l]
        w[:, l, 2, :] = inp["c_g_up"][l]
    return w


def _rowp(inp):
    return np.concatenate([inp["a_A_log"], inp["a_dt_bias"]], axis=1).astype(np.float32)


def _amask():
    kp = np.arange(128)[:, None]
    q = np.arange(512)[None, :]
    m = np.zeros((128, 4, 512), np.float32)
    for j in range(4):
        m[:, j, :] = (2 * j + kp // 64) <= (q // 64)
    return m.astype(ml_dtypes.bfloat16)


def _rope_table(pos):
    half = 32
    inv = (10000.0 ** (-2.0 * np.arange(half, dtype=np.float32) / 64)).astype(np.float32)
    ang = pos.astype(np.float32)[:, None] * inv[None, :]
    return np.concatenate([np.cos(ang), np.sin(ang)], axis=1).astype(np.float32)


def _cols(inp):
    c = np.zeros((128, 64), np.float32)
    for l in range(DEPTH):
        for nm, base in (("norm_mix", 0), ("norm_ffn", 8), ("norm_ple", 16)):
            c[:, l * 24 + base:l * 24 + base + 8] = inp[nm][l].reshape(8, 128).T
        lam_init = 0.8 - 0.6 * math.exp(-0.3 * l)
        c[:, 48 + l] = inp["b_norm"][l]
    c[:, 50] = NORM_EPS
    c[:, 59] = C_LN_EPS
    c[:, 51:59] = inp["norm_final"].reshape(8, 128).T
    return c


_CACHE = {}


def run(inputs, seq, n_cores, stages=("prompt", "A", "C", "sample"), trace=False):
    key = (seq, stages)
    if key not in _CACHE:
        b = Builder(seq, stages=stages)
        b.build()
        _CACHE[key] = b
    b = _CACHE[key]
    cols = _cols(inputs)
    consts = _consts()
    amask = _amask()
    rope = _rope_table(np.arange(seq))
    lamrow = np.stack([np.stack([inputs[n][l] for n in ("b_lam_q1", "b_lam_k1", "b_lam_q2", "b_lam_k2")])
                       for l in range(DEPTH)]).astype(np.float32)
    shared = {"cols": cols, "consts": consts, "amask": amask, "rope": rope, "lamrow": lamrow,
              "cols2": _cols2(inputs), "rowp": _rowp(inputs), "cw3": _cw3(inputs)}
    for nm in ("w_in", "w_out", "w_ff1", "w_ff2", "w_ple_gate", "w_ple_proj"):
        shared[nm] = inputs[nm]
    if "sample" in stages:
        past = inputs["cache_b_k"].shape[2]
        dec = inputs["x_sample"].shape[1]
        shared["rope_s"] = _rope_table(np.arange(past, past + dec))
    in_maps = []
    for c in range(n_cores):
        m = dict(shared)
        m["x"] = np.ascontiguousarray(inputs["x_prompt"][c])
        m["p"] = np.ascontiguousarray(inputs["p_prompt"][:, c])
        if "sample" in stages:
            m["xs"] = np.ascontiguousarray(inputs["x_sample"][c])
            m["psm"] = np.ascontiguousarray(inputs["p_sample"][:, c])
            m["cache_k"] = np.ascontiguousarray(inputs["cache_b_k"][:, c]).reshape(DEPTH, past, 512)
            m["cache_v"] = np.ascontiguousarray(inputs["cache_b_v"][:, c]).reshape(DEPTH, past, 512)
            m["st_conv"] = np.ascontiguousarray(inputs["state_a_conv"][:, c])
            m["st_delta"] = np.ascontiguousarray(inputs["state_a_delta"][:, c])
            m["st_shift"] = np.ascontiguousarray(inputs["state_c_shift"][:, c])
            m["st_wkv"] = np.ascontiguousarray(inputs["state_c_wkv"][:, c])
        in_maps.append(m)
    res = run_bass_kernel_spmd(b.nc, in_maps, core_ids=list(range(n_cores)), trace=trace)
    return res


def kernel(**inputs):
    inputs = {k: np.asarray(v) for k, v in inputs.items()}
    n, seq = inputs["x_prompt"].shape[0], inputs["x_prompt"].shape[1]
    dec = inputs["x_sample"].shape[1]
    r = run(inputs, seq, n).results

    def st(name, axis):
        return np.stack([np.asarray(r[c][name]) for c in range(n)], axis=axis)
    return (st("y", 0), st("y_s", 0),
            st("a_conv", 1), st("a_delta", 1),
            st("b_k", 1).reshape(DEPTH, n, seq, 4, 128), st("b_v", 1).reshape(DEPTH, n, seq, 4, 128),
            st("c_shift", 1), st("c_wkv", 1),
            st("a_conv_s", 1), st("a_delta_s", 1),
            st("b_k_s", 1).reshape(DEPTH, n, dec, 4, 128), st("b_v_s", 1).reshape(DEPTH, n, dec, 4, 128),
            st("c_shift_s", 1), st("c_wkv_s", 1))
```
